# Optimizing a Trainium2 kernel written in Bass

```python
import jax, jax.numpy as jnp
from jax import lax
import numpy as np

D_MODEL = 1024
BATCH = 8
SEQ = 2048
DEPTH = 4

N_MEM = 256
N_MIXERS = 4
D_MIX = D_MODEL
GROUP = D_MIX // N_MIXERS
HEAD_DIM = 64
N_HEADS_G = GROUP // HEAD_DIM
SG_CHUNK = 128
RW_DECAY_LORA = 64
RW_AAA_LORA = 64
RW_GATE_LORA = 128
RW_LNX_EPS = 64e-5
GLA_DK = HEAD_DIM // 2
GLA_LORA = 16
GLA_TAU = 16.0
GLA_CHUNK = 64
GLA_NORM_EPS = 1e-5
FOX_BLOCK = 128
CA_HEADS = 4
CA_HEAD_DIM = D_MODEL // CA_HEADS
D_FF = 2816
CONV_W = 3
DN_ALPHA = (2.0 * DEPTH) ** 0.25
DN_BETA = (8.0 * DEPTH) ** -0.25
LN_EPS = 1e-5

SG_COLS = (GROUP, GROUP)
RW_COLS = (GROUP, GROUP, GROUP, RW_DECAY_LORA, RW_AAA_LORA, RW_GATE_LORA)
GLA_COLS = (N_HEADS_G * GLA_DK, N_HEADS_G * GLA_DK, GROUP, GROUP, GLA_LORA)
FOX_COLS = (GROUP, GROUP, GROUP, N_HEADS_G)
MIXER_COLS = (sum(SG_COLS), sum(RW_COLS), sum(GLA_COLS), sum(FOX_COLS))
P_IN = sum(MIXER_COLS)

kernel_name = "hybrid_sg_rwkv7_gla_fox_deepnorm_trunk"


def _split_cols(z, widths):
    outs, off = [], 0
    for w in widths:
        outs.append(z[..., off:off + w])
        off += w
    return outs


def _layernorm(x, g, b, eps=LN_EPS):
    xf = x.astype(jnp.float32)
    mu = jnp.mean(xf, axis=-1, keepdims=True)
    var = jnp.mean(jnp.square(xf - mu), axis=-1, keepdims=True)
    return (((xf - mu) * lax.rsqrt(var + eps)) * g + b).astype(x.dtype)


def _token_shift(z):
    return jnp.pad(z[:, :-1], ((0, 0), (1, 0), (0, 0)))


def _spatial_gating(u, v, ln_g, ln_b, w_s, b_s):
    B, S, _ = u.shape
    nc = S // SG_CHUNK
    v = v.reshape(B, nc, SG_CHUNK, N_HEADS_G, HEAD_DIM)
    v = _layernorm(v, ln_g.reshape(N_HEADS_G, HEAD_DIM), ln_b.reshape(N_HEADS_G, HEAD_DIM))
    causal = jnp.tril(jnp.ones((SG_CHUNK, SG_CHUNK), dtype=bool))
    w = jnp.where(causal, w_s, jnp.zeros((), w_s.dtype))
    s = jnp.einsum('hts,bnshd->bnthd', w, v) + b_s.T[None, None, :, :, None]
    return u * s.reshape(B, S, GROUP)


def _rwkv7_scan(r, decay, k, v, kk, a):
    B, S, H, N = r.shape
    xs = tuple(jnp.moveaxis(t, 1, 0) for t in (r, decay, k, v, kk, a))

    def step(state, inp):
        r_t, w_t, k_t, v_t, kk_t, a_t = inp
        sa = jnp.einsum('bhvk,bhk->bhv', state, -kk_t)
        state = (state * w_t[:, :, None, :]
                 + sa[..., None] * (kk_t * a_t)[:, :, None, :]
                 + v_t[..., None] * k_t[:, :, None, :])
        y = jnp.einsum('bhvk,bhk->bhv', state, r_t)
        return state, y

    s0 = jnp.zeros((B, H, N, N), jnp.float32)
    _, ys = lax.scan(step, s0, xs)
    return jnp.moveaxis(ys, 0, 1)


def _rwkv7(r, k, v, wd, ad, gd, w0, w2, a0, a2, g2, k_k, k_a, r_k, lnx_g, lnx_b):
    B, S, _ = r.shape
    w = -jax.nn.softplus(-(w0 + jnp.tanh(wd) @ w2)) - 0.5
    decay = jnp.exp(-jnp.exp(w.astype(jnp.float32)))
    a = jax.nn.sigmoid(a0 + ad @ a2)
    g = jax.nn.sigmoid(gd) @ g2
    heads = lambda t: t.reshape(B, S, N_HEADS_G, HEAD_DIM).astype(jnp.float32)
    kk = heads(k * k_k)
    kk = kk / jnp.maximum(jnp.sqrt(jnp.sum(kk * kk, axis=-1, keepdims=True)), 1e-12)
    k = k * (1.0 + (a - 1.0) * k_a)
    rh, kh, vh, ah, dh = heads(r), heads(k), heads(v), heads(a), heads(decay)
    y = _rwkv7_scan(rh, dh, kh, vh, kk, ah)
    mu = jnp.mean(y, axis=-1, keepdims=True)
    var = jnp.mean(jnp.square(y - mu), axis=-1, keepdims=True)
    y = (y - mu) * lax.rsqrt(var + RW_LNX_EPS)
    y = y * lnx_g.reshape(N_HEADS_G, HEAD_DIM) + lnx_b.reshape(N_HEADS_G, HEAD_DIM)
    y = y + jnp.sum(rh * kh * r_k, axis=-1, keepdims=True) * vh
    return (y.reshape(B, S, GROUP) * g).astype(r.dtype)


def _gla(q, k, v, g, ad, a_up, a_b, norm_g):
    B, S, _ = q.shape
    C = GLA_CHUNK
    nc = S // C
    H = N_HEADS_G
    shape_k = (B, nc, C, H, GLA_DK)
    lg = jax.nn.log_sigmoid((ad @ a_up + a_b).astype(jnp.float32)) / GLA_TAU
    qf = q.astype(jnp.float32).reshape(shape_k) * GLA_DK ** -0.5
    kf = k.astype(jnp.float32).reshape(shape_k)
    vf = v.astype(jnp.float32).reshape(B, nc, C, H, HEAD_DIM)
    b = jnp.cumsum(lg.reshape(shape_k), axis=2)
    b_ref = b[:, :, C // 2][:, :, None]
    b_last = b[:, :, -1]
    causal = jnp.tril(jnp.ones((C, C), dtype=bool))
    att = jnp.einsum('bnthd,bnshd->bnhts', qf * jnp.exp(b - b_ref), kf * jnp.exp(b_ref - b))
    att = jnp.where(causal, att, 0.0)
    o_intra = jnp.einsum('bnhts,bnshv->bnthv', att, vf)
    u = jnp.einsum('bnshd,bnshv->bnhdv', kf * jnp.exp(b_last[:, :, None] - b), vf)
    dec = jnp.exp(b_last)

    def step(state, inp):
        dec_n, u_n = inp
        return dec_n[..., None] * state + u_n, state

    s0 = jnp.zeros((B, H, GLA_DK, HEAD_DIM), jnp.float32)
    _, s_prev = lax.scan(step, s0, (jnp.moveaxis(dec, 1, 0), jnp.moveaxis(u, 1, 0)))
    s_prev = jnp.moveaxis(s_prev, 0, 1)
    o_inter = jnp.einsum('bnthd,bnhdv->bnthv', qf * jnp.exp(b), s_prev)
    o = (o_intra + o_inter).reshape(B, S, H, HEAD_DIM)
    o = o * lax.rsqrt(jnp.mean(o * o, axis=-1, keepdims=True) + GLA_NORM_EPS)
    o = o.reshape(B, S, GROUP) * norm_g * jax.nn.silu(g.astype(jnp.float32))
    return o.astype(q.dtype)


def _forgetting_attention(q, k, v, f_logit, f_bias):
    B, S, _ = q.shape
    H = N_HEADS_G
    heads = lambda t: jnp.moveaxis(t.reshape(B, S, H, HEAD_DIM), 2, 1).astype(jnp.float32)
    qh, kh, vh = heads(q), heads(k), heads(v)
    log_f = jax.nn.log_sigmoid((f_logit + f_bias).astype(jnp.float32))
    c = jnp.transpose(jnp.cumsum(log_f, axis=1), (0, 2, 1))
    scale = HEAD_DIM ** -0.5
    outs = []
    for i in range(S // FOX_BLOCK):
        q0 = i * FOX_BLOCK
        end = q0 + FOX_BLOCK
        logits = jnp.einsum('bhtd,bhsd->bhts', qh[:, :, q0:end], kh[:, :, :end]) * scale
        logits = logits + c[:, :, q0:end, None] - c[:, :, None, :end]
        causal = jnp.arange(end)[None, :] <= (q0 + jnp.arange(FOX_BLOCK))[:, None]
        p = jax.nn.softmax(jnp.where(causal, logits, -jnp.inf), axis=-1)
        outs.append(jnp.einsum('bhts,bhsd->bhtd', p, vh[:, :, :end]))
    o = jnp.concatenate(outs, axis=2)
    return jnp.moveaxis(o, 1, 2).reshape(B, S, GROUP).astype(q.dtype)


def _memory_attention(x, memn, wq, wk, wv, wo):
    B, S, _ = x.shape
    M = memn.shape[1]
    q = (x @ wq).reshape(B, S, CA_HEADS, CA_HEAD_DIM)
    k = (memn @ wk).reshape(B, M, CA_HEADS, CA_HEAD_DIM)
    v = (memn @ wv).reshape(B, M, CA_HEADS, CA_HEAD_DIM)
    logits = jnp.einsum('bthd,bmhd->bhtm', q, k).astype(jnp.float32) * CA_HEAD_DIM ** -0.5
    p = jax.nn.softmax(logits, axis=-1).astype(x.dtype)
    o = jnp.einsum('bhtm,bmhd->bthd', p, v).reshape(B, S, D_MODEL)
    return o @ wo


def _conv_ffn(x, w_up, b_up, w_conv, b_conv, w_down):
    h = x @ w_up + b_up
    h = lax.conv_general_dilated(h, w_conv[:, None, :], window_strides=(1,),
                                 padding=[(CONV_W - 1, 0)],
                                 dimension_numbers=('NWC', 'WIO', 'NWC'),
                                 feature_group_count=h.shape[-1]) + b_conv
    gate, val = jnp.split(h, 2, axis=-1)
    return (jax.nn.gelu(gate, approximate=False) * val) @ w_down


def setup_inputs(seed: int = 0) -> dict:
    key = jax.random.key(seed)
    ks = iter(jax.random.split(key, 48))
    L, D, H = DEPTH, D_MODEL, N_HEADS_G
    nrm = lambda shape, s: s * jax.random.normal(next(ks), shape, jnp.float32)
    uni = lambda shape, lo, hi: jax.random.uniform(next(ks), shape, jnp.float32, lo, hi)
    gain = lambda shape: 1.0 + nrm(shape, 0.02)
    return {
        "x": nrm((BATCH, SEQ, D), 1.0),
        "mem": nrm((BATCH, N_MEM, D), 1.0),
        "mem_ln_g": gain((D,)),
        "mem_ln_b": nrm((D,), 0.02),
        "w_in": nrm((L, D, P_IN), D ** -0.5),
        "w_out": nrm((L, D_MIX, D), D_MIX ** -0.5 * DN_BETA),
        "sg_ln_g": gain((L, GROUP)),
        "sg_ln_b": nrm((L, GROUP), 0.02),
        "sg_w": nrm((L, H, SG_CHUNK, SG_CHUNK), 0.5 * SG_CHUNK ** -0.5),
        "sg_b": 1.0 + nrm((L, H, SG_CHUNK), 0.02),
        "rw_mu": uni((L, sum(RW_COLS)), 0.0, 1.0),
        "rw_w0": uni((L, GROUP), -6.0, 1.0),
        "rw_w2": nrm((L, RW_DECAY_LORA, GROUP), 0.1 * RW_DECAY_LORA ** -0.5),
        "rw_a0": nrm((L, GROUP), 0.1),
        "rw_a2": nrm((L, RW_AAA_LORA, GROUP), 0.1 * RW_AAA_LORA ** -0.5),
        "rw_g2": nrm((L, RW_GATE_LORA, GROUP), RW_GATE_LORA ** -0.5),
        "rw_kk": 0.85 + nrm((L, GROUP), 0.02),
        "rw_ka": gain((L, GROUP)),
        "rw_rk": nrm((L, H, HEAD_DIM), 0.1),
        "rw_lnx_g": gain((L, GROUP)),
        "rw_lnx_b": nrm((L, GROUP), 0.02),
        "gla_a_up": nrm((L, GLA_LORA, H * GLA_DK), GLA_LORA ** -0.5),
        "gla_a_b": nrm((L, H * GLA_DK), 0.1),
        "gla_norm_g": gain((L, GROUP)),
        "fox_fb": uni((L, H), 1.0, 4.0),
        "ln1_g": gain((L, D)),
        "ln1_b": nrm((L, D), 0.02),
        "ca_wq": nrm((L, D, D), D ** -0.5),
        "ca_wk": nrm((L, D, D), D ** -0.5),
        "ca_wv": nrm((L, D, D), D ** -0.5),
        "ca_wo": nrm((L, D, D), D ** -0.5 * DN_BETA),
        "ln2_g": gain((L, D)),
        "ln2_b": nrm((L, D), 0.02),
        "ffn_up": nrm((L, D, 2 * D_FF), D ** -0.5),
        "ffn_up_b": nrm((L, 2 * D_FF), 0.02),
        "ffn_conv": nrm((L, CONV_W, 2 * D_FF), CONV_W ** -0.5),
        "ffn_conv_b": nrm((L, 2 * D_FF), 0.02),
        "ffn_down": nrm((L, D_FF, D), D_FF ** -0.5 * DN_BETA),
        "ln3_g": gain((L, D)),
        "ln3_b": nrm((L, D), 0.02),
    }


def reference(x, mem, mem_ln_g, mem_ln_b, w_in, w_out, sg_ln_g, sg_ln_b, sg_w, sg_b,
              rw_mu, rw_w0, rw_w2, rw_a0, rw_a2, rw_g2, rw_kk, rw_ka, rw_rk, rw_lnx_g, rw_lnx_b,
              gla_a_up, gla_a_b, gla_norm_g, fox_fb, ln1_g, ln1_b,
              ca_wq, ca_wk, ca_wv, ca_wo, ln2_g, ln2_b,
              ffn_up, ffn_up_b, ffn_conv, ffn_conv_b, ffn_down, ln3_g, ln3_b):
    memn = _layernorm(mem, mem_ln_g, mem_ln_b)
    for l in range(DEPTH):
        z = x @ w_in[l]
        z_sg, z_rw, z_gla, z_fox = _split_cols(z, MIXER_COLS)
        sg_u, sg_v = _split_cols(z_sg, SG_COLS)
        y_a = _spatial_gating(sg_u, sg_v, sg_ln_g[l], sg_ln_b[l], sg_w[l], sg_b[l])
        z_rw = z_rw + rw_mu[l] * (_token_shift(z_rw) - z_rw)
        rw_r, rw_k, rw_v, rw_wd, rw_ad, rw_gd = _split_cols(z_rw, RW_COLS)
        y_b = _rwkv7(rw_r, rw_k, rw_v, rw_wd, rw_ad, rw_gd, rw_w0[l], rw_w2[l], rw_a0[l], rw_a2[l],
                     rw_g2[l], rw_kk[l], rw_ka[l], rw_rk[l], rw_lnx_g[l], rw_lnx_b[l])
        gl_q, gl_k, gl_v, gl_g, gl_ad = _split_cols(z_gla, GLA_COLS)
        y_c = _gla(gl_q, gl_k, gl_v, gl_g, gl_ad, gla_a_up[l], gla_a_b[l], gla_norm_g[l])
        fx_q, fx_k, fx_v, fx_f = _split_cols(z_fox, FOX_COLS)
        y_d = _forgetting_attention(fx_q, fx_k, fx_v, fx_f, fox_fb[l])
        mix = jnp.concatenate([y_a, y_b, y_c, y_d], axis=-1) @ w_out[l]
        x = _layernorm(DN_ALPHA * x + mix, ln1_g[l], ln1_b[l])
        ca = _memory_attention(x, memn, ca_wq[l], ca_wk[l], ca_wv[l], ca_wo[l])
        x = _layernorm(DN_ALPHA * x + ca, ln2_g[l], ln2_b[l])
        ff = _conv_ffn(x, ffn_up[l], ffn_up_b[l], ffn_conv[l], ffn_conv_b[l], ffn_down[l])
        x = _layernorm(DN_ALPHA * x + ff, ln3_g[l], ln3_b[l])
    return x
```

```python
from contextlib import ExitStack
from concourse.bass_utils import run_bass_kernel_spmd
import numpy as np
import concourse.bass as bass
import concourse.mybir as mybir

F32 = mybir.dt.float32
BF16 = mybir.dt.bfloat16
AF = mybir.ActivationFunctionType
ALU = mybir.AluOpType
AX = mybir.AxisListType

ENGS = ("pe", "act", "dve", "pool", "sp")


class V:
    __slots__ = ("ap", "keys")

    def __init__(self, ap, keys):
        self.ap = ap
        self.keys = keys


class T:
    def __init__(self, handle, name, shape):
        self.h = handle
        self.name = name
        self.shape = shape

    def __getitem__(self, idx):
        return V(self.h[idx], (self.name,))

    def k(self, sub):
        return _TK(self, sub)


class _TK:
    def __init__(self, t, sub):
        self.t = t
        self.sub = sub

    def __getitem__(self, idx):
        return V(self.t.h[idx], ((self.t.name, self.sub),))


class Op:
    __slots__ = ("eng", "fn", "reads", "writes", "idx", "deps", "signal", "sigidx",
                 "is_dma", "dslot", "dcnt", "snap", "xreads")

    def __init__(self, eng, fn, reads, writes, is_dma=False):
        self.eng = eng
        self.fn = fn
        self.reads = reads
        self.writes = writes
        self.is_dma = is_dma
        self.deps = []
        self.signal = False
        self.sigidx = 0
        self.dslot = -1
        self.dcnt = 0
        self.snap = None


class Prog:
    N_DSEM = 24

    def __init__(self, nc):
        self.nc = nc
        self.ops = []
        self._stack = None
        self.ntile = 0
        self.psum_names = set()

    def enter(self, stack):
        self._stack = stack

    def sb(self, name, shape, dt=F32):
        h = self._stack.enter_context(self.nc.sbuf_tensor("s_" + name, list(shape), dt))
        return T(h, name, shape)

    def ps(self, name, shape, dt=F32):
        h = self._stack.enter_context(self.nc.psum_tensor("p_" + name, list(shape), dt))
        self.psum_names.add(name)
        return T(h, name, shape)

    def _keys(self, vs):
        ks = []
        for v in vs:
            if v is None or isinstance(v, (int, float)):
                continue
            ks.extend(v.keys)
        return ks

    def add(self, eng, fn, reads, writes, is_dma=False):
        op = Op(eng, fn, self._keys(reads), self._keys(writes), is_dma)
        op.xreads = [k for k in op.reads if (k if isinstance(k, str) else k[0]) in self.psum_names and k not in op.writes]
        self.ops.append(op)
        return op

    def dma(self, out, in_, eng="sp", **kw):
        def fn(e, out=out, in_=in_):
            return e.dma_start(out=out.ap, in_=in_.ap, **kw)
        return self.add(eng, fn, [in_], [out], is_dma=True)

    def mm(self, out, lhsT, rhs, start=True, stop=True, **kw):
        def fn(e):
            return e.matmul(out.ap, lhsT.ap, rhs.ap, start=start, stop=stop, **kw)
        return self.add("pe", fn, [lhsT, rhs], [out])

    def transpose(self, out, in_, ident):
        def fn(e):
            return e.transpose(out.ap, in_.ap, ident.ap)
        return self.add("pe", fn, [in_, ident], [out])

    def act(self, out, in_, func, bias=0.0, scale=1.0, eng="act", accum_out=None):
        def fn(e):
            kw = {}
            if accum_out is not None:
                kw["accum_out"] = accum_out.ap
            return e.activation(out.ap, in_.ap, func,
                                bias=(bias.ap if isinstance(bias, V) else bias),
                                scale=(scale.ap if isinstance(scale, V) else scale), **kw)
        return self.add("act", fn, [in_, bias, scale], [out, accum_out])

    def tt(self, out, in0, in1, op, eng="dve"):
        def fn(e):
            return e.tensor_tensor(out.ap, in0.ap, in1.ap, op)
        return self.add(eng, fn, [in0, in1], [out])

    def ts(self, out, in0, s1, op0, s2=None, op1=None, eng="dve", accum_out=None):
        def fn(e):
            a1 = s1.ap if isinstance(s1, V) else s1
            a2 = s2.ap if isinstance(s2, V) else s2
            kw = {}
            if accum_out is not None:
                kw["accum_out"] = accum_out.ap
            if op1 is None:
                return e.tensor_scalar(out.ap, in0.ap, a1, None, op0, **kw)
            return e.tensor_scalar(out.ap, in0.ap, a1, a2, op0, op1, **kw)
        return self.add(eng, fn, [in0, s1, s2], [out, accum_out])

    def stt(self, out, in0, scalar, in1, op0, op1, eng="dve"):
        def fn(e):
            s = scalar.ap if isinstance(scalar, V) else scalar
            return e.scalar_tensor_tensor(out.ap, in0.ap, s, in1.ap, op0, op1)
        return self.add(eng, fn, [in0, scalar, in1], [out])

    def copy(self, out, in_, eng="dve"):
        if eng == "act":
            def fn(e):
                return e.copy(out.ap, in_.ap)
        else:
            def fn(e):
                return e.tensor_copy(out.ap, in_.ap)
        return self.add(eng, fn, [in_], [out])

    def memset(self, out, val, eng="dve"):
        def fn(e):
            return e.memset(out.ap, val)
        return self.add(eng, fn, [], [out])

    def reduce(self, out, in_, op, axis=AX.X, eng="dve"):
        def fn(e):
            return e.tensor_reduce(out.ap, in_.ap, axis, op)
        return self.add(eng, fn, [in_], [out])

    def recip(self, out, in_):
        def fn(e):
            return e.reciprocal(out.ap, in_.ap)
        return self.add("dve", fn, [in_], [out])

    def generic(self, eng, fn, reads, writes):
        return self.add(eng, fn, reads, writes)

    def finalize(self, out_keys=()):
        nc = self.nc
        ops = self.ops
        last_w = {}
        readers = {}
        for i, op in enumerate(ops):
            op.idx = i
            deps = set()
            for k in op.reads:
                w = last_w.get(k)
                if w is not None:
                    deps.add(w)
            for k in list(op.writes) + op.xreads:
                w = last_w.get(k)
                if w is not None:
                    deps.add(w)
                latest = {}
                for r in readers.get(k, ()):
                    ro = ops[r]
                    if ro.is_dma:
                        deps.add(r)
                    else:
                        latest[ro.eng] = r
                for r in latest.values():
                    deps.add(r)
            deps.discard(i)
            op.deps = sorted(deps)
            for k in op.reads:
                lst = readers.setdefault(k, [])
                if not op.is_dma:
                    lst[:] = [r for r in lst if ops[r].is_dma or ops[r].eng != op.eng]
                lst.append(i)
            for k in op.writes:
                last_w[k] = i
                readers[k] = []
            for k in op.xreads:
                readers[k] = [i]
        for op in ops:
            need = []
            for d in op.deps:
                p = ops[d]
                if p.is_dma:
                    need.append(d)
                    continue
                if p.eng == op.eng:
                    if op.is_dma:
                        need.append(d)
                        continue
                    if op.eng == "pe":
                        continue
                    raw = any(k in p.writes for k in op.reads)
                    if raw:
                        need.append(d)
                    continue
                need.append(d)
            op.deps = need
            for d in need:
                if not ops[d].is_dma:
                    ops[d].signal = True
        cnt = {e: 0 for e in ENGS}
        for op in ops:
            if op.is_dma:
                continue
            if op.signal:
                cnt[op.eng] += 1
                op.sigidx = cnt[op.eng]
        dcount = [0] * self.N_DSEM
        nd = 0
        for op in ops:
            if op.is_dma:
                op.dslot = nd % self.N_DSEM
                dcount[op.dslot] += 1
                op.dcnt = dcount[op.dslot]
                nd += 1
        self.n_dma = nd
        self.sig_counts = cnt
        return self

    def emit(self, stack):
        nc = self.nc
        ops = self.ops
        sems = {e: stack.enter_context(nc.semaphore("S_" + e)) for e in ENGS if e != "sp"}
        dsems = [stack.enter_context(nc.semaphore("D%d" % i)) for i in range(self.N_DSEM)]
        block = stack.enter_context(nc.Block())
        seen = {e: {x: 0 for x in ENGS} for e in ENGS}
        seen_d = {e: [0] * self.N_DSEM for e in ENGS}
        plan = {e: [] for e in ENGS}
        last_dma_on_slot = [None] * self.N_DSEM
        for op in ops:
            e = op.eng
            waits = []
            if op.is_dma:
                prev = last_dma_on_slot[op.dslot]
                if prev is not None and seen_d[e][op.dslot] < prev.dcnt * 16:
                    waits.append((dsems[op.dslot], prev.dcnt * 16))
                    seen_d[e][op.dslot] = prev.dcnt * 16
                last_dma_on_slot[op.dslot] = op
            for d in op.deps:
                p = ops[d]
                if p.is_dma:
                    v = p.dcnt * 16
                    if seen_d[e][p.dslot] < v:
                        waits.append((dsems[p.dslot], v))
                        seen_d[e][p.dslot] = v
                else:
                    v = p.sigidx
                    if seen[e][p.eng] < v:
                        waits.append((sems[p.eng], v))
                        seen[e][p.eng] = v
                        for x in ENGS:
                            if p.snap[x] > seen[e][x]:
                                seen[e][x] = p.snap[x]
            if not op.is_dma:
                snap = dict(seen[e])
                if op.signal:
                    snap[e] = max(snap[e], op.sigidx)
                op.snap = snap
            plan[e].append((waits, op))
        final_waits = []
        for s in range(self.N_DSEM):
            lp = last_dma_on_slot[s]
            if lp is not None:
                final_waits.append((dsems[s], lp.dcnt * 16))

        def run(engname, e):
            for waits, op in plan[engname]:
                for (s, v) in waits:
                    e.wait_ge(s, v)
                ins = op.fn(e)
                if op.is_dma:
                    ins.then_inc(dsems[op.dslot], 16)
                elif op.signal:
                    ins.then_inc(sems[op.eng], 1)

        @block.tensor
        def _(e):
            run("pe", e)

        @block.scalar
        def _(e):
            run("act", e)

        @block.vector
        def _(e):
            run("dve", e)

        @block.gpsimd
        def _(e):
            run("pool", e)

        @block.sync
        def _(e):
            run("sp", e)
            for (s, v) in final_waits:
                e.wait_ge(s, v)
        self.stats = {e: len(plan[e]) for e in ENGS}
        self.nwaits = {e: sum(len(w) for w, _ in plan[e]) for e in ENGS}


S = 2048
D = 1024
L = 4
NMEM = 256
DFF = 2816
ALPHA = (2.0 * L) ** 0.25
LN_EPS = 1e-5
NEG = -30000.0

CONST_LAYOUT = {}
_off = 0
for _n, _w in [("ident", 128), ("tri_le", 128), ("tri_lt", 64), ("negmask", 128), ("selneg", 4 * 128),
               ("ones", 128), ("sel_even", 64), ("sel_odd", 128), ("tri_gt", 64), ("blk64", 128), ("mask2", 128)]:
    CONST_LAYOUT[_n] = (_off, _w)
    _off += _w
NCONST = _off


def make_consts():
    c = np.zeros((128, NCONST), np.float32)
    def put(n, a):
        o, w = CONST_LAYOUT[n]
        c[:a.shape[0], o:o + w] = a
    i = np.arange(128)
    put("ident", np.eye(128, dtype=np.float32))
    put("tri_le", (i[:, None] <= i[None, :]).astype(np.float32))
    put("tri_lt", (i[:, None] < i[None, :]).astype(np.float32)[:, :64])
    put("tri_gt", (i[:, None] > i[None, :]).astype(np.float32)[:, :64])
    put("negmask", np.where(i[:, None] <= i[None, :], 0.0, NEG).astype(np.float32))
    sn = np.zeros((4, 4 * 128), np.float32)
    for h in range(4):
        sn[h, h * 128:(h + 1) * 128] = -1.0
    put("selneg", sn)
    put("ones", np.ones((128, 128), np.float32))
    se = np.zeros((65, 64), np.float32); se[64, :] = 1.0
    put("sel_even", se)
    so = np.zeros((128, 128), np.float32); so[0, 64:128] = 1.0
    put("sel_odd", so)
    put("blk64", ((i[:, None] // 64) == (i[None, :] // 64)).astype(np.float32) / 64.0)
    put("mask2", ((i[:, None] <= i[None, :]) & ((i[:, None] // 64) == (i[None, :] // 64))).astype(np.float32))
    return c


def col_layout():
    lay = {}
    off = 0
    def add(n, w):
        nonlocal off
        lay[n] = (off, w)
        off += w
    add("mem_ln_g", 8); add("mem_ln_b", 8)
    for n in ("ln1_g", "ln1_b", "ln2_g", "ln2_b", "ln3_g", "ln3_b"):
        add(n, 8)
    add("ffn_up_b", 44); add("ffn_conv_b", 44); add("ffn_conv", 132)
    add("fox_fb", 1)
    add("gla_a_b", 2)
    add("gla_norm_g", 2)
    add("rw_mu", 8)
    add("rw_mu64_", 16)
    add("rw_h64_", 4 * 9)
    for n in list(lay.keys()):
        for l in range(L):
            lay["%s%d" % (n, l)] = lay[n]
    return lay, off


COL_LAYOUT, NCOL = col_layout()


def chunkcols(v, p=128):
    return np.ascontiguousarray(v.reshape(-1, p).T)


def make_colp(inp):
    call = np.zeros((L, 128, NCOL), np.float32)
    for l in range(L):
        c = call[l]
        def put(n, a):
            o, w = COL_LAYOUT[n]
            assert a.shape[1] == w, (n, a.shape, w)
            c[:a.shape[0], o:o + w] = a
        put("mem_ln_g", chunkcols(inp["mem_ln_g"])); put("mem_ln_b", chunkcols(inp["mem_ln_b"]))
        for n in ("ln1_g", "ln1_b", "ln2_g", "ln2_b", "ln3_g", "ln3_b"):
            put(n, chunkcols(inp[n][l]))
        put("ffn_up_b", chunkcols(inp["ffn_up_b"][l]))
        put("ffn_conv_b", chunkcols(inp["ffn_conv_b"][l]))
        put("ffn_conv", np.concatenate([chunkcols(inp["ffn_conv"][l][j]) for j in range(3)], 1))
        put("fox_fb", inp["fox_fb"][l].reshape(4, 1))
        put("gla_a_b", chunkcols(inp["gla_a_b"][l], 64))
        put("gla_norm_g", chunkcols(inp["gla_norm_g"][l]))
        put("rw_mu", chunkcols(inp["rw_mu"][l]))
        put("rw_mu64_", chunkcols(inp["rw_mu"][l], 64))
        h64 = np.zeros((64, 36), np.float32)
        for h in range(4):
            sl = slice(h * 64, (h + 1) * 64)
            for j, nm in enumerate(["rw_w0", "rw_a0", "rw_kk", "rw_ka", None, "rw_lnx_g", "rw_lnx_b"]):
                if nm is not None:
                    h64[:, h * 9 + j] = inp[nm][l][sl]
            h64[:, h * 9 + 4] = inp["rw_rk"][l][h]
        put("rw_h64_", h64)
    return call


ROW_LAYOUT = {}
_off = 0
for _n, _w in [("sg_ln_g", 256), ("sg_ln_b", 256), ("sgb", 256)]:
    ROW_LAYOUT[_n] = (_off, _w)
    _off += _w
NROW = _off


def make_rowp(inp):
    r = np.zeros((L, 128, NROW), np.float32)
    for l in range(L):
        def put(n, a):
            o, w = ROW_LAYOUT[n]
            r[l, :, o:o + w] = a
        put("sg_ln_g", np.tile(inp["sg_ln_g"][l][None], (128, 1)))
        put("sg_ln_b", np.tile(inp["sg_ln_b"][l][None], (128, 1)))
        sgb = np.zeros((128, 2, 128), np.float32)
        for p in range(128):
            for pair in range(2):
                sgb[p, pair] = inp["sg_b"][l][pair * 2 + p // 64]
        put("sgb", sgb.reshape(128, 256))
    return r


class WV:
    def __init__(self, tile, kc, cols):
        self.tile = tile
        self.kc = kc
        self.cols = cols
        self.ap = tile.h[:, 0:kc * cols].rearrange("p (k c) -> p k c", k=kc)

    def __getitem__(self, idx):
        return V(self.ap[idx], (self.tile.name,))


class MK:
    def __init__(self, nc, nlayers=L, stages=("sg", "fox", "gla", "rw", "ln1", "ca", "ln2", "ffn", "ln3"), dbg=()):
        self.nc = nc
        self.nlayers = nlayers
        self.stages = stages
        self.dbg = dbg
        self.P = Prog(nc)
        self.dbg_out = {}

    def decl(self):
        nc = self.nc
        di = lambda n, shp: nc.dram_tensor(n, list(shp), F32, kind="ExternalInput").ap()
        self.h_xT = di("xT", [D, S])
        self.h_memT = di("memT", [D, NMEM])
        self.h_consts = di("consts", [128, NCONST])
        self.h_colp = di("colp", [L, 128, NCOL])
        self.h_rowp = di("rowp", [L, 128, NROW])
        self.h_w_in = di("w_in", [L, D, 3092])
        self.h_w_out = di("w_out", [L, D, D])
        self.h_sgwT = di("sgwT", [L, 128, 4, 128])
        self.h_gla_a_up = di("gla_a_up", [L, 16, 128])
        self.h_rw_w2 = di("rw_w2", [L, 64, 256])
        self.h_rw_a2 = di("rw_a2", [L, 64, 256])
        self.h_rw_g2 = di("rw_g2", [L, 128, 256])
        self.h_ca_wq = di("ca_wq", [L, D, D]); self.h_ca_wk = di("ca_wk", [L, D, D])
        self.h_ca_wv = di("ca_wv", [L, D, D]); self.h_ca_wo = di("ca_wo", [L, D, D])
        self.h_ffn_up = di("ffn_up", [L, D, 2 * DFF]); self.h_ffn_down = di("ffn_down", [L, DFF, D])
        self.h_outT = nc.dram_tensor("outT", [D, S], F32, kind="ExternalOutput").ap()

    def hv(self, ap, key):
        return V(ap, (key,))

    def alloc(self):
        P = self.P
        self.xT = P.sb("xT", [128, 8, S], F32)
        self.xb = P.sb("xb", [128, 8, S], BF16)
        self.cur_xb = None
        self.yT = P.sb("yT", [128, 2, S], BF16)
        self.consts = P.sb("consts", [128, NCONST], F32)
        self.colp = P.sb("colp", [128, NCOL], F32)
        self.ident_bf = P.sb("ident_bf", [128, 128], BF16)
        self.ones_s = P.sb("ones_s", [128, 128], BF16)
        self.blk64_bf = P.sb("blk64_bf", [128, 128], BF16)
        self.wb = [P.sb("wb%d" % i, [128, 4096], BF16) for i in range(2)]
        self.pb = [P.ps("pb%d" % i, [128, 512], F32) for i in range(8)]
        self.t512 = [P.sb("t512_%d" % i, [128, 512], F32) for i in range(5)]
        self.b512 = [P.sb("b512_%d" % i, [128, 512], BF16) for i in range(4)]
        self.st512 = [P.sb("st512_%d" % i, [128, 512], F32) for i in range(2)]
        self.scr = P.sb("scr", [128, 20480], BF16)
        self._ffh_i = 0
        self._ps_i = 0
        self._wb_i = 0
        self._wo_i = 0
        self._t_i = 0
        self._b_i = 0

    def load_xb(self, tb, eng="pool"):
        xb = self.xb
        class _B:
            def __getitem__(s_, idx):
                p, k, c = idx
                return V(xb.h[p, k, slice(tb * 512 + c.start, tb * 512 + c.stop)], (xb.name,))
        self.cur_xb = _B()
        return self.cur_xb

    def sub256(self, t):
        class _S:
            name = t.name
            class _H:
                def __getitem__(s2, idx):
                    p, c = idx
                    c = slice(c.start or 0, 256 if c.stop is None else c.stop)
                    return t.h[p, c]
            h = _H()
            def __getitem__(s_, idx):
                if not isinstance(idx, tuple):
                    idx = (idx, slice(0, 256))
                p, c = idx
                c = slice(c.start or 0, 256 if c.stop is None else c.stop)
                return V(t.h[p, c], (t.name,))
        return _S()

    def nextps(self):
        p = self.pb[self._ps_i % 6]
        self._ps_i += 1
        return p

    def nextacc(self):
        self._acc_i = getattr(self, "_acc_i", 0) + 1
        return self.pb[6 + self._acc_i % 2]

    def nt(self):
        t = self.t512[self._t_i % 5]
        self._t_i += 1
        return t

    def nf(self):
        return self.nt()

    def nb(self):
        t = self.b512[self._b_i % 4]
        self._b_i += 1
        return t

    def C(self, name, rows=128, c0=0, c1=None):
        o, w = CONST_LAYOUT[name]
        if c1 is None:
            c1 = w
        return self.consts[0:rows, o + c0:o + c1]

    def col(self, name, j, rows=128):
        o, w = COL_LAYOUT[name]
        return self.colp[0:rows, o + j:o + j + 1]

    def row(self, name, c0=0, c1=None):
        o, w = ROW_LAYOUT[name]
        if c1 is None:
            c1 = w
        return V(self.scr.h[:, :].bitcast(F32)[:, 2560 + o + c0:2560 + o + c1], ("sg_rowp",))

    def load_w(self, hbm_ap, kc, cols, key, ring="wb"):
        P = self.P
        t = self.wb[self._wb_i % 2]; self._wb_i += 1
        wv = WV(t, kc, cols)
        src = hbm_ap.rearrange("(k p) c -> p k c", p=128)
        step = max(1, (2048 // cols) if cols <= 2048 else 1)
        k = 0
        while k < kc:
            k2 = min(kc, k + step)
            P.dma(V(wv.ap[:, k:k2, :], (t.name,)), V(src[:, k:k2, :], (key,)), eng="pool")
            k = k2
        return wv

    def scr_phase(self, new_keys):
        old = getattr(self, "_scr_keys", [])
        if not hasattr(self, "_dummy"):
            self._dummy = self.P.sb("phase_dummy", [128, 8], F32)
        d = self._dummy
        self.P.add("pool", lambda e: e.memset(d.h[:, :], 0.0), [V(None, tuple(old))], [V(None, tuple(new_keys) + (d.name,))])
        self._scr_keys = list(new_keys)

    def debug_dump(self, name, view, shape):
        if name not in self.dbg:
            return
        h = self.nc.dram_tensor("dbg_" + name, list(shape), view.ap.dtype, kind="ExternalOutput").ap()
        self.P.dma(V(h, ("dbg_" + name,)), view)
        self.dbg_out[name] = shape

    def proj_fm(self, w, c0, M, tb, evac, xsrc=None, ncol=512):
        P = self.P
        ps = self.nextps()
        for kc in range(w.kc):
            if xsrc is None:
                rhs = self.cur_xb[:, kc, 0:ncol]
            else:
                rhs = xsrc[:, kc, tb * ncol:(tb + 1) * ncol]
            P.mm(ps[0:M, 0:ncol], w[:, kc, c0:c0 + M], rhs,
                 start=(kc == 0), stop=(kc == w.kc - 1))
        evac(ps)

    def proj_tm(self, w, c0, N, t0, evac, ntok=128):
        P = self.P
        ps = self.nextps()
        for kc in range(w.kc):
            P.mm(ps[0:ntok, 0:N], self.cur_xb[:, kc, (t0 % 512):(t0 % 512) + ntok], w[:, kc, c0:c0 + N],
                 start=(kc == 0), stop=(kc == w.kc - 1))
        evac(ps)

    def acc_out(self, hbm_w, yT, nck, key, first):
        P = self.P
        w = self.load_w(hbm_w, nck, 1024, key, ring="wb")
        for tb in range(4):
            for o in range(8):
                ps = self.nextps()
                for c in range(nck):
                    P.mm(ps[:, :], w[:, c, o * 128:(o + 1) * 128], yT[:, c, tb * 512:(tb + 1) * 512],
                         start=(c == 0), stop=(c == nck - 1))
                xv = self.xT[:, o, tb * 512:(tb + 1) * 512]
                if first:
                    P.stt(xv, xv, ALPHA, ps[:, :], ALU.mult, ALU.add)
                else:
                    P.tt(xv, xv, ps[:, :], ALU.add)

    def layernorm(self, gname, bname, src=None, dst32=None, dstb=None, ntok=S, eps=LN_EPS):
        P = self.P
        src = self.xT if src is None else src
        dst32 = self.xT if dst32 is None else (None if dst32 is False else dst32)
        dstb = self.xb if dstb is None else dstb
        nblk = (ntok + 511) // 512
        for tb in range(nblk):
            w = min(512, ntok - tb * 512)
            sl = slice(tb * 512, tb * 512 + w)
            psm = self.nextps(); psq = self.nextps()
            for c in range(8):
                xb_ = self.nb(); sq = self.nb()
                P.copy(xb_[:, 0:w], src[:, c, sl], eng="pool")
                P.act(sq[:, 0:w], src[:, c, sl], AF.Square)
                P.mm(psm[:, 0:w], self.ones_s[:, :], xb_[:, 0:w], start=(c == 0), stop=(c == 7))
                P.mm(psq[:, 0:w], self.ones_s[:, :], sq[:, 0:w], start=(c == 0), stop=(c == 7))
            mean = self.st512[0]; rstd = self.st512[1]
            P.copy(mean[:, 0:w], psm[:, 0:w])
            msq = self.nt()
            P.tt(msq[:, 0:w], mean[:, 0:w], mean[:, 0:w], ALU.mult)
            P.tt(msq[:, 0:w], psq[:, 0:w], msq[:, 0:w], ALU.subtract)
            P.act(msq[:, 0:w], msq[:, 0:w], AF.Ln, bias=eps)
            P.act(rstd[:, 0:w], msq[:, 0:w], AF.Exp, scale=-0.5)
            for c in range(8):
                u = self.nt()
                P.tt(u[:, 0:w], src[:, c, sl], mean[:, 0:w], ALU.subtract)
                P.stt(u[:, 0:w], u[:, 0:w], self.col(gname, c), rstd[:, 0:w], ALU.mult, ALU.mult)
                if dst32 is not None:
                    P.act(dst32[:, c, sl], u[:, 0:w], AF.Identity, bias=self.col(bname, c))
                if dstb is not None:
                    P.ts(dstb[:, c, sl], u[:, 0:w], self.col(bname, c), ALU.add, eng="pool")

    def mixer_sg(self, l):
        P = self.P
        w = self.load_w(self.h_w_in[l][:, 0:512], 8, 512, "h_w_in")
        if not hasattr(self, "sg_t"):
            self.sg_t = dict(
                wmT=P.sb("sg_wmT", [128, 4, 128], BF16),
                stat=[P.sb("sg_stat%d" % i, [128, 16], F32) for i in range(2)],
                vnp=[P.sb("sg_vnp%d" % i, [128, 2, 2, 128], BF16) for i in range(2)],
            )
            for v_ in self.sg_t["vnp"]:
                P.memset(v_[:], 0.0, eng="pool")
        self.scr_phase(["sg_u0", "sg_u1", "sg_wraw", "sg_rowp"])
        P.dma(V(self.scr.h[:, :].bitcast(F32)[:, 2560:2560 + NROW], ("sg_rowp",)), self.hv(self.h_rowp[l], "h_rowp"))
        scr32 = self.scr.h[:, :].bitcast(F32)
        class _C:
            def __init__(s_, ap, key):
                s_.ap = ap; s_.key = key
            def __getitem__(s_, idx):
                return V(s_.ap[idx], (s_.key,))
        usb = [_C(scr32[:, i * 1024:(i + 1) * 1024].rearrange("p (c t) -> p c t", c=2), "sg_u%d" % i) for i in range(2)]
        wraw = _C(scr32[:, 2048:2560].rearrange("p (h t) -> p h t", h=4), "sg_wraw")
        T_ = self.sg_t
        P.dma(wraw[:], self.hv(self.h_sgwT[l], "h_sgwT"))
        tri = V(self.C("tri_le").ap.unsqueeze(1).to_broadcast([128, 4, 128]), self.consts[:].keys)
        P.tt(T_["wmT"][:], wraw[:], tri, ALU.mult)
        for tb in range(4):
            self.load_xb(tb)
            u_sb = usb[tb % 2]
            for c in range(2):
                self.proj_fm(w, c * 128, 128, tb, lambda ps, c=c: P.copy(u_sb[:, c, :], ps[:, :], eng="act"))
            for n4 in range(4):
                n = tb * 4 + n4
                vs = self.sub256(self.nf()); sq = self.sub256(self.nf()); stat = T_["stat"][n % 2]; vnp = T_["vnp"][n % 2]
                def ev(ps):
                    P.copy(vs[:], ps[:, 0:256], eng="act")
                    P.act(sq[:], ps[:, 0:256], AF.Square)
                self.proj_tm(w, 256, 256, n * 128, ev)
                v3 = V(vs.h[:, :].rearrange("p (h d) -> p h d", h=4), vs[:].keys)
                q3 = V(sq.h[:, :].rearrange("p (h d) -> p h d", h=4), sq[:].keys)
                P.reduce(stat[:, 0:4], v3, ALU.add)
                P.reduce(stat[:, 4:8], q3, ALU.add)
                P.ts(stat[:, 0:4], stat[:, 0:4], 1.0 / 64, ALU.mult)
                P.tt(stat[:, 8:12], stat[:, 0:4], stat[:, 0:4], ALU.mult)
                P.stt(stat[:, 4:8], stat[:, 4:8], 1.0 / 64, stat[:, 8:12], ALU.mult, ALU.subtract)
                P.act(stat[:, 4:8], stat[:, 4:8], AF.Ln, bias=LN_EPS)
                P.act(stat[:, 12:16], stat[:, 4:8], AF.Exp, scale=-0.5)
                for h in range(4):
                    P.ts(sq[:, h * 64:(h + 1) * 64], vs[:, h * 64:(h + 1) * 64], stat[:, h:h + 1], ALU.subtract,
                         stat[:, 12 + h:13 + h], ALU.mult)
                P.tt(sq[:], sq[:], self.row("sg_ln_g"), ALU.mult)
                s4 = sq.h[:, :].rearrange("p (a b d) -> p a b d", a=2, b=2)
                rb = self.row("sg_ln_b").ap.rearrange("p (a b d) -> p a b d", a=2, b=2)
                for hh in range(2):
                    P.tt(V(vnp.h[:, :, hh, hh * 64:(hh + 1) * 64], vnp[:].keys), V(s4[:, :, hh, :], sq[:].keys),
                         V(rb[:, :, hh, :], ("sg_rowp",)), ALU.add)
                for pair in range(2):
                    ps = self.nextps()
                    for hh in range(2):
                        P.mm(ps[:, 0:128], V(vnp.h[:, pair, hh, :], vnp[:].keys), T_["wmT"][:, pair * 2 + hh, :],
                             start=(hh == 0), stop=(hh == 1))
                    t2 = self.nf()
                    P.tt(t2[:, 0:128], ps[:, 0:128], self.row("sgb", pair * 128, (pair + 1) * 128), ALU.add)
                    P.tt(self.yT[:, pair, n * 128:(n + 1) * 128], t2[:, 0:128], u_sb[:, pair, n4 * 128:(n4 + 1) * 128], ALU.mult)

    def mixer_fox(self, l):
        P = self.P
        w = self.load_w(self.h_w_in[l][:, 2320:2832], 8, 512, "h_w_in")
        w2 = self.load_w(self.h_w_in[l][:, 2832:3092], 8, 260, "h_w_in")
        if not hasattr(self, "fx"):
            self.fx = dict(
                negcT=P.sb("fx_negcT", [128, 16, 4], F32),
                nfb=P.sb("fx_nfb", [4, 1], F32),
            )
        F = self.fx
        scr = self.scr
        qT = V(scr.h[:, 0:4096].rearrange("p (c t) -> p c t", c=2), ("fx_qT",))
        kT = V(scr.h[:, 4096:8192].rearrange("p (c t) -> p c t", c=2), ("fx_kT",))
        v1ap = scr.h[:, 8192:16384].rearrange("p (n h m) -> p n h m", n=16, h=4)
        v1 = lambda idx: V(v1ap[idx], ("fx_v1",))
        self.scr_phase(["fx_qT", "fx_kT", "fx_v1", "fx_negc"])
        negc_ap = scr.h[:, 16384:20480].bitcast(F32)
        class _N:
            def __getitem__(s_, idx):
                return V(negc_ap[idx], ("fx_negc",))
        F["negc"] = _N(); F["nlf"] = F["negc"]
        P.memset(v1((slice(None),)), 0.0, eng="pool")
        for n in range(16):
            for h in range(4):
                col = 64 if h % 2 == 0 else 0
                P.memset(v1((slice(None), n, h, slice(col, col + 1))), 1.0, eng="pool")
        P.ts(F["nfb"][:], self.col("fox_fb%d" % l, 0, rows=4), -1.0, ALU.mult)
        for tb in range(4):
            self.load_xb(tb)
            sl = slice(tb * 512, (tb + 1) * 512)
            for c in range(2):
                self.proj_fm(w, c * 128, 128, tb,
                             lambda ps, c=c: P.act(V(qT.ap[:, c, sl], qT.keys), ps[:, :], AF.Copy, scale=0.125))
                self.proj_fm(w, 256 + c * 128, 128, tb,
                             lambda ps, c=c: P.copy(V(kT.ap[:, c, sl], kT.keys), ps[:, :], eng="dve"))
            def evf(ps):
                P.act(F["nlf"][0:4, sl], ps[0:4, :], AF.Exp, bias=F["nfb"][:, 0:1], scale=-1.0)
                P.act(F["nlf"][0:4, sl], F["nlf"][0:4, sl], AF.Ln, bias=1.0)
            self.proj_fm(w2, 256, 4, tb, evf)
            for n4 in range(4):
                n = tb * 4 + n4
                def evv(ps, n=n):
                    for h in range(4):
                        col = 0 if h % 2 == 0 else 64
                        P.copy(v1((slice(None), n, h, slice(col, col + 64))), ps[:, h * 64:(h + 1) * 64],
                               eng=("act" if h % 2 else "dve"))
                self.proj_tm(w2, 0, 256, n * 128, evv)
        ones_b = self.C("ones", rows=4, c0=0, c1=1).ap.to_broadcast([4, S])
        P.generic("dve", lambda e: e.tensor_tensor_scan(negc_ap[0:4, :], ones_b, negc_ap[0:4, :], 0.0,
                                                         ALU.mult, ALU.add),
                  [self.consts[:], F["negc"][0:4, :]], [F["negc"][0:4, :]])
        self.debug_dump("fox_negc", F["negc"][0:4, :], [4, S])
        for J in range(16):
            ps = self.nextps()
            P.transpose(ps[:, 0:4], F["negc"][0:4, J * 128:(J + 1) * 128], self.C("ident", rows=4, c0=0, c1=4))
            P.copy(F["negcT"][:, J, :], ps[:, 0:4])
        for h in range(4):
            c = h // 2
            p0 = (h % 2) * 64
            even = (h % 2 == 0)
            for Q in range(4):
                ops_ = self.nextacc()
                nJ = 4 * Q + 4
                for J in range(nJ):
                    c_lo = max(0, (J - 4 * Q) * 128)
                    lg = self.nextps()
                    P.mm(lg[:, c_lo:512], V(kT.ap[p0:p0 + 64, c, J * 128:(J + 1) * 128], kT.keys),
                         V(qT.ap[p0:p0 + 64, c, Q * 512 + c_lo:(Q + 1) * 512], qT.keys), start=True, stop=False)
                    P.mm(lg[:, c_lo:512], self.C("selneg", rows=4, c0=h * 128, c1=(h + 1) * 128),
                         F["negc"][0:4, Q * 512 + c_lo:(Q + 1) * 512], start=False, stop=True)
                    if J >= 4 * Q:
                        P.tt(lg[:, c_lo:c_lo + 128], lg[:, c_lo:c_lo + 128], self.C("negmask"), ALU.add)
                    pT = self.nb()
                    P.act(pT[:, c_lo:512], lg[:, c_lo:512], AF.Exp, bias=F["negcT"][:, J, h:h + 1])
                    M = 65 if even else 128
                    P.mm(ops_[0:M, c_lo:512], v1((slice(None), J, h, slice(0, M))), pT[:, c_lo:512],
                         start=(J == 0), stop=(J == nJ - 1))
                osb = self.nt()
                if even:
                    P.copy(osb[0:65, :], ops_[0:65, :], eng="act")
                    P.recip(osb[64:65, :], osb[64:65, :])
                    bp = self.nextps()
                    P.mm(bp[0:64, :], self.C("sel_even", rows=65), osb[0:65, :])
                    P.tt(self.yT[0:64, c, Q * 512:(Q + 1) * 512], osb[0:64, :], bp[0:64, :], ALU.mult)
                else:
                    P.copy(osb[:, :], ops_[:, :], eng="act")
                    P.recip(osb[0:1, :], osb[0:1, :])
                    bp = self.nextps()
                    P.mm(bp[:, :], self.C("sel_odd"), osb[:, :])
                    P.tt(self.yT[64:128, c, Q * 512:(Q + 1) * 512], osb[64:128, :], bp[64:128, :], ALU.mult)


    def mixer_gla(self, l):
        P = self.P
        w1 = self.load_w(self.h_w_in[l][:, 1536:2048], 8, 512, "h_w_in")
        w2 = self.load_w(self.h_w_in[l][:, 2048:2320], 8, 272, "h_w_in")
        if not hasattr(self, "gl"):
            self.gl = dict(
                aup=P.sb("gl_aup", [16, 128], BF16),
                nab=P.sb("gl_nab", [64, 2], F32),
                gps=P.sb("gl_gps", [64, 32], F32),
                bl=P.sb("gl_bl", [64, 32], F32),
                dec=P.sb("gl_dec", [64, 2, 32], F32),
                S=[P.sb("gl_S%d" % g, [64, 128], F32) for g in range(2)],
                Sbf=[[P.sb("gl_Sbf%d_%d" % (g, i), [64, 128], BF16) for i in range(4)] for g in range(2)],
                osb=[P.sb("gl_osb%d" % i, [128, 128], F32) for i in range(2)],
            )
        G = self.gl
        scr = self.scr
        self.scr_phase(["gl_qe", "gl_ke", "gl_v", "gl_ketm", "gl_G", "gl_adT"])
        class _C:
            def __init__(s_, ap, key):
                s_.ap = ap; s_.key = key
            def __getitem__(s_, idx):
                return V(s_.ap[idx], (s_.key,))
        qe = _C(scr.h[:, 0:4096].rearrange("p (g t) -> p g t", g=2), "gl_qe")
        ke = _C(scr.h[:, 4096:8192].rearrange("p (g t) -> p g t", g=2), "gl_ke")
        v128 = _C(scr.h[:, 8192:12288].rearrange("p (b c) -> p b c", b=16), "gl_v")
        ketm = _C(scr.h[:, 12288:14336].rearrange("p (b c) -> p b c", b=16), "gl_ketm")
        Gt = _C(scr.h[:, 14336:18432].bitcast(F32), "gl_G")
        Gt3 = _C(scr.h[:, 14336:18432].bitcast(F32).rearrange("p (n c) -> p n c", n=32), "gl_G")
        adT = _C(scr.h[:, 18432:20480], "gl_adT")
        P.dma(G["aup"][:], self.hv(self.h_gla_a_up[l], "h_gla_a_up"), eng="pool")
        P.ts(G["nab"][:], V(self.colp.h[0:64, COL_LAYOUT["gla_a_b%d" % l][0]:COL_LAYOUT["gla_a_b%d" % l][0] + 2], self.colp[:].keys),
             -1.0, ALU.mult)
        for tb in range(4):
            self.load_xb(tb)
            sl = slice(tb * 512, (tb + 1) * 512)
            self.proj_fm(w2, 256, 16, tb, lambda ps: P.copy(adT[0:16, sl], ps[0:16, :], eng="act"))
            for c in range(2):
                def evg(ps, c=c):
                    t = self.nt()
                    P.act(t[:], ps[:, :], AF.Silu)
                    P.ts(self.yT[:, c, sl], t[:], self.col("gla_norm_g%d" % l, c), ALU.mult)
                self.proj_fm(w2, c * 128, 128, tb, evg)
            for b4 in range(4):
                bk = tb * 4 + b4
                self.proj_tm(w1, 256, 256, bk * 128, lambda ps, bk=bk: P.copy(v128[:, bk, :], ps[:, 0:256], eng="act"))
        import os
        stop = int(os.environ.get('GLA_STOP', '99'))
        if stop <= 1:
            return
        ones_b = self.C("ones", rows=64, c0=0, c1=1).ap.to_broadcast([64, S])
        for g in range(2):
            for tb in range(4):
                sl = slice(tb * 512, (tb + 1) * 512)
                ps = self.nextps()
                P.mm(ps[0:64, :], G["aup"][:, g * 64:(g + 1) * 64], adT[0:16, sl])
                P.act(Gt[0:64, sl], ps[0:64, :], AF.Exp, bias=G["nab"][:, g:g + 1], scale=-1.0)
                P.act(Gt[0:64, sl], Gt[0:64, sl], AF.Ln, bias=1.0)
            P.generic("dve", lambda e: e.tensor_tensor_scan(Gt.ap[0:64, :], ones_b, Gt.ap[0:64, :], 0.0, ALU.mult, ALU.add),
                      [self.consts[:], Gt[0:64, :]], [Gt[0:64, :]])
            if stop <= 2:
                continue
            P.memset(G["gps"][:, 0:1], 0.0)
            P.copy(G["gps"][:, 1:32], Gt3[0:64, 0:31, 63])
            P.tt(Gt3[0:64, :, :], Gt3[0:64, :, :], V(G["gps"].h[:, :].unsqueeze(2).to_broadcast([64, 32, 64]), G["gps"][:].keys),
                 ALU.subtract)
            P.copy(G["bl"][:], Gt3[0:64, :, 63])
            P.act(G["dec"][:, g, :], G["bl"][:], AF.Exp, scale=-1.0 / 16)
            for tb in range(4):
                sl = slice(tb * 512, (tb + 1) * 512)
                self.load_xb(tb)
                Eb = self.nt(); Enb = self.nt()
                P.act(Eb[0:64, :], Gt[0:64, sl], AF.Exp, scale=-1.0 / 16)
                P.act(Enb[0:64, :], Gt[0:64, sl], AF.Exp, scale=1.0 / 16)
                self.proj_fm(w1, g * 64, 64, tb,
                             lambda ps: P.stt(qe[0:64, g, sl], ps[0:64, :], 32.0 ** -0.5, Eb[0:64, :], ALU.mult, ALU.mult))
                self.proj_fm(w1, 128 + g * 64, 64, tb,
                             lambda ps: P.tt(ke[0:64, g, sl], ps[0:64, :], Enb[0:64, :], ALU.mult))
            if stop <= 3:
                continue
            for bk in range(16):
                ps = self.nextps()
                pbf = V(ps.h[:, :].bitcast(BF16), ps[:].keys)
                P.transpose(V(pbf.ap[:, 0:64], pbf.keys), ke[0:64, g, bk * 128:(bk + 1) * 128], self.ident_bf[0:64, 0:64])
                P.copy(ketm[:, bk, g * 64:(g + 1) * 64], V(pbf.ap[:, 0:64], pbf.keys), eng="act")
        self.debug_dump("gl_qe", qe[0:64, :, :], [64, 2, S])
        self.debug_dump("gl_ke", ke[0:64, :, :], [64, 2, S])
        if stop <= 4:
            return
        for g in range(2):
            P.memset(G["S"][g][:], 0.0)
            P.memset(G["Sbf"][g][0][:], 0.0)
        for bk in range(16):
            for g in range(2):
                attm = []
                for hh in range(2):
                    ps = self.nextps()
                    P.mm(ps[:, 0:128], ke[hh * 32:hh * 32 + 32, g, bk * 128:(bk + 1) * 128],
                         qe[hh * 32:hh * 32 + 32, g, bk * 128:(bk + 1) * 128])
                    am = self.nb()
                    P.tt(am[:, 0:128], ps[:, 0:128], self.C("mask2"), ALU.mult)
                    attm.append(am)
                if stop <= 5:
                    continue
                for cc in range(2):
                    n = 2 * bk + cc
                    if n == 31:
                        break
                    psU = self.nextps()
                    r0 = cc * 64
                    P.mm(psU[0:64, 0:128], ketm[r0:r0 + 64, bk, g * 64:(g + 1) * 64], v128[r0:r0 + 64, bk, g * 128:(g + 1) * 128])
                    P.tt(G["S"][g][:], G["S"][g][:], psU[0:64, 0:128], ALU.add)
                    P.ts(G["S"][g][:], G["S"][g][:], G["dec"][:, g, n:n + 1], ALU.mult)
                    P.copy(G["Sbf"][g][(n + 1) % 4][:], G["S"][g][:], eng="act")
                if stop <= 6:
                    continue
                psO = self.nextps()
                for hh in range(2):
                    c0 = hh * 128
                    P.mm(psO[:, c0:c0 + 128], v128[:, bk, g * 128:(g + 1) * 128], attm[hh][:, 0:128], start=True, stop=False)
                    for cc in range(2):
                        n = 2 * bk + cc
                        P.mm(psO[:, c0 + cc * 64:c0 + (cc + 1) * 64], G["Sbf"][g][n % 4][hh * 32:hh * 32 + 32, :],
                             qe[hh * 32:hh * 32 + 32, g, n * 64:(n + 1) * 64], start=False, stop=(cc == 1))
                if stop <= 7:
                    continue
                osb = G["osb"][g]
                osq = self.nb()
                for hh in range(2):
                    r0 = hh * 64
                    var = os.environ.get('GLA_VAR', 'ab')
                    if 'a' in var:
                        P.copy(osb[r0:r0 + 64, :], psO[r0:r0 + 64, hh * 128:(hh + 1) * 128], eng="dve")
                    if 'b' in var:
                        P.act(osq[r0:r0 + 64, 0:128], psO[r0:r0 + 64, hh * 128:(hh + 1) * 128], AF.Square)
                if stop <= 8:
                    continue
                pss = self.nextps()
                P.mm(pss[:, 0:128], self.blk64_bf[:, :], osq[:, 0:128])
                if stop <= 9:
                    continue
                rs = self.nf()
                P.act(rs[:, 0:128], pss[:, 0:128], AF.Ln, bias=1e-5)
                P.act(rs[:, 0:128], rs[:, 0:128], AF.Exp, scale=-0.5)
                if stop <= 10:
                    continue
                P.tt(osb[:, :], osb[:, :], rs[:, 0:128], ALU.mult)
                if stop <= 11:
                    continue
                yv = self.yT[:, g, bk * 128:(bk + 1) * 128]
                P.tt(yv, osb[:, :], yv, ALU.mult)


    def mixer_rw(self, l):
        P = self.P
        wA = self.load_w(self.h_w_in[l][:, 512:1024], 8, 512, "h_w_in")
        wB = self.load_w(self.h_w_in[l][:, 1024:1536], 8, 512, "h_w_in")
        if not hasattr(self, "rw"):
            self.rw = dict(
                sm=P.sb("rw_sm", [128, 512], BF16),
                omm=P.sb("rw_omm", [128, 24], F32),
                nwa=P.sb("rw_nwa", [64, 8], F32),
                carry=P.sb("rw_carry", [128, 8], F32),
                maskq=P.sb("rw_maskq", [64, 128], F32),
                PC=P.sb("rw_PC", [64, 32], F32),
                gs=P.sb("rw_gs", [64, 4], F32),
                S32=P.sb("rw_S32", [64, 64], F32),
                Sb=[P.sb("rw_Sb%d" % i, [64, 64], BF16) for i in range(2)],
                QA=[P.sb("rw_QA%d" % i, [64, 128], BF16) for i in range(2)],
                QB=[P.sb("rw_QB%d" % i, [64, 128], BF16) for i in range(2)],
                Nn=[P.sb("rw_Nn%d" % i, [64, 64], BF16) for i in range(2)],
                TM=[P.sb("rw_TM%d" % i, [64, 192], BF16) for i in range(2)],
                XN=[P.sb("rw_XN%d" % i, [64, 128], BF16) for i in range(3)],
                W=[P.sb("rw_W%d" % i, [64, 64], BF16) for i in range(3)],
                RU=[P.sb("rw_RU%d" % i, [64, 128], BF16) for i in range(2)],
            )
            P.copy(self.rw["maskq"][:, 0:64], self.C("tri_le", rows=64, c0=0, c1=64))
            P.copy(self.rw["maskq"][:, 64:128], self.C("tri_lt", rows=64, c0=0, c1=64))
        R = self.rw
        scr = self.scr
        self.scr_phase(["rw_twa", "rw_sgd", "rw_KB", "rw_RA", "rw_VT", "rw_bv"])
        class _C:
            def __init__(s_, ap, key):
                s_.ap = ap; s_.key = key
            def __getitem__(s_, idx):
                return V(s_.ap[idx], (s_.key,))
        twa = _C(scr.h[:, 0:2048], "rw_twa")
        sgd = _C(scr.h[:, 2048:4096], "rw_sgd")
        KB = _C(scr.h[:, 4096:8192].rearrange("p (n c) -> p n c", n=32), "rw_KB")
        RA = _C(scr.h[:, 8192:12288].rearrange("p (n c) -> p n c", n=32), "rw_RA")
        VT = _C(scr.h[:, 12288:14336], "rw_VT")
        bv = _C(scr.h[:, 14336:16384], "rw_bv")
        P.dma(R["sm"][0:64, 0:256], self.hv(self.h_rw_w2[l], "h_rw_w2"), eng="pool")
        P.dma(R["sm"][64:128, 0:256], self.hv(self.h_rw_a2[l], "h_rw_a2"), eng="pool")
        P.dma(R["sm"][:, 256:512], self.hv(self.h_rw_g2[l], "h_rw_g2"), eng="pool")
        o_mu, _ = COL_LAYOUT["rw_mu%d" % l]; o_mu64, _ = COL_LAYOUT["rw_mu64_%d" % l]
        mu128 = lambda c: self.colp[:, o_mu + c:o_mu + c + 1]
        mu64 = lambda c: self.colp[0:64, o_mu64 + c:o_mu64 + c + 1]
        P.ts(R["omm"][:, 0:8], self.colp[:, o_mu:o_mu + 8], -1.0, ALU.mult, 1.0, ALU.add)
        P.ts(R["omm"][0:64, 8:24], self.colp[0:64, o_mu64:o_mu64 + 16], -1.0, ALU.mult, 1.0, ALU.add)
        o_h, _ = COL_LAYOUT["rw_h64_%d" % l]
        hcol = lambda h, j: self.colp[0:64, o_h + h * 9 + j:o_h + h * 9 + j + 1]
        for h in range(4):
            P.ts(R["nwa"][:, 2 * h:2 * h + 1], hcol(h, 0), -1.0, ALU.mult)
            P.ts(R["nwa"][:, 2 * h + 1:2 * h + 2], hcol(h, 1), -1.0, ALU.mult)
        pool_tiles = self.t512 + self.st512
        def tmp(i, rows=64):
            t = pool_tiles[i // 2]
            c0 = (i % 2) * 256
            class _T:
                name = t.name
                def __getitem__(s_, idx):
                    if not isinstance(idx, tuple):
                        idx = (idx, slice(0, 256))
                    p, c = idx
                    c = slice(c0 + (c.start or 0), c0 + (256 if c.stop is None else c.stop))
                    return V(t.h[p, c], (t.name,))
                def v3(s_, rows_):
                    return V(t.h[0:rows_, c0:c0 + 256].rearrange("p (n c) -> p n c", n=4), (t.name,))
            return _T()

        def shiftmix(ps, M, mu_ap, omm_ap, cslot, out_t, blk):
            zr = pool_tiles[6]
            if blk == 0:
                P.memset(zr[0:M, 0:1], 0.0)
            else:
                P.copy(zr[0:M, 0:1], R["carry"][0:M, cslot:cslot + 1])
            P.copy(zr[0:M, 1:257], ps[0:M, 0:256], eng="act")
            P.copy(R["carry"][0:M, cslot:cslot + 1], zr[0:M, 256:257])
            P.ts(out_t[0:M, :], zr[0:M, 1:257], omm_ap, ALU.mult)
            P.stt(out_t[0:M, :], zr[0:M, 0:256], mu_ap, out_t[0:M, :], ALU.mult, ALU.add)

        class _XB:
            def __init__(s_, xb, t0):
                s_.xb = xb; s_.t0 = t0
            def __getitem__(s_, idx):
                p, k, c = idx
                return V(s_.xb.h[p, k, slice(s_.t0 + c.start, s_.t0 + c.stop)], (s_.xb.name,))

        for blk in range(8):
            self.cur_xb = _XB(self.xb, blk * 256)
            sl = slice(blk * 256, (blk + 1) * 256)
            z = tmp(0)
            self.proj_fm(wB, 256, 128, 0, lambda ps: shiftmix(ps, 128, mu128(6), R["omm"][:, 6:7], 0, z, blk), ncol=256)
            P.act(twa[0:64, sl], z[0:64, :], AF.Tanh)
            P.copy(twa[64:128, sl], z[64:128, :], eng="pool")
            z2 = tmp(1)
            self.proj_fm(wB, 384, 128, 0, lambda ps: shiftmix(ps, 128, mu128(7), R["omm"][:, 7:8], 1, z2, blk), ncol=256)
            P.act(z2[:, :], z2[:, :], AF.Exp, scale=-1.0)
            P.ts(z2[:, :], z2[:, :], 1.0, ALU.add)
            P.recip(z2[:, :], z2[:, :])
            P.copy(sgd[:, sl], z2[:, :], eng="pool")
        import os
        rstop = int(os.environ.get("RW_STOP", "99"))
        nheads = int(os.environ.get("RW_HEADS", "4"))
        for h in range(nheads):
            for blk in range(8):
                self.cur_xb = _XB(self.xb, blk * 256)
                sl = slice(blk * 256, (blk + 1) * 256)
                n0 = blk * 4
                zr_ = tmp(0); zk_ = tmp(1); zv_ = tmp(2)
                self.proj_fm(wA, h * 64, 64, 0, lambda ps: shiftmix(ps, 64, mu64(h), R["omm"][0:64, 8 + h:9 + h], 2, zr_, blk), ncol=256)
                self.proj_fm(wA, 256 + h * 64, 64, 0, lambda ps: shiftmix(ps, 64, mu64(4 + h), R["omm"][0:64, 12 + h:13 + h], 3, zk_, blk), ncol=256)
                self.proj_fm(wB, h * 64, 64, 0, lambda ps: shiftmix(ps, 64, mu64(8 + h), R["omm"][0:64, 16 + h:17 + h], 4, zv_, blk), ncol=256)
                P.copy(VT[0:64, sl], zv_[0:64, :], eng="pool")
                LD = tmp(3)
                ps = self.nextps()
                P.mm(ps[0:64, 0:256], R["sm"][0:64, h * 64:(h + 1) * 64], twa[0:64, sl])
                P.act(LD[0:64, :], ps[0:64, 0:256], AF.Exp, bias=R["nwa"][:, 2 * h:2 * h + 1], scale=-1.0)
                P.ts(LD[0:64, :], LD[0:64, :], 1.0, ALU.add)
                P.recip(LD[0:64, :], LD[0:64, :])
                P.ts(LD[0:64, :], LD[0:64, :], -0.6065306597126334, ALU.mult)
                A = tmp(4)
                ps = self.nextps()
                P.mm(ps[0:64, 0:256], R["sm"][64:128, h * 64:(h + 1) * 64], twa[64:128, sl])
                P.act(A[0:64, :], ps[0:64, 0:256], AF.Exp, bias=R["nwa"][:, 2 * h + 1:2 * h + 2], scale=-1.0)
                P.ts(A[0:64, :], A[0:64, :], 1.0, ALU.add)
                P.recip(A[0:64, :], A[0:64, :])
                KK = tmp(5)
                P.ts(KK[0:64, :], zk_[0:64, :], hcol(h, 2), ALU.mult)
                sq = self.nb()
                P.act(sq[0:64, 0:256], KK[0:64, :], AF.Square)
                ps = self.nextps()
                P.mm(ps[0:64, 0:256], self.blk64_bf[0:64, 0:64], sq[0:64, 0:256])
                RN = tmp(10)
                P.act(RN[0:64, :], ps[0:64, 0:256], AF.Ln, bias=1e-24, scale=64.0)
                P.act(RN[0:64, :], RN[0:64, :], AF.Exp, scale=-0.5)
                P.tt(KK[0:64, :], KK[0:64, :], RN[0:64, :], ALU.mult)
                K2 = tmp(6)
                P.ts(K2[0:64, :], A[0:64, :], -1.0, ALU.add, hcol(h, 3), ALU.mult)
                P.stt(K2[0:64, :], K2[0:64, :], 1.0, zk_[0:64, :], ALU.add, ALU.mult)
                KKA = tmp(7)
                P.tt(KKA[0:64, :], KK[0:64, :], A[0:64, :], ALU.mult)
                pr = self.nb()
                P.stt(pr[0:64, 0:256], zr_[0:64, :], hcol(h, 4), K2[0:64, :], ALU.mult, ALU.mult)
                ps = self.nextps()
                P.mm(ps[0:64, 0:256], self.blk64_bf[0:64, 0:64], pr[0:64, 0:256])
                P.stt(bv[0:64, sl], ps[0:64, 0:256], 64.0, zv_[0:64, :], ALU.mult, ALU.mult)
                Gb = tmp(8)
                ones_b = self.C("ones", rows=64, c0=0, c1=1).ap.to_broadcast([64, 256])
                P.generic("dve", lambda e, Gb=Gb, LD=LD: e.tensor_tensor_scan(Gb[0:64, :].ap, ones_b, LD[0:64, :].ap, 0.0, ALU.mult, ALU.add),
                          [self.consts[:], LD[0:64, :]], [Gb[0:64, :]])
                P.memset(R["gs"][:, 0:1], 0.0)
                G3 = Gb.v3(64)
                P.copy(R["gs"][:, 1:4], V(G3.ap[:, 0:3, 63], G3.keys))
                P.tt(G3, G3, V(R["gs"].h[:, :].unsqueeze(2).to_broadcast([64, 4, 64]), R["gs"][:].keys), ALU.subtract)
                P.act(R["PC"][:, n0:n0 + 4], V(G3.ap[:, :, 63], G3.keys), AF.Exp)
                Ep = tmp(10); Em = tmp(11); Epm1 = tmp(9)
                P.act(Ep[0:64, :], Gb[0:64, :], AF.Exp)
                P.act(Em[0:64, :], Gb[0:64, :], AF.Exp, scale=-1.0)
                P.tt(Epm1[0:64, :], Gb[0:64, :], LD[0:64, :], ALU.subtract)
                P.act(Epm1[0:64, :], Epm1[0:64, :], AF.Exp)
                P.tt(V(RA.ap[0:64, n0:n0 + 4, 0:64], ("rw_RA",)), zr_.v3(64), Ep.v3(64), ALU.mult)
                P.stt(V(RA.ap[0:64, n0:n0 + 4, 64:128], ("rw_RA",)), KK.v3(64), -1.0, Epm1.v3(64), ALU.mult, ALU.mult)
                P.tt(V(KB.ap[0:64, n0:n0 + 4, 0:64], ("rw_KB",)), K2.v3(64), Em.v3(64), ALU.mult)
                P.tt(V(KB.ap[0:64, n0:n0 + 4, 64:128], ("rw_KB",)), KKA.v3(64), Em.v3(64), ALU.mult)
            if h == 0:
                self.debug_dump("rw_RA", RA[0:64, :, :], [64, 32, 128])
                self.debug_dump("rw_KB", KB[0:64, :, :], [64, 32, 128])
                self.debug_dump("rw_PC", R["PC"][:, :], [64, 32])
            if rstop <= 1:
                continue
            P.memset(R["S32"][:], 0.0)
            P.memset(R["Sb"][0][:], 0.0)
            psY = None
            for n in range(32):
                i3 = n % 2
                QA = R["QA"][i3]; QB = R["QB"][i3]; Nn = R["Nn"][i3]; TM = R["TM"][i3]
                Kt = KB[0:64, n, 0:64]; Bt = KB[0:64, n, 64:128]; Rt = RA[0:64, n, 0:64]; At = RA[0:64, n, 64:128]
                ps = self.nextps()
                P.mm(ps[0:64, 0:128], Kt, RA[0:64, n, :])
                P.tt(QA[:, :], ps[0:64, 0:128], R["maskq"][:, :], ALU.mult)
                ps = self.nextps()
                P.mm(ps[0:64, 0:128], Bt, RA[0:64, n, :])
                P.tt(QB[:, :], ps[0:64, 0:128], R["maskq"][:, :], ALU.mult)
                ps = self.nextps()
                P.mm(ps[0:64, 0:64], At, Bt)
                P.tt(Nn[:, :], ps[0:64, 0:64], self.C("tri_gt", rows=64, c0=0, c1=64), ALU.mult)
                ps = self.nextps()
                pbf = lambda a, b: V(ps.h[0:64, :].bitcast(BF16)[:, a:b], ps[:].keys)
                P.transpose(pbf(0, 64), Kt, self.ident_bf[0:64, 0:64])
                P.transpose(pbf(64, 128), Bt, self.ident_bf[0:64, 0:64])
                P.transpose(pbf(128, 192), VT[0:64, n * 64:(n + 1) * 64], self.ident_bf[0:64, 0:64])
                P.copy(TM[:, :], pbf(0, 192), eng="act")
                wi = 0
                W = R["W"][wi]
                P.tt(W[:, :], QB[:, 64:128], self.C("ident", rows=64, c0=0, c1=64), ALU.add)
                Xp = QB[:, 64:128]; Np = Nn[:, :]
                for lev in range(5):
                    XN = R["XN"][(n * 5 + lev) % 3]
                    ps = self.nextps()
                    P.mm(ps[0:64, 64:128], Xp, Np)
                    if lev < 4:
                        P.mm(ps[0:64, 0:64], Np, Xp)
                        P.copy(XN[:, :], ps[0:64, 0:128], eng="act")
                    else:
                        P.copy(XN[:, 64:128], ps[0:64, 64:128], eng="act")
                    ps2 = self.nextps()
                    P.mm(ps2[0:64, 0:64], XN[:, 64:128], W[:, :])
                    wi += 1
                    Wn = R["W"][wi % 3]
                    P.tt(Wn[:, :], ps2[0:64, 0:64], W[:, :], ALU.add)
                    W = Wn
                    Xp = XN[:, 0:64]; Np = XN[:, 64:128]
                if rstop <= 2:
                    continue
                Sb = R["Sb"][n % 2]; Sbn = R["Sb"][(n + 1) % 2]
                RU = R["RU"][n % 2]
                psA = self.nextps()
                P.mm(psA[0:64, 0:64], At, Sb[:, :], start=True, stop=False)
                P.mm(psA[0:64, 0:64], QA[:, 64:128], TM[:, 128:192], start=False, stop=True)
                P.copy(RU[:, 0:64], psA[0:64, 0:64], eng="act")
                psU = self.nextps()
                P.mm(psU[0:64, 0:64], W[:, :], RU[:, 0:64])
                P.copy(RU[:, 64:128], psU[0:64, 0:64], eng="act")
                if n % 8 == 0:
                    psY = self.nextacc()
                yc = slice((n % 8) * 64, (n % 8 + 1) * 64)
                P.mm(psY[0:64, yc], Sb[:, :], Rt, start=True, stop=False)
                P.mm(psY[0:64, yc], TM[:, 128:192], QA[:, 0:64], start=False, stop=False)
                P.mm(psY[0:64, yc], RU[:, 64:128], QB[:, 0:64], start=False, stop=True)
                psS = self.nextps()
                P.mm(psS[0:64, 0:64], TM[:, 0:64], TM[:, 128:192], start=True, stop=False)
                P.mm(psS[0:64, 0:64], TM[:, 64:128], RU[:, 64:128], start=False, stop=True)
                P.tt(R["S32"][:, :], R["S32"][:, :], psS[0:64, 0:64], ALU.add)
                P.ts(R["S32"][:, :], R["S32"][:, :], R["PC"][:, n:n + 1], ALU.mult)
                P.copy(Sbn[:, :], R["S32"][:, :], eng="act")
                if n % 8 == 7:
                    tb = n // 8
                    sl = slice(tb * 512, (tb + 1) * 512)
                    Y = self.nt()
                    P.copy(Y[0:64, :], psY[0:64, :], eng="act")
                    if h == 0:
                        self.debug_dump("rw_scan%d" % tb, Y[0:64, :], [64, 512])
                    ysq = self.nb(); ybf = self.nb()
                    P.act(ysq[0:64, :], Y[0:64, :], AF.Square)
                    P.copy(ybf[0:64, :], Y[0:64, :], eng="pool")
                    psm = self.nextps(); psq = self.nextps()
                    P.mm(psm[0:64, :], self.blk64_bf[0:64, 0:64], ybf[0:64, :])
                    P.mm(psq[0:64, :], self.blk64_bf[0:64, 0:64], ysq[0:64, :])
                    m2 = self.nt()
                    P.tt(Y[0:64, :], Y[0:64, :], psm[0:64, :], ALU.subtract)
                    P.act(m2[0:64, :], psm[0:64, :], AF.Square)
                    P.tt(m2[0:64, :], psq[0:64, :], m2[0:64, :], ALU.subtract)
                    P.act(m2[0:64, :], m2[0:64, :], AF.Ln, bias=64e-5)
                    P.act(m2[0:64, :], m2[0:64, :], AF.Exp, scale=-0.5)
                    P.stt(Y[0:64, :], Y[0:64, :], hcol(h, 5), m2[0:64, :], ALU.mult, ALU.mult)
                    P.stt(Y[0:64, :], Y[0:64, :], hcol(h, 6), bv[0:64, sl], ALU.add, ALU.add)
                    psg = self.nextps()
                    P.mm(psg[0:64, :], R["sm"][:, 256 + h * 64:256 + (h + 1) * 64], sgd[:, sl])
                    P.tt(self.yT[0:64, 0, sl], Y[0:64, :], psg[0:64, :], ALU.mult)
            if rstop <= 2:
                continue
            self.debug_dump("y_rwh%d" % h, self.yT[0:64, 0, :], [64, S])
            if "ln1" in self.stages:
                self.acc_out_rw(l, h)

    def acc_out_rw(self, l, h):
        P = self.P
        if not hasattr(self, "rw_wo"):
            self.rw_wo = P.sb("rw_wo", [64, 1024], BF16)
        t = self.rw_wo
        wv = WV(t, 1, 1024)
        P.dma(V(wv.ap[0:64, :, :], (t.name,)),
              V(self.h_w_out[l][256 + h * 64:256 + (h + 1) * 64, :].rearrange("(k p) c -> p k c", p=64), ("h_w_out",)), eng="pool")
        for tb in range(4):
            for o in range(8):
                ps = self.nextps()
                P.mm(ps[:, :], V(wv.ap[0:64, 0, o * 128:(o + 1) * 128], (t.name,)), self.yT[0:64, 0, tb * 512:(tb + 1) * 512])
                xv = self.xT[:, o, tb * 512:(tb + 1) * 512]
                if self._first_acc:
                    P.stt(xv, xv, ALPHA, ps[:, :], ALU.mult, ALU.add)
                else:
                    P.tt(xv, xv, ps[:, :], ALU.add)
        self._first_acc = False


    def mem_ln(self):
        P = self.P
        self.memnb = P.sb("memnb", [128, 8, NMEM], BF16)
        self.scr_phase(["mem_raw"])
        class _C:
            def __init__(s_, ap, key):
                s_.ap = ap; s_.key = key
            def __getitem__(s_, idx):
                return V(s_.ap[idx], (s_.key,))
        raw = _C(self.scr.h[:, :].bitcast(F32)[:, 0:8 * NMEM].rearrange("p (c t) -> p c t", c=8), "mem_raw")
        P.dma(raw[:, :, :], self.hv(self.h_memT.rearrange("(c p) t -> p c t", p=128), "h_memT"))
        self.layernorm("mem_ln_g", "mem_ln_b", src=raw, dst32=False, dstb=self.memnb, ntok=NMEM)

    def cross_attn(self, l):
        P = self.P
        self.scr_phase(["ca_KT", "ca_V", "ca_oT"])
        scr = self.scr
        class _C:
            def __init__(s_, ap, key):
                s_.ap = ap; s_.key = key
            def __getitem__(s_, idx):
                return V(s_.ap[idx], (s_.key,))
        KT = _C(scr.h[:, 0:2048].rearrange("p (c m) -> p c m", c=8), "ca_KT")
        Vt = _C(scr.h[:, 2048:4096].rearrange("p (b c) -> p b c", b=2), "ca_V")
        oT = _C(scr.h[:, 4096:20480].rearrange("p (c t) -> p c t", c=8), "ca_oT")
        if not hasattr(self, "ones_bf"):
            self.ones_bf = P.sb("ones_bf", [128, 128], BF16)
            P.copy(self.ones_bf[:], self.C("ones"))
        for half in range(2):
            wk = self.load_w(self.h_ca_wk[l][:, half * 512:(half + 1) * 512], 8, 512, "h_ca_wk")
            for oc in range(4):
                ps = self.nextps()
                for kc in range(8):
                    P.mm(ps[:, 0:NMEM], wk[:, kc, oc * 128:(oc + 1) * 128], self.memnb[:, kc, :], start=(kc == 0), stop=(kc == 7))
                P.copy(KT[:, half * 4 + oc, :], ps[:, 0:NMEM], eng="act")
        for half in range(2):
            wv = self.load_w(self.h_ca_wv[l][:, half * 512:(half + 1) * 512], 8, 512, "h_ca_wv")
            for mb in range(2):
                ps = self.nextps()
                for kc in range(8):
                    P.mm(ps[:, :], self.memnb[:, kc, mb * 128:(mb + 1) * 128], wv[:, kc, :], start=(kc == 0), stop=(kc == 7))
                P.copy(Vt[:, mb, half * 512:(half + 1) * 512], ps[:, :], eng="dve")
        for h in range(4):
            wq = self.load_w(self.h_ca_wq[l][:, h * 256:(h + 1) * 256], 8, 256, "h_ca_wq")
            for tb in range(4):
                self.load_xb(tb)
                sl = slice(tb * 512, (tb + 1) * 512)
                qT = [self.nb(), self.nb()]
                for c in range(2):
                    self.proj_fm(wq, c * 128, 128, tb, lambda ps, c=c: P.act(qT[c][:, :], ps[:, :], AF.Copy, scale=1.0 / 16))
                PT = []
                for mb in range(2):
                    ps = self.nextps()
                    for c in range(2):
                        P.mm(ps[:, :], KT[:, h * 2 + c, mb * 128:(mb + 1) * 128], qT[c][:, :], start=(c == 0), stop=(c == 1))
                    pt = self.nb()
                    P.act(pt[:, :], ps[:, :], AF.Exp)
                    PT.append(pt)
                den = self.nextps()
                for mb in range(2):
                    P.mm(den[:, :], self.ones_bf[:, :], PT[mb][:, :], start=(mb == 0), stop=(mb == 1))
                rden = self.nt()
                P.recip(rden[:, :], den[:, :])
                for c2 in range(2):
                    ps = self.nextps()
                    for mb in range(2):
                        P.mm(ps[:, :], Vt[:, mb, h * 256 + c2 * 128:h * 256 + (c2 + 1) * 128], PT[mb][:, :], start=(mb == 0), stop=(mb == 1))
                    P.tt(oT[:, h * 2 + c2, sl], ps[:, :], rden[:, :], ALU.mult)
        for half in range(2):
            wo = self.load_w(self.h_ca_wo[l][:, half * 512:(half + 1) * 512], 8, 512, "h_ca_wo")
            for tb in range(4):
                sl = slice(tb * 512, (tb + 1) * 512)
                for oc in range(4):
                    ps = self.nextps()
                    for kc in range(8):
                        P.mm(ps[:, :], wo[:, kc, oc * 128:(oc + 1) * 128], oT[:, kc, sl], start=(kc == 0), stop=(kc == 7))
                    xv = self.xT[:, half * 4 + oc, sl]
                    P.stt(xv, xv, ALPHA, ps[:, :], ALU.mult, ALU.add)

    def conv_ffn(self, l):
        P = self.P
        self.scr_phase(["ff_w0", "ff_w1", "ff_w2", "ff_w3", "ff_pr0", "ff_pr1"])
        scr = self.scr
        if not hasattr(self, "ff"):
            self.ff = dict(halo=P.sb("ff_halo", [128, 8, 2], F32))
        y32 = self.yT.h[:, :, :].rearrange("p a b -> p (a b)").bitcast(F32)
        class _H:
            def __init__(s_, i):
                s_.i = i
            def __getitem__(s_, idx):
                p, c = idx
                return V(y32[p, slice(s_.i * 520 + c.start, s_.i * 520 + c.stop)], ("ffh%d" % s_.i,))
        self.ffh = [_H(i) for i in range(3)]
        self.P.add("pool", lambda e: e.memset(self._dummy.h[:, :], 0.0), [V(None, ("yT",))],
                   [V(None, ("ffh0", "ffh1", "ffh2", self._dummy.name))])
        class _WB:
            def __init__(s_, ap, key, kc, cols):
                s_.ap = ap[:, 0:kc * cols].rearrange("p (k c) -> p k c", k=kc); s_.key = key; s_.kc = kc
            def __getitem__(s_, idx):
                return V(s_.ap[idx], (s_.key,))
        bufs = [(self.wb[0].h[:, :], "wb0"), (self.wb[1].h[:, :], "wb1")] + \
               [(scr.h[:, i * 4096:(i + 1) * 4096], "ff_w%d" % i) for i in range(4)]
        prs = [(scr.h[:, 16384 + i * 2048:16384 + (i + 1) * 2048].rearrange("p (j t) -> p j t", j=4), "ff_pr%d" % i) for i in range(2)]
        def loadw(bi, hbm_ap, kc, cols, key):
            ap, k = bufs[bi]
            wv = _WB(ap, k, kc, cols)
            src = hbm_ap.rearrange("(k p) c -> p k c", p=128)
            step = 4 if cols <= 512 else 2
            kk = 0
            while kk < kc:
                k2 = min(kc, kk + step)
                P.dma(V(wv.ap[:, kk:k2, :], (k,)), V(src[:, kk:k2, :], (key,)), eng="pool")
                kk = k2
            return wv
        o_ub, _ = COL_LAYOUT["ffn_up_b"]; o_cb, _ = COL_LAYOUT["ffn_conv_b"]; o_cw, _ = COL_LAYOUT["ffn_conv"]
        colv = lambda o: self.colp[:, o:o + 1]
        npg = 6
        for pg in range(npg):
            nj = 4 if pg < 5 else 2
            b0 = (pg % 2) * 3
            wg = loadw(b0, self.h_ffn_up[l][:, pg * 512:pg * 512 + nj * 128], 8, nj * 128, "h_ffn_up")
            wvv = loadw(b0 + 1, self.h_ffn_up[l][:, DFF + pg * 512:DFF + pg * 512 + nj * 128], 8, nj * 128, "h_ffn_up")
            wd = loadw(b0 + 2, self.h_ffn_down[l][pg * 512:pg * 512 + nj * 128, :], nj, 1024, "h_ffn_down")
            for tb in range(4):
                self.load_xb(tb)
                sl = slice(tb * 512, (tb + 1) * 512)
                prap, prk = prs[(pg * 4 + tb) % 2]
                for jj in range(nj):
                    res = []
                    for part, wsrc in enumerate((wg, wvv)):
                        ch = part * 22 + pg * 4 + jj
                        hb = self.nt()
                        hs = part * 4 + jj
                        ps = self.nextps()
                        for kc in range(8):
                            P.mm(ps[:, :], wsrc[:, kc, jj * 128:(jj + 1) * 128], self.cur_xb[:, kc, 0:512], start=(kc == 0), stop=(kc == 7))
                        hbuf = self.ffh[self._ffh_i % 3]; self._ffh_i += 1
                        if tb == 0:
                            P.memset(hbuf[:, 0:2], 0.0, eng="pool")
                        else:
                            P.copy(hbuf[:, 0:2], self.ff["halo"][:, hs, :], eng="pool")
                        P.act(hbuf[:, 2:514], ps[:, :], AF.Identity, bias=colv(o_ub + ch))
                        P.copy(self.ff["halo"][:, hs, :], hbuf[:, 512:514], eng="pool")
                        P.ts(hb[:, :], hbuf[:, 0:512], colv(o_cw + ch), ALU.mult, colv(o_cb + ch), ALU.add, eng="pool")
                        P.stt(hb[:, :], hbuf[:, 1:513], colv(o_cw + 44 + ch), hb[:, :], ALU.mult, ALU.add)
                        P.stt(hb[:, :], hbuf[:, 2:514], colv(o_cw + 88 + ch), hb[:, :], ALU.mult, ALU.add)
                        res.append(hb)
                    P.act(res[0][:, :], res[0][:, :], AF.Gelu)
                    P.tt(V(prap[:, jj, :], (prk,)), res[0][:, :], res[1][:, :], ALU.mult)
                for o in range(8):
                    ps = self.nextps()
                    for jj in range(nj):
                        P.mm(ps[:, :], wd[:, jj, o * 128:(o + 1) * 128], V(prap[:, jj, :], (prk,)), start=(jj == 0), stop=(jj == nj - 1))
                    xv = self.xT[:, o, sl]
                    if pg == 0:
                        P.stt(xv, xv, ALPHA, ps[:, :], ALU.mult, ALU.add)
                    else:
                        P.tt(xv, xv, ps[:, :], ALU.add)

    def build(self):
        P = self.P
        with ExitStack() as st:
            P.enter(st)
            self.decl()
            self.alloc()
            P.dma(self.consts[:], self.hv(self.h_consts, "h_consts"))
            P.dma(self.colp[:], self.hv(self.h_colp[0], "h_colp"))
            xsrc = self.h_xT.rearrange("(c p) t -> p c t", p=128)
            for tb in range(4):
                sl = slice(tb * 512, (tb + 1) * 512)
                P.dma(self.xT[:, :, sl], self.hv(xsrc[:, :, sl], "h_xT"))
                P.copy(self.xb[:, :, sl], self.xT[:, :, sl], eng=("act" if tb % 2 else "dve"))
            P.copy(self.ident_bf[:], self.C("ident"))
            P.ts(self.ones_s[:], self.C("ones"), 1.0 / 1024, ALU.mult)
            P.copy(self.blk64_bf[:], self.C("blk64"))
            if "ca" in self.stages:
                self.mem_ln()
            for l in range(self.nlayers):
                self.layer(l)
            osrc = self.h_outT.rearrange("(c p) t -> p c t", p=128)
            for tb in range(4):
                sl = slice(tb * 512, (tb + 1) * 512)
                P.dma(self.hv(osrc[:, :, sl], "h_outT"), self.xT[:, :, sl])
            P.finalize()
            P.emit(st)
        return self.nc

    def layer(self, l):
        P = self.P
        if l > 0:
            P.dma(self.colp[:], self.hv(self.h_colp[l], "h_colp"))
        self._first_acc = True
        if l > 0 and "ffn" in self.stages:
            self.P.add("pool", lambda e: e.memset(self._dummy.h[:, :], 0.0), [V(None, ("ffh0", "ffh1", "ffh2"))],
                       [V(None, ("yT", self._dummy.name))])
        for m, (name, fn) in enumerate([("sg", self.mixer_sg), ("rw", self.mixer_rw), ("gla", self.mixer_gla), ("fox", self.mixer_fox)]):
            if name in self.stages and fn is not None:
                fn(l)
                if name == "rw":
                    continue
                self.debug_dump("y_%s%d" % (name, l), self.yT[:], [128, 2, S])
                if "ln1" in self.stages:
                    self.acc_out(self.h_w_out[l][m * 256:(m + 1) * 256, :], self.yT, 2, "h_w_out", self._first_acc)
                    self._first_acc = False
        if "ln1" in self.stages:
            self.layernorm("ln1_g%d" % l, "ln1_b%d" % l)
            self.debug_dump("x1_%d" % l, self.xT[:], [128, 8, S])
        if "ca" in self.stages:
            self.cross_attn(l)
            self.layernorm("ln2_g%d" % l, "ln2_b%d" % l)
            self.debug_dump("x2_%d" % l, self.xT[:], [128, 8, S])
        if "ffn" in self.stages:
            self.conv_ffn(l)
            self.layernorm("ln3_g%d" % l, "ln3_b%d" % l)


_CACHE = {}


def kernel(**inputs):
    inp = {k: np.ascontiguousarray(np.asarray(v, dtype=np.float32)) for k, v in inputs.items()}
    n = 8
    nc = bass.Bass("TRN2", target_bir_lowering=False)
    mk = MK(nc)
    mk.build()
    consts = make_consts()
    colp = make_colp(inp)
    rowp = make_rowp(inp)
    sgwT = np.ascontiguousarray(inp["sg_w"].transpose(0, 3, 1, 2))
    shared = dict(consts=consts, colp=colp, rowp=rowp, sgwT=sgwT,
                  w_in=inp["w_in"], w_out=inp["w_out"], gla_a_up=inp["gla_a_up"],
                  rw_w2=inp["rw_w2"], rw_a2=inp["rw_a2"], rw_g2=inp["rw_g2"],
                  ca_wq=inp["ca_wq"], ca_wk=inp["ca_wk"], ca_wv=inp["ca_wv"], ca_wo=inp["ca_wo"],
                  ffn_up=inp["ffn_up"], ffn_down=inp["ffn_down"])
    maps = []
    for b in range(n):
        m = dict(shared)
        m["xT"] = np.ascontiguousarray(inp["x"][b].T)
        m["memT"] = np.ascontiguousarray(inp["mem"][b].T)
        maps.append(m)
    res = run_bass_kernel_spmd(nc, maps, core_ids=list(range(n)))
    out = np.stack([np.asarray(res.results[b]["outT"]).T for b in range(n)]).astype(np.float32)
    return np.ascontiguousarray(out)
```

```python
from contextlib import ExitStack
from concourse.bass_utils import run_bass_kernel_spmd
import numpy as np
import concourse.bass as bass
import concourse.mybir as mybir

F32 = mybir.dt.float32
BF16 = mybir.dt.bfloat16
AF = mybir.ActivationFunctionType
ALU = mybir.AluOpType
AX = mybir.AxisListType

ENGS = ("pe", "act", "dve", "pool", "sp")


class V:
    __slots__ = ("ap", "keys")

    def __init__(self, ap, keys):
        self.ap = ap
        self.keys = keys


class T:
    def __init__(self, handle, name, shape):
        self.h = handle
        self.name = name
        self.shape = shape

    def __getitem__(self, idx):
        return V(self.h[idx], (self.name,))

    def k(self, sub):
        return _TK(self, sub)


class _TK:
    def __init__(self, t, sub):
        self.t = t
        self.sub = sub

    def __getitem__(self, idx):
        return V(self.t.h[idx], ((self.t.name, self.sub),))


class Op:
    __slots__ = ("eng", "fn", "reads", "writes", "idx", "deps", "signal", "sigidx",
                 "is_dma", "dslot", "dcnt", "snap", "xreads")

    def __init__(self, eng, fn, reads, writes, is_dma=False):
        self.eng = eng
        self.fn = fn
        self.reads = reads
        self.writes = writes
        self.is_dma = is_dma
        self.deps = []
        self.signal = False
        self.sigidx = 0
        self.dslot = -1
        self.dcnt = 0
        self.snap = None


class Prog:
    N_DSEM = 24

    def __init__(self, nc):
        self.nc = nc
        self.ops = []
        self._stack = None
        self.ntile = 0
        self.psum_names = set()

    def enter(self, stack):
        self._stack = stack

    def sb(self, name, shape, dt=F32):
        h = self._stack.enter_context(self.nc.sbuf_tensor("s_" + name, list(shape), dt))
        return T(h, name, shape)

    def ps(self, name, shape, dt=F32):
        h = self._stack.enter_context(self.nc.psum_tensor("p_" + name, list(shape), dt))
        self.psum_names.add(name)
        return T(h, name, shape)

    def _keys(self, vs):
        ks = []
        for v in vs:
            if v is None or isinstance(v, (int, float)):
                continue
            ks.extend(v.keys)
        return ks

    def add(self, eng, fn, reads, writes, is_dma=False):
        op = Op(eng, fn, self._keys(reads), self._keys(writes), is_dma)
        op.xreads = [k for k in op.reads if (k if isinstance(k, str) else k[0]) in self.psum_names and k not in op.writes]
        self.ops.append(op)
        return op

    def dma(self, out, in_, eng="sp", **kw):
        def fn(e, out=out, in_=in_):
            return e.dma_start(out=out.ap, in_=in_.ap, **kw)
        return self.add(eng, fn, [in_], [out], is_dma=True)

    def mm(self, out, lhsT, rhs, start=True, stop=True, **kw):
        def fn(e):
            return e.matmul(out.ap, lhsT.ap, rhs.ap, start=start, stop=stop, **kw)
        return self.add("pe", fn, [lhsT, rhs], [out])

    def transpose(self, out, in_, ident):
        def fn(e):
            return e.transpose(out.ap, in_.ap, ident.ap)
        return self.add("pe", fn, [in_, ident], [out])

    def act(self, out, in_, func, bias=0.0, scale=1.0, eng="act", accum_out=None):
        def fn(e):
            kw = {}
            if accum_out is not None:
                kw["accum_out"] = accum_out.ap
            return e.activation(out.ap, in_.ap, func,
                                bias=(bias.ap if isinstance(bias, V) else bias),
                                scale=(scale.ap if isinstance(scale, V) else scale), **kw)
        return self.add("act", fn, [in_, bias, scale], [out, accum_out])

    def tt(self, out, in0, in1, op, eng="dve"):
        def fn(e):
            return e.tensor_tensor(out.ap, in0.ap, in1.ap, op)
        return self.add(eng, fn, [in0, in1], [out])

    def ts(self, out, in0, s1, op0, s2=None, op1=None, eng="dve", accum_out=None):
        def fn(e):
            a1 = s1.ap if isinstance(s1, V) else s1
            a2 = s2.ap if isinstance(s2, V) else s2
            kw = {}
            if accum_out is not None:
                kw["accum_out"] = accum_out.ap
            if op1 is None:
                return e.tensor_scalar(out.ap, in0.ap, a1, None, op0, **kw)
            return e.tensor_scalar(out.ap, in0.ap, a1, a2, op0, op1, **kw)
        return self.add(eng, fn, [in0, s1, s2], [out, accum_out])

    def stt(self, out, in0, scalar, in1, op0, op1, eng="dve"):
        def fn(e):
            s = scalar.ap if isinstance(scalar, V) else scalar
            return e.scalar_tensor_tensor(out.ap, in0.ap, s, in1.ap, op0, op1)
        return self.add(eng, fn, [in0, scalar, in1], [out])

    def copy(self, out, in_, eng="dve"):
        if eng == "act":
            def fn(e):
                return e.copy(out.ap, in_.ap)
        else:
            def fn(e):
                return e.tensor_copy(out.ap, in_.ap)
        return self.add(eng, fn, [in_], [out])

    def memset(self, out, val, eng="dve"):
        def fn(e):
            return e.memset(out.ap, val)
        return self.add(eng, fn, [], [out])

    def reduce(self, out, in_, op, axis=AX.X, eng="dve"):
        def fn(e):
            return e.tensor_reduce(out.ap, in_.ap, axis, op)
        return self.add(eng, fn, [in_], [out])

    def recip(self, out, in_):
        def fn(e):
            return e.reciprocal(out.ap, in_.ap)
        return self.add("dve", fn, [in_], [out])

    def generic(self, eng, fn, reads, writes):
        return self.add(eng, fn, reads, writes)

    def finalize(self, out_keys=()):
        nc = self.nc
        ops = self.ops
        last_w = {}
        readers = {}
        for i, op in enumerate(ops):
            op.idx = i
            deps = set()
            for k in op.reads:
                w = last_w.get(k)
                if w is not None:
                    deps.add(w)
            for k in list(op.writes) + op.xreads:
                w = last_w.get(k)
                if w is not None:
                    deps.add(w)
                latest = {}
                for r in readers.get(k, ()):
                    ro = ops[r]
                    if ro.is_dma:
                        deps.add(r)
                    else:
                        latest[ro.eng] = r
                for r in latest.values():
                    deps.add(r)
            deps.discard(i)
            op.deps = sorted(deps)
            for k in op.reads:
                lst = readers.setdefault(k, [])
                if not op.is_dma:
                    lst[:] = [r for r in lst if ops[r].is_dma or ops[r].eng != op.eng]
                lst.append(i)
            for k in op.writes:
                last_w[k] = i
                readers[k] = []
            for k in op.xreads:
                readers[k] = [i]
        for op in ops:
            need = []
            for d in op.deps:
                p = ops[d]
                if p.is_dma:
                    need.append(d)
                    continue
                if p.eng == op.eng:
                    if op.is_dma:
                        need.append(d)
                        continue
                    if op.eng == "pe":
                        continue
                    raw = any(k in p.writes for k in op.reads)
                    if raw:
                        need.append(d)
                    continue
                need.append(d)
            op.deps = need
            for d in need:
                if not ops[d].is_dma:
                    ops[d].signal = True
        cnt = {e: 0 for e in ENGS}
        for op in ops:
            if op.is_dma:
                continue
            if op.signal:
                cnt[op.eng] += 1
                op.sigidx = cnt[op.eng]
        dcount = [0] * self.N_DSEM
        nd = 0
        for op in ops:
            if op.is_dma:
                op.dslot = nd % self.N_DSEM
                dcount[op.dslot] += 1
                op.dcnt = dcount[op.dslot]
                nd += 1
        self.n_dma = nd
        self.sig_counts = cnt
        return self

    def emit(self, stack):
        nc = self.nc
        ops = self.ops
        sems = {e: stack.enter_context(nc.semaphore("S_" + e)) for e in ENGS if e != "sp"}
        dsems = [stack.enter_context(nc.semaphore("D%d" % i)) for i in range(self.N_DSEM)]
        block = stack.enter_context(nc.Block())
        seen = {e: {x: 0 for x in ENGS} for e in ENGS}
        seen_d = {e: [0] * self.N_DSEM for e in ENGS}
        plan = {e: [] for e in ENGS}
        last_dma_on_slot = [None] * self.N_DSEM
        for op in ops:
            e = op.eng
            waits = []
            if op.is_dma:
                prev = last_dma_on_slot[op.dslot]
                if prev is not None and seen_d[e][op.dslot] < prev.dcnt * 16:
                    waits.append((dsems[op.dslot], prev.dcnt * 16))
                    seen_d[e][op.dslot] = prev.dcnt * 16
                last_dma_on_slot[op.dslot] = op
            for d in op.deps:
                p = ops[d]
                if p.is_dma:
                    v = p.dcnt * 16
                    if seen_d[e][p.dslot] < v:
                        waits.append((dsems[p.dslot], v))
                        seen_d[e][p.dslot] = v
                else:
                    v = p.sigidx
                    if seen[e][p.eng] < v:
                        waits.append((sems[p.eng], v))
                        seen[e][p.eng] = v
                        for x in ENGS:
                            if p.snap[x] > seen[e][x]:
                                seen[e][x] = p.snap[x]
            if not op.is_dma:
                snap = dict(seen[e])
                if op.signal:
                    snap[e] = max(snap[e], op.sigidx)
                op.snap = snap
            plan[e].append((waits, op))
        final_waits = []
        for s in range(self.N_DSEM):
            lp = last_dma_on_slot[s]
            if lp is not None:
                final_waits.append((dsems[s], lp.dcnt * 16))

        def run(engname, e):
            for waits, op in plan[engname]:
                for (s, v) in waits:
                    e.wait_ge(s, v)
                ins = op.fn(e)
                if op.is_dma:
                    ins.then_inc(dsems[op.dslot], 16)
                elif op.signal:
                    ins.then_inc(sems[op.eng], 1)

        @block.tensor
        def _(e):
            run("pe", e)

        @block.scalar
        def _(e):
            run("act", e)

        @block.vector
        def _(e):
            run("dve", e)

        @block.gpsimd
        def _(e):
            run("pool", e)

        @block.sync
        def _(e):
            run("sp", e)
            for (s, v) in final_waits:
                e.wait_ge(s, v)
        self.stats = {e: len(plan[e]) for e in ENGS}
        self.nwaits = {e: sum(len(w) for w, _ in plan[e]) for e in ENGS}


S = 2048
D = 1024
L = 4
NMEM = 256
DFF = 2816
ALPHA = (2.0 * L) ** 0.25
LN_EPS = 1e-5
NEG = -30000.0

CONST_LAYOUT = {}
_off = 0
for _n, _w in [("ident", 128), ("tri_le", 128), ("tri_lt", 64), ("negmask", 128), ("selneg", 4 * 128),
               ("ones", 128), ("sel_even", 64), ("sel_odd", 128), ("tri_gt", 64), ("blk64", 128), ("mask2", 128)]:
    CONST_LAYOUT[_n] = (_off, _w)
    _off += _w
NCONST = _off


def make_consts():
    c = np.zeros((128, NCONST), np.float32)
    def put(n, a):
        o, w = CONST_LAYOUT[n]
        c[:a.shape[0], o:o + w] = a
    i = np.arange(128)
    put("ident", np.eye(128, dtype=np.float32))
    put("tri_le", (i[:, None] <= i[None, :]).astype(np.float32))
    put("tri_lt", (i[:, None] < i[None, :]).astype(np.float32)[:, :64])
    put("tri_gt", (i[:, None] > i[None, :]).astype(np.float32)[:, :64])
    put("negmask", np.where(i[:, None] <= i[None, :], 0.0, NEG).astype(np.float32))
    sn = np.zeros((4, 4 * 128), np.float32)
    for h in range(4):
        sn[h, h * 128:(h + 1) * 128] = -1.0
    put("selneg", sn)
    put("ones", np.ones((128, 128), np.float32))
    se = np.zeros((65, 64), np.float32); se[64, :] = 1.0
    put("sel_even", se)
    so = np.zeros((128, 128), np.float32); so[0, 64:128] = 1.0
    put("sel_odd", so)
    put("blk64", ((i[:, None] // 64) == (i[None, :] // 64)).astype(np.float32) / 64.0)
    put("mask2", ((i[:, None] <= i[None, :]) & ((i[:, None] // 64) == (i[None, :] // 64))).astype(np.float32))
    return c


def col_layout():
    lay = {}
    off = 0
    def add(n, w):
        nonlocal off
        lay[n] = (off, w)
        off += w
    add("mem_ln_g", 8); add("mem_ln_b", 8)
    for n in ("ln1_g", "ln1_b", "ln2_g", "ln2_b", "ln3_g", "ln3_b"):
        add(n, 8)
    add("ffn_up_b", 44); add("ffn_conv_b", 44); add("ffn_conv", 132)
    add("fox_fb", 1)
    add("gla_a_b", 2)
    add("gla_norm_g", 2)
    add("rw_mu", 8)
    add("rw_mu64_", 16)
    add("rw_h64_", 4 * 9)
    for n in list(lay.keys()):
        for l in range(L):
            lay["%s%d" % (n, l)] = lay[n]
    return lay, off


COL_LAYOUT, NCOL = col_layout()


def chunkcols(v, p=128):
    return np.ascontiguousarray(v.reshape(-1, p).T)


def make_colp(inp):
    call = np.zeros((L, 128, NCOL), np.float32)
    for l in range(L):
        c = call[l]
        def put(n, a):
            o, w = COL_LAYOUT[n]
            assert a.shape[1] == w, (n, a.shape, w)
            c[:a.shape[0], o:o + w] = a
        put("mem_ln_g", chunkcols(inp["mem_ln_g"])); put("mem_ln_b", chunkcols(inp["mem_ln_b"]))
        for n in ("ln1_g", "ln1_b", "ln2_g", "ln2_b", "ln3_g", "ln3_b"):
            put(n, chunkcols(inp[n][l]))
        put("ffn_up_b", chunkcols(inp["ffn_up_b"][l]))
        put("ffn_conv_b", chunkcols(inp["ffn_conv_b"][l]))
        put("ffn_conv", np.concatenate([chunkcols(inp["ffn_conv"][l][j]) for j in range(3)], 1))
        put("fox_fb", inp["fox_fb"][l].reshape(4, 1))
        put("gla_a_b", chunkcols(inp["gla_a_b"][l], 64))
        put("gla_norm_g", chunkcols(inp["gla_norm_g"][l]))
        put("rw_mu", chunkcols(inp["rw_mu"][l]))
        put("rw_mu64_", chunkcols(inp["rw_mu"][l], 64))
        h64 = np.zeros((64, 36), np.float32)
        for h in range(4):
            sl = slice(h * 64, (h + 1) * 64)
            for j, nm in enumerate(["rw_w0", "rw_a0", "rw_kk", "rw_ka", None, "rw_lnx_g", "rw_lnx_b"]):
                if nm is not None:
                    h64[:, h * 9 + j] = inp[nm][l][sl]
            h64[:, h * 9 + 4] = inp["rw_rk"][l][h]
        put("rw_h64_", h64)
    return call


ROW_LAYOUT = {}
_off = 0
for _n, _w in [("sg_ln_g", 256), ("sg_ln_b", 256), ("sgb", 256)]:
    ROW_LAYOUT[_n] = (_off, _w)
    _off += _w
NROW = _off


def make_rowp(inp):
    r = np.zeros((L, 128, NROW), np.float32)
    for l in range(L):
        def put(n, a):
            o, w = ROW_LAYOUT[n]
            r[l, :, o:o + w] = a
        put("sg_ln_g", np.tile(inp["sg_ln_g"][l][None], (128, 1)))
        put("sg_ln_b", np.tile(inp["sg_ln_b"][l][None], (128, 1)))
        sgb = np.zeros((128, 2, 128), np.float32)
        for p in range(128):
            for pair in range(2):
                sgb[p, pair] = inp["sg_b"][l][pair * 2 + p // 64]
        put("sgb", sgb.reshape(128, 256))
    return r


class WV:
    def __init__(self, tile, kc, cols):
        self.tile = tile
        self.kc = kc
        self.cols = cols
        self.ap = tile.h[:, 0:kc * cols].rearrange("p (k c) -> p k c", k=kc)

    def __getitem__(self, idx):
        return V(self.ap[idx], (self.tile.name,))


class MK:
    def __init__(self, nc, nlayers=L, stages=("sg", "fox", "gla", "rw", "ln1", "ca", "ln2", "ffn", "ln3"), dbg=()):
        self.nc = nc
        self.nlayers = nlayers
        self.stages = stages
        self.dbg = dbg
        self.P = Prog(nc)
        self.dbg_out = {}

    def decl(self):
        nc = self.nc
        di = lambda n, shp: nc.dram_tensor(n, list(shp), F32, kind="ExternalInput").ap()
        self.h_xT = di("xT", [D, S])
        self.h_memT = di("memT", [D, NMEM])
        self.h_consts = di("consts", [128, NCONST])
        self.h_colp = di("colp", [L, 128, NCOL])
        self.h_rowp = di("rowp", [L, 128, NROW])
        self.h_w_in = di("w_in", [L, D, 3092])
        self.h_w_out = di("w_out", [L, D, D])
        self.h_sgwT = di("sgwT", [L, 128, 4, 128])
        self.h_gla_a_up = di("gla_a_up", [L, 16, 128])
        self.h_rw_w2 = di("rw_w2", [L, 64, 256])
        self.h_rw_a2 = di("rw_a2", [L, 64, 256])
        self.h_rw_g2 = di("rw_g2", [L, 128, 256])
        self.h_ca_wq = di("ca_wq", [L, D, D]); self.h_ca_wk = di("ca_wk", [L, D, D])
        self.h_ca_wv = di("ca_wv", [L, D, D]); self.h_ca_wo = di("ca_wo", [L, D, D])
        self.h_ffn_up = di("ffn_up", [L, D, 2 * DFF]); self.h_ffn_down = di("ffn_down", [L, DFF, D])
        self.h_outT = nc.dram_tensor("outT", [D, S], F32, kind="ExternalOutput").ap()

    def hv(self, ap, key):
        return V(ap, (key,))

    def alloc(self):
        P = self.P
        self.xT = P.sb("xT", [128, 8, S], F32)
        self.xb = P.sb("xb", [128, 8, S], BF16)
        self.cur_xb = None
        self.yT = P.sb("yT", [128, 2, S], BF16)
        self.consts = P.sb("consts", [128, NCONST], F32)
        self.colp = P.sb("colp", [128, NCOL], F32)
        self.ident_bf = P.sb("ident_bf", [128, 128], BF16)
        self.ones_s = P.sb("ones_s", [128, 128], BF16)
        self.blk64_bf = P.sb("blk64_bf", [128, 128], BF16)
        self.wb = [P.sb("wb%d" % i, [128, 4096], BF16) for i in range(2)]
        self.pb = [P.ps("pb%d" % i, [128, 512], F32) for i in range(8)]
        self.t512 = [P.sb("t512_%d" % i, [128, 512], F32) for i in range(5)]
        self.b512 = [P.sb("b512_%d" % i, [128, 512], BF16) for i in range(4)]
        self.st512 = [P.sb("st512_%d" % i, [128, 512], F32) for i in range(2)]
        self.scr = P.sb("scr", [128, 20480], BF16)
        self._ffh_i = 0
        self._ps_i = 0
        self._wb_i = 0
        self._wo_i = 0
        self._t_i = 0
        self._b_i = 0

    def load_xb(self, tb, eng="pool"):
        xb = self.xb
        class _B:
            def __getitem__(s_, idx):
                p, k, c = idx
                return V(xb.h[p, k, slice(tb * 512 + c.start, tb * 512 + c.stop)], (xb.name,))
        self.cur_xb = _B()
        return self.cur_xb

    def sub256(self, t):
        class _S:
            name = t.name
            class _H:
                def __getitem__(s2, idx):
                    p, c = idx
                    c = slice(c.start or 0, 256 if c.stop is None else c.stop)
                    return t.h[p, c]
            h = _H()
            def __getitem__(s_, idx):
                if not isinstance(idx, tuple):
                    idx = (idx, slice(0, 256))
                p, c = idx
                c = slice(c.start or 0, 256 if c.stop is None else c.stop)
                return V(t.h[p, c], (t.name,))
        return _S()

    def nextps(self):
        p = self.pb[self._ps_i % 6]
        self._ps_i += 1
        return p

    def nextacc(self):
        self._acc_i = getattr(self, "_acc_i", 0) + 1
        return self.pb[6 + self._acc_i % 2]

    def nt(self):
        t = self.t512[self._t_i % 5]
        self._t_i += 1
        return t

    def nf(self):
        return self.nt()

    def nb(self):
        t = self.b512[self._b_i % 4]
        self._b_i += 1
        return t

    def C(self, name, rows=128, c0=0, c1=None):
        o, w = CONST_LAYOUT[name]
        if c1 is None:
            c1 = w
        return self.consts[0:rows, o + c0:o + c1]

    def col(self, name, j, rows=128):
        o, w = COL_LAYOUT[name]
        return self.colp[0:rows, o + j:o + j + 1]

    def row(self, name, c0=0, c1=None):
        o, w = ROW_LAYOUT[name]
        if c1 is None:
            c1 = w
        return V(self.scr.h[:, :].bitcast(F32)[:, 2560 + o + c0:2560 + o + c1], ("sg_rowp",))

    def load_w(self, hbm_ap, kc, cols, key, ring="wb"):
        P = self.P
        t = self.wb[self._wb_i % 2]; self._wb_i += 1
        wv = WV(t, kc, cols)
        src = hbm_ap.rearrange("(k p) c -> p k c", p=128)
        step = max(1, (2048 // cols) if cols <= 2048 else 1)
        k = 0
        while k < kc:
            k2 = min(kc, k + step)
            P.dma(V(wv.ap[:, k:k2, :], (t.name,)), V(src[:, k:k2, :], (key,)), eng="pool")
            k = k2
        return wv

    def scr_phase(self, new_keys):
        old = getattr(self, "_scr_keys", [])
        if not hasattr(self, "_dummy"):
            self._dummy = self.P.sb("phase_dummy", [128, 8], F32)
        d = self._dummy
        self.P.add("pool", lambda e: e.memset(d.h[:, :], 0.0), [V(None, tuple(old))], [V(None, tuple(new_keys) + (d.name,))])
        self._scr_keys = list(new_keys)

    def debug_dump(self, name, view, shape):
        if name not in self.dbg:
            return
        h = self.nc.dram_tensor("dbg_" + name, list(shape), view.ap.dtype, kind="ExternalOutput").ap()
        self.P.dma(V(h, ("dbg_" + name,)), view)
        self.dbg_out[name] = shape

    def proj_fm(self, w, c0, M, tb, evac, xsrc=None, ncol=512):
        P = self.P
        ps = self.nextps()
        for kc in range(w.kc):
            if xsrc is None:
                rhs = self.cur_xb[:, kc, 0:ncol]
            else:
                rhs = xsrc[:, kc, tb * ncol:(tb + 1) * ncol]
            P.mm(ps[0:M, 0:ncol], w[:, kc, c0:c0 + M], rhs,
                 start=(kc == 0), stop=(kc == w.kc - 1))
        evac(ps)

    def proj_tm(self, w, c0, N, t0, evac, ntok=128):
        P = self.P
        ps = self.nextps()
        for kc in range(w.kc):
            P.mm(ps[0:ntok, 0:N], self.cur_xb[:, kc, (t0 % 512):(t0 % 512) + ntok], w[:, kc, c0:c0 + N],
                 start=(kc == 0), stop=(kc == w.kc - 1))
        evac(ps)

    def acc_out(self, hbm_w, yT, nck, key, first):
        P = self.P
        w = self.load_w(hbm_w, nck, 1024, key, ring="wb")
        for tb in range(4):
            for o in range(8):
                ps = self.nextps()
                for c in range(nck):
                    P.mm(ps[:, :], w[:, c, o * 128:(o + 1) * 128], yT[:, c, tb * 512:(tb + 1) * 512],
                         start=(c == 0), stop=(c == nck - 1))
                xv = self.xT[:, o, tb * 512:(tb + 1) * 512]
                if first:
                    P.stt(xv, xv, ALPHA, ps[:, :], ALU.mult, ALU.add)
                else:
                    P.tt(xv, xv, ps[:, :], ALU.add)

    def layernorm(self, gname, bname, src=None, dst32=None, dstb=None, ntok=S, eps=LN_EPS):
        P = self.P
        src = self.xT if src is None else src
        dst32 = self.xT if dst32 is None else (None if dst32 is False else dst32)
        dstb = self.xb if dstb is None else dstb
        nblk = (ntok + 511) // 512
        for tb in range(nblk):
            w = min(512, ntok - tb * 512)
            sl = slice(tb * 512, tb * 512 + w)
            psm = self.nextps(); psq = self.nextps()
            for c in range(8):
                xb_ = self.nb(); sq = self.nb()
                P.copy(xb_[:, 0:w], src[:, c, sl], eng="pool")
                P.act(sq[:, 0:w], src[:, c, sl], AF.Square)
                P.mm(psm[:, 0:w], self.ones_s[:, :], xb_[:, 0:w], start=(c == 0), stop=(c == 7))
                P.mm(psq[:, 0:w], self.ones_s[:, :], sq[:, 0:w], start=(c == 0), stop=(c == 7))
            mean = self.st512[0]; rstd = self.st512[1]
            P.copy(mean[:, 0:w], psm[:, 0:w])
            msq = self.nt()
            P.tt(msq[:, 0:w], mean[:, 0:w], mean[:, 0:w], ALU.mult)
            P.tt(msq[:, 0:w], psq[:, 0:w], msq[:, 0:w], ALU.subtract)
            P.act(msq[:, 0:w], msq[:, 0:w], AF.Ln, bias=eps)
            P.act(rstd[:, 0:w], msq[:, 0:w], AF.Exp, scale=-0.5)
            for c in range(8):
                u = self.nt()
                P.tt(u[:, 0:w], src[:, c, sl], mean[:, 0:w], ALU.subtract)
                P.stt(u[:, 0:w], u[:, 0:w], self.col(gname, c), rstd[:, 0:w], ALU.mult, ALU.mult)
                if dst32 is not None:
                    P.act(dst32[:, c, sl], u[:, 0:w], AF.Identity, bias=self.col(bname, c))
                if dstb is not None:
                    P.ts(dstb[:, c, sl], u[:, 0:w], self.col(bname, c), ALU.add, eng="pool")

    def mixer_sg(self, l):
        P = self.P
        w = self.load_w(self.h_w_in[l][:, 0:512], 8, 512, "h_w_in")
        if not hasattr(self, "sg_t"):
            self.sg_t = dict(
                wmT=P.sb("sg_wmT", [128, 4, 128], BF16),
                stat=[P.sb("sg_stat%d" % i, [128, 16], F32) for i in range(2)],
                vnp=[P.sb("sg_vnp%d" % i, [128, 2, 2, 128], BF16) for i in range(2)],
            )
            for v_ in self.sg_t["vnp"]:
                P.memset(v_[:], 0.0, eng="pool")
        self.scr_phase(["sg_u0", "sg_u1", "sg_wraw", "sg_rowp"])
        P.dma(V(self.scr.h[:, :].bitcast(F32)[:, 2560:2560 + NROW], ("sg_rowp",)), self.hv(self.h_rowp[l], "h_rowp"))
        scr32 = self.scr.h[:, :].bitcast(F32)
        class _C:
            def __init__(s_, ap, key):
                s_.ap = ap; s_.key = key
            def __getitem__(s_, idx):
                return V(s_.ap[idx], (s_.key,))
        usb = [_C(scr32[:, i * 1024:(i + 1) * 1024].rearrange("p (c t) -> p c t", c=2), "sg_u%d" % i) for i in range(2)]
        wraw = _C(scr32[:, 2048:2560].rearrange("p (h t) -> p h t", h=4), "sg_wraw")
        T_ = self.sg_t
        P.dma(wraw[:], self.hv(self.h_sgwT[l], "h_sgwT"))
        tri = V(self.C("tri_le").ap.unsqueeze(1).to_broadcast([128, 4, 128]), self.consts[:].keys)
        P.tt(T_["wmT"][:], wraw[:], tri, ALU.mult)
        for tb in range(4):
            self.load_xb(tb)
            u_sb = usb[tb % 2]
            for c in range(2):
                self.proj_fm(w, c * 128, 128, tb, lambda ps, c=c: P.copy(u_sb[:, c, :], ps[:, :], eng="act"))
            for n4 in range(4):
                n = tb * 4 + n4
                vs = self.sub256(self.nf()); sq = self.sub256(self.nf()); stat = T_["stat"][n % 2]; vnp = T_["vnp"][n % 2]
                def ev(ps):
                    P.copy(vs[:], ps[:, 0:256], eng="act")
                    P.act(sq[:], ps[:, 0:256], AF.Square)
                self.proj_tm(w, 256, 256, n * 128, ev)
                v3 = V(vs.h[:, :].rearrange("p (h d) -> p h d", h=4), vs[:].keys)
                q3 = V(sq.h[:, :].rearrange("p (h d) -> p h d", h=4), sq[:].keys)
                P.reduce(stat[:, 0:4], v3, ALU.add)
                P.reduce(stat[:, 4:8], q3, ALU.add)
                P.ts(stat[:, 0:4], stat[:, 0:4], 1.0 / 64, ALU.mult)
                P.tt(stat[:, 8:12], stat[:, 0:4], stat[:, 0:4], ALU.mult)
                P.stt(stat[:, 4:8], stat[:, 4:8], 1.0 / 64, stat[:, 8:12], ALU.mult, ALU.subtract)
                P.act(stat[:, 4:8], stat[:, 4:8], AF.Ln, bias=LN_EPS)
                P.act(stat[:, 12:16], stat[:, 4:8], AF.Exp, scale=-0.5)
                for h in range(4):
                    P.ts(sq[:, h * 64:(h + 1) * 64], vs[:, h * 64:(h + 1) * 64], stat[:, h:h + 1], ALU.subtract,
                         stat[:, 12 + h:13 + h], ALU.mult)
                P.tt(sq[:], sq[:], self.row("sg_ln_g"), ALU.mult)
                s4 = sq.h[:, :].rearrange("p (a b d) -> p a b d", a=2, b=2)
                rb = self.row("sg_ln_b").ap.rearrange("p (a b d) -> p a b d", a=2, b=2)
                for hh in range(2):
                    P.tt(V(vnp.h[:, :, hh, hh * 64:(hh + 1) * 64], vnp[:].keys), V(s4[:, :, hh, :], sq[:].keys),
                         V(rb[:, :, hh, :], ("sg_rowp",)), ALU.add)
                for pair in range(2):
                    ps = self.nextps()
                    for hh in range(2):
                        P.mm(ps[:, 0:128], V(vnp.h[:, pair, hh, :], vnp[:].keys), T_["wmT"][:, pair * 2 + hh, :],
                             start=(hh == 0), stop=(hh == 1))
                    t2 = self.nf()
                    P.tt(t2[:, 0:128], ps[:, 0:128], self.row("sgb", pair * 128, (pair + 1) * 128), ALU.add)
                    P.tt(self.yT[:, pair, n * 128:(n + 1) * 128], t2[:, 0:128], u_sb[:, pair, n4 * 128:(n4 + 1) * 128], ALU.mult)

    def mixer_fox(self, l):
        P = self.P
        w = self.load_w(self.h_w_in[l][:, 2320:2832], 8, 512, "h_w_in")
        w2 = self.load_w(self.h_w_in[l][:, 2832:3092], 8, 260, "h_w_in")
        if not hasattr(self, "fx"):
            self.fx = dict(
                negcT=P.sb("fx_negcT", [128, 16, 4], F32),
                nfb=P.sb("fx_nfb", [4, 1], F32),
            )
        F = self.fx
        scr = self.scr
        qT = V(scr.h[:, 0:4096].rearrange("p (c t) -> p c t", c=2), ("fx_qT",))
        kT = V(scr.h[:, 4096:8192].rearrange("p (c t) -> p c t", c=2), ("fx_kT",))
        v1ap = scr.h[:, 8192:16384].rearrange("p (n h m) -> p n h m", n=16, h=4)
        v1 = lambda idx: V(v1ap[idx], ("fx_v1",))
        self.scr_phase(["fx_qT", "fx_kT", "fx_v1", "fx_negc"])
        negc_ap = scr.h[:, 16384:20480].bitcast(F32)
        class _N:
            def __getitem__(s_, idx):
                return V(negc_ap[idx], ("fx_negc",))
        F["negc"] = _N(); F["nlf"] = F["negc"]
        P.memset(v1((slice(None),)), 0.0, eng="pool")
        for n in range(16):
            for h in range(4):
                col = 64 if h % 2 == 0 else 0
                P.memset(v1((slice(None), n, h, slice(col, col + 1))), 1.0, eng="pool")
        P.ts(F["nfb"][:], self.col("fox_fb%d" % l, 0, rows=4), -1.0, ALU.mult)
        for tb in range(4):
            self.load_xb(tb)
            sl = slice(tb * 512, (tb + 1) * 512)
            for c in range(2):
                self.proj_fm(w, c * 128, 128, tb,
                             lambda ps, c=c: P.act(V(qT.ap[:, c, sl], qT.keys), ps[:, :], AF.Copy, scale=0.125))
                self.proj_fm(w, 256 + c * 128, 128, tb,
                             lambda ps, c=c: P.copy(V(kT.ap[:, c, sl], kT.keys), ps[:, :], eng="dve"))
            def evf(ps):
                P.act(F["nlf"][0:4, sl], ps[0:4, :], AF.Exp, bias=F["nfb"][:, 0:1], scale=-1.0)
                P.act(F["nlf"][0:4, sl], F["nlf"][0:4, sl], AF.Ln, bias=1.0)
            self.proj_fm(w2, 256, 4, tb, evf)
            for n4 in range(4):
                n = tb * 4 + n4
                def evv(ps, n=n):
                    for h in range(4):
                        col = 0 if h % 2 == 0 else 64
                        P.copy(v1((slice(None), n, h, slice(col, col + 64))), ps[:, h * 64:(h + 1) * 64],
                               eng=("act" if h % 2 else "dve"))
                self.proj_tm(w2, 0, 256, n * 128, evv)
        ones_b = self.C("ones", rows=4, c0=0, c1=1).ap.to_broadcast([4, S])
        P.generic("dve", lambda e: e.tensor_tensor_scan(negc_ap[0:4, :], ones_b, negc_ap[0:4, :], 0.0,
                                                         ALU.mult, ALU.add),
                  [self.consts[:], F["negc"][0:4, :]], [F["negc"][0:4, :]])
        self.debug_dump("fox_negc", F["negc"][0:4, :], [4, S])
        for J in range(16):
            ps = self.nextps()
            P.transpose(ps[:, 0:4], F["negc"][0:4, J * 128:(J + 1) * 128], self.C("ident", rows=4, c0=0, c1=4))
            P.copy(F["negcT"][:, J, :], ps[:, 0:4])
        for h in range(4):
            c = h // 2
            p0 = (h % 2) * 64
            even = (h % 2 == 0)
            for Q in range(4):
                ops_ = self.nextacc()
                nJ = 4 * Q + 4
                for J in range(nJ):
                    c_lo = max(0, (J - 4 * Q) * 128)
                    lg = self.nextps()
                    P.mm(lg[:, c_lo:512], V(kT.ap[p0:p0 + 64, c, J * 128:(J + 1) * 128], kT.keys),
                         V(qT.ap[p0:p0 + 64, c, Q * 512 + c_lo:(Q + 1) * 512], qT.keys), start=True, stop=False)
                    P.mm(lg[:, c_lo:512], self.C("selneg", rows=4, c0=h * 128, c1=(h + 1) * 128),
                         F["negc"][0:4, Q * 512 + c_lo:(Q + 1) * 512], start=False, stop=True)
                    if J >= 4 * Q:
                        P.tt(lg[:, c_lo:c_lo + 128], lg[:, c_lo:c_lo + 128], self.C("negmask"), ALU.add)
                    pT = self.nb()
                    P.act(pT[:, c_lo:512], lg[:, c_lo:512], AF.Exp, bias=F["negcT"][:, J, h:h + 1])
                    M = 65 if even else 128
                    P.mm(ops_[0:M, c_lo:512], v1((slice(None), J, h, slice(0, M))), pT[:, c_lo:512],
                         start=(J == 0), stop=(J == nJ - 1))
                osb = self.nt()
                if even:
                    P.copy(osb[0:65, :], ops_[0:65, :], eng="act")
                    P.recip(osb[64:65, :], osb[64:65, :])
                    bp = self.nextps()
                    P.mm(bp[0:64, :], self.C("sel_even", rows=65), osb[0:65, :])
                    P.tt(self.yT[0:64, c, Q * 512:(Q + 1) * 512], osb[0:64, :], bp[0:64, :], ALU.mult)
                else:
                    P.copy(osb[:, :], ops_[:, :], eng="act")
                    P.recip(osb[0:1, :], osb[0:1, :])
                    bp = self.nextps()
                    P.mm(bp[:, :], self.C("sel_odd"), osb[:, :])
                    P.tt(self.yT[64:128, c, Q * 512:(Q + 1) * 512], osb[64:128, :], bp[64:128, :], ALU.mult)


    def mixer_gla(self, l):
        P = self.P
        w1 = self.load_w(self.h_w_in[l][:, 1536:2048], 8, 512, "h_w_in")
        w2 = self.load_w(self.h_w_in[l][:, 2048:2320], 8, 272, "h_w_in")
        if not hasattr(self, "gl"):
            self.gl = dict(
                aup=P.sb("gl_aup", [16, 128], BF16),
                nab=P.sb("gl_nab", [64, 2], F32),
                gps=P.sb("gl_gps", [64, 32], F32),
                bl=P.sb("gl_bl", [64, 32], F32),
                dec=P.sb("gl_dec", [64, 2, 32], F32),
                S=[P.sb("gl_S%d" % g, [64, 128], F32) for g in range(2)],
                Sbf=[[P.sb("gl_Sbf%d_%d" % (g, i), [64, 128], BF16) for i in range(4)] for g in range(2)],
                osb=[P.sb("gl_osb%d" % i, [128, 128], F32) for i in range(2)],
            )
        G = self.gl
        scr = self.scr
        self.scr_phase(["gl_qe", "gl_ke", "gl_v", "gl_ketm", "gl_G", "gl_adT"])
        class _C:
            def __init__(s_, ap, key):
                s_.ap = ap; s_.key = key
            def __getitem__(s_, idx):
                return V(s_.ap[idx], (s_.key,))
        qe = _C(scr.h[:, 0:4096].rearrange("p (g t) -> p g t", g=2), "gl_qe")
        ke = _C(scr.h[:, 4096:8192].rearrange("p (g t) -> p g t", g=2), "gl_ke")
        v128 = _C(scr.h[:, 8192:12288].rearrange("p (b c) -> p b c", b=16), "gl_v")
        ketm = _C(scr.h[:, 12288:14336].rearrange("p (b c) -> p b c", b=16), "gl_ketm")
        Gt = _C(scr.h[:, 14336:18432].bitcast(F32), "gl_G")
        Gt3 = _C(scr.h[:, 14336:18432].bitcast(F32).rearrange("p (n c) -> p n c", n=32), "gl_G")
        adT = _C(scr.h[:, 18432:20480], "gl_adT")
        P.dma(G["aup"][:], self.hv(self.h_gla_a_up[l], "h_gla_a_up"), eng="pool")
        P.ts(G["nab"][:], V(self.colp.h[0:64, COL_LAYOUT["gla_a_b%d" % l][0]:COL_LAYOUT["gla_a_b%d" % l][0] + 2], self.colp[:].keys),
             -1.0, ALU.mult)
        for tb in range(4):
            self.load_xb(tb)
            sl = slice(tb * 512, (tb + 1) * 512)
            self.proj_fm(w2, 256, 16, tb, lambda ps: P.copy(adT[0:16, sl], ps[0:16, :], eng="act"))
            for c in range(2):
                def evg(ps, c=c):
                    t = self.nt()
                    P.act(t[:], ps[:, :], AF.Silu)
                    P.ts(self.yT[:, c, sl], t[:], self.col("gla_norm_g%d" % l, c), ALU.mult)
                self.proj_fm(w2, c * 128, 128, tb, evg)
            for b4 in range(4):
                bk = tb * 4 + b4
                self.proj_tm(w1, 256, 256, bk * 128, lambda ps, bk=bk: P.copy(v128[:, bk, :], ps[:, 0:256], eng="act"))
        import os
        stop = int(os.environ.get('GLA_STOP', '99'))
        if stop <= 1:
            return
        ones_b = self.C("ones", rows=64, c0=0, c1=1).ap.to_broadcast([64, S])
        for g in range(2):
            for tb in range(4):
                sl = slice(tb * 512, (tb + 1) * 512)
                ps = self.nextps()
                P.mm(ps[0:64, :], G["aup"][:, g * 64:(g + 1) * 64], adT[0:16, sl])
                P.act(Gt[0:64, sl], ps[0:64, :], AF.Exp, bias=G["nab"][:, g:g + 1], scale=-1.0)
                P.act(Gt[0:64, sl], Gt[0:64, sl], AF.Ln, bias=1.0)
            P.generic("dve", lambda e: e.tensor_tensor_scan(Gt.ap[0:64, :], ones_b, Gt.ap[0:64, :], 0.0, ALU.mult, ALU.add),
                      [self.consts[:], Gt[0:64, :]], [Gt[0:64, :]])
            if stop <= 2:
                continue
            P.memset(G["gps"][:, 0:1], 0.0)
            P.copy(G["gps"][:, 1:32], Gt3[0:64, 0:31, 63])
            P.tt(Gt3[0:64, :, :], Gt3[0:64, :, :], V(G["gps"].h[:, :].unsqueeze(2).to_broadcast([64, 32, 64]), G["gps"][:].keys),
                 ALU.subtract)
            P.copy(G["bl"][:], Gt3[0:64, :, 63])
            P.act(G["dec"][:, g, :], G["bl"][:], AF.Exp, scale=-1.0 / 16)
            for tb in range(4):
                sl = slice(tb * 512, (tb + 1) * 512)
                self.load_xb(tb)
                Eb = self.nt(); Enb = self.nt()
                P.act(Eb[0:64, :], Gt[0:64, sl], AF.Exp, scale=-1.0 / 16)
                P.act(Enb[0:64, :], Gt[0:64, sl], AF.Exp, scale=1.0 / 16)
                self.proj_fm(w1, g * 64, 64, tb,
                             lambda ps: P.stt(qe[0:64, g, sl], ps[0:64, :], 32.0 ** -0.5, Eb[0:64, :], ALU.mult, ALU.mult))
                self.proj_fm(w1, 128 + g * 64, 64, tb,
                             lambda ps: P.tt(ke[0:64, g, sl], ps[0:64, :], Enb[0:64, :], ALU.mult))
            if stop <= 3:
                continue
            for bk in range(16):
                ps = self.nextps()
                pbf = V(ps.h[:, :].bitcast(BF16), ps[:].keys)
                P.transpose(V(pbf.ap[:, 0:64], pbf.keys), ke[0:64, g, bk * 128:(bk + 1) * 128], self.ident_bf[0:64, 0:64])
                P.copy(ketm[:, bk, g * 64:(g + 1) * 64], V(pbf.ap[:, 0:64], pbf.keys), eng="act")
        self.debug_dump("gl_qe", qe[0:64, :, :], [64, 2, S])
        self.debug_dump("gl_ke", ke[0:64, :, :], [64, 2, S])
        if stop <= 4:
            return
        for g in range(2):
            P.memset(G["S"][g][:], 0.0)
            P.memset(G["Sbf"][g][0][:], 0.0)
        for bk in range(16):
            for g in range(2):
                attm = []
                for hh in range(2):
                    ps = self.nextps()
                    P.mm(ps[:, 0:128], ke[hh * 32:hh * 32 + 32, g, bk * 128:(bk + 1) * 128],
                         qe[hh * 32:hh * 32 + 32, g, bk * 128:(bk + 1) * 128])
                    am = self.nb()
                    P.tt(am[:, 0:128], ps[:, 0:128], self.C("mask2"), ALU.mult)
                    attm.append(am)
                if stop <= 5:
                    continue
                for cc in range(2):
                    n = 2 * bk + cc
                    if n == 31:
                        break
                    psU = self.nextps()
                    r0 = cc * 64
                    P.mm(psU[0:64, 0:128], ketm[r0:r0 + 64, bk, g * 64:(g + 1) * 64], v128[r0:r0 + 64, bk, g * 128:(g + 1) * 128])
                    P.tt(G["S"][g][:], G["S"][g][:], psU[0:64, 0:128], ALU.add)
                    P.ts(G["S"][g][:], G["S"][g][:], G["dec"][:, g, n:n + 1], ALU.mult)
                    P.copy(G["Sbf"][g][(n + 1) % 4][:], G["S"][g][:], eng="act")
                if stop <= 6:
                    continue
                psO = self.nextps()
                for hh in range(2):
                    c0 = hh * 128
                    P.mm(psO[:, c0:c0 + 128], v128[:, bk, g * 128:(g + 1) * 128], attm[hh][:, 0:128], start=True, stop=False)
                    for cc in range(2):
                        n = 2 * bk + cc
                        P.mm(psO[:, c0 + cc * 64:c0 + (cc + 1) * 64], G["Sbf"][g][n % 4][hh * 32:hh * 32 + 32, :],
                             qe[hh * 32:hh * 32 + 32, g, n * 64:(n + 1) * 64], start=False, stop=(cc == 1))
                if stop <= 7:
                    continue
                osb = G["osb"][g]
                osq = self.nb()
                for hh in range(2):
                    r0 = hh * 64
                    var = os.environ.get('GLA_VAR', 'ab')
                    if 'a' in var:
                        P.copy(osb[r0:r0 + 64, :], psO[r0:r0 + 64, hh * 128:(hh + 1) * 128], eng="dve")
                    if 'b' in var:
                        P.act(osq[r0:r0 + 64, 0:128], psO[r0:r0 + 64, hh * 128:(hh + 1) * 128], AF.Square)
                if stop <= 8:
                    continue
                pss = self.nextps()
                P.mm(pss[:, 0:128], self.blk64_bf[:, :], osq[:, 0:128])
                if stop <= 9:
                    continue
                rs = self.nf()
                P.act(rs[:, 0:128], pss[:, 0:128], AF.Ln, bias=1e-5)
                P.act(rs[:, 0:128], rs[:, 0:128], AF.Exp, scale=-0.5)
                if stop <= 10:
                    continue
                P.tt(osb[:, :], osb[:, :], rs[:, 0:128], ALU.mult)
                if stop <= 11:
                    continue
                yv = self.yT[:, g, bk * 128:(bk + 1) * 128]
                P.tt(yv, osb[:, :], yv, ALU.mult)


    def mixer_rw(self, l):
        P = self.P
        wA = self.load_w(self.h_w_in[l][:, 512:1024], 8, 512, "h_w_in")
        wB = self.load_w(self.h_w_in[l][:, 1024:1536], 8, 512, "h_w_in")
        if not hasattr(self, "rw"):
            self.rw = dict(
                sm=P.sb("rw_sm", [128, 512], BF16),
                omm=P.sb("rw_omm", [128, 24], F32),
                nwa=P.sb("rw_nwa", [64, 8], F32),
                carry=P.sb("rw_carry", [128, 8], F32),
                maskq=P.sb("rw_maskq", [64, 128], F32),
                PC=P.sb("rw_PC", [64, 32], F32),
                gs=P.sb("rw_gs", [64, 4], F32),
                S32=P.sb("rw_S32", [64, 64], F32),
                Sb=[P.sb("rw_Sb%d" % i, [64, 64], BF16) for i in range(2)],
                Nn=[P.sb("rw_Nn%d" % i, [64, 64], BF16) for i in range(2)],
                XN=[[P.sb("rw_XN%d_%d" % (i, j), [64, 128], BF16) for j in range(2)] for i in range(2)],
                Wt=[[P.sb("rw_Wt%d_%d" % (i, j), [64, 64], BF16) for j in range(2)] for i in range(2)],
                Wf=[P.sb("rw_Wf%d" % i, [64, 64], BF16) for i in range(4)],
                RU=[P.sb("rw_RU%d" % i, [64, 128], BF16) for i in range(2)],
            )
            P.copy(self.rw["maskq"][:, 0:64], self.C("tri_le", rows=64, c0=0, c1=64))
            P.copy(self.rw["maskq"][:, 64:128], self.C("tri_lt", rows=64, c0=0, c1=64))
        R = self.rw
        scr = self.scr
        self.scr_phase(["rw_twa", "rw_sgd", "rw_KB", "rw_RA", "rw_VT", "rw_bv"])
        class _C:
            def __init__(s_, ap, key):
                s_.ap = ap; s_.key = key
            def __getitem__(s_, idx):
                return V(s_.ap[idx], (s_.key,))
        twa = _C(scr.h[:, 0:2048], "rw_twa")
        sgd = _C(scr.h[:, 2048:4096], "rw_sgd")
        KB = _C(scr.h[:, 4096:8192].rearrange("p (n c) -> p n c", n=32), "rw_KB")
        RA = _C(scr.h[:, 8192:12288].rearrange("p (n c) -> p n c", n=32), "rw_RA")
        VT = _C(scr.h[:, 12288:14336], "rw_VT")
        bv = _C(scr.h[:, 14336:16384], "rw_bv")
        y1 = self.yT.h[0:64, 1, :]
        class _A:
            def __init__(s_, c0, w, key):
                s_.c0 = c0; s_.w = w; s_.key = key
            def __getitem__(s_, idx):
                p, c = idx
                c = slice(s_.c0 + (c.start or 0), s_.c0 + (s_.w if c.stop is None else c.stop))
                return V(y1[:, c], (s_.key,))
        RQA = [_A(i * 128, 128, "rwq_QA%d" % i) for i in range(4)]
        RQB = [_A(512 + i * 128, 128, "rwq_QB%d" % i) for i in range(4)]
        RTM = [_A(1024 + i * 192, 192, "rwq_TM%d" % i) for i in range(4)]
        ring_keys = tuple(x.key for x in RQA + RQB + RTM)
        P.add("pool", lambda e: e.memset(self._dummy.h[:, :], 0.0), [V(None, ("yT",))], [V(None, ring_keys + (self._dummy.name,))])
        P.dma(R["sm"][0:64, 0:256], self.hv(self.h_rw_w2[l], "h_rw_w2"), eng="pool")
        P.dma(R["sm"][64:128, 0:256], self.hv(self.h_rw_a2[l], "h_rw_a2"), eng="pool")
        P.dma(R["sm"][:, 256:512], self.hv(self.h_rw_g2[l], "h_rw_g2"), eng="pool")
        o_mu, _ = COL_LAYOUT["rw_mu%d" % l]; o_mu64, _ = COL_LAYOUT["rw_mu64_%d" % l]
        mu128 = lambda c: self.colp[:, o_mu + c:o_mu + c + 1]
        mu64 = lambda c: self.colp[0:64, o_mu64 + c:o_mu64 + c + 1]
        P.ts(R["omm"][:, 0:8], self.colp[:, o_mu:o_mu + 8], -1.0, ALU.mult, 1.0, ALU.add)
        P.ts(R["omm"][0:64, 8:24], self.colp[0:64, o_mu64:o_mu64 + 16], -1.0, ALU.mult, 1.0, ALU.add)
        o_h, _ = COL_LAYOUT["rw_h64_%d" % l]
        hcol = lambda h, j: self.colp[0:64, o_h + h * 9 + j:o_h + h * 9 + j + 1]
        for h in range(4):
            P.ts(R["nwa"][:, 2 * h:2 * h + 1], hcol(h, 0), -1.0, ALU.mult)
            P.ts(R["nwa"][:, 2 * h + 1:2 * h + 2], hcol(h, 1), -1.0, ALU.mult)
        pool_tiles = self.t512 + self.st512
        def tmp(i, rows=64):
            t = pool_tiles[i // 2]
            c0 = (i % 2) * 256
            class _T:
                name = t.name
                def __getitem__(s_, idx):
                    if not isinstance(idx, tuple):
                        idx = (idx, slice(0, 256))
                    p, c = idx
                    c = slice(c0 + (c.start or 0), c0 + (256 if c.stop is None else c.stop))
                    return V(t.h[p, c], (t.name,))
                def v3(s_, rows_):
                    return V(t.h[0:rows_, c0:c0 + 256].rearrange("p (n c) -> p n c", n=4), (t.name,))
            return _T()

        def shiftmix(ps, M, mu_ap, omm_ap, cslot, out_t, blk):
            zr = pool_tiles[6]
            if blk == 0:
                P.memset(zr[0:M, 0:1], 0.0)
            else:
                P.copy(zr[0:M, 0:1], R["carry"][0:M, cslot:cslot + 1])
            P.copy(zr[0:M, 1:257], ps[0:M, 0:256], eng="act")
            P.copy(R["carry"][0:M, cslot:cslot + 1], zr[0:M, 256:257])
            P.ts(out_t[0:M, :], zr[0:M, 1:257], omm_ap, ALU.mult)
            P.stt(out_t[0:M, :], zr[0:M, 0:256], mu_ap, out_t[0:M, :], ALU.mult, ALU.add)

        class _XB:
            def __init__(s_, xb, t0):
                s_.xb = xb; s_.t0 = t0
            def __getitem__(s_, idx):
                p, k, c = idx
                return V(s_.xb.h[p, k, slice(s_.t0 + c.start, s_.t0 + c.stop)], (s_.xb.name,))

        for blk in range(8):
            self.cur_xb = _XB(self.xb, blk * 256)
            sl = slice(blk * 256, (blk + 1) * 256)
            z = tmp(0)
            self.proj_fm(wB, 256, 128, 0, lambda ps: shiftmix(ps, 128, mu128(6), R["omm"][:, 6:7], 0, z, blk), ncol=256)
            P.act(twa[0:64, sl], z[0:64, :], AF.Tanh)
            P.copy(twa[64:128, sl], z[64:128, :], eng="pool")
            z2 = tmp(1)
            self.proj_fm(wB, 384, 128, 0, lambda ps: shiftmix(ps, 128, mu128(7), R["omm"][:, 7:8], 1, z2, blk), ncol=256)
            P.act(z2[:, :], z2[:, :], AF.Exp, scale=-1.0)
            P.ts(z2[:, :], z2[:, :], 1.0, ALU.add)
            P.recip(z2[:, :], z2[:, :])
            P.copy(sgd[:, sl], z2[:, :], eng="pool")
        import os
        rstop = int(os.environ.get("RW_STOP", "99"))
        nheads = int(os.environ.get("RW_HEADS", "4"))
        for h in range(nheads):
            for blk in range(8):
                self.cur_xb = _XB(self.xb, blk * 256)
                sl = slice(blk * 256, (blk + 1) * 256)
                n0 = blk * 4
                zr_ = tmp(0); zk_ = tmp(1); zv_ = tmp(2)
                self.proj_fm(wA, h * 64, 64, 0, lambda ps: shiftmix(ps, 64, mu64(h), R["omm"][0:64, 8 + h:9 + h], 2, zr_, blk), ncol=256)
                self.proj_fm(wA, 256 + h * 64, 64, 0, lambda ps: shiftmix(ps, 64, mu64(4 + h), R["omm"][0:64, 12 + h:13 + h], 3, zk_, blk), ncol=256)
                self.proj_fm(wB, h * 64, 64, 0, lambda ps: shiftmix(ps, 64, mu64(8 + h), R["omm"][0:64, 16 + h:17 + h], 4, zv_, blk), ncol=256)
                P.copy(VT[0:64, sl], zv_[0:64, :], eng="pool")
                LD = tmp(3)
                ps = self.nextps()
                P.mm(ps[0:64, 0:256], R["sm"][0:64, h * 64:(h + 1) * 64], twa[0:64, sl])
                P.act(LD[0:64, :], ps[0:64, 0:256], AF.Exp, bias=R["nwa"][:, 2 * h:2 * h + 1], scale=-1.0)
                P.ts(LD[0:64, :], LD[0:64, :], 1.0, ALU.add)
                P.recip(LD[0:64, :], LD[0:64, :])
                P.ts(LD[0:64, :], LD[0:64, :], -0.6065306597126334, ALU.mult)
                A = tmp(4)
                ps = self.nextps()
                P.mm(ps[0:64, 0:256], R["sm"][64:128, h * 64:(h + 1) * 64], twa[64:128, sl])
                P.act(A[0:64, :], ps[0:64, 0:256], AF.Exp, bias=R["nwa"][:, 2 * h + 1:2 * h + 2], scale=-1.0)
                P.ts(A[0:64, :], A[0:64, :], 1.0, ALU.add)
                P.recip(A[0:64, :], A[0:64, :])
                KK = tmp(5)
                P.ts(KK[0:64, :], zk_[0:64, :], hcol(h, 2), ALU.mult)
                sq = self.nb()
                P.act(sq[0:64, 0:256], KK[0:64, :], AF.Square)
                ps = self.nextps()
                P.mm(ps[0:64, 0:256], self.blk64_bf[0:64, 0:64], sq[0:64, 0:256])
                RN = tmp(10)
                P.act(RN[0:64, :], ps[0:64, 0:256], AF.Ln, bias=1e-24, scale=64.0)
                P.act(RN[0:64, :], RN[0:64, :], AF.Exp, scale=-0.5)
                P.tt(KK[0:64, :], KK[0:64, :], RN[0:64, :], ALU.mult)
                K2 = tmp(6)
                P.ts(K2[0:64, :], A[0:64, :], -1.0, ALU.add, hcol(h, 3), ALU.mult)
                P.stt(K2[0:64, :], K2[0:64, :], 1.0, zk_[0:64, :], ALU.add, ALU.mult)
                KKA = tmp(7)
                P.tt(KKA[0:64, :], KK[0:64, :], A[0:64, :], ALU.mult)
                pr = self.nb()
                P.stt(pr[0:64, 0:256], zr_[0:64, :], hcol(h, 4), K2[0:64, :], ALU.mult, ALU.mult)
                ps = self.nextps()
                P.mm(ps[0:64, 0:256], self.blk64_bf[0:64, 0:64], pr[0:64, 0:256])
                P.stt(bv[0:64, sl], ps[0:64, 0:256], 64.0, zv_[0:64, :], ALU.mult, ALU.mult)
                Gb = tmp(8)
                ones_b = self.C("ones", rows=64, c0=0, c1=1).ap.to_broadcast([64, 256])
                P.generic("dve", lambda e, Gb=Gb, LD=LD: e.tensor_tensor_scan(Gb[0:64, :].ap, ones_b, LD[0:64, :].ap, 0.0, ALU.mult, ALU.add),
                          [self.consts[:], LD[0:64, :]], [Gb[0:64, :]])
                P.memset(R["gs"][:, 0:1], 0.0)
                G3 = Gb.v3(64)
                P.copy(R["gs"][:, 1:4], V(G3.ap[:, 0:3, 63], G3.keys))
                P.tt(G3, G3, V(R["gs"].h[:, :].unsqueeze(2).to_broadcast([64, 4, 64]), R["gs"][:].keys), ALU.subtract)
                P.act(R["PC"][:, n0:n0 + 4], V(G3.ap[:, :, 63], G3.keys), AF.Exp)
                Ep = tmp(10); Em = tmp(11); Epm1 = tmp(9)
                P.act(Ep[0:64, :], Gb[0:64, :], AF.Exp)
                P.act(Em[0:64, :], Gb[0:64, :], AF.Exp, scale=-1.0)
                P.tt(Epm1[0:64, :], Gb[0:64, :], LD[0:64, :], ALU.subtract)
                P.act(Epm1[0:64, :], Epm1[0:64, :], AF.Exp)
                P.tt(V(RA.ap[0:64, n0:n0 + 4, 0:64], ("rw_RA",)), zr_.v3(64), Ep.v3(64), ALU.mult)
                P.stt(V(RA.ap[0:64, n0:n0 + 4, 64:128], ("rw_RA",)), KK.v3(64), -1.0, Epm1.v3(64), ALU.mult, ALU.mult)
                P.tt(V(KB.ap[0:64, n0:n0 + 4, 0:64], ("rw_KB",)), K2.v3(64), Em.v3(64), ALU.mult)
                P.tt(V(KB.ap[0:64, n0:n0 + 4, 64:128], ("rw_KB",)), KKA.v3(64), Em.v3(64), ALU.mult)
            if h == 0:
                self.debug_dump("rw_RA", RA[0:64, :, :], [64, 32, 128])
                self.debug_dump("rw_KB", KB[0:64, :, :], [64, 32, 128])
                self.debug_dump("rw_PC", R["PC"][:, :], [64, 32])
            if rstop <= 1:
                continue
            P.memset(R["S32"][:], 0.0)
            P.memset(R["Sb"][0][:], 0.0)
            self._psY = None
            BS = 2
            def slot(n):
                return n % (2 * BS)
            def pre_rounds(ns):
                rounds = []
                def r0():
                    for n in ns:
                        s4 = slot(n)
                        QA = RQA[s4]; QB = RQB[s4]; TM = RTM[s4]; Nn = R["Nn"][n % BS]
                        Kt = KB[0:64, n, 0:64]; Bt = KB[0:64, n, 64:128]; At = RA[0:64, n, 64:128]
                        ps = self.nextps()
                        P.mm(ps[0:64, 0:128], Kt, RA[0:64, n, :])
                        P.mm(ps[0:64, 128:256], Bt, RA[0:64, n, :])
                        P.mm(ps[0:64, 256:320], At, Bt)
                        P.tt(QA[:, :], ps[0:64, 0:128], R["maskq"][:, :], ALU.mult)
                        P.tt(QB[:, :], ps[0:64, 128:256], R["maskq"][:, :], ALU.mult)
                        P.tt(Nn[:, :], ps[0:64, 256:320], self.C("tri_gt", rows=64, c0=0, c1=64), ALU.mult)
                        ps2 = self.nextps()
                        pbf = lambda a_, b_, ps2=ps2: V(ps2.h[0:64, :].bitcast(BF16)[:, a_:b_], ps2[:].keys)
                        P.transpose(pbf(0, 64), Kt, self.ident_bf[0:64, 0:64])
                        P.transpose(pbf(64, 128), Bt, self.ident_bf[0:64, 0:64])
                        P.transpose(pbf(128, 192), VT[0:64, n * 64:(n + 1) * 64], self.ident_bf[0:64, 0:64])
                        P.copy(TM[:, :], pbf(0, 192), eng="act")
                rounds.append(r0)
                st = {}
                def r1():
                    for n in ns:
                        QB = RQB[slot(n)]
                        W = R["Wt"][n % BS][0]
                        P.tt(W[:, :], QB[:, 64:128], self.C("ident", rows=64, c0=0, c1=64), ALU.add)
                        st[n] = dict(W=W, Xp=QB[:, 64:128], Np=R["Nn"][n % BS][:, :], wi=0)
                rounds.append(r1)
                for lev in range(5):
                    def ra(lev=lev):
                        for n in ns:
                            d = st[n]
                            XN = R["XN"][n % BS][lev % 2]
                            ps = self.nextps()
                            P.mm(ps[0:64, 64:128], d["Xp"], d["Np"])
                            if lev < 4:
                                P.mm(ps[0:64, 0:64], d["Np"], d["Xp"])
                                P.copy(XN[:, :], ps[0:64, 0:128], eng="act")
                            else:
                                P.copy(XN[:, 64:128], ps[0:64, 64:128], eng="act")
                            d["XN"] = XN
                    def rb(lev=lev):
                        for n in ns:
                            d = st[n]
                            XN = d["XN"]
                            ps2 = self.nextps()
                            P.mm(ps2[0:64, 0:64], XN[:, 64:128], d["W"][:, :])
                            if lev < 4:
                                d["wi"] += 1
                                Wn = R["Wt"][n % BS][d["wi"] % 2]
                            else:
                                Wn = R["Wf"][slot(n)]
                            P.tt(Wn[:, :], ps2[0:64, 0:64], d["W"][:, :], ALU.add)
                            d["W"] = Wn
                            d["Xp"] = XN[:, 0:64]; d["Np"] = XN[:, 64:128]
                    rounds.append(ra); rounds.append(rb)
                return rounds

            def chain_hops(ns):
                hops = []
                for n in ns:
                    s4 = slot(n)
                    QA = RQA[s4]; QB = RQB[s4]; TM = RTM[s4]; W = R["Wf"][s4]
                    Rt = RA[0:64, n, 0:64]; At = RA[0:64, n, 64:128]
                    Sb = R["Sb"][n % 2]; Sbn = R["Sb"][(n + 1) % 2]
                    RU = R["RU"][n % 2]
                    def h1(n=n, QA=QA, TM=TM, At=At, Sb=Sb, RU=RU):
                        psA = self.nextps()
                        P.mm(psA[0:64, 0:64], At, Sb[:, :], start=True, stop=False)
                        P.mm(psA[0:64, 0:64], QA[:, 64:128], TM[:, 128:192], start=False, stop=True)
                        P.copy(RU[:, 0:64], psA[0:64, 0:64], eng="act")
                        P.ts(R["S32"][:, :], R["S32"][:, :], R["PC"][:, n:n + 1], ALU.mult)
                    def h2(n=n, W=W, RU=RU):
                        psU = self.nextps()
                        P.mm(psU[0:64, 0:64], W[:, :], RU[:, 0:64])
                        P.copy(RU[:, 64:128], psU[0:64, 0:64], eng="act")
                    def h3(n=n, QA=QA, QB=QB, TM=TM, Rt=Rt, Sb=Sb, Sbn=Sbn, RU=RU):
                        psS = self.nextps()
                        P.mm(psS[0:64, 0:64], TM[:, 0:64], TM[:, 128:192], start=True, stop=False)
                        P.mm(psS[0:64, 0:64], TM[:, 64:128], RU[:, 64:128], start=False, stop=True)
                        P.stt(Sbn[:, :], psS[0:64, 0:64], R["PC"][:, n:n + 1], R["S32"][:, :], ALU.mult, ALU.add)
                        P.stt(R["S32"][:, :], psS[0:64, 0:64], R["PC"][:, n:n + 1], R["S32"][:, :], ALU.mult, ALU.add)
                        if n % 8 == 0:
                            self._psY = self.nextacc()
                        psY = self._psY
                        yc = slice((n % 8) * 64, (n % 8 + 1) * 64)
                        P.mm(psY[0:64, yc], Sb[:, :], Rt, start=True, stop=False)
                        P.mm(psY[0:64, yc], TM[:, 128:192], QA[:, 0:64], start=False, stop=False)
                        P.mm(psY[0:64, yc], RU[:, 64:128], QB[:, 0:64], start=False, stop=True)
                        if n % 8 == 7:
                            post(n // 8, psY)
                    hops += [h1, h2, h3]
                return hops

            def post(tb, psY):
                sl = slice(tb * 512, (tb + 1) * 512)
                Y = self.nt()
                P.copy(Y[0:64, :], psY[0:64, :], eng="act")
                if h == 0:
                    self.debug_dump("rw_scan%d" % tb, Y[0:64, :], [64, 512])
                ysq = self.nb(); ybf = self.nb()
                P.act(ysq[0:64, :], Y[0:64, :], AF.Square)
                P.copy(ybf[0:64, :], Y[0:64, :], eng="pool")
                psm = self.nextps(); psq = self.nextps()
                P.mm(psm[0:64, :], self.blk64_bf[0:64, 0:64], ybf[0:64, :])
                P.mm(psq[0:64, :], self.blk64_bf[0:64, 0:64], ysq[0:64, :])
                m2 = self.nt()
                P.tt(Y[0:64, :], Y[0:64, :], psm[0:64, :], ALU.subtract)
                P.act(m2[0:64, :], psm[0:64, :], AF.Square)
                P.tt(m2[0:64, :], psq[0:64, :], m2[0:64, :], ALU.subtract)
                P.act(m2[0:64, :], m2[0:64, :], AF.Ln, bias=64e-5)
                P.act(m2[0:64, :], m2[0:64, :], AF.Exp, scale=-0.5)
                P.stt(Y[0:64, :], Y[0:64, :], hcol(h, 5), m2[0:64, :], ALU.mult, ALU.mult)
                P.stt(Y[0:64, :], Y[0:64, :], hcol(h, 6), bv[0:64, sl], ALU.add, ALU.add)
                psg = self.nextps()
                P.mm(psg[0:64, :], R["sm"][:, 256 + h * 64:256 + (h + 1) * 64], sgd[:, sl])
                P.tt(self.yT[0:64, 0, sl], Y[0:64, :], psg[0:64, :], ALU.mult)

            nb_ = 32 // BS
            for k in range(nb_ + 1):
                pr = pre_rounds(list(range(k * BS, (k + 1) * BS))) if k < nb_ else []
                ch = chain_hops(list(range((k - 1) * BS, k * BS))) if k >= 1 else []
                i = j = 0
                while i < len(pr) or j < len(ch):
                    if j < len(ch):
                        ch[j](); j += 1
                    for _ in range(2):
                        if i < len(pr):
                            pr[i](); i += 1
            if rstop <= 2:
                continue
            self.debug_dump("y_rwh%d" % h, self.yT[0:64, 0, :], [64, S])
            if "ln1" in self.stages:
                self.acc_out_rw(l, h)
        P.add("pool", lambda e: e.memset(self._dummy.h[:, :], 0.0), [V(None, ring_keys)], [V(None, ("yT", self._dummy.name))])

    def acc_out_rw(self, l, h):
        P = self.P
        if not hasattr(self, "rw_wo"):
            self.rw_wo = P.sb("rw_wo", [64, 1024], BF16)
        t = self.rw_wo
        wv = WV(t, 1, 1024)
        P.dma(V(wv.ap[0:64, :, :], (t.name,)),
              V(self.h_w_out[l][256 + h * 64:256 + (h + 1) * 64, :].rearrange("(k p) c -> p k c", p=64), ("h_w_out",)), eng="pool")
        for tb in range(4):
            for o in range(8):
                ps = self.nextps()
                P.mm(ps[:, :], V(wv.ap[0:64, 0, o * 128:(o + 1) * 128], (t.name,)), self.yT[0:64, 0, tb * 512:(tb + 1) * 512])
                xv = self.xT[:, o, tb * 512:(tb + 1) * 512]
                if self._first_acc:
                    P.stt(xv, xv, ALPHA, ps[:, :], ALU.mult, ALU.add)
                else:
                    P.tt(xv, xv, ps[:, :], ALU.add)
        self._first_acc = False


    def mem_ln(self):
        P = self.P
        self.memnb = P.sb("memnb", [128, 8, NMEM], BF16)
        self.scr_phase(["mem_raw"])
        class _C:
            def __init__(s_, ap, key):
                s_.ap = ap; s_.key = key
            def __getitem__(s_, idx):
                return V(s_.ap[idx], (s_.key,))
        raw = _C(self.scr.h[:, :].bitcast(F32)[:, 0:8 * NMEM].rearrange("p (c t) -> p c t", c=8), "mem_raw")
        P.dma(raw[:, :, :], self.hv(self.h_memT.rearrange("(c p) t -> p c t", p=128), "h_memT"))
        self.layernorm("mem_ln_g", "mem_ln_b", src=raw, dst32=False, dstb=self.memnb, ntok=NMEM)

    def cross_attn(self, l):
        P = self.P
        self.scr_phase(["ca_KT", "ca_V", "ca_oT"])
        scr = self.scr
        class _C:
            def __init__(s_, ap, key):
                s_.ap = ap; s_.key = key
            def __getitem__(s_, idx):
                return V(s_.ap[idx], (s_.key,))
        KT = _C(scr.h[:, 0:2048].rearrange("p (c m) -> p c m", c=8), "ca_KT")
        Vt = _C(scr.h[:, 2048:4096].rearrange("p (b c) -> p b c", b=2), "ca_V")
        oT = _C(scr.h[:, 4096:20480].rearrange("p (c t) -> p c t", c=8), "ca_oT")
        if not hasattr(self, "ones_bf"):
            self.ones_bf = P.sb("ones_bf", [128, 128], BF16)
            P.copy(self.ones_bf[:], self.C("ones"))
        stages = []
        def ld_k(half):
            return self.load_w(self.h_ca_wk[l][:, half * 512:(half + 1) * 512], 8, 512, "h_ca_wk")
        def cp_k(wk, half):
            for oc in range(4):
                ps = self.nextps()
                for kc in range(8):
                    P.mm(ps[:, 0:NMEM], wk[:, kc, oc * 128:(oc + 1) * 128], self.memnb[:, kc, :], start=(kc == 0), stop=(kc == 7))
                P.copy(KT[:, half * 4 + oc, :], ps[:, 0:NMEM], eng="act")
        def ld_v(half):
            return self.load_w(self.h_ca_wv[l][:, half * 512:(half + 1) * 512], 8, 512, "h_ca_wv")
        def cp_v(wv, half):
            for mb in range(2):
                ps = self.nextps()
                for kc in range(8):
                    P.mm(ps[:, :], self.memnb[:, kc, mb * 128:(mb + 1) * 128], wv[:, kc, :], start=(kc == 0), stop=(kc == 7))
                P.copy(Vt[:, mb, half * 512:(half + 1) * 512], ps[:, :], eng="dve")
        def ld_q(h):
            return self.load_w(self.h_ca_wq[l][:, h * 256:(h + 1) * 256], 8, 256, "h_ca_wq")
        def cp_q(wq, h):
            for tb in range(4):
                self.load_xb(tb)
                sl = slice(tb * 512, (tb + 1) * 512)
                qT = [self.nb(), self.nb()]
                for c in range(2):
                    self.proj_fm(wq, c * 128, 128, tb, lambda ps, c=c: P.act(qT[c][:, :], ps[:, :], AF.Copy, scale=1.0 / 16))
                PT = []
                for mb in range(2):
                    ps = self.nextps()
                    for c in range(2):
                        P.mm(ps[:, :], KT[:, h * 2 + c, mb * 128:(mb + 1) * 128], qT[c][:, :], start=(c == 0), stop=(c == 1))
                    pt = self.nb()
                    P.act(pt[:, :], ps[:, :], AF.Exp)
                    PT.append(pt)
                den = self.nextps()
                for mb in range(2):
                    P.mm(den[:, :], self.ones_bf[:, :], PT[mb][:, :], start=(mb == 0), stop=(mb == 1))
                rden = self.nt()
                P.recip(rden[:, :], den[:, :])
                for c2 in range(2):
                    ps = self.nextps()
                    for mb in range(2):
                        P.mm(ps[:, :], Vt[:, mb, h * 256 + c2 * 128:h * 256 + (c2 + 1) * 128], PT[mb][:, :], start=(mb == 0), stop=(mb == 1))
                    P.tt(oT[:, h * 2 + c2, sl], ps[:, :], rden[:, :], ALU.mult)
        def ld_o(half):
            return self.load_w(self.h_ca_wo[l][:, half * 512:(half + 1) * 512], 8, 512, "h_ca_wo")
        def cp_o(wo, half):
            for tb in range(4):
                sl = slice(tb * 512, (tb + 1) * 512)
                for oc in range(4):
                    ps = self.nextps()
                    for kc in range(8):
                        P.mm(ps[:, :], wo[:, kc, oc * 128:(oc + 1) * 128], oT[:, kc, sl], start=(kc == 0), stop=(kc == 7))
                    xv = self.xT[:, half * 4 + oc, sl]
                    P.stt(xv, xv, ALPHA, ps[:, :], ALU.mult, ALU.add)
        for half in range(2):
            stages.append((ld_k, cp_k, half))
        for half in range(2):
            stages.append((ld_v, cp_v, half))
        for h in range(4):
            stages.append((ld_q, cp_q, h))
        for half in range(2):
            stages.append((ld_o, cp_o, half))
        cur = stages[0][0](stages[0][2])
        for i, (ld, cp, arg) in enumerate(stages):
            nxt = None
            if i + 1 < len(stages):
                nxt = stages[i + 1][0](stages[i + 1][2])
            cp(cur, arg)
            cur = nxt

    def conv_ffn(self, l):
        P = self.P
        self.scr_phase(["ff_w0", "ff_w1", "ff_w2", "ff_w3", "ff_pr0", "ff_pr1"])
        scr = self.scr
        if not hasattr(self, "ff"):
            self.ff = dict(halo=P.sb("ff_halo", [128, 8, 2], F32))
        y32 = self.yT.h[:, :, :].rearrange("p a b -> p (a b)").bitcast(F32)
        class _H:
            def __init__(s_, i):
                s_.i = i
            def __getitem__(s_, idx):
                p, c = idx
                return V(y32[p, slice(s_.i * 520 + c.start, s_.i * 520 + c.stop)], ("ffh%d" % s_.i,))
        self.ffh = [_H(i) for i in range(3)]
        self.P.add("pool", lambda e: e.memset(self._dummy.h[:, :], 0.0), [V(None, ("yT",))],
                   [V(None, ("ffh0", "ffh1", "ffh2", self._dummy.name))])
        class _WB:
            def __init__(s_, ap, key, kc, cols):
                s_.ap = ap[:, 0:kc * cols].rearrange("p (k c) -> p k c", k=kc); s_.key = key; s_.kc = kc
            def __getitem__(s_, idx):
                return V(s_.ap[idx], (s_.key,))
        bufs = [(self.wb[0].h[:, :], "wb0"), (self.wb[1].h[:, :], "wb1")] + \
               [(scr.h[:, i * 4096:(i + 1) * 4096], "ff_w%d" % i) for i in range(4)]
        prs = [(scr.h[:, 16384 + i * 2048:16384 + (i + 1) * 2048].rearrange("p (j t) -> p j t", j=4), "ff_pr%d" % i) for i in range(2)]
        def loadw(bi, hbm_ap, kc, cols, key):
            ap, k = bufs[bi]
            wv = _WB(ap, k, kc, cols)
            src = hbm_ap.rearrange("(k p) c -> p k c", p=128)
            step = 4 if cols <= 512 else 2
            kk = 0
            while kk < kc:
                k2 = min(kc, kk + step)
                P.dma(V(wv.ap[:, kk:k2, :], (k,)), V(src[:, kk:k2, :], (key,)), eng="pool")
                kk = k2
            return wv
        o_ub, _ = COL_LAYOUT["ffn_up_b"]; o_cb, _ = COL_LAYOUT["ffn_conv_b"]; o_cw, _ = COL_LAYOUT["ffn_conv"]
        colv = lambda o: self.colp[:, o:o + 1]
        npg = 6
        def load_pg(pg):
            nj = 4 if pg < 5 else 2
            b0 = (pg % 2) * 3
            wg = loadw(b0, self.h_ffn_up[l][:, pg * 512:pg * 512 + nj * 128], 8, nj * 128, "h_ffn_up")
            wvv = loadw(b0 + 1, self.h_ffn_up[l][:, DFF + pg * 512:DFF + pg * 512 + nj * 128], 8, nj * 128, "h_ffn_up")
            wd = loadw(b0 + 2, self.h_ffn_down[l][pg * 512:pg * 512 + nj * 128, :], nj, 1024, "h_ffn_down")
            return wg, wvv, wd
        nxt = load_pg(0)
        for pg in range(npg):
            nj = 4 if pg < 5 else 2
            wg, wvv, wd = nxt
            if pg + 1 < npg:
                nxt = load_pg(pg + 1)
            for tb in range(4):
                self.load_xb(tb)
                sl = slice(tb * 512, (tb + 1) * 512)
                prap, prk = prs[(pg * 4 + tb) % 2]
                for jj in range(nj):
                    res = []
                    for part, wsrc in enumerate((wg, wvv)):
                        ch = part * 22 + pg * 4 + jj
                        hb = self.nt()
                        hs = part * 4 + jj
                        ps = self.nextps()
                        for kc in range(8):
                            P.mm(ps[:, :], wsrc[:, kc, jj * 128:(jj + 1) * 128], self.cur_xb[:, kc, 0:512], start=(kc == 0), stop=(kc == 7))
                        hbuf = self.ffh[self._ffh_i % 3]; self._ffh_i += 1
                        if tb == 0:
                            P.memset(hbuf[:, 0:2], 0.0, eng="dve")
                        else:
                            P.copy(hbuf[:, 0:2], self.ff["halo"][:, hs, :], eng="dve")
                        P.act(hbuf[:, 2:514], ps[:, :], AF.Identity, bias=colv(o_ub + ch))
                        P.copy(self.ff["halo"][:, hs, :], hbuf[:, 512:514], eng="dve")
                        P.act(hb[:, :], hbuf[:, 0:512], AF.Identity, bias=colv(o_cb + ch), scale=colv(o_cw + ch))
                        P.stt(hb[:, :], hbuf[:, 1:513], colv(o_cw + 44 + ch), hb[:, :], ALU.mult, ALU.add)
                        P.stt(hb[:, :], hbuf[:, 2:514], colv(o_cw + 88 + ch), hb[:, :], ALU.mult, ALU.add)
                        res.append(hb)
                    P.act(res[0][:, :], res[0][:, :], AF.Gelu)
                    P.tt(V(prap[:, jj, :], (prk,)), res[0][:, :], res[1][:, :], ALU.mult)
                for o in range(8):
                    ps = self.nextps()
                    for jj in range(nj):
                        P.mm(ps[:, :], wd[:, jj, o * 128:(o + 1) * 128], V(prap[:, jj, :], (prk,)), start=(jj == 0), stop=(jj == nj - 1))
                    xv = self.xT[:, o, sl]
                    if pg == 0:
                        P.stt(xv, xv, ALPHA, ps[:, :], ALU.mult, ALU.add)
                    else:
                        P.tt(xv, xv, ps[:, :], ALU.add)

    def build(self):
        P = self.P
        with ExitStack() as st:
            P.enter(st)
            self.decl()
            self.alloc()
            P.dma(self.consts[:], self.hv(self.h_consts, "h_consts"))
            P.dma(self.colp[:], self.hv(self.h_colp[0], "h_colp"))
            xsrc = self.h_xT.rearrange("(c p) t -> p c t", p=128)
            for tb in range(4):
                sl = slice(tb * 512, (tb + 1) * 512)
                P.dma(self.xT[:, :, sl], self.hv(xsrc[:, :, sl], "h_xT"))
                P.copy(self.xb[:, :, sl], self.xT[:, :, sl], eng=("act" if tb % 2 else "dve"))
            P.copy(self.ident_bf[:], self.C("ident"))
            P.ts(self.ones_s[:], self.C("ones"), 1.0 / 1024, ALU.mult)
            P.copy(self.blk64_bf[:], self.C("blk64"))
            if "ca" in self.stages:
                self.mem_ln()
            for l in range(self.nlayers):
                self.layer(l)
            osrc = self.h_outT.rearrange("(c p) t -> p c t", p=128)
            for tb in range(4):
                sl = slice(tb * 512, (tb + 1) * 512)
                P.dma(self.hv(osrc[:, :, sl], "h_outT"), self.xT[:, :, sl])
            P.finalize()
            P.emit(st)
        return self.nc

    def layer(self, l):
        P = self.P
        if l > 0:
            P.dma(self.colp[:], self.hv(self.h_colp[l], "h_colp"))
        self._first_acc = True
        if l > 0 and "ffn" in self.stages:
            self.P.add("pool", lambda e: e.memset(self._dummy.h[:, :], 0.0), [V(None, ("ffh0", "ffh1", "ffh2"))],
                       [V(None, ("yT", self._dummy.name))])
        for m, (name, fn) in enumerate([("sg", self.mixer_sg), ("rw", self.mixer_rw), ("gla", self.mixer_gla), ("fox", self.mixer_fox)]):
            if name in self.stages and fn is not None:
                fn(l)
                if name == "rw":
                    continue
                self.debug_dump("y_%s%d" % (name, l), self.yT[:], [128, 2, S])
                if "ln1" in self.stages:
                    self.acc_out(self.h_w_out[l][m * 256:(m + 1) * 256, :], self.yT, 2, "h_w_out", self._first_acc)
                    self._first_acc = False
        if "ln1" in self.stages:
            self.layernorm("ln1_g%d" % l, "ln1_b%d" % l)
            self.debug_dump("x1_%d" % l, self.xT[:], [128, 8, S])
        if "ca" in self.stages:
            self.cross_attn(l)
            self.layernorm("ln2_g%d" % l, "ln2_b%d" % l)
            self.debug_dump("x2_%d" % l, self.xT[:], [128, 8, S])
        if "ffn" in self.stages:
            self.conv_ffn(l)
            self.layernorm("ln3_g%d" % l, "ln3_b%d" % l)


_CACHE = {}


def kernel(**inputs):
    inp = {k: np.ascontiguousarray(np.asarray(v, dtype=np.float32)) for k, v in inputs.items()}
    n = 8
    nc = bass.Bass("TRN2", target_bir_lowering=False)
    mk = MK(nc)
    mk.build()
    consts = make_consts()
    colp = make_colp(inp)
    rowp = make_rowp(inp)
    sgwT = np.ascontiguousarray(inp["sg_w"].transpose(0, 3, 1, 2))
    shared = dict(consts=consts, colp=colp, rowp=rowp, sgwT=sgwT,
                  w_in=inp["w_in"], w_out=inp["w_out"], gla_a_up=inp["gla_a_up"],
                  rw_w2=inp["rw_w2"], rw_a2=inp["rw_a2"], rw_g2=inp["rw_g2"],
                  ca_wq=inp["ca_wq"], ca_wk=inp["ca_wk"], ca_wv=inp["ca_wv"], ca_wo=inp["ca_wo"],
                  ffn_up=inp["ffn_up"], ffn_down=inp["ffn_down"])
    maps = []
    for b in range(n):
        m = dict(shared)
        m["xT"] = np.ascontiguousarray(inp["x"][b].T)
        m["memT"] = np.ascontiguousarray(inp["mem"][b].T)
        maps.append(m)
    res = run_bass_kernel_spmd(nc, maps, core_ids=list(range(n)))
    out = np.stack([np.asarray(res.results[b]["outT"]).T for b in range(n)]).astype(np.float32)
    return np.ascontiguousarray(out)
```

```python
from contextlib import ExitStack
from concourse.bass_utils import run_bass_kernel_spmd
import numpy as np
import concourse.bass as bass
import concourse.mybir as mybir

F32 = mybir.dt.float32
BF16 = mybir.dt.bfloat16
AF = mybir.ActivationFunctionType
ALU = mybir.AluOpType
AX = mybir.AxisListType

ENGS = ("pe", "act", "dve", "pool", "sp")


class V:
    __slots__ = ("ap", "keys")

    def __init__(self, ap, keys):
        self.ap = ap
        self.keys = keys


class T:
    def __init__(self, handle, name, shape):
        self.h = handle
        self.name = name
        self.shape = shape

    def __getitem__(self, idx):
        return V(self.h[idx], (self.name,))

    def k(self, sub):
        return _TK(self, sub)


class _TK:
    def __init__(self, t, sub):
        self.t = t
        self.sub = sub

    def __getitem__(self, idx):
        return V(self.t.h[idx], ((self.t.name, self.sub),))


class Op:
    __slots__ = ("eng", "fn", "reads", "writes", "idx", "deps", "signal", "sigidx",
                 "is_dma", "dslot", "dcnt", "snap", "xreads")

    def __init__(self, eng, fn, reads, writes, is_dma=False):
        self.eng = eng
        self.fn = fn
        self.reads = reads
        self.writes = writes
        self.is_dma = is_dma
        self.deps = []
        self.signal = False
        self.sigidx = 0
        self.dslot = -1
        self.dcnt = 0
        self.snap = None


class Prog:
    N_DSEM = 24

    def __init__(self, nc):
        self.nc = nc
        self.ops = []
        self._stack = None
        self.ntile = 0
        self.psum_names = set()

    def enter(self, stack):
        self._stack = stack

    def sb(self, name, shape, dt=F32):
        h = self._stack.enter_context(self.nc.sbuf_tensor("s_" + name, list(shape), dt))
        return T(h, name, shape)

    def ps(self, name, shape, dt=F32):
        h = self._stack.enter_context(self.nc.psum_tensor("p_" + name, list(shape), dt))
        self.psum_names.add(name)
        return T(h, name, shape)

    def _keys(self, vs):
        ks = []
        for v in vs:
            if v is None or isinstance(v, (int, float)):
                continue
            ks.extend(v.keys)
        return ks

    def add(self, eng, fn, reads, writes, is_dma=False):
        op = Op(eng, fn, self._keys(reads), self._keys(writes), is_dma)
        op.xreads = [k for k in op.reads if (k if isinstance(k, str) else k[0]) in self.psum_names and k not in op.writes]
        self.ops.append(op)
        return op

    def dma(self, out, in_, eng="sp", **kw):
        def fn(e, out=out, in_=in_):
            return e.dma_start(out=out.ap, in_=in_.ap, **kw)
        return self.add(eng, fn, [in_], [out], is_dma=True)

    def mm(self, out, lhsT, rhs, start=True, stop=True, **kw):
        def fn(e):
            return e.matmul(out.ap, lhsT.ap, rhs.ap, start=start, stop=stop, **kw)
        return self.add("pe", fn, [lhsT, rhs], [out])

    def transpose(self, out, in_, ident):
        def fn(e):
            return e.transpose(out.ap, in_.ap, ident.ap)
        return self.add("pe", fn, [in_, ident], [out])

    def act(self, out, in_, func, bias=0.0, scale=1.0, eng="act", accum_out=None):
        def fn(e):
            kw = {}
            if accum_out is not None:
                kw["accum_out"] = accum_out.ap
            return e.activation(out.ap, in_.ap, func,
                                bias=(bias.ap if isinstance(bias, V) else bias),
                                scale=(scale.ap if isinstance(scale, V) else scale), **kw)
        return self.add("act", fn, [in_, bias, scale], [out, accum_out])

    def tt(self, out, in0, in1, op, eng="dve"):
        def fn(e):
            return e.tensor_tensor(out.ap, in0.ap, in1.ap, op)
        return self.add(eng, fn, [in0, in1], [out])

    def ts(self, out, in0, s1, op0, s2=None, op1=None, eng="dve", accum_out=None):
        def fn(e):
            a1 = s1.ap if isinstance(s1, V) else s1
            a2 = s2.ap if isinstance(s2, V) else s2
            kw = {}
            if accum_out is not None:
                kw["accum_out"] = accum_out.ap
            if op1 is None:
                return e.tensor_scalar(out.ap, in0.ap, a1, None, op0, **kw)
            return e.tensor_scalar(out.ap, in0.ap, a1, a2, op0, op1, **kw)
        return self.add(eng, fn, [in0, s1, s2], [out, accum_out])

    def stt(self, out, in0, scalar, in1, op0, op1, eng="dve"):
        def fn(e):
            s = scalar.ap if isinstance(scalar, V) else scalar
            return e.scalar_tensor_tensor(out.ap, in0.ap, s, in1.ap, op0, op1)
        return self.add(eng, fn, [in0, scalar, in1], [out])

    def copy(self, out, in_, eng="dve"):
        if eng == "act":
            def fn(e):
                return e.copy(out.ap, in_.ap)
        else:
            def fn(e):
                return e.tensor_copy(out.ap, in_.ap)
        return self.add(eng, fn, [in_], [out])

    def memset(self, out, val, eng="dve"):
        def fn(e):
            return e.memset(out.ap, val)
        return self.add(eng, fn, [], [out])

    def reduce(self, out, in_, op, axis=AX.X, eng="dve"):
        def fn(e):
            return e.tensor_reduce(out.ap, in_.ap, axis, op)
        return self.add(eng, fn, [in_], [out])

    def recip(self, out, in_):
        def fn(e):
            return e.reciprocal(out.ap, in_.ap)
        return self.add("dve", fn, [in_], [out])

    def generic(self, eng, fn, reads, writes):
        return self.add(eng, fn, reads, writes)

    def finalize(self, out_keys=()):
        nc = self.nc
        ops = self.ops
        last_w = {}
        readers = {}
        for i, op in enumerate(ops):
            op.idx = i
            deps = set()
            for k in op.reads:
                w = last_w.get(k)
                if w is not None:
                    deps.add(w)
            for k in list(op.writes) + op.xreads:
                w = last_w.get(k)
                if w is not None:
                    deps.add(w)
                latest = {}
                for r in readers.get(k, ()):
                    ro = ops[r]
                    if ro.is_dma:
                        deps.add(r)
                    else:
                        latest[ro.eng] = r
                for r in latest.values():
                    deps.add(r)
            deps.discard(i)
            op.deps = sorted(deps)
            for k in op.reads:
                lst = readers.setdefault(k, [])
                if not op.is_dma:
                    lst[:] = [r for r in lst if ops[r].is_dma or ops[r].eng != op.eng]
                lst.append(i)
            for k in op.writes:
                last_w[k] = i
                readers[k] = []
            for k in op.xreads:
                readers[k] = [i]
        for op in ops:
            need = []
            for d in op.deps:
                p = ops[d]
                if p.is_dma:
                    need.append(d)
                    continue
                if p.eng == op.eng:
                    if op.is_dma:
                        need.append(d)
                        continue
                    if op.eng == "pe":
                        continue
                    raw = any(k in p.writes for k in op.reads)
                    if raw:
                        need.append(d)
                    continue
                need.append(d)
            op.deps = need
            for d in need:
                if not ops[d].is_dma:
                    ops[d].signal = True
        cnt = {e: 0 for e in ENGS}
        for op in ops:
            if op.is_dma:
                continue
            if op.signal:
                cnt[op.eng] += 1
                op.sigidx = cnt[op.eng]
        dcount = [0] * self.N_DSEM
        nd = 0
        for op in ops:
            if op.is_dma:
                op.dslot = nd % self.N_DSEM
                dcount[op.dslot] += 1
                op.dcnt = dcount[op.dslot]
                nd += 1
        self.n_dma = nd
        self.sig_counts = cnt
        return self

    def emit(self, stack):
        nc = self.nc
        ops = self.ops
        sems = {e: stack.enter_context(nc.semaphore("S_" + e)) for e in ENGS if e != "sp"}
        dsems = [stack.enter_context(nc.semaphore("D%d" % i)) for i in range(self.N_DSEM)]
        block = stack.enter_context(nc.Block())
        seen = {e: {x: 0 for x in ENGS} for e in ENGS}
        seen_d = {e: [0] * self.N_DSEM for e in ENGS}
        plan = {e: [] for e in ENGS}
        last_dma_on_slot = [None] * self.N_DSEM
        for op in ops:
            e = op.eng
            waits = []
            if op.is_dma:
                prev = last_dma_on_slot[op.dslot]
                if prev is not None and seen_d[e][op.dslot] < prev.dcnt * 16:
                    waits.append((dsems[op.dslot], prev.dcnt * 16))
                    seen_d[e][op.dslot] = prev.dcnt * 16
                last_dma_on_slot[op.dslot] = op
            for d in op.deps:
                p = ops[d]
                if p.is_dma:
                    v = p.dcnt * 16
                    if seen_d[e][p.dslot] < v:
                        waits.append((dsems[p.dslot], v))
                        seen_d[e][p.dslot] = v
                else:
                    v = p.sigidx
                    if seen[e][p.eng] < v:
                        waits.append((sems[p.eng], v))
                        seen[e][p.eng] = v
                        for x in ENGS:
                            if p.snap[x] > seen[e][x]:
                                seen[e][x] = p.snap[x]
            if not op.is_dma:
                snap = dict(seen[e])
                if op.signal:
                    snap[e] = max(snap[e], op.sigidx)
                op.snap = snap
            plan[e].append((waits, op))
        final_waits = []
        for s in range(self.N_DSEM):
            lp = last_dma_on_slot[s]
            if lp is not None:
                final_waits.append((dsems[s], lp.dcnt * 16))

        def run(engname, e):
            for waits, op in plan[engname]:
                for (s, v) in waits:
                    e.wait_ge(s, v)
                ins = op.fn(e)
                if op.is_dma:
                    ins.then_inc(dsems[op.dslot], 16)
                elif op.signal:
                    ins.then_inc(sems[op.eng], 1)

        @block.tensor
        def _(e):
            run("pe", e)

        @block.scalar
        def _(e):
            run("act", e)

        @block.vector
        def _(e):
            run("dve", e)

        @block.gpsimd
        def _(e):
            run("pool", e)

        @block.sync
        def _(e):
            run("sp", e)
            for (s, v) in final_waits:
                e.wait_ge(s, v)
        self.stats = {e: len(plan[e]) for e in ENGS}
        self.nwaits = {e: sum(len(w) for w, _ in plan[e]) for e in ENGS}


S = 2048
D = 1024
L = 4
NMEM = 256
DFF = 2816
ALPHA = (2.0 * L) ** 0.25
LN_EPS = 1e-5
NEG = -30000.0

CONST_LAYOUT = {}
_off = 0
for _n, _w in [("ident", 128), ("tri_le", 128), ("tri_lt", 64), ("negmask", 128), ("selneg", 4 * 128),
               ("ones", 128), ("sel_even", 64), ("sel_odd", 128), ("tri_gt", 64), ("blk64", 128), ("mask2", 128)]:
    CONST_LAYOUT[_n] = (_off, _w)
    _off += _w
NCONST = _off


def make_consts():
    c = np.zeros((128, NCONST), np.float32)
    def put(n, a):
        o, w = CONST_LAYOUT[n]
        c[:a.shape[0], o:o + w] = a
    i = np.arange(128)
    put("ident", np.eye(128, dtype=np.float32))
    put("tri_le", (i[:, None] <= i[None, :]).astype(np.float32))
    put("tri_lt", (i[:, None] < i[None, :]).astype(np.float32)[:, :64])
    put("tri_gt", (i[:, None] > i[None, :]).astype(np.float32)[:, :64])
    put("negmask", np.where(i[:, None] <= i[None, :], 0.0, NEG).astype(np.float32))
    sn = np.zeros((4, 4 * 128), np.float32)
    for h in range(4):
        sn[h, h * 128:(h + 1) * 128] = -1.0
    put("selneg", sn)
    put("ones", np.ones((128, 128), np.float32))
    se = np.zeros((65, 64), np.float32); se[64, :] = 1.0
    put("sel_even", se)
    so = np.zeros((128, 128), np.float32); so[0, 64:128] = 1.0
    put("sel_odd", so)
    put("blk64", ((i[:, None] // 64) == (i[None, :] // 64)).astype(np.float32) / 64.0)
    put("mask2", ((i[:, None] <= i[None, :]) & ((i[:, None] // 64) == (i[None, :] // 64))).astype(np.float32))
    return c


def col_layout():
    lay = {}
    off = 0
    def add(n, w):
        nonlocal off
        lay[n] = (off, w)
        off += w
    add("mem_ln_g", 8); add("mem_ln_b", 8)
    for n in ("ln1_g", "ln1_b", "ln2_g", "ln2_b", "ln3_g", "ln3_b"):
        add(n, 8)
    add("ffn_up_b", 44); add("ffn_conv_b", 44); add("ffn_conv", 132)
    add("fox_fb", 1)
    add("gla_a_b", 2)
    add("gla_norm_g", 2)
    add("rw_mu", 8)
    add("rw_mu64_", 16)
    add("rw_h64_", 4 * 9)
    for n in list(lay.keys()):
        for l in range(L):
            lay["%s%d" % (n, l)] = lay[n]
    return lay, off


COL_LAYOUT, NCOL = col_layout()


def chunkcols(v, p=128):
    return np.ascontiguousarray(v.reshape(-1, p).T)


def make_colp(inp):
    call = np.zeros((L, 128, NCOL), np.float32)
    for l in range(L):
        c = call[l]
        def put(n, a):
            o, w = COL_LAYOUT[n]
            assert a.shape[1] == w, (n, a.shape, w)
            c[:a.shape[0], o:o + w] = a
        put("mem_ln_g", chunkcols(inp["mem_ln_g"])); put("mem_ln_b", chunkcols(inp["mem_ln_b"]))
        for n in ("ln1_g", "ln1_b", "ln2_g", "ln2_b", "ln3_g", "ln3_b"):
            put(n, chunkcols(inp[n][l]))
        put("ffn_up_b", chunkcols(inp["ffn_up_b"][l]))
        put("ffn_conv_b", chunkcols(inp["ffn_conv_b"][l]))
        put("ffn_conv", np.concatenate([chunkcols(inp["ffn_conv"][l][j]) for j in range(3)], 1))
        put("fox_fb", inp["fox_fb"][l].reshape(4, 1))
        put("gla_a_b", chunkcols(inp["gla_a_b"][l], 64))
        put("gla_norm_g", chunkcols(inp["gla_norm_g"][l]))
        put("rw_mu", chunkcols(inp["rw_mu"][l]))
        put("rw_mu64_", chunkcols(inp["rw_mu"][l], 64))
        h64 = np.zeros((64, 36), np.float32)
        for h in range(4):
            sl = slice(h * 64, (h + 1) * 64)
            for j, nm in enumerate(["rw_w0", "rw_a0", "rw_kk", "rw_ka", None, "rw_lnx_g", "rw_lnx_b"]):
                if nm is not None:
                    h64[:, h * 9 + j] = inp[nm][l][sl]
            h64[:, h * 9 + 4] = inp["rw_rk"][l][h]
        put("rw_h64_", h64)
    return call


ROW_LAYOUT = {}
_off = 0
for _n, _w in [("sg_ln_g", 256), ("sg_ln_b", 256), ("sgb", 256)]:
    ROW_LAYOUT[_n] = (_off, _w)
    _off += _w
NROW = _off


def make_rowp(inp):
    r = np.zeros((L, 128, NROW), np.float32)
    for l in range(L):
        def put(n, a):
            o, w = ROW_LAYOUT[n]
            r[l, :, o:o + w] = a
        put("sg_ln_g", np.tile(inp["sg_ln_g"][l][None], (128, 1)))
        put("sg_ln_b", np.tile(inp["sg_ln_b"][l][None], (128, 1)))
        sgb = np.zeros((128, 2, 128), np.float32)
        for p in range(128):
            for pair in range(2):
                sgb[p, pair] = inp["sg_b"][l][pair * 2 + p // 64]
        put("sgb", sgb.reshape(128, 256))
    return r


class WV:
    def __init__(self, tile, kc, cols):
        self.tile = tile
        self.kc = kc
        self.cols = cols
        self.ap = tile.h[:, 0:kc * cols].rearrange("p (k c) -> p k c", k=kc)

    def __getitem__(self, idx):
        return V(self.ap[idx], (self.tile.name,))


class MK:
    def __init__(self, nc, nlayers=L, stages=("sg", "fox", "gla", "rw", "ln1", "ca", "ln2", "ffn", "ln3"), dbg=()):
        self.nc = nc
        self.nlayers = nlayers
        self.stages = stages
        self.dbg = dbg
        self.P = Prog(nc)
        self.dbg_out = {}

    def decl(self):
        nc = self.nc
        di = lambda n, shp: nc.dram_tensor(n, list(shp), F32, kind="ExternalInput").ap()
        self.h_xT = di("xT", [D, S])
        self.h_memT = di("memT", [D, NMEM])
        self.h_consts = di("consts", [128, NCONST])
        self.h_colp = di("colp", [L, 128, NCOL])
        self.h_rowp = di("rowp", [L, 128, NROW])
        self.h_w_in = di("w_in", [L, D, 3092])
        self.h_w_out = di("w_out", [L, D, D])
        self.h_sgwT = di("sgwT", [L, 128, 4, 128])
        self.h_gla_a_up = di("gla_a_up", [L, 16, 128])
        self.h_rw_w2 = di("rw_w2", [L, 64, 256])
        self.h_rw_a2 = di("rw_a2", [L, 64, 256])
        self.h_rw_g2 = di("rw_g2", [L, 128, 256])
        self.h_ca_wq = di("ca_wq", [L, D, D]); self.h_ca_wk = di("ca_wk", [L, D, D])
        self.h_ca_wv = di("ca_wv", [L, D, D]); self.h_ca_wo = di("ca_wo", [L, D, D])
        self.h_ffn_up = di("ffn_up", [L, D, 2 * DFF]); self.h_ffn_down = di("ffn_down", [L, DFF, D])
        self.h_outT = nc.dram_tensor("outT", [D, S], F32, kind="ExternalOutput").ap()

    def hv(self, ap, key):
        return V(ap, (key,))

    def alloc(self):
        P = self.P
        self.xT = P.sb("xT", [128, 8, S], F32)
        self.xb = P.sb("xb", [128, 8, S], BF16)
        self.cur_xb = None
        self.yT = P.sb("yT", [128, 2, S], BF16)
        self.consts = P.sb("consts", [128, NCONST], F32)
        self.colp = P.sb("colp", [128, NCOL], F32)
        self.ident_bf = P.sb("ident_bf", [128, 128], BF16)
        self.ones_s = P.sb("ones_s", [128, 128], BF16)
        self.blk64_bf = P.sb("blk64_bf", [128, 128], BF16)
        self.wb = [P.sb("wb%d" % i, [128, 4096], BF16) for i in range(2)]
        self.pb = [P.ps("pb%d" % i, [128, 512], F32) for i in range(8)]
        self.t512 = [P.sb("t512_%d" % i, [128, 512], F32) for i in range(5)]
        self.b512 = [P.sb("b512_%d" % i, [128, 512], BF16) for i in range(4)]
        self.st512 = [P.sb("st512_%d" % i, [128, 512], F32) for i in range(2)]
        self.scr = P.sb("scr", [128, 20480], BF16)
        self._ffh_i = 0
        self._ps_i = 0
        self._wb_i = 0
        self._wo_i = 0
        self._t_i = 0
        self._b_i = 0

    def load_xb(self, tb, eng="pool"):
        xb = self.xb
        class _B:
            def __getitem__(s_, idx):
                p, k, c = idx
                return V(xb.h[p, k, slice(tb * 512 + c.start, tb * 512 + c.stop)], (xb.name,))
        self.cur_xb = _B()
        return self.cur_xb

    def sub256(self, t):
        class _S:
            name = t.name
            class _H:
                def __getitem__(s2, idx):
                    p, c = idx
                    c = slice(c.start or 0, 256 if c.stop is None else c.stop)
                    return t.h[p, c]
            h = _H()
            def __getitem__(s_, idx):
                if not isinstance(idx, tuple):
                    idx = (idx, slice(0, 256))
                p, c = idx
                c = slice(c.start or 0, 256 if c.stop is None else c.stop)
                return V(t.h[p, c], (t.name,))
        return _S()

    def nextps(self):
        p = self.pb[self._ps_i % 6]
        self._ps_i += 1
        return p

    def nextacc(self):
        self._acc_i = getattr(self, "_acc_i", 0) + 1
        return self.pb[6 + self._acc_i % 2]

    def nt(self):
        t = self.t512[self._t_i % 5]
        self._t_i += 1
        return t

    def nf(self):
        return self.nt()

    def nb(self):
        t = self.b512[self._b_i % 4]
        self._b_i += 1
        return t

    def C(self, name, rows=128, c0=0, c1=None):
        o, w = CONST_LAYOUT[name]
        if c1 is None:
            c1 = w
        return self.consts[0:rows, o + c0:o + c1]

    def col(self, name, j, rows=128):
        o, w = COL_LAYOUT[name]
        return self.colp[0:rows, o + j:o + j + 1]

    def row(self, name, c0=0, c1=None):
        o, w = ROW_LAYOUT[name]
        if c1 is None:
            c1 = w
        return V(self.scr.h[:, :].bitcast(F32)[:, 2560 + o + c0:2560 + o + c1], ("sg_rowp",))

    def load_w(self, hbm_ap, kc, cols, key, ring="wb"):
        P = self.P
        t = self.wb[self._wb_i % 2]; self._wb_i += 1
        wv = WV(t, kc, cols)
        src = hbm_ap.rearrange("(k p) c -> p k c", p=128)
        step = max(1, (2048 // cols) if cols <= 2048 else 1)
        k = 0
        while k < kc:
            k2 = min(kc, k + step)
            P.dma(V(wv.ap[:, k:k2, :], (t.name,)), V(src[:, k:k2, :], (key,)), eng="pool")
            k = k2
        return wv

    def scr_phase(self, new_keys):
        old = getattr(self, "_scr_keys", [])
        if not hasattr(self, "_dummy"):
            self._dummy = self.P.sb("phase_dummy", [128, 8], F32)
        d = self._dummy
        self.P.add("pool", lambda e: e.memset(d.h[:, :], 0.0), [V(None, tuple(old))], [V(None, tuple(new_keys) + (d.name,))])
        self._scr_keys = list(new_keys)

    def debug_dump(self, name, view, shape):
        if name not in self.dbg:
            return
        h = self.nc.dram_tensor("dbg_" + name, list(shape), view.ap.dtype, kind="ExternalOutput").ap()
        self.P.dma(V(h, ("dbg_" + name,)), view)
        self.dbg_out[name] = shape

    def proj_fm(self, w, c0, M, tb, evac, xsrc=None, ncol=512):
        P = self.P
        ps = self.nextps()
        for kc in range(w.kc):
            if xsrc is None:
                rhs = self.cur_xb[:, kc, 0:ncol]
            else:
                rhs = xsrc[:, kc, tb * ncol:(tb + 1) * ncol]
            P.mm(ps[0:M, 0:ncol], w[:, kc, c0:c0 + M], rhs,
                 start=(kc == 0), stop=(kc == w.kc - 1))
        evac(ps)

    def proj_tm(self, w, c0, N, t0, evac, ntok=128):
        P = self.P
        ps = self.nextps()
        for kc in range(w.kc):
            P.mm(ps[0:ntok, 0:N], self.cur_xb[:, kc, (t0 % 512):(t0 % 512) + ntok], w[:, kc, c0:c0 + N],
                 start=(kc == 0), stop=(kc == w.kc - 1))
        evac(ps)

    def acc_out(self, hbm_w, yT, nck, key, first):
        P = self.P
        w = self.load_w(hbm_w, nck, 1024, key, ring="wb")
        for tb in range(4):
            for o in range(8):
                ps = self.nextps()
                for c in range(nck):
                    P.mm(ps[:, :], w[:, c, o * 128:(o + 1) * 128], yT[:, c, tb * 512:(tb + 1) * 512],
                         start=(c == 0), stop=(c == nck - 1))
                xv = self.xT[:, o, tb * 512:(tb + 1) * 512]
                if first:
                    P.stt(xv, xv, ALPHA, ps[:, :], ALU.mult, ALU.add)
                else:
                    P.tt(xv, xv, ps[:, :], ALU.add)

    def layernorm(self, gname, bname, src=None, dst32=None, dstb=None, ntok=S, eps=LN_EPS):
        P = self.P
        src = self.xT if src is None else src
        dst32 = self.xT if dst32 is None else (None if dst32 is False else dst32)
        dstb = self.xb if dstb is None else dstb
        nblk = (ntok + 511) // 512
        for tb in range(nblk):
            w = min(512, ntok - tb * 512)
            sl = slice(tb * 512, tb * 512 + w)
            psm = self.nextps(); psq = self.nextps()
            for c in range(8):
                xb_ = self.nb(); sq = self.nb()
                P.copy(xb_[:, 0:w], src[:, c, sl], eng="dve")
                P.act(sq[:, 0:w], src[:, c, sl], AF.Square)
                P.mm(psm[:, 0:w], self.ones_s[:, :], xb_[:, 0:w], start=(c == 0), stop=(c == 7))
                P.mm(psq[:, 0:w], self.ones_s[:, :], sq[:, 0:w], start=(c == 0), stop=(c == 7))
            mean = self.st512[0]; rstd = self.st512[1]
            P.copy(mean[:, 0:w], psm[:, 0:w])
            msq = self.nt()
            P.tt(msq[:, 0:w], mean[:, 0:w], mean[:, 0:w], ALU.mult)
            P.tt(msq[:, 0:w], psq[:, 0:w], msq[:, 0:w], ALU.subtract)
            P.act(msq[:, 0:w], msq[:, 0:w], AF.Ln, bias=eps)
            P.act(rstd[:, 0:w], msq[:, 0:w], AF.Exp, scale=-0.5)
            for c in range(8):
                u = self.nt()
                P.tt(u[:, 0:w], src[:, c, sl], mean[:, 0:w], ALU.subtract)
                P.stt(u[:, 0:w], u[:, 0:w], self.col(gname, c), rstd[:, 0:w], ALU.mult, ALU.mult)
                if dst32 is not None:
                    P.act(dst32[:, c, sl], u[:, 0:w], AF.Identity, bias=self.col(bname, c))
                if dstb is not None:
                    P.act(dstb[:, c, sl], u[:, 0:w], AF.Identity, bias=self.col(bname, c))

    def mixer_sg(self, l):
        P = self.P
        w = self.load_w(self.h_w_in[l][:, 0:512], 8, 512, "h_w_in")
        if not hasattr(self, "sg_t"):
            self.sg_t = dict(
                wmT=P.sb("sg_wmT", [128, 4, 128], BF16),
                stat=[P.sb("sg_stat%d" % i, [128, 16], F32) for i in range(2)],
                vnp=[P.sb("sg_vnp%d" % i, [128, 2, 2, 128], BF16) for i in range(2)],
            )
            for v_ in self.sg_t["vnp"]:
                P.memset(v_[:], 0.0, eng="pool")
        self.scr_phase(["sg_u0", "sg_u1", "sg_wraw", "sg_rowp"])
        P.dma(V(self.scr.h[:, :].bitcast(F32)[:, 2560:2560 + NROW], ("sg_rowp",)), self.hv(self.h_rowp[l], "h_rowp"))
        scr32 = self.scr.h[:, :].bitcast(F32)
        class _C:
            def __init__(s_, ap, key):
                s_.ap = ap; s_.key = key
            def __getitem__(s_, idx):
                return V(s_.ap[idx], (s_.key,))
        usb = [_C(scr32[:, i * 1024:(i + 1) * 1024].rearrange("p (c t) -> p c t", c=2), "sg_u%d" % i) for i in range(2)]
        wraw = _C(scr32[:, 2048:2560].rearrange("p (h t) -> p h t", h=4), "sg_wraw")
        T_ = self.sg_t
        P.dma(wraw[:], self.hv(self.h_sgwT[l], "h_sgwT"))
        tri = V(self.C("tri_le").ap.unsqueeze(1).to_broadcast([128, 4, 128]), self.consts[:].keys)
        P.tt(T_["wmT"][:], wraw[:], tri, ALU.mult)
        for tb in range(4):
            self.load_xb(tb)
            u_sb = usb[tb % 2]
            for c in range(2):
                self.proj_fm(w, c * 128, 128, tb, lambda ps, c=c: P.copy(u_sb[:, c, :], ps[:, :], eng="act"))
            for n4 in range(4):
                n = tb * 4 + n4
                vs = self.sub256(self.nf()); sq = self.sub256(self.nf()); stat = T_["stat"][n % 2]; vnp = T_["vnp"][n % 2]
                def ev(ps):
                    P.copy(vs[:], ps[:, 0:256], eng="act")
                    P.act(sq[:], ps[:, 0:256], AF.Square)
                self.proj_tm(w, 256, 256, n * 128, ev)
                v3 = V(vs.h[:, :].rearrange("p (h d) -> p h d", h=4), vs[:].keys)
                q3 = V(sq.h[:, :].rearrange("p (h d) -> p h d", h=4), sq[:].keys)
                P.reduce(stat[:, 0:4], v3, ALU.add)
                P.reduce(stat[:, 4:8], q3, ALU.add)
                P.ts(stat[:, 0:4], stat[:, 0:4], 1.0 / 64, ALU.mult)
                P.tt(stat[:, 8:12], stat[:, 0:4], stat[:, 0:4], ALU.mult)
                P.stt(stat[:, 4:8], stat[:, 4:8], 1.0 / 64, stat[:, 8:12], ALU.mult, ALU.subtract)
                P.act(stat[:, 4:8], stat[:, 4:8], AF.Ln, bias=LN_EPS)
                P.act(stat[:, 12:16], stat[:, 4:8], AF.Exp, scale=-0.5)
                for h in range(4):
                    P.ts(sq[:, h * 64:(h + 1) * 64], vs[:, h * 64:(h + 1) * 64], stat[:, h:h + 1], ALU.subtract,
                         stat[:, 12 + h:13 + h], ALU.mult)
                P.tt(sq[:], sq[:], self.row("sg_ln_g"), ALU.mult)
                s4 = sq.h[:, :].rearrange("p (a b d) -> p a b d", a=2, b=2)
                rb = self.row("sg_ln_b").ap.rearrange("p (a b d) -> p a b d", a=2, b=2)
                for hh in range(2):
                    P.tt(V(vnp.h[:, :, hh, hh * 64:(hh + 1) * 64], vnp[:].keys), V(s4[:, :, hh, :], sq[:].keys),
                         V(rb[:, :, hh, :], ("sg_rowp",)), ALU.add)
                for pair in range(2):
                    ps = self.nextps()
                    for hh in range(2):
                        P.mm(ps[:, 0:128], V(vnp.h[:, pair, hh, :], vnp[:].keys), T_["wmT"][:, pair * 2 + hh, :],
                             start=(hh == 0), stop=(hh == 1))
                    t2 = self.nf()
                    P.tt(t2[:, 0:128], ps[:, 0:128], self.row("sgb", pair * 128, (pair + 1) * 128), ALU.add)
                    P.tt(self.yT[:, pair, n * 128:(n + 1) * 128], t2[:, 0:128], u_sb[:, pair, n4 * 128:(n4 + 1) * 128], ALU.mult)

    def mixer_fox(self, l):
        P = self.P
        w = self.load_w(self.h_w_in[l][:, 2320:2832], 8, 512, "h_w_in")
        w2 = self.load_w(self.h_w_in[l][:, 2832:3092], 8, 260, "h_w_in")
        if not hasattr(self, "fx"):
            self.fx = dict(
                negcT=P.sb("fx_negcT", [128, 16, 4], F32),
                nfb=P.sb("fx_nfb", [4, 1], F32),
            )
        F = self.fx
        scr = self.scr
        qT = V(scr.h[:, 0:4096].rearrange("p (c t) -> p c t", c=2), ("fx_qT",))
        kT = V(scr.h[:, 4096:8192].rearrange("p (c t) -> p c t", c=2), ("fx_kT",))
        v1ap = scr.h[:, 8192:16384].rearrange("p (n h m) -> p n h m", n=16, h=4)
        v1 = lambda idx: V(v1ap[idx], ("fx_v1",))
        self.scr_phase(["fx_qT", "fx_kT", "fx_v1", "fx_negc"])
        negc_ap = scr.h[:, 16384:20480].bitcast(F32)
        class _N:
            def __getitem__(s_, idx):
                return V(negc_ap[idx], ("fx_negc",))
        F["negc"] = _N(); F["nlf"] = F["negc"]
        P.memset(v1((slice(None),)), 0.0, eng="pool")
        for n in range(16):
            for h in range(4):
                col = 64 if h % 2 == 0 else 0
                P.memset(v1((slice(None), n, h, slice(col, col + 1))), 1.0, eng="pool")
        P.ts(F["nfb"][:], self.col("fox_fb%d" % l, 0, rows=4), -1.0, ALU.mult)
        for tb in range(4):
            self.load_xb(tb)
            sl = slice(tb * 512, (tb + 1) * 512)
            for c in range(2):
                self.proj_fm(w, c * 128, 128, tb,
                             lambda ps, c=c: P.act(V(qT.ap[:, c, sl], qT.keys), ps[:, :], AF.Copy, scale=0.125))
                self.proj_fm(w, 256 + c * 128, 128, tb,
                             lambda ps, c=c: P.copy(V(kT.ap[:, c, sl], kT.keys), ps[:, :], eng="dve"))
            def evf(ps):
                P.act(F["nlf"][0:4, sl], ps[0:4, :], AF.Exp, bias=F["nfb"][:, 0:1], scale=-1.0)
                P.act(F["nlf"][0:4, sl], F["nlf"][0:4, sl], AF.Ln, bias=1.0)
            self.proj_fm(w2, 256, 4, tb, evf)
            for n4 in range(4):
                n = tb * 4 + n4
                def evv(ps, n=n):
                    for h in range(4):
                        col = 0 if h % 2 == 0 else 64
                        P.copy(v1((slice(None), n, h, slice(col, col + 64))), ps[:, h * 64:(h + 1) * 64],
                               eng=("act" if h % 2 else "dve"))
                self.proj_tm(w2, 0, 256, n * 128, evv)
        ones_b = self.C("ones", rows=4, c0=0, c1=1).ap.to_broadcast([4, S])
        P.generic("dve", lambda e: e.tensor_tensor_scan(negc_ap[0:4, :], ones_b, negc_ap[0:4, :], 0.0,
                                                         ALU.mult, ALU.add),
                  [self.consts[:], F["negc"][0:4, :]], [F["negc"][0:4, :]])
        self.debug_dump("fox_negc", F["negc"][0:4, :], [4, S])
        for J in range(16):
            ps = self.nextps()
            P.transpose(ps[:, 0:4], F["negc"][0:4, J * 128:(J + 1) * 128], self.C("ident", rows=4, c0=0, c1=4))
            P.copy(F["negcT"][:, J, :], ps[:, 0:4])
        for h in range(4):
            c = h // 2
            p0 = (h % 2) * 64
            even = (h % 2 == 0)
            for Q in range(4):
                ops_ = self.nextacc()
                nJ = 4 * Q + 4
                for J in range(nJ):
                    c_lo = max(0, (J - 4 * Q) * 128)
                    lg = self.nextps()
                    P.mm(lg[:, c_lo:512], V(kT.ap[p0:p0 + 64, c, J * 128:(J + 1) * 128], kT.keys),
                         V(qT.ap[p0:p0 + 64, c, Q * 512 + c_lo:(Q + 1) * 512], qT.keys), start=True, stop=False)
                    P.mm(lg[:, c_lo:512], self.C("selneg", rows=4, c0=h * 128, c1=(h + 1) * 128),
                         F["negc"][0:4, Q * 512 + c_lo:(Q + 1) * 512], start=False, stop=True)
                    if J >= 4 * Q:
                        P.tt(lg[:, c_lo:c_lo + 128], lg[:, c_lo:c_lo + 128], self.C("negmask"), ALU.add)
                    pT = self.nb()
                    P.act(pT[:, c_lo:512], lg[:, c_lo:512], AF.Exp, bias=F["negcT"][:, J, h:h + 1])
                    M = 65 if even else 128
                    P.mm(ops_[0:M, c_lo:512], v1((slice(None), J, h, slice(0, M))), pT[:, c_lo:512],
                         start=(J == 0), stop=(J == nJ - 1))
                osb = self.nt()
                if even:
                    P.copy(osb[0:65, :], ops_[0:65, :], eng="act")
                    P.recip(osb[64:65, :], osb[64:65, :])
                    bp = self.nextps()
                    P.mm(bp[0:64, :], self.C("sel_even", rows=65), osb[0:65, :])
                    P.tt(self.yT[0:64, c, Q * 512:(Q + 1) * 512], osb[0:64, :], bp[0:64, :], ALU.mult)
                else:
                    P.copy(osb[:, :], ops_[:, :], eng="act")
                    P.recip(osb[0:1, :], osb[0:1, :])
                    bp = self.nextps()
                    P.mm(bp[:, :], self.C("sel_odd"), osb[:, :])
                    P.tt(self.yT[64:128, c, Q * 512:(Q + 1) * 512], osb[64:128, :], bp[64:128, :], ALU.mult)


    def mixer_gla(self, l):
        P = self.P
        w1 = self.load_w(self.h_w_in[l][:, 1536:2048], 8, 512, "h_w_in")
        w2 = self.load_w(self.h_w_in[l][:, 2048:2320], 8, 272, "h_w_in")
        if not hasattr(self, "gl"):
            self.gl = dict(
                aup=P.sb("gl_aup", [16, 128], BF16),
                nab=P.sb("gl_nab", [64, 2], F32),
                gps=P.sb("gl_gps", [64, 32], F32),
                bl=P.sb("gl_bl", [64, 32], F32),
                dec=P.sb("gl_dec", [64, 2, 32], F32),
                S=[P.sb("gl_S%d" % g, [64, 128], F32) for g in range(2)],
                Sbf=[[P.sb("gl_Sbf%d_%d" % (g, i), [64, 128], BF16) for i in range(4)] for g in range(2)],
                osb=[P.sb("gl_osb%d" % i, [128, 128], F32) for i in range(2)],
            )
        G = self.gl
        scr = self.scr
        self.scr_phase(["gl_qe", "gl_ke", "gl_v", "gl_ketm", "gl_G", "gl_adT"])
        class _C:
            def __init__(s_, ap, key):
                s_.ap = ap; s_.key = key
            def __getitem__(s_, idx):
                return V(s_.ap[idx], (s_.key,))
        qe = _C(scr.h[:, 0:4096].rearrange("p (g t) -> p g t", g=2), "gl_qe")
        ke = _C(scr.h[:, 4096:8192].rearrange("p (g t) -> p g t", g=2), "gl_ke")
        v128 = _C(scr.h[:, 8192:12288].rearrange("p (b c) -> p b c", b=16), "gl_v")
        ketm = _C(scr.h[:, 12288:14336].rearrange("p (b c) -> p b c", b=16), "gl_ketm")
        Gt = _C(scr.h[:, 14336:18432].bitcast(F32), "gl_G")
        Gt3 = _C(scr.h[:, 14336:18432].bitcast(F32).rearrange("p (n c) -> p n c", n=32), "gl_G")
        adT = _C(scr.h[:, 18432:20480], "gl_adT")
        P.dma(G["aup"][:], self.hv(self.h_gla_a_up[l], "h_gla_a_up"), eng="pool")
        P.ts(G["nab"][:], V(self.colp.h[0:64, COL_LAYOUT["gla_a_b%d" % l][0]:COL_LAYOUT["gla_a_b%d" % l][0] + 2], self.colp[:].keys),
             -1.0, ALU.mult)
        for tb in range(4):
            self.load_xb(tb)
            sl = slice(tb * 512, (tb + 1) * 512)
            self.proj_fm(w2, 256, 16, tb, lambda ps: P.copy(adT[0:16, sl], ps[0:16, :], eng="act"))
            for c in range(2):
                def evg(ps, c=c):
                    t = self.nt()
                    P.act(t[:], ps[:, :], AF.Silu)
                    P.ts(self.yT[:, c, sl], t[:], self.col("gla_norm_g%d" % l, c), ALU.mult)
                self.proj_fm(w2, c * 128, 128, tb, evg)
            for b4 in range(4):
                bk = tb * 4 + b4
                self.proj_tm(w1, 256, 256, bk * 128, lambda ps, bk=bk: P.copy(v128[:, bk, :], ps[:, 0:256], eng="act"))
        import os
        stop = int(os.environ.get('GLA_STOP', '99'))
        if stop <= 1:
            return
        ones_b = self.C("ones", rows=64, c0=0, c1=1).ap.to_broadcast([64, S])
        for g in range(2):
            for tb in range(4):
                sl = slice(tb * 512, (tb + 1) * 512)
                ps = self.nextps()
                P.mm(ps[0:64, :], G["aup"][:, g * 64:(g + 1) * 64], adT[0:16, sl])
                P.act(Gt[0:64, sl], ps[0:64, :], AF.Exp, bias=G["nab"][:, g:g + 1], scale=-1.0)
                P.act(Gt[0:64, sl], Gt[0:64, sl], AF.Ln, bias=1.0)
            P.generic("dve", lambda e: e.tensor_tensor_scan(Gt.ap[0:64, :], ones_b, Gt.ap[0:64, :], 0.0, ALU.mult, ALU.add),
                      [self.consts[:], Gt[0:64, :]], [Gt[0:64, :]])
            if stop <= 2:
                continue
            P.memset(G["gps"][:, 0:1], 0.0)
            P.copy(G["gps"][:, 1:32], Gt3[0:64, 0:31, 63])
            P.tt(Gt3[0:64, :, :], Gt3[0:64, :, :], V(G["gps"].h[:, :].unsqueeze(2).to_broadcast([64, 32, 64]), G["gps"][:].keys),
                 ALU.subtract)
            P.copy(G["bl"][:], Gt3[0:64, :, 63])
            P.act(G["dec"][:, g, :], G["bl"][:], AF.Exp, scale=-1.0 / 16)
            for tb in range(4):
                sl = slice(tb * 512, (tb + 1) * 512)
                self.load_xb(tb)
                Eb = self.nt(); Enb = self.nt()
                P.act(Eb[0:64, :], Gt[0:64, sl], AF.Exp, scale=-1.0 / 16)
                P.act(Enb[0:64, :], Gt[0:64, sl], AF.Exp, scale=1.0 / 16)
                self.proj_fm(w1, g * 64, 64, tb,
                             lambda ps: P.stt(qe[0:64, g, sl], ps[0:64, :], 32.0 ** -0.5, Eb[0:64, :], ALU.mult, ALU.mult))
                self.proj_fm(w1, 128 + g * 64, 64, tb,
                             lambda ps: P.tt(ke[0:64, g, sl], ps[0:64, :], Enb[0:64, :], ALU.mult))
            if stop <= 3:
                continue
            for bk in range(16):
                ps = self.nextps()
                pbf = V(ps.h[:, :].bitcast(BF16), ps[:].keys)
                P.transpose(V(pbf.ap[:, 0:64], pbf.keys), ke[0:64, g, bk * 128:(bk + 1) * 128], self.ident_bf[0:64, 0:64])
                P.copy(ketm[:, bk, g * 64:(g + 1) * 64], V(pbf.ap[:, 0:64], pbf.keys), eng="act")
        self.debug_dump("gl_qe", qe[0:64, :, :], [64, 2, S])
        self.debug_dump("gl_ke", ke[0:64, :, :], [64, 2, S])
        if stop <= 4:
            return
        for g in range(2):
            P.memset(G["S"][g][:], 0.0)
            P.memset(G["Sbf"][g][0][:], 0.0)
        for bk in range(16):
            for g in range(2):
                attm = []
                for hh in range(2):
                    ps = self.nextps()
                    P.mm(ps[:, 0:128], ke[hh * 32:hh * 32 + 32, g, bk * 128:(bk + 1) * 128],
                         qe[hh * 32:hh * 32 + 32, g, bk * 128:(bk + 1) * 128])
                    am = self.nb()
                    P.tt(am[:, 0:128], ps[:, 0:128], self.C("mask2"), ALU.mult)
                    attm.append(am)
                if stop <= 5:
                    continue
                for cc in range(2):
                    n = 2 * bk + cc
                    if n == 31:
                        break
                    psU = self.nextps()
                    r0 = cc * 64
                    P.mm(psU[0:64, 0:128], ketm[r0:r0 + 64, bk, g * 64:(g + 1) * 64], v128[r0:r0 + 64, bk, g * 128:(g + 1) * 128])
                    P.tt(G["S"][g][:], G["S"][g][:], psU[0:64, 0:128], ALU.add)
                    P.ts(G["S"][g][:], G["S"][g][:], G["dec"][:, g, n:n + 1], ALU.mult)
                    P.copy(G["Sbf"][g][(n + 1) % 4][:], G["S"][g][:], eng="act")
                if stop <= 6:
                    continue
                psO = self.nextps()
                for hh in range(2):
                    c0 = hh * 128
                    P.mm(psO[:, c0:c0 + 128], v128[:, bk, g * 128:(g + 1) * 128], attm[hh][:, 0:128], start=True, stop=False)
                    for cc in range(2):
                        n = 2 * bk + cc
                        P.mm(psO[:, c0 + cc * 64:c0 + (cc + 1) * 64], G["Sbf"][g][n % 4][hh * 32:hh * 32 + 32, :],
                             qe[hh * 32:hh * 32 + 32, g, n * 64:(n + 1) * 64], start=False, stop=(cc == 1))
                if stop <= 7:
                    continue
                osb = G["osb"][g]
                osq = self.nb()
                for hh in range(2):
                    r0 = hh * 64
                    var = os.environ.get('GLA_VAR', 'ab')
                    if 'a' in var:
                        P.copy(osb[r0:r0 + 64, :], psO[r0:r0 + 64, hh * 128:(hh + 1) * 128], eng="dve")
                    if 'b' in var:
                        P.act(osq[r0:r0 + 64, 0:128], psO[r0:r0 + 64, hh * 128:(hh + 1) * 128], AF.Square)
                if stop <= 8:
                    continue
                pss = self.nextps()
                P.mm(pss[:, 0:128], self.blk64_bf[:, :], osq[:, 0:128])
                if stop <= 9:
                    continue
                rs = self.nf()
                P.act(rs[:, 0:128], pss[:, 0:128], AF.Ln, bias=1e-5)
                P.act(rs[:, 0:128], rs[:, 0:128], AF.Exp, scale=-0.5)
                if stop <= 10:
                    continue
                P.tt(osb[:, :], osb[:, :], rs[:, 0:128], ALU.mult)
                if stop <= 11:
                    continue
                yv = self.yT[:, g, bk * 128:(bk + 1) * 128]
                P.tt(yv, osb[:, :], yv, ALU.mult)


    def mixer_rw(self, l):
        P = self.P
        wA = self.load_w(self.h_w_in[l][:, 512:1024], 8, 512, "h_w_in")
        wB = self.load_w(self.h_w_in[l][:, 1024:1536], 8, 512, "h_w_in")
        if not hasattr(self, "rw"):
            self.rw = dict(
                sm=P.sb("rw_sm", [128, 512], BF16),
                omm=P.sb("rw_omm", [128, 24], F32),
                nwa=P.sb("rw_nwa", [64, 8], F32),
                carry=P.sb("rw_carry", [128, 8], F32),
                maskq=P.sb("rw_maskq", [64, 128], F32),
                PC=P.sb("rw_PC", [64, 32], F32),
                gs=P.sb("rw_gs", [64, 4], F32),
                S32=P.sb("rw_S32", [64, 64], F32),
                Sb=[P.sb("rw_Sb%d" % i, [64, 64], BF16) for i in range(2)],
                Nn=[P.sb("rw_Nn%d" % i, [64, 64], BF16) for i in range(2)],
                XN=[[P.sb("rw_XN%d_%d" % (i, j), [64, 128], BF16) for j in range(2)] for i in range(2)],
                Wt=[[P.sb("rw_Wt%d_%d" % (i, j), [64, 64], BF16) for j in range(2)] for i in range(2)],
                Wf=[P.sb("rw_Wf%d" % i, [64, 64], BF16) for i in range(4)],
                RU=[P.sb("rw_RU%d" % i, [64, 128], BF16) for i in range(2)],
            )
            P.copy(self.rw["maskq"][:, 0:64], self.C("tri_le", rows=64, c0=0, c1=64))
            P.copy(self.rw["maskq"][:, 64:128], self.C("tri_lt", rows=64, c0=0, c1=64))
        R = self.rw
        scr = self.scr
        self.scr_phase(["rw_twa", "rw_sgd", "rw_KB", "rw_RA", "rw_VT", "rw_bv"])
        class _C:
            def __init__(s_, ap, key):
                s_.ap = ap; s_.key = key
            def __getitem__(s_, idx):
                return V(s_.ap[idx], (s_.key,))
        twa = _C(scr.h[:, 0:2048], "rw_twa")
        sgd = _C(scr.h[:, 2048:4096], "rw_sgd")
        KB = _C(scr.h[:, 4096:8192].rearrange("p (n c) -> p n c", n=32), "rw_KB")
        RA = _C(scr.h[:, 8192:12288].rearrange("p (n c) -> p n c", n=32), "rw_RA")
        VT = _C(scr.h[:, 12288:14336], "rw_VT")
        bv = _C(scr.h[:, 14336:16384], "rw_bv")
        y1 = self.yT.h[0:64, 1, :]
        class _A:
            def __init__(s_, c0, w, key):
                s_.c0 = c0; s_.w = w; s_.key = key
            def __getitem__(s_, idx):
                p, c = idx
                c = slice(s_.c0 + (c.start or 0), s_.c0 + (s_.w if c.stop is None else c.stop))
                return V(y1[:, c], (s_.key,))
        RQA = [_A(i * 128, 128, "rwq_QA%d" % i) for i in range(4)]
        RQB = [_A(512 + i * 128, 128, "rwq_QB%d" % i) for i in range(4)]
        RTM = [_A(1024 + i * 192, 192, "rwq_TM%d" % i) for i in range(4)]
        ring_keys = tuple(x.key for x in RQA + RQB + RTM)
        P.add("pool", lambda e: e.memset(self._dummy.h[:, :], 0.0), [V(None, ("yT",))], [V(None, ring_keys + (self._dummy.name,))])
        P.dma(R["sm"][0:64, 0:256], self.hv(self.h_rw_w2[l], "h_rw_w2"), eng="pool")
        P.dma(R["sm"][64:128, 0:256], self.hv(self.h_rw_a2[l], "h_rw_a2"), eng="pool")
        P.dma(R["sm"][:, 256:512], self.hv(self.h_rw_g2[l], "h_rw_g2"), eng="pool")
        o_mu, _ = COL_LAYOUT["rw_mu%d" % l]; o_mu64, _ = COL_LAYOUT["rw_mu64_%d" % l]
        mu128 = lambda c: self.colp[:, o_mu + c:o_mu + c + 1]
        mu64 = lambda c: self.colp[0:64, o_mu64 + c:o_mu64 + c + 1]
        P.ts(R["omm"][:, 0:8], self.colp[:, o_mu:o_mu + 8], -1.0, ALU.mult, 1.0, ALU.add)
        P.ts(R["omm"][0:64, 8:24], self.colp[0:64, o_mu64:o_mu64 + 16], -1.0, ALU.mult, 1.0, ALU.add)
        o_h, _ = COL_LAYOUT["rw_h64_%d" % l]
        hcol = lambda h, j: self.colp[0:64, o_h + h * 9 + j:o_h + h * 9 + j + 1]
        for h in range(4):
            P.ts(R["nwa"][:, 2 * h:2 * h + 1], hcol(h, 0), -1.0, ALU.mult)
            P.ts(R["nwa"][:, 2 * h + 1:2 * h + 2], hcol(h, 1), -1.0, ALU.mult)
        pool_tiles = self.t512 + self.st512
        def tmp(i, rows=64):
            t = pool_tiles[i // 2]
            c0 = (i % 2) * 256
            class _T:
                name = t.name
                def __getitem__(s_, idx):
                    if not isinstance(idx, tuple):
                        idx = (idx, slice(0, 256))
                    p, c = idx
                    c = slice(c0 + (c.start or 0), c0 + (256 if c.stop is None else c.stop))
                    return V(t.h[p, c], (t.name,))
                def v3(s_, rows_):
                    return V(t.h[0:rows_, c0:c0 + 256].rearrange("p (n c) -> p n c", n=4), (t.name,))
            return _T()

        def shiftmix(ps, M, mu_ap, omm_ap, cslot, out_t, blk):
            zr = pool_tiles[6]
            if blk == 0:
                P.memset(zr[0:M, 0:1], 0.0)
            else:
                P.copy(zr[0:M, 0:1], R["carry"][0:M, cslot:cslot + 1])
            P.copy(zr[0:M, 1:257], ps[0:M, 0:256], eng="act")
            P.copy(R["carry"][0:M, cslot:cslot + 1], zr[0:M, 256:257])
            P.ts(out_t[0:M, :], zr[0:M, 1:257], omm_ap, ALU.mult)
            P.stt(out_t[0:M, :], zr[0:M, 0:256], mu_ap, out_t[0:M, :], ALU.mult, ALU.add)

        class _XB:
            def __init__(s_, xb, t0):
                s_.xb = xb; s_.t0 = t0
            def __getitem__(s_, idx):
                p, k, c = idx
                return V(s_.xb.h[p, k, slice(s_.t0 + c.start, s_.t0 + c.stop)], (s_.xb.name,))

        for blk in range(8):
            self.cur_xb = _XB(self.xb, blk * 256)
            sl = slice(blk * 256, (blk + 1) * 256)
            z = tmp(0)
            self.proj_fm(wB, 256, 128, 0, lambda ps: shiftmix(ps, 128, mu128(6), R["omm"][:, 6:7], 0, z, blk), ncol=256)
            P.act(twa[0:64, sl], z[0:64, :], AF.Tanh)
            P.copy(twa[64:128, sl], z[64:128, :], eng="pool")
            z2 = tmp(1)
            self.proj_fm(wB, 384, 128, 0, lambda ps: shiftmix(ps, 128, mu128(7), R["omm"][:, 7:8], 1, z2, blk), ncol=256)
            P.act(z2[:, :], z2[:, :], AF.Exp, scale=-1.0)
            P.ts(z2[:, :], z2[:, :], 1.0, ALU.add)
            P.recip(z2[:, :], z2[:, :])
            P.copy(sgd[:, sl], z2[:, :], eng="pool")
        import os
        rstop = int(os.environ.get("RW_STOP", "99"))
        nheads = int(os.environ.get("RW_HEADS", "4"))
        for h in range(nheads):
            for blk in range(8):
                self.cur_xb = _XB(self.xb, blk * 256)
                sl = slice(blk * 256, (blk + 1) * 256)
                n0 = blk * 4
                zr_ = tmp(0); zk_ = tmp(1); zv_ = tmp(2)
                self.proj_fm(wA, h * 64, 64, 0, lambda ps: shiftmix(ps, 64, mu64(h), R["omm"][0:64, 8 + h:9 + h], 2, zr_, blk), ncol=256)
                self.proj_fm(wA, 256 + h * 64, 64, 0, lambda ps: shiftmix(ps, 64, mu64(4 + h), R["omm"][0:64, 12 + h:13 + h], 3, zk_, blk), ncol=256)
                self.proj_fm(wB, h * 64, 64, 0, lambda ps: shiftmix(ps, 64, mu64(8 + h), R["omm"][0:64, 16 + h:17 + h], 4, zv_, blk), ncol=256)
                P.copy(VT[0:64, sl], zv_[0:64, :], eng="pool")
                LD = tmp(3)
                ps = self.nextps()
                P.mm(ps[0:64, 0:256], R["sm"][0:64, h * 64:(h + 1) * 64], twa[0:64, sl])
                P.act(LD[0:64, :], ps[0:64, 0:256], AF.Exp, bias=R["nwa"][:, 2 * h:2 * h + 1], scale=-1.0)
                P.ts(LD[0:64, :], LD[0:64, :], 1.0, ALU.add)
                P.recip(LD[0:64, :], LD[0:64, :])
                P.ts(LD[0:64, :], LD[0:64, :], -0.6065306597126334, ALU.mult)
                A = tmp(4)
                ps = self.nextps()
                P.mm(ps[0:64, 0:256], R["sm"][64:128, h * 64:(h + 1) * 64], twa[64:128, sl])
                P.act(A[0:64, :], ps[0:64, 0:256], AF.Exp, bias=R["nwa"][:, 2 * h + 1:2 * h + 2], scale=-1.0)
                P.ts(A[0:64, :], A[0:64, :], 1.0, ALU.add)
                P.recip(A[0:64, :], A[0:64, :])
                KK = tmp(5)
                P.ts(KK[0:64, :], zk_[0:64, :], hcol(h, 2), ALU.mult)
                sq = self.nb()
                P.act(sq[0:64, 0:256], KK[0:64, :], AF.Square)
                ps = self.nextps()
                P.mm(ps[0:64, 0:256], self.blk64_bf[0:64, 0:64], sq[0:64, 0:256])
                RN = tmp(10)
                P.act(RN[0:64, :], ps[0:64, 0:256], AF.Ln, bias=1e-24, scale=64.0)
                P.act(RN[0:64, :], RN[0:64, :], AF.Exp, scale=-0.5)
                P.tt(KK[0:64, :], KK[0:64, :], RN[0:64, :], ALU.mult)
                K2 = tmp(6)
                P.ts(K2[0:64, :], A[0:64, :], -1.0, ALU.add, hcol(h, 3), ALU.mult)
                P.stt(K2[0:64, :], K2[0:64, :], 1.0, zk_[0:64, :], ALU.add, ALU.mult)
                KKA = tmp(7)
                P.tt(KKA[0:64, :], KK[0:64, :], A[0:64, :], ALU.mult)
                pr = self.nb()
                P.stt(pr[0:64, 0:256], zr_[0:64, :], hcol(h, 4), K2[0:64, :], ALU.mult, ALU.mult)
                ps = self.nextps()
                P.mm(ps[0:64, 0:256], self.blk64_bf[0:64, 0:64], pr[0:64, 0:256])
                P.stt(bv[0:64, sl], ps[0:64, 0:256], 64.0, zv_[0:64, :], ALU.mult, ALU.mult)
                Gb = tmp(8)
                ones_b = self.C("ones", rows=64, c0=0, c1=1).ap.to_broadcast([64, 256])
                P.generic("dve", lambda e, Gb=Gb, LD=LD: e.tensor_tensor_scan(Gb[0:64, :].ap, ones_b, LD[0:64, :].ap, 0.0, ALU.mult, ALU.add),
                          [self.consts[:], LD[0:64, :]], [Gb[0:64, :]])
                P.memset(R["gs"][:, 0:1], 0.0)
                G3 = Gb.v3(64)
                P.copy(R["gs"][:, 1:4], V(G3.ap[:, 0:3, 63], G3.keys))
                P.tt(G3, G3, V(R["gs"].h[:, :].unsqueeze(2).to_broadcast([64, 4, 64]), R["gs"][:].keys), ALU.subtract)
                P.act(R["PC"][:, n0:n0 + 4], V(G3.ap[:, :, 63], G3.keys), AF.Exp)
                Ep = tmp(10); Em = tmp(11); Epm1 = tmp(9)
                P.act(Ep[0:64, :], Gb[0:64, :], AF.Exp)
                P.act(Em[0:64, :], Gb[0:64, :], AF.Exp, scale=-1.0)
                P.tt(Epm1[0:64, :], Gb[0:64, :], LD[0:64, :], ALU.subtract)
                P.act(Epm1[0:64, :], Epm1[0:64, :], AF.Exp)
                P.tt(V(RA.ap[0:64, n0:n0 + 4, 0:64], ("rw_RA",)), zr_.v3(64), Ep.v3(64), ALU.mult)
                P.stt(V(RA.ap[0:64, n0:n0 + 4, 64:128], ("rw_RA",)), KK.v3(64), -1.0, Epm1.v3(64), ALU.mult, ALU.mult)
                P.tt(V(KB.ap[0:64, n0:n0 + 4, 0:64], ("rw_KB",)), K2.v3(64), Em.v3(64), ALU.mult)
                P.tt(V(KB.ap[0:64, n0:n0 + 4, 64:128], ("rw_KB",)), KKA.v3(64), Em.v3(64), ALU.mult)
            if h == 0:
                self.debug_dump("rw_RA", RA[0:64, :, :], [64, 32, 128])
                self.debug_dump("rw_KB", KB[0:64, :, :], [64, 32, 128])
                self.debug_dump("rw_PC", R["PC"][:, :], [64, 32])
            if rstop <= 1:
                continue
            P.memset(R["S32"][:], 0.0)
            P.memset(R["Sb"][0][:], 0.0)
            self._psY = None
            BS = 2
            def slot(n):
                return n % (2 * BS)
            def pre_rounds(ns):
                rounds = []
                def r0():
                    for n in ns:
                        s4 = slot(n)
                        QA = RQA[s4]; QB = RQB[s4]; TM = RTM[s4]; Nn = R["Nn"][n % BS]
                        Kt = KB[0:64, n, 0:64]; Bt = KB[0:64, n, 64:128]; At = RA[0:64, n, 64:128]
                        ps = self.nextps()
                        P.mm(ps[0:64, 0:128], Kt, RA[0:64, n, :])
                        P.mm(ps[0:64, 128:256], Bt, RA[0:64, n, :])
                        P.mm(ps[0:64, 256:320], At, Bt)
                        P.tt(QA[:, :], ps[0:64, 0:128], R["maskq"][:, :], ALU.mult)
                        P.tt(QB[:, :], ps[0:64, 128:256], R["maskq"][:, :], ALU.mult)
                        P.tt(Nn[:, :], ps[0:64, 256:320], self.C("tri_gt", rows=64, c0=0, c1=64), ALU.mult)
                        ps2 = self.nextps()
                        pbf = lambda a_, b_, ps2=ps2: V(ps2.h[0:64, :].bitcast(BF16)[:, a_:b_], ps2[:].keys)
                        P.transpose(pbf(0, 64), Kt, self.ident_bf[0:64, 0:64])
                        P.transpose(pbf(64, 128), Bt, self.ident_bf[0:64, 0:64])
                        P.transpose(pbf(128, 192), VT[0:64, n * 64:(n + 1) * 64], self.ident_bf[0:64, 0:64])
                        P.copy(TM[:, :], pbf(0, 192), eng="act")
                rounds.append(r0)
                st = {}
                def r1():
                    for n in ns:
                        QB = RQB[slot(n)]
                        W = R["Wt"][n % BS][0]
                        P.tt(W[:, :], QB[:, 64:128], self.C("ident", rows=64, c0=0, c1=64), ALU.add)
                        st[n] = dict(W=W, Xp=QB[:, 64:128], Np=R["Nn"][n % BS][:, :], wi=0)
                rounds.append(r1)
                for lev in range(5):
                    def ra(lev=lev):
                        for n in ns:
                            d = st[n]
                            XN = R["XN"][n % BS][lev % 2]
                            ps = self.nextps()
                            P.mm(ps[0:64, 64:128], d["Xp"], d["Np"])
                            if lev < 4:
                                P.mm(ps[0:64, 0:64], d["Np"], d["Xp"])
                                P.copy(XN[:, :], ps[0:64, 0:128], eng="act")
                            else:
                                P.copy(XN[:, 64:128], ps[0:64, 64:128], eng="act")
                            d["XN"] = XN
                    def rb(lev=lev):
                        for n in ns:
                            d = st[n]
                            XN = d["XN"]
                            ps2 = self.nextps()
                            P.mm(ps2[0:64, 0:64], XN[:, 64:128], d["W"][:, :])
                            if lev < 4:
                                d["wi"] += 1
                                Wn = R["Wt"][n % BS][d["wi"] % 2]
                            else:
                                Wn = R["Wf"][slot(n)]
                            P.tt(Wn[:, :], ps2[0:64, 0:64], d["W"][:, :], ALU.add)
                            d["W"] = Wn
                            d["Xp"] = XN[:, 0:64]; d["Np"] = XN[:, 64:128]
                    rounds.append(ra); rounds.append(rb)
                return rounds

            def chain_hops(ns):
                hops = []
                for n in ns:
                    s4 = slot(n)
                    QA = RQA[s4]; QB = RQB[s4]; TM = RTM[s4]; W = R["Wf"][s4]
                    Rt = RA[0:64, n, 0:64]; At = RA[0:64, n, 64:128]
                    Sb = R["Sb"][n % 2]; Sbn = R["Sb"][(n + 1) % 2]
                    RU = R["RU"][n % 2]
                    def h1(n=n, QA=QA, TM=TM, At=At, Sb=Sb, RU=RU):
                        psA = self.nextps()
                        P.mm(psA[0:64, 0:64], At, Sb[:, :], start=True, stop=False)
                        P.mm(psA[0:64, 0:64], QA[:, 64:128], TM[:, 128:192], start=False, stop=True)
                        P.copy(RU[:, 0:64], psA[0:64, 0:64], eng="act")
                        P.ts(R["S32"][:, :], R["S32"][:, :], R["PC"][:, n:n + 1], ALU.mult)
                    def h2(n=n, W=W, RU=RU):
                        psU = self.nextps()
                        P.mm(psU[0:64, 0:64], W[:, :], RU[:, 0:64])
                        P.copy(RU[:, 64:128], psU[0:64, 0:64], eng="act")
                    def h3(n=n, QA=QA, QB=QB, TM=TM, Rt=Rt, Sb=Sb, Sbn=Sbn, RU=RU):
                        psS = self.nextps()
                        P.mm(psS[0:64, 0:64], TM[:, 0:64], TM[:, 128:192], start=True, stop=False)
                        P.mm(psS[0:64, 0:64], TM[:, 64:128], RU[:, 64:128], start=False, stop=True)
                        P.stt(Sbn[:, :], psS[0:64, 0:64], R["PC"][:, n:n + 1], R["S32"][:, :], ALU.mult, ALU.add)
                        P.stt(R["S32"][:, :], psS[0:64, 0:64], R["PC"][:, n:n + 1], R["S32"][:, :], ALU.mult, ALU.add)
                        if n % 8 == 0:
                            self._psY = self.nextacc()
                        psY = self._psY
                        yc = slice((n % 8) * 64, (n % 8 + 1) * 64)
                        P.mm(psY[0:64, yc], Sb[:, :], Rt, start=True, stop=False)
                        P.mm(psY[0:64, yc], TM[:, 128:192], QA[:, 0:64], start=False, stop=False)
                        P.mm(psY[0:64, yc], RU[:, 64:128], QB[:, 0:64], start=False, stop=True)
                        if n % 8 == 7:
                            post(n // 8, psY)
                    hops += [h1, h2, h3]
                return hops

            def post(tb, psY):
                sl = slice(tb * 512, (tb + 1) * 512)
                Y = self.nt()
                P.copy(Y[0:64, :], psY[0:64, :], eng="act")
                if h == 0:
                    self.debug_dump("rw_scan%d" % tb, Y[0:64, :], [64, 512])
                ysq = self.nb(); ybf = self.nb()
                P.act(ysq[0:64, :], Y[0:64, :], AF.Square)
                P.copy(ybf[0:64, :], Y[0:64, :], eng="pool")
                psm = self.nextps(); psq = self.nextps()
                P.mm(psm[0:64, :], self.blk64_bf[0:64, 0:64], ybf[0:64, :])
                P.mm(psq[0:64, :], self.blk64_bf[0:64, 0:64], ysq[0:64, :])
                m2 = self.nt()
                P.tt(Y[0:64, :], Y[0:64, :], psm[0:64, :], ALU.subtract)
                P.act(m2[0:64, :], psm[0:64, :], AF.Square)
                P.tt(m2[0:64, :], psq[0:64, :], m2[0:64, :], ALU.subtract)
                P.act(m2[0:64, :], m2[0:64, :], AF.Ln, bias=64e-5)
                P.act(m2[0:64, :], m2[0:64, :], AF.Exp, scale=-0.5)
                P.stt(Y[0:64, :], Y[0:64, :], hcol(h, 5), m2[0:64, :], ALU.mult, ALU.mult)
                P.stt(Y[0:64, :], Y[0:64, :], hcol(h, 6), bv[0:64, sl], ALU.add, ALU.add)
                psg = self.nextps()
                P.mm(psg[0:64, :], R["sm"][:, 256 + h * 64:256 + (h + 1) * 64], sgd[:, sl])
                P.tt(self.yT[0:64, 0, sl], Y[0:64, :], psg[0:64, :], ALU.mult)

            nb_ = 32 // BS
            for k in range(nb_ + 1):
                pr = pre_rounds(list(range(k * BS, (k + 1) * BS))) if k < nb_ else []
                ch = chain_hops(list(range((k - 1) * BS, k * BS))) if k >= 1 else []
                i = j = 0
                while i < len(pr) or j < len(ch):
                    if j < len(ch):
                        ch[j](); j += 1
                    for _ in range(2):
                        if i < len(pr):
                            pr[i](); i += 1
            if rstop <= 2:
                continue
            self.debug_dump("y_rwh%d" % h, self.yT[0:64, 0, :], [64, S])
            if "ln1" in self.stages:
                self.acc_out_rw(l, h)
        P.add("pool", lambda e: e.memset(self._dummy.h[:, :], 0.0), [V(None, ring_keys)], [V(None, ("yT", self._dummy.name))])

    def acc_out_rw(self, l, h):
        P = self.P
        if not hasattr(self, "rw_wo"):
            self.rw_wo = P.sb("rw_wo", [64, 1024], BF16)
        t = self.rw_wo
        wv = WV(t, 1, 1024)
        P.dma(V(wv.ap[0:64, :, :], (t.name,)),
              V(self.h_w_out[l][256 + h * 64:256 + (h + 1) * 64, :].rearrange("(k p) c -> p k c", p=64), ("h_w_out",)), eng="pool")
        for tb in range(4):
            for o in range(8):
                ps = self.nextps()
                P.mm(ps[:, :], V(wv.ap[0:64, 0, o * 128:(o + 1) * 128], (t.name,)), self.yT[0:64, 0, tb * 512:(tb + 1) * 512])
                xv = self.xT[:, o, tb * 512:(tb + 1) * 512]
                if self._first_acc:
                    P.stt(xv, xv, ALPHA, ps[:, :], ALU.mult, ALU.add)
                else:
                    P.tt(xv, xv, ps[:, :], ALU.add)
        self._first_acc = False


    def mem_ln(self):
        P = self.P
        self.memnb = P.sb("memnb", [128, 8, NMEM], BF16)
        self.scr_phase(["mem_raw"])
        class _C:
            def __init__(s_, ap, key):
                s_.ap = ap; s_.key = key
            def __getitem__(s_, idx):
                return V(s_.ap[idx], (s_.key,))
        raw = _C(self.scr.h[:, :].bitcast(F32)[:, 0:8 * NMEM].rearrange("p (c t) -> p c t", c=8), "mem_raw")
        P.dma(raw[:, :, :], self.hv(self.h_memT.rearrange("(c p) t -> p c t", p=128), "h_memT"))
        self.layernorm("mem_ln_g", "mem_ln_b", src=raw, dst32=False, dstb=self.memnb, ntok=NMEM)

    def cross_attn(self, l):
        P = self.P
        self.scr_phase(["ca_KT", "ca_V", "ca_oT"])
        scr = self.scr
        class _C:
            def __init__(s_, ap, key):
                s_.ap = ap; s_.key = key
            def __getitem__(s_, idx):
                return V(s_.ap[idx], (s_.key,))
        KT = _C(scr.h[:, 0:2048].rearrange("p (c m) -> p c m", c=8), "ca_KT")
        Vt = _C(scr.h[:, 2048:4096].rearrange("p (b c) -> p b c", b=2), "ca_V")
        oT = _C(scr.h[:, 4096:20480].rearrange("p (c t) -> p c t", c=8), "ca_oT")
        if not hasattr(self, "ones_bf"):
            self.ones_bf = P.sb("ones_bf", [128, 128], BF16)
            P.copy(self.ones_bf[:], self.C("ones"))
        stages = []
        def ld_k(half):
            return self.load_w(self.h_ca_wk[l][:, half * 512:(half + 1) * 512], 8, 512, "h_ca_wk")
        def cp_k(wk, half):
            for oc in range(4):
                ps = self.nextps()
                for kc in range(8):
                    P.mm(ps[:, 0:NMEM], wk[:, kc, oc * 128:(oc + 1) * 128], self.memnb[:, kc, :], start=(kc == 0), stop=(kc == 7))
                P.copy(KT[:, half * 4 + oc, :], ps[:, 0:NMEM], eng="act")
        def ld_v(half):
            return self.load_w(self.h_ca_wv[l][:, half * 512:(half + 1) * 512], 8, 512, "h_ca_wv")
        def cp_v(wv, half):
            for mb in range(2):
                ps = self.nextps()
                for kc in range(8):
                    P.mm(ps[:, :], self.memnb[:, kc, mb * 128:(mb + 1) * 128], wv[:, kc, :], start=(kc == 0), stop=(kc == 7))
                P.copy(Vt[:, mb, half * 512:(half + 1) * 512], ps[:, :], eng="dve")
        def ld_q(h):
            return self.load_w(self.h_ca_wq[l][:, h * 256:(h + 1) * 256], 8, 256, "h_ca_wq")
        def cp_q(wq, h):
            for tb in range(4):
                self.load_xb(tb)
                sl = slice(tb * 512, (tb + 1) * 512)
                qT = [self.nb(), self.nb()]
                for c in range(2):
                    self.proj_fm(wq, c * 128, 128, tb, lambda ps, c=c: P.act(qT[c][:, :], ps[:, :], AF.Copy, scale=1.0 / 16))
                PT = []
                for mb in range(2):
                    ps = self.nextps()
                    for c in range(2):
                        P.mm(ps[:, :], KT[:, h * 2 + c, mb * 128:(mb + 1) * 128], qT[c][:, :], start=(c == 0), stop=(c == 1))
                    pt = self.nb()
                    P.act(pt[:, :], ps[:, :], AF.Exp)
                    PT.append(pt)
                den = self.nextps()
                for mb in range(2):
                    P.mm(den[:, :], self.ones_bf[:, :], PT[mb][:, :], start=(mb == 0), stop=(mb == 1))
                rden = self.nt()
                P.recip(rden[:, :], den[:, :])
                for c2 in range(2):
                    ps = self.nextps()
                    for mb in range(2):
                        P.mm(ps[:, :], Vt[:, mb, h * 256 + c2 * 128:h * 256 + (c2 + 1) * 128], PT[mb][:, :], start=(mb == 0), stop=(mb == 1))
                    P.tt(oT[:, h * 2 + c2, sl], ps[:, :], rden[:, :], ALU.mult)
        def ld_o(half):
            return self.load_w(self.h_ca_wo[l][:, half * 512:(half + 1) * 512], 8, 512, "h_ca_wo")
        def cp_o(wo, half):
            for tb in range(4):
                sl = slice(tb * 512, (tb + 1) * 512)
                for oc in range(4):
                    ps = self.nextps()
                    for kc in range(8):
                        P.mm(ps[:, :], wo[:, kc, oc * 128:(oc + 1) * 128], oT[:, kc, sl], start=(kc == 0), stop=(kc == 7))
                    xv = self.xT[:, half * 4 + oc, sl]
                    P.stt(xv, xv, ALPHA, ps[:, :], ALU.mult, ALU.add)
        for half in range(2):
            stages.append((ld_k, cp_k, half))
        for half in range(2):
            stages.append((ld_v, cp_v, half))
        for h in range(4):
            stages.append((ld_q, cp_q, h))
        for half in range(2):
            stages.append((ld_o, cp_o, half))
        cur = stages[0][0](stages[0][2])
        for i, (ld, cp, arg) in enumerate(stages):
            nxt = None
            if i + 1 < len(stages):
                nxt = stages[i + 1][0](stages[i + 1][2])
            cp(cur, arg)
            cur = nxt

    def conv_ffn(self, l):
        P = self.P
        self.scr_phase(["ff_w0", "ff_w1", "ff_w2", "ff_w3", "ff_pr0", "ff_pr1"])
        scr = self.scr
        if not hasattr(self, "ff"):
            self.ff = dict(halo=P.sb("ff_halo", [128, 8, 2], F32))
        y32 = self.yT.h[:, :, :].rearrange("p a b -> p (a b)").bitcast(F32)
        class _H:
            def __init__(s_, i):
                s_.i = i
            def __getitem__(s_, idx):
                p, c = idx
                return V(y32[p, slice(s_.i * 520 + c.start, s_.i * 520 + c.stop)], ("ffh%d" % s_.i,))
        self.ffh = [_H(i) for i in range(3)]
        self.P.add("pool", lambda e: e.memset(self._dummy.h[:, :], 0.0), [V(None, ("yT",))],
                   [V(None, ("ffh0", "ffh1", "ffh2", self._dummy.name))])
        class _WB:
            def __init__(s_, ap, key, kc, cols):
                s_.ap = ap[:, 0:kc * cols].rearrange("p (k c) -> p k c", k=kc); s_.key = key; s_.kc = kc
            def __getitem__(s_, idx):
                return V(s_.ap[idx], (s_.key,))
        bufs = [(self.wb[0].h[:, :], "wb0"), (self.wb[1].h[:, :], "wb1")] + \
               [(scr.h[:, i * 4096:(i + 1) * 4096], "ff_w%d" % i) for i in range(4)]
        prs = [(scr.h[:, 16384 + i * 2048:16384 + (i + 1) * 2048].rearrange("p (j t) -> p j t", j=4), "ff_pr%d" % i) for i in range(2)]
        def loadw(bi, hbm_ap, kc, cols, key):
            ap, k = bufs[bi]
            wv = _WB(ap, k, kc, cols)
            src = hbm_ap.rearrange("(k p) c -> p k c", p=128)
            step = 4 if cols <= 512 else 2
            kk = 0
            while kk < kc:
                k2 = min(kc, kk + step)
                P.dma(V(wv.ap[:, kk:k2, :], (k,)), V(src[:, kk:k2, :], (key,)), eng="pool")
                kk = k2
            return wv
        o_ub, _ = COL_LAYOUT["ffn_up_b"]; o_cb, _ = COL_LAYOUT["ffn_conv_b"]; o_cw, _ = COL_LAYOUT["ffn_conv"]
        colv = lambda o: self.colp[:, o:o + 1]
        npg = 6
        def load_pg(pg):
            nj = 4 if pg < 5 else 2
            b0 = (pg % 2) * 3
            wg = loadw(b0, self.h_ffn_up[l][:, pg * 512:pg * 512 + nj * 128], 8, nj * 128, "h_ffn_up")
            wvv = loadw(b0 + 1, self.h_ffn_up[l][:, DFF + pg * 512:DFF + pg * 512 + nj * 128], 8, nj * 128, "h_ffn_up")
            wd = loadw(b0 + 2, self.h_ffn_down[l][pg * 512:pg * 512 + nj * 128, :], nj, 1024, "h_ffn_down")
            return wg, wvv, wd
        nxt = load_pg(0)
        for pg in range(npg):
            nj = 4 if pg < 5 else 2
            wg, wvv, wd = nxt
            if pg + 1 < npg:
                nxt = load_pg(pg + 1)
            for tb in range(4):
                self.load_xb(tb)
                sl = slice(tb * 512, (tb + 1) * 512)
                prap, prk = prs[(pg * 4 + tb) % 2]
                for jj in range(nj):
                    res = []
                    for part, wsrc in enumerate((wg, wvv)):
                        ch = part * 22 + pg * 4 + jj
                        hb = self.nt()
                        hs = part * 4 + jj
                        ps = self.nextps()
                        for kc in range(8):
                            P.mm(ps[:, :], wsrc[:, kc, jj * 128:(jj + 1) * 128], self.cur_xb[:, kc, 0:512], start=(kc == 0), stop=(kc == 7))
                        hbuf = self.ffh[self._ffh_i % 3]; self._ffh_i += 1
                        if tb == 0:
                            P.memset(hbuf[:, 0:2], 0.0, eng="dve")
                        else:
                            P.copy(hbuf[:, 0:2], self.ff["halo"][:, hs, :], eng="dve")
                        P.act(hbuf[:, 2:514], ps[:, :], AF.Identity, bias=colv(o_ub + ch))
                        P.copy(self.ff["halo"][:, hs, :], hbuf[:, 512:514], eng="dve")
                        P.act(hb[:, :], hbuf[:, 0:512], AF.Identity, bias=colv(o_cb + ch), scale=colv(o_cw + ch))
                        P.stt(hb[:, :], hbuf[:, 1:513], colv(o_cw + 44 + ch), hb[:, :], ALU.mult, ALU.add)
                        P.stt(hb[:, :], hbuf[:, 2:514], colv(o_cw + 88 + ch), hb[:, :], ALU.mult, ALU.add)
                        res.append(hb)
                    P.act(res[0][:, :], res[0][:, :], AF.Gelu)
                    P.tt(V(prap[:, jj, :], (prk,)), res[0][:, :], res[1][:, :], ALU.mult)
                for o in range(8):
                    ps = self.nextps()
                    for jj in range(nj):
                        P.mm(ps[:, :], wd[:, jj, o * 128:(o + 1) * 128], V(prap[:, jj, :], (prk,)), start=(jj == 0), stop=(jj == nj - 1))
                    xv = self.xT[:, o, sl]
                    if pg == 0:
                        P.stt(xv, xv, ALPHA, ps[:, :], ALU.mult, ALU.add)
                    else:
                        P.tt(xv, xv, ps[:, :], ALU.add)

    def build(self):
        P = self.P
        with ExitStack() as st:
            P.enter(st)
            self.decl()
            self.alloc()
            P.dma(self.consts[:], self.hv(self.h_consts, "h_consts"))
            P.dma(self.colp[:], self.hv(self.h_colp[0], "h_colp"))
            xsrc = self.h_xT.rearrange("(c p) t -> p c t", p=128)
            for tb in range(4):
                sl = slice(tb * 512, (tb + 1) * 512)
                P.dma(self.xT[:, :, sl], self.hv(xsrc[:, :, sl], "h_xT"))
                P.copy(self.xb[:, :, sl], self.xT[:, :, sl], eng=("act" if tb % 2 else "dve"))
            P.copy(self.ident_bf[:], self.C("ident"))
            P.ts(self.ones_s[:], self.C("ones"), 1.0 / 1024, ALU.mult)
            P.copy(self.blk64_bf[:], self.C("blk64"))
            if "ca" in self.stages:
                self.mem_ln()
            for l in range(self.nlayers):
                self.layer(l)
            osrc = self.h_outT.rearrange("(c p) t -> p c t", p=128)
            for tb in range(4):
                sl = slice(tb * 512, (tb + 1) * 512)
                P.dma(self.hv(osrc[:, :, sl], "h_outT"), self.xT[:, :, sl])
            P.finalize()
            P.emit(st)
        return self.nc

    def layer(self, l):
        P = self.P
        if l > 0:
            P.dma(self.colp[:], self.hv(self.h_colp[l], "h_colp"))
        self._first_acc = True
        if l > 0 and "ffn" in self.stages:
            self.P.add("pool", lambda e: e.memset(self._dummy.h[:, :], 0.0), [V(None, ("ffh0", "ffh1", "ffh2"))],
                       [V(None, ("yT", self._dummy.name))])
        for m, (name, fn) in enumerate([("sg", self.mixer_sg), ("rw", self.mixer_rw), ("gla", self.mixer_gla), ("fox", self.mixer_fox)]):
            if name in self.stages and fn is not None:
                fn(l)
                if name == "rw":
                    continue
                self.debug_dump("y_%s%d" % (name, l), self.yT[:], [128, 2, S])
                if "ln1" in self.stages:
                    self.acc_out(self.h_w_out[l][m * 256:(m + 1) * 256, :], self.yT, 2, "h_w_out", self._first_acc)
                    self._first_acc = False
        if "ln1" in self.stages:
            self.layernorm("ln1_g%d" % l, "ln1_b%d" % l)
            self.debug_dump("x1_%d" % l, self.xT[:], [128, 8, S])
        if "ca" in self.stages:
            self.cross_attn(l)
            self.layernorm("ln2_g%d" % l, "ln2_b%d" % l)
            self.debug_dump("x2_%d" % l, self.xT[:], [128, 8, S])
        if "ffn" in self.stages:
            self.conv_ffn(l)
            self.layernorm("ln3_g%d" % l, "ln3_b%d" % l)


_CACHE = {}


def kernel(**inputs):
    inp = {k: np.ascontiguousarray(np.asarray(v, dtype=np.float32)) for k, v in inputs.items()}
    n = 8
    nc = bass.Bass("TRN2", target_bir_lowering=False)
    mk = MK(nc)
    mk.build()
    consts = make_consts()
    colp = make_colp(inp)
    rowp = make_rowp(inp)
    sgwT = np.ascontiguousarray(inp["sg_w"].transpose(0, 3, 1, 2))
    shared = dict(consts=consts, colp=colp, rowp=rowp, sgwT=sgwT,
                  w_in=inp["w_in"], w_out=inp["w_out"], gla_a_up=inp["gla_a_up"],
                  rw_w2=inp["rw_w2"], rw_a2=inp["rw_a2"], rw_g2=inp["rw_g2"],
                  ca_wq=inp["ca_wq"], ca_wk=inp["ca_wk"], ca_wv=inp["ca_wv"], ca_wo=inp["ca_wo"],
                  ffn_up=inp["ffn_up"], ffn_down=inp["ffn_down"])
    maps = []
    for b in range(n):
        m = dict(shared)
        m["xT"] = np.ascontiguousarray(inp["x"][b].T)
        m["memT"] = np.ascontiguousarray(inp["mem"][b].T)
        maps.append(m)
    res = run_bass_kernel_spmd(nc, maps, core_ids=list(range(n)))
    out = np.stack([np.asarray(res.results[b]["outT"]).T for b in range(n)]).astype(np.float32)
    return np.ascontiguousarray(out)
```

```python
from contextlib import ExitStack
from concourse.bass_utils import run_bass_kernel_spmd
import numpy as np
import concourse.bass as bass
import concourse.mybir as mybir

F32 = mybir.dt.float32
BF16 = mybir.dt.bfloat16
AF = mybir.ActivationFunctionType
ALU = mybir.AluOpType
AX = mybir.AxisListType

ENGS = ("pe", "act", "dve", "pool", "sp")


class V:
    __slots__ = ("ap", "keys")

    def __init__(self, ap, keys):
        self.ap = ap
        self.keys = keys


class T:
    def __init__(self, handle, name, shape):
        self.h = handle
        self.name = name
        self.shape = shape

    def __getitem__(self, idx):
        return V(self.h[idx], (self.name,))

    def k(self, sub):
        return _TK(self, sub)


class _TK:
    def __init__(self, t, sub):
        self.t = t
        self.sub = sub

    def __getitem__(self, idx):
        return V(self.t.h[idx], ((self.t.name, self.sub),))


class Op:
    __slots__ = ("eng", "fn", "reads", "writes", "idx", "deps", "signal", "sigidx",
                 "is_dma", "dslot", "dcnt", "snap", "xreads")

    def __init__(self, eng, fn, reads, writes, is_dma=False):
        self.eng = eng
        self.fn = fn
        self.reads = reads
        self.writes = writes
        self.is_dma = is_dma
        self.deps = []
        self.signal = False
        self.sigidx = 0
        self.dslot = -1
        self.dcnt = 0
        self.snap = None


class Prog:
    N_DSEM = 24

    def __init__(self, nc):
        self.nc = nc
        self.ops = []
        self._stack = None
        self.ntile = 0
        self.psum_names = set()

    def enter(self, stack):
        self._stack = stack

    def sb(self, name, shape, dt=F32):
        h = self._stack.enter_context(self.nc.sbuf_tensor("s_" + name, list(shape), dt))
        return T(h, name, shape)

    def ps(self, name, shape, dt=F32):
        h = self._stack.enter_context(self.nc.psum_tensor("p_" + name, list(shape), dt))
        self.psum_names.add(name)
        return T(h, name, shape)

    def _keys(self, vs):
        ks = []
        for v in vs:
            if v is None or isinstance(v, (int, float)):
                continue
            ks.extend(v.keys)
        return ks

    def add(self, eng, fn, reads, writes, is_dma=False):
        op = Op(eng, fn, self._keys(reads), self._keys(writes), is_dma)
        op.xreads = [k for k in op.reads if (k if isinstance(k, str) else k[0]) in self.psum_names and k not in op.writes]
        self.ops.append(op)
        return op

    def dma(self, out, in_, eng="sp", **kw):
        def fn(e, out=out, in_=in_):
            return e.dma_start(out=out.ap, in_=in_.ap, **kw)
        return self.add(eng, fn, [in_], [out], is_dma=True)

    def mm(self, out, lhsT, rhs, start=True, stop=True, **kw):
        def fn(e):
            return e.matmul(out.ap, lhsT.ap, rhs.ap, start=start, stop=stop, **kw)
        return self.add("pe", fn, [lhsT, rhs], [out])

    def transpose(self, out, in_, ident):
        def fn(e):
            return e.transpose(out.ap, in_.ap, ident.ap)
        return self.add("pe", fn, [in_, ident], [out])

    def act(self, out, in_, func, bias=0.0, scale=1.0, eng="act", accum_out=None):
        def fn(e):
            kw = {}
            if accum_out is not None:
                kw["accum_out"] = accum_out.ap
            return e.activation(out.ap, in_.ap, func,
                                bias=(bias.ap if isinstance(bias, V) else bias),
                                scale=(scale.ap if isinstance(scale, V) else scale), **kw)
        return self.add("act", fn, [in_, bias, scale], [out, accum_out])

    def tt(self, out, in0, in1, op, eng="dve"):
        def fn(e):
            return e.tensor_tensor(out.ap, in0.ap, in1.ap, op)
        return self.add(eng, fn, [in0, in1], [out])

    def ts(self, out, in0, s1, op0, s2=None, op1=None, eng="dve", accum_out=None):
        def fn(e):
            a1 = s1.ap if isinstance(s1, V) else s1
            a2 = s2.ap if isinstance(s2, V) else s2
            kw = {}
            if accum_out is not None:
                kw["accum_out"] = accum_out.ap
            if op1 is None:
                return e.tensor_scalar(out.ap, in0.ap, a1, None, op0, **kw)
            return e.tensor_scalar(out.ap, in0.ap, a1, a2, op0, op1, **kw)
        return self.add(eng, fn, [in0, s1, s2], [out, accum_out])

    def stt(self, out, in0, scalar, in1, op0, op1, eng="dve"):
        def fn(e):
            s = scalar.ap if isinstance(scalar, V) else scalar
            return e.scalar_tensor_tensor(out.ap, in0.ap, s, in1.ap, op0, op1)
        return self.add(eng, fn, [in0, scalar, in1], [out])

    def copy(self, out, in_, eng="dve"):
        if eng == "act":
            def fn(e):
                return e.copy(out.ap, in_.ap)
        else:
            def fn(e):
                return e.tensor_copy(out.ap, in_.ap)
        return self.add(eng, fn, [in_], [out])

    def memset(self, out, val, eng="dve"):
        def fn(e):
            return e.memset(out.ap, val)
        return self.add(eng, fn, [], [out])

    def reduce(self, out, in_, op, axis=AX.X, eng="dve"):
        def fn(e):
            return e.tensor_reduce(out.ap, in_.ap, axis, op)
        return self.add(eng, fn, [in_], [out])

    def recip(self, out, in_):
        def fn(e):
            return e.reciprocal(out.ap, in_.ap)
        return self.add("dve", fn, [in_], [out])

    def generic(self, eng, fn, reads, writes):
        return self.add(eng, fn, reads, writes)

    def finalize(self, out_keys=()):
        nc = self.nc
        ops = self.ops
        last_w = {}
        readers = {}
        for i, op in enumerate(ops):
            op.idx = i
            deps = set()
            for k in op.reads:
                w = last_w.get(k)
                if w is not None:
                    deps.add(w)
            for k in list(op.writes) + op.xreads:
                w = last_w.get(k)
                if w is not None:
                    deps.add(w)
                latest = {}
                for r in readers.get(k, ()):
                    ro = ops[r]
                    if ro.is_dma:
                        deps.add(r)
                    else:
                        latest[ro.eng] = r
                for r in latest.values():
                    deps.add(r)
            deps.discard(i)
            op.deps = sorted(deps)
            for k in op.reads:
                lst = readers.setdefault(k, [])
                if not op.is_dma:
                    lst[:] = [r for r in lst if ops[r].is_dma or ops[r].eng != op.eng]
                lst.append(i)
            for k in op.writes:
                last_w[k] = i
                readers[k] = []
            for k in op.xreads:
                readers[k] = [i]
        for op in ops:
            need = []
            for d in op.deps:
                p = ops[d]
                if p.is_dma:
                    need.append(d)
                    continue
                if p.eng == op.eng:
                    if op.is_dma:
                        need.append(d)
                        continue
                    if op.eng == "pe":
                        continue
                    raw = any(k in p.writes for k in op.reads)
                    if raw:
                        need.append(d)
                    continue
                need.append(d)
            op.deps = need
            for d in need:
                if not ops[d].is_dma:
                    ops[d].signal = True
        cnt = {e: 0 for e in ENGS}
        for op in ops:
            if op.is_dma:
                continue
            if op.signal:
                cnt[op.eng] += 1
                op.sigidx = cnt[op.eng]
        dcount = [0] * self.N_DSEM
        nd = 0
        for op in ops:
            if op.is_dma:
                op.dslot = nd % self.N_DSEM
                dcount[op.dslot] += 1
                op.dcnt = dcount[op.dslot]
                nd += 1
        self.n_dma = nd
        self.sig_counts = cnt
        return self

    def emit(self, stack):
        nc = self.nc
        ops = self.ops
        sems = {e: stack.enter_context(nc.semaphore("S_" + e)) for e in ENGS if e != "sp"}
        dsems = [stack.enter_context(nc.semaphore("D%d" % i)) for i in range(self.N_DSEM)]
        block = stack.enter_context(nc.Block())
        seen = {e: {x: 0 for x in ENGS} for e in ENGS}
        seen_d = {e: [0] * self.N_DSEM for e in ENGS}
        plan = {e: [] for e in ENGS}
        last_dma_on_slot = [None] * self.N_DSEM
        for op in ops:
            e = op.eng
            waits = []
            if op.is_dma:
                prev = last_dma_on_slot[op.dslot]
                if prev is not None and seen_d[e][op.dslot] < prev.dcnt * 16:
                    waits.append((dsems[op.dslot], prev.dcnt * 16))
                    seen_d[e][op.dslot] = prev.dcnt * 16
                last_dma_on_slot[op.dslot] = op
            for d in op.deps:
                p = ops[d]
                if p.is_dma:
                    v = p.dcnt * 16
                    if seen_d[e][p.dslot] < v:
                        waits.append((dsems[p.dslot], v))
                        seen_d[e][p.dslot] = v
                else:
                    v = p.sigidx
                    if seen[e][p.eng] < v:
                        waits.append((sems[p.eng], v))
                        seen[e][p.eng] = v
                        for x in ENGS:
                            if p.snap[x] > seen[e][x]:
                                seen[e][x] = p.snap[x]
            if not op.is_dma:
                snap = dict(seen[e])
                if op.signal:
                    snap[e] = max(snap[e], op.sigidx)
                op.snap = snap
            plan[e].append((waits, op))
        final_waits = []
        for s in range(self.N_DSEM):
            lp = last_dma_on_slot[s]
            if lp is not None:
                final_waits.append((dsems[s], lp.dcnt * 16))

        def run(engname, e):
            for waits, op in plan[engname]:
                for (s, v) in waits:
                    e.wait_ge(s, v)
                ins = op.fn(e)
                if op.is_dma:
                    ins.then_inc(dsems[op.dslot], 16)
                elif op.signal:
                    ins.then_inc(sems[op.eng], 1)

        @block.tensor
        def _(e):
            run("pe", e)

        @block.scalar
        def _(e):
            run("act", e)

        @block.vector
        def _(e):
            run("dve", e)

        @block.gpsimd
        def _(e):
            run("pool", e)

        @block.sync
        def _(e):
            run("sp", e)
            for (s, v) in final_waits:
                e.wait_ge(s, v)
        self.stats = {e: len(plan[e]) for e in ENGS}
        self.nwaits = {e: sum(len(w) for w, _ in plan[e]) for e in ENGS}


S = 2048
D = 1024
L = 4
NMEM = 256
DFF = 2816
ALPHA = (2.0 * L) ** 0.25
LN_EPS = 1e-5
NEG = -30000.0

CONST_LAYOUT = {}
_off = 0
for _n, _w in [("ident", 128), ("tri_le", 128), ("tri_lt", 64), ("negmask", 128), ("selneg", 4 * 128),
               ("ones", 128), ("sel_even", 64), ("sel_odd", 128), ("tri_gt", 64), ("blk64", 128), ("mask2", 128)]:
    CONST_LAYOUT[_n] = (_off, _w)
    _off += _w
NCONST = _off


def make_consts():
    c = np.zeros((128, NCONST), np.float32)
    def put(n, a):
        o, w = CONST_LAYOUT[n]
        c[:a.shape[0], o:o + w] = a
    i = np.arange(128)
    put("ident", np.eye(128, dtype=np.float32))
    put("tri_le", (i[:, None] <= i[None, :]).astype(np.float32))
    put("tri_lt", (i[:, None] < i[None, :]).astype(np.float32)[:, :64])
    put("tri_gt", (i[:, None] > i[None, :]).astype(np.float32)[:, :64])
    put("negmask", np.where(i[:, None] <= i[None, :], 0.0, NEG).astype(np.float32))
    sn = np.zeros((12, 4 * 128), np.float32)
    for h in range(4):
        for j in range(3):
            sn[j * 4 + h, h * 128:(h + 1) * 128] = -1.0
    put("selneg", sn)
    put("ones", np.ones((128, 128), np.float32))
    se = np.zeros((65, 64), np.float32); se[64, :] = 1.0
    put("sel_even", se)
    so = np.zeros((128, 128), np.float32); so[0, 64:128] = 1.0
    put("sel_odd", so)
    put("blk64", ((i[:, None] // 64) == (i[None, :] // 64)).astype(np.float32) / 64.0)
    put("mask2", ((i[:, None] <= i[None, :]) & ((i[:, None] // 64) == (i[None, :] // 64))).astype(np.float32))
    return c


def col_layout():
    lay = {}
    off = 0
    def add(n, w):
        nonlocal off
        lay[n] = (off, w)
        off += w
    add("mem_ln_g", 8); add("mem_ln_b", 8)
    for n in ("ln1_g", "ln1_b", "ln2_g", "ln2_b", "ln3_g", "ln3_b"):
        add(n, 8)
    add("ffn_up_b", 44); add("ffn_conv_b", 44); add("ffn_conv", 132)
    add("fox_fb", 1)
    add("gla_a_b", 2)
    add("gla_norm_g", 2)
    add("rw_mu", 8)
    add("rw_mu64_", 16)
    add("rw_h64_", 4 * 9)
    for n in list(lay.keys()):
        for l in range(L):
            lay["%s%d" % (n, l)] = lay[n]
    return lay, off


COL_LAYOUT, NCOL = col_layout()


def chunkcols(v, p=128):
    return np.ascontiguousarray(v.reshape(-1, p).T)


def make_colp(inp):
    call = np.zeros((L, 128, NCOL), np.float32)
    for l in range(L):
        c = call[l]
        def put(n, a):
            o, w = COL_LAYOUT[n]
            assert a.shape[1] == w, (n, a.shape, w)
            c[:a.shape[0], o:o + w] = a
        put("mem_ln_g", chunkcols(inp["mem_ln_g"])); put("mem_ln_b", chunkcols(inp["mem_ln_b"]))
        for n in ("ln1_g", "ln1_b", "ln2_g", "ln2_b", "ln3_g", "ln3_b"):
            put(n, chunkcols(inp[n][l]))
        put("ffn_up_b", chunkcols(inp["ffn_up_b"][l]))
        put("ffn_conv_b", chunkcols(inp["ffn_conv_b"][l]))
        put("ffn_conv", np.concatenate([chunkcols(inp["ffn_conv"][l][j]) for j in range(3)], 1))
        put("fox_fb", inp["fox_fb"][l].reshape(4, 1))
        put("gla_a_b", chunkcols(inp["gla_a_b"][l], 64))
        put("gla_norm_g", chunkcols(inp["gla_norm_g"][l]))
        put("rw_mu", chunkcols(inp["rw_mu"][l]))
        put("rw_mu64_", chunkcols(inp["rw_mu"][l], 64))
        h64 = np.zeros((64, 36), np.float32)
        for h in range(4):
            sl = slice(h * 64, (h + 1) * 64)
            for j, nm in enumerate(["rw_w0", "rw_a0", "rw_kk", "rw_ka", None, "rw_lnx_g", "rw_lnx_b"]):
                if nm is not None:
                    h64[:, h * 9 + j] = inp[nm][l][sl]
            h64[:, h * 9 + 4] = inp["rw_rk"][l][h]
        put("rw_h64_", h64)
    return call


ROW_LAYOUT = {}
_off = 0
for _n, _w in [("sg_ln_g", 256), ("sg_ln_b", 256), ("sgb", 256)]:
    ROW_LAYOUT[_n] = (_off, _w)
    _off += _w
NROW = _off


def make_rowp(inp):
    r = np.zeros((L, 128, NROW), np.float32)
    for l in range(L):
        def put(n, a):
            o, w = ROW_LAYOUT[n]
            r[l, :, o:o + w] = a
        put("sg_ln_g", np.tile(inp["sg_ln_g"][l][None], (128, 1)))
        put("sg_ln_b", np.tile(inp["sg_ln_b"][l][None], (128, 1)))
        sgb = np.zeros((128, 2, 128), np.float32)
        for p in range(128):
            for pair in range(2):
                sgb[p, pair] = inp["sg_b"][l][pair * 2 + p // 64]
        put("sgb", sgb.reshape(128, 256))
    return r


class WV:
    def __init__(self, tile, kc, cols):
        self.tile = tile
        self.kc = kc
        self.cols = cols
        self.ap = tile.h[:, 0:kc * cols].rearrange("p (k c) -> p k c", k=kc)

    def __getitem__(self, idx):
        return V(self.ap[idx], (self.tile.name,))


class MK:
    def __init__(self, nc, nlayers=L, stages=("sg", "fox", "gla", "rw", "ln1", "ca", "ln2", "ffn", "ln3"), dbg=()):
        self.nc = nc
        self.nlayers = nlayers
        self.stages = stages
        self.dbg = dbg
        self.P = Prog(nc)
        self.dbg_out = {}

    def decl(self):
        nc = self.nc
        di = lambda n, shp: nc.dram_tensor(n, list(shp), F32, kind="ExternalInput").ap()
        self.h_xT = di("xT", [D, S])
        self.h_memT = di("memT", [D, NMEM])
        self.h_consts = di("consts", [128, NCONST])
        self.h_colp = di("colp", [L, 128, NCOL])
        self.h_rowp = di("rowp", [L, 128, NROW])
        self.h_w_in = di("w_in", [L, D, 3092])
        self.h_w_out = di("w_out", [L, D, D])
        self.h_sgwT = di("sgwT", [L, 128, 4, 128])
        self.h_gla_a_up = di("gla_a_up", [L, 16, 128])
        self.h_rw_w2 = di("rw_w2", [L, 64, 256])
        self.h_rw_a2 = di("rw_a2", [L, 64, 256])
        self.h_rw_g2 = di("rw_g2", [L, 128, 256])
        self.h_ca_wq = di("ca_wq", [L, D, D]); self.h_ca_wk = di("ca_wk", [L, D, D])
        self.h_ca_wv = di("ca_wv", [L, D, D]); self.h_ca_wo = di("ca_wo", [L, D, D])
        self.h_ffn_up = di("ffn_up", [L, D, 2 * DFF]); self.h_ffn_down = di("ffn_down", [L, DFF, D])
        self.h_outT = nc.dram_tensor("outT", [D, S], F32, kind="ExternalOutput").ap()

    def hv(self, ap, key):
        return V(ap, (key,))

    def alloc(self):
        P = self.P
        self.xT = P.sb("xT", [128, 8, S], F32)
        self.xb = P.sb("xb", [128, 8, S], BF16)
        self.cur_xb = None
        self.yT = P.sb("yT", [128, 2, S], BF16)
        self.consts = P.sb("consts", [128, NCONST], F32)
        self.colp = P.sb("colp", [128, NCOL], F32)
        self.ident_bf = P.sb("ident_bf", [128, 128], BF16)
        self.ones_s = P.sb("ones_s", [128, 128], BF16)
        self.blk64_bf = P.sb("blk64_bf", [128, 128], BF16)
        self.wb = [P.sb("wb%d" % i, [128, 4096], BF16) for i in range(2)]
        self.pb = [P.ps("pb%d" % i, [128, 512], F32) for i in range(8)]
        self.t512 = [P.sb("t512_%d" % i, [128, 512], F32) for i in range(5)]
        self.b512 = [P.sb("b512_%d" % i, [128, 512], BF16) for i in range(4)]
        self.st512 = [P.sb("st512_%d" % i, [128, 512], F32) for i in range(2)]
        self.scr = P.sb("scr", [128, 20480], BF16)
        self._ffh_i = 0
        self._ps_i = 0
        self._wb_i = 0
        self._wo_i = 0
        self._t_i = 0
        self._b_i = 0

    def load_xb(self, tb, eng="pool"):
        xb = self.xb
        class _B:
            def __getitem__(s_, idx):
                p, k, c = idx
                return V(xb.h[p, k, slice(tb * 512 + c.start, tb * 512 + c.stop)], (xb.name,))
        self.cur_xb = _B()
        return self.cur_xb

    def sub256(self, t):
        class _S:
            name = t.name
            class _H:
                def __getitem__(s2, idx):
                    p, c = idx
                    c = slice(c.start or 0, 256 if c.stop is None else c.stop)
                    return t.h[p, c]
            h = _H()
            def __getitem__(s_, idx):
                if not isinstance(idx, tuple):
                    idx = (idx, slice(0, 256))
                p, c = idx
                c = slice(c.start or 0, 256 if c.stop is None else c.stop)
                return V(t.h[p, c], (t.name,))
        return _S()

    def nextps(self):
        p = self.pb[self._ps_i % 6]
        self._ps_i += 1
        return p

    def nextacc(self):
        self._acc_i = getattr(self, "_acc_i", 0) + 1
        return self.pb[6 + self._acc_i % 2]

    def nt(self):
        t = self.t512[self._t_i % 5]
        self._t_i += 1
        return t

    def nf(self):
        return self.nt()

    def nb(self):
        t = self.b512[self._b_i % 4]
        self._b_i += 1
        return t

    def C(self, name, rows=128, c0=0, c1=None):
        o, w = CONST_LAYOUT[name]
        if c1 is None:
            c1 = w
        return self.consts[0:rows, o + c0:o + c1]

    def col(self, name, j, rows=128):
        o, w = COL_LAYOUT[name]
        return self.colp[0:rows, o + j:o + j + 1]

    def row(self, name, c0=0, c1=None):
        o, w = ROW_LAYOUT[name]
        if c1 is None:
            c1 = w
        return V(self.scr.h[:, :].bitcast(F32)[:, 2560 + o + c0:2560 + o + c1], ("sg_rowp",))

    def load_w(self, hbm_ap, kc, cols, key, ring="wb"):
        P = self.P
        t = self.wb[self._wb_i % 2]; self._wb_i += 1
        wv = WV(t, kc, cols)
        src = hbm_ap.rearrange("(k p) c -> p k c", p=128)
        step = max(1, (2048 // cols) if cols <= 2048 else 1)
        k = 0
        while k < kc:
            k2 = min(kc, k + step)
            P.dma(V(wv.ap[:, k:k2, :], (t.name,)), V(src[:, k:k2, :], (key,)), eng="pool")
            k = k2
        return wv

    def scr_phase(self, new_keys):
        old = getattr(self, "_scr_keys", [])
        if not hasattr(self, "_dummy"):
            self._dummy = self.P.sb("phase_dummy", [128, 8], F32)
        d = self._dummy
        self.P.add("pool", lambda e: e.memset(d.h[:, :], 0.0), [V(None, tuple(old))], [V(None, tuple(new_keys) + (d.name,))])
        self._scr_keys = list(new_keys)

    def debug_dump(self, name, view, shape):
        if name not in self.dbg:
            return
        h = self.nc.dram_tensor("dbg_" + name, list(shape), view.ap.dtype, kind="ExternalOutput").ap()
        self.P.dma(V(h, ("dbg_" + name,)), view)
        self.dbg_out[name] = shape

    def proj_fm(self, w, c0, M, tb, evac, xsrc=None, ncol=512):
        P = self.P
        ps = self.nextps()
        for kc in range(w.kc):
            if xsrc is None:
                rhs = self.cur_xb[:, kc, 0:ncol]
            else:
                rhs = xsrc[:, kc, tb * ncol:(tb + 1) * ncol]
            P.mm(ps[0:M, 0:ncol], w[:, kc, c0:c0 + M], rhs,
                 start=(kc == 0), stop=(kc == w.kc - 1))
        evac(ps)

    def proj_tm(self, w, c0, N, t0, evac, ntok=128):
        P = self.P
        ps = self.nextps()
        for kc in range(w.kc):
            P.mm(ps[0:ntok, 0:N], self.cur_xb[:, kc, (t0 % 512):(t0 % 512) + ntok], w[:, kc, c0:c0 + N],
                 start=(kc == 0), stop=(kc == w.kc - 1))
        evac(ps)

    def acc_out(self, hbm_w, yT, nck, key, first):
        P = self.P
        w = self.load_w(hbm_w, nck, 1024, key, ring="wb")
        for tb in range(4):
            for o in range(8):
                ps = self.nextps()
                for c in range(nck):
                    P.mm(ps[:, :], w[:, c, o * 128:(o + 1) * 128], yT[:, c, tb * 512:(tb + 1) * 512],
                         start=(c == 0), stop=(c == nck - 1))
                xv = self.xT[:, o, tb * 512:(tb + 1) * 512]
                if first:
                    P.stt(xv, xv, ALPHA, ps[:, :], ALU.mult, ALU.add)
                else:
                    P.tt(xv, xv, ps[:, :], ALU.add)

    def layernorm(self, gname, bname, src=None, dst32=None, dstb=None, ntok=S, eps=LN_EPS):
        P = self.P
        src = self.xT if src is None else src
        dst32 = self.xT if dst32 is None else (None if dst32 is False else dst32)
        dstb = self.xb if dstb is None else dstb
        nblk = (ntok + 511) // 512
        for tb in range(nblk):
            w = min(512, ntok - tb * 512)
            sl = slice(tb * 512, tb * 512 + w)
            psm = self.nextps(); psq = self.nextps()
            for c in range(8):
                xb_ = self.nb(); sq = self.nb()
                P.copy(xb_[:, 0:w], src[:, c, sl], eng="dve")
                P.act(sq[:, 0:w], src[:, c, sl], AF.Square)
                P.mm(psm[:, 0:w], self.ones_s[:, :], xb_[:, 0:w], start=(c == 0), stop=(c == 7))
                P.mm(psq[:, 0:w], self.ones_s[:, :], sq[:, 0:w], start=(c == 0), stop=(c == 7))
            mean = self.st512[0]; rstd = self.st512[1]
            P.copy(mean[:, 0:w], psm[:, 0:w])
            msq = self.nt()
            P.tt(msq[:, 0:w], mean[:, 0:w], mean[:, 0:w], ALU.mult)
            P.tt(msq[:, 0:w], psq[:, 0:w], msq[:, 0:w], ALU.subtract)
            P.act(msq[:, 0:w], msq[:, 0:w], AF.Ln, bias=eps)
            P.act(rstd[:, 0:w], msq[:, 0:w], AF.Exp, scale=-0.5)
            for c in range(8):
                u = self.nt()
                P.tt(u[:, 0:w], src[:, c, sl], mean[:, 0:w], ALU.subtract)
                P.stt(u[:, 0:w], u[:, 0:w], self.col(gname, c), rstd[:, 0:w], ALU.mult, ALU.mult)
                if dst32 is not None:
                    P.act(dst32[:, c, sl], u[:, 0:w], AF.Identity, bias=self.col(bname, c))
                if dstb is not None:
                    P.act(dstb[:, c, sl], u[:, 0:w], AF.Identity, bias=self.col(bname, c))

    def mixer_sg(self, l):
        P = self.P
        w = self.load_w(self.h_w_in[l][:, 0:512], 8, 512, "h_w_in")
        if not hasattr(self, "sg_t"):
            self.sg_t = dict(
                wmT=P.sb("sg_wmT", [128, 4, 128], BF16),
                stat=[P.sb("sg_stat%d" % i, [128, 16], F32) for i in range(2)],
                vnp=[P.sb("sg_vnp%d" % i, [128, 2, 2, 128], BF16) for i in range(2)],
            )
            for v_ in self.sg_t["vnp"]:
                P.memset(v_[:], 0.0, eng="pool")
        self.scr_phase(["sg_u0", "sg_u1", "sg_wraw", "sg_rowp"])
        P.dma(V(self.scr.h[:, :].bitcast(F32)[:, 2560:2560 + NROW], ("sg_rowp",)), self.hv(self.h_rowp[l], "h_rowp"))
        scr32 = self.scr.h[:, :].bitcast(F32)
        class _C:
            def __init__(s_, ap, key):
                s_.ap = ap; s_.key = key
            def __getitem__(s_, idx):
                return V(s_.ap[idx], (s_.key,))
        usb = [_C(scr32[:, i * 1024:(i + 1) * 1024].rearrange("p (c t) -> p c t", c=2), "sg_u%d" % i) for i in range(2)]
        wraw = _C(scr32[:, 2048:2560].rearrange("p (h t) -> p h t", h=4), "sg_wraw")
        T_ = self.sg_t
        P.dma(wraw[:], self.hv(self.h_sgwT[l], "h_sgwT"))
        tri = V(self.C("tri_le").ap.unsqueeze(1).to_broadcast([128, 4, 128]), self.consts[:].keys)
        P.tt(T_["wmT"][:], wraw[:], tri, ALU.mult)
        for tb in range(4):
            self.load_xb(tb)
            u_sb = usb[tb % 2]
            for c in range(2):
                self.proj_fm(w, c * 128, 128, tb, lambda ps, c=c: P.copy(u_sb[:, c, :], ps[:, :], eng="act"))
            for n4 in range(4):
                n = tb * 4 + n4
                vs = self.sub256(self.nf()); sq = self.sub256(self.nf()); stat = T_["stat"][n % 2]; vnp = T_["vnp"][n % 2]
                def ev(ps):
                    P.copy(vs[:], ps[:, 0:256], eng="act")
                    P.act(sq[:], ps[:, 0:256], AF.Square)
                self.proj_tm(w, 256, 256, n * 128, ev)
                v3 = V(vs.h[:, :].rearrange("p (h d) -> p h d", h=4), vs[:].keys)
                q3 = V(sq.h[:, :].rearrange("p (h d) -> p h d", h=4), sq[:].keys)
                P.reduce(stat[:, 0:4], v3, ALU.add)
                P.reduce(stat[:, 4:8], q3, ALU.add)
                P.ts(stat[:, 0:4], stat[:, 0:4], 1.0 / 64, ALU.mult)
                P.tt(stat[:, 8:12], stat[:, 0:4], stat[:, 0:4], ALU.mult)
                P.stt(stat[:, 4:8], stat[:, 4:8], 1.0 / 64, stat[:, 8:12], ALU.mult, ALU.subtract)
                P.act(stat[:, 4:8], stat[:, 4:8], AF.Ln, bias=LN_EPS)
                P.act(stat[:, 12:16], stat[:, 4:8], AF.Exp, scale=-0.5)
                for h in range(4):
                    P.ts(sq[:, h * 64:(h + 1) * 64], vs[:, h * 64:(h + 1) * 64], stat[:, h:h + 1], ALU.subtract,
                         stat[:, 12 + h:13 + h], ALU.mult)
                P.tt(sq[:], sq[:], self.row("sg_ln_g"), ALU.mult)
                s4 = sq.h[:, :].rearrange("p (a b d) -> p a b d", a=2, b=2)
                rb = self.row("sg_ln_b").ap.rearrange("p (a b d) -> p a b d", a=2, b=2)
                for hh in range(2):
                    P.tt(V(vnp.h[:, :, hh, hh * 64:(hh + 1) * 64], vnp[:].keys), V(s4[:, :, hh, :], sq[:].keys),
                         V(rb[:, :, hh, :], ("sg_rowp",)), ALU.add)
                for pair in range(2):
                    ps = self.nextps()
                    for hh in range(2):
                        P.mm(ps[:, 0:128], V(vnp.h[:, pair, hh, :], vnp[:].keys), T_["wmT"][:, pair * 2 + hh, :],
                             start=(hh == 0), stop=(hh == 1))
                    t2 = self.nf()
                    P.tt(t2[:, 0:128], ps[:, 0:128], self.row("sgb", pair * 128, (pair + 1) * 128), ALU.add)
                    P.tt(self.yT[:, pair, n * 128:(n + 1) * 128], t2[:, 0:128], u_sb[:, pair, n4 * 128:(n4 + 1) * 128], ALU.mult)

    def mixer_fox(self, l):
        P = self.P
        w = self.load_w(self.h_w_in[l][:, 2320:2832], 8, 512, "h_w_in")
        w2 = self.load_w(self.h_w_in[l][:, 2832:3092], 8, 260, "h_w_in")
        if not hasattr(self, "fx"):
            self.fx = dict(
                negcT=P.sb("fx_negcT", [128, 16, 4], F32),
                nfb=P.sb("fx_nfb", [4, 1], F32),
            )
        F = self.fx
        scr = self.scr
        qT = V(scr.h[:, 0:4096].rearrange("p (c t) -> p c t", c=2), ("fx_qT",))
        kT = V(scr.h[:, 4096:8192].rearrange("p (c t) -> p c t", c=2), ("fx_kT",))
        v1ap = scr.h[:, 8192:16384].rearrange("p (n h m) -> p n h m", n=16, h=4)
        v1 = lambda idx: V(v1ap[idx], ("fx_v1",))
        self.scr_phase(["fx_qT", "fx_kT", "fx_v1", "fx_negc"])
        negc_ap = scr.h[:, 16384:20480].bitcast(F32)
        class _N:
            def __getitem__(s_, idx):
                return V(negc_ap[idx], ("fx_negc",))
        F["negc"] = _N(); F["nlf"] = F["negc"]
        P.memset(v1((slice(None),)), 0.0, eng="dve")
        for h in range(4):
            col = 64 if h % 2 == 0 else 0
            P.memset(v1((slice(None), slice(None), h, slice(col, col + 1))), 1.0, eng="dve")
        P.ts(F["nfb"][:], self.col("fox_fb%d" % l, 0, rows=4), -1.0, ALU.mult)
        for tb in range(4):
            self.load_xb(tb)
            sl = slice(tb * 512, (tb + 1) * 512)
            for c in range(2):
                self.proj_fm(w, c * 128, 128, tb,
                             lambda ps, c=c: P.act(V(qT.ap[:, c, sl], qT.keys), ps[:, :], AF.Copy, scale=0.125))
                self.proj_fm(w, 256 + c * 128, 128, tb,
                             lambda ps, c=c: P.copy(V(kT.ap[:, c, sl], kT.keys), ps[:, :], eng="dve"))
            def evf(ps):
                P.act(F["nlf"][0:4, sl], ps[0:4, :], AF.Exp, bias=F["nfb"][:, 0:1], scale=-1.0)
                P.act(F["nlf"][0:4, sl], F["nlf"][0:4, sl], AF.Ln, bias=1.0)
            self.proj_fm(w2, 256, 4, tb, evf)
            for n4 in range(4):
                n = tb * 4 + n4
                def evv(ps, n=n):
                    for h in range(4):
                        col = 0 if h % 2 == 0 else 64
                        P.copy(v1((slice(None), n, h, slice(col, col + 64))), ps[:, h * 64:(h + 1) * 64],
                               eng=("act" if h % 2 else "dve"))
                self.proj_tm(w2, 0, 256, n * 128, evv)
        ones_b = self.C("ones", rows=4, c0=0, c1=1).ap.to_broadcast([4, S])
        P.generic("dve", lambda e: e.tensor_tensor_scan(negc_ap[0:4, :], ones_b, negc_ap[0:4, :], 0.0,
                                                         ALU.mult, ALU.add),
                  [self.consts[:], F["negc"][0:4, :]], [F["negc"][0:4, :]])
        self.debug_dump("fox_negc", F["negc"][0:4, :], [4, S])
        for J in range(16):
            ps = self.nextps()
            P.transpose(ps[:, 0:4], F["negc"][0:4, J * 128:(J + 1) * 128], self.C("ident", rows=4, c0=0, c1=4))
            P.copy(F["negcT"][:, J, :], ps[:, 0:4])
        if "sel12" not in F:
            F["sel12"] = P.sb("fx_sel12", [12, 512], BF16)
            P.copy(F["sel12"][:], self.C("selneg", rows=12))
        cs_t = w2.tile
        cs_ap = cs_t.h[0:12, 0:S]
        cs3 = lambda r0, r1, c0, c1: V(cs_ap[r0:r1, c0:c1], (cs_t.name,))
        for tb in range(4):
            c0 = tb * 512; c1 = c0 + 512
            r1 = self.nt(); r2 = self.nt(); m_ = self.nb(); l_ = self.nb()
            P.copy(cs3(0, 4, c0, c1), F["negc"][0:4, c0:c1])
            P.tt(r1[0:4, :], F["negc"][0:4, c0:c1], cs3(0, 4, c0, c1), ALU.subtract)
            P.copy(m_[0:4, :], r1[0:4, :])
            P.tt(r2[0:4, :], r1[0:4, :], m_[0:4, :], ALU.subtract)
            P.copy(l_[0:4, :], r2[0:4, :])
            P.dma(cs3(4, 8, c0, c1), m_[0:4, :])
            P.dma(cs3(8, 12, c0, c1), l_[0:4, :])
        for h in range(4):
            c = h // 2
            p0 = (h % 2) * 64
            even = (h % 2 == 0)
            for Q in range(4):
                ops_ = self.nextacc()
                nJ = 4 * Q + 4
                for J in range(nJ):
                    c_lo = max(0, (J - 4 * Q) * 128)
                    lg = self.nextps()
                    P.mm(lg[:, c_lo:512], V(kT.ap[p0:p0 + 64, c, J * 128:(J + 1) * 128], kT.keys),
                         V(qT.ap[p0:p0 + 64, c, Q * 512 + c_lo:(Q + 1) * 512], qT.keys), start=True, stop=False)
                    P.mm(lg[:, c_lo:512], F["sel12"][:, h * 128:(h + 1) * 128],
                         cs3(0, 12, Q * 512 + c_lo, (Q + 1) * 512), start=False, stop=True)
                    if J >= 4 * Q:
                        P.tt(lg[:, c_lo:c_lo + 128], lg[:, c_lo:c_lo + 128], self.C("negmask"), ALU.add)
                    pT = self.nb()
                    P.act(pT[:, c_lo:512], lg[:, c_lo:512], AF.Exp, bias=F["negcT"][:, J, h:h + 1])
                    M = 65 if even else 128
                    P.mm(ops_[0:M, c_lo:512], v1((slice(None), J, h, slice(0, M))), pT[:, c_lo:512],
                         start=(J == 0), stop=(J == nJ - 1))
                osb = self.nt()
                if even:
                    P.copy(osb[0:65, :], ops_[0:65, :], eng="act")
                    P.recip(osb[64:65, :], osb[64:65, :])
                    bp = self.nextps()
                    P.mm(bp[0:64, :], self.C("sel_even", rows=65), osb[0:65, :])
                    P.tt(self.yT[0:64, c, Q * 512:(Q + 1) * 512], osb[0:64, :], bp[0:64, :], ALU.mult)
                else:
                    P.copy(osb[:, :], ops_[:, :], eng="act")
                    P.recip(osb[0:1, :], osb[0:1, :])
                    bp = self.nextps()
                    P.mm(bp[:, :], self.C("sel_odd"), osb[:, :])
                    P.tt(self.yT[64:128, c, Q * 512:(Q + 1) * 512], osb[64:128, :], bp[64:128, :], ALU.mult)


    def mixer_gla(self, l):
        P = self.P
        w1 = self.load_w(self.h_w_in[l][:, 1536:2048], 8, 512, "h_w_in")
        w2 = self.load_w(self.h_w_in[l][:, 2048:2320], 8, 272, "h_w_in")
        if not hasattr(self, "gl"):
            self.gl = dict(
                aup=P.sb("gl_aup", [16, 128], BF16),
                nab=P.sb("gl_nab", [64, 2], F32),
                gps=P.sb("gl_gps", [64, 32], F32),
                bl=P.sb("gl_bl", [64, 32], F32),
                dec=P.sb("gl_dec", [64, 2, 32], F32),
                S=[P.sb("gl_S%d" % g, [64, 128], F32) for g in range(2)],
                Sbf=[[P.sb("gl_Sbf%d_%d" % (g, i), [64, 128], BF16) for i in range(4)] for g in range(2)],
                osb=[P.sb("gl_osb%d" % i, [128, 128], F32) for i in range(2)],
            )
        G = self.gl
        scr = self.scr
        self.scr_phase(["gl_qe", "gl_ke", "gl_v", "gl_ketm", "gl_G", "gl_adT"])
        class _C:
            def __init__(s_, ap, key):
                s_.ap = ap; s_.key = key
            def __getitem__(s_, idx):
                return V(s_.ap[idx], (s_.key,))
        qe = _C(scr.h[:, 0:4096].rearrange("p (g t) -> p g t", g=2), "gl_qe")
        ke = _C(scr.h[:, 4096:8192].rearrange("p (g t) -> p g t", g=2), "gl_ke")
        v128 = _C(scr.h[:, 8192:12288].rearrange("p (b c) -> p b c", b=16), "gl_v")
        ketm = _C(scr.h[:, 12288:14336].rearrange("p (b c) -> p b c", b=16), "gl_ketm")
        Gt = _C(scr.h[:, 14336:18432].bitcast(F32), "gl_G")
        Gt3 = _C(scr.h[:, 14336:18432].bitcast(F32).rearrange("p (n c) -> p n c", n=32), "gl_G")
        adT = _C(scr.h[:, 18432:20480], "gl_adT")
        P.dma(G["aup"][:], self.hv(self.h_gla_a_up[l], "h_gla_a_up"), eng="pool")
        P.ts(G["nab"][:], V(self.colp.h[0:64, COL_LAYOUT["gla_a_b%d" % l][0]:COL_LAYOUT["gla_a_b%d" % l][0] + 2], self.colp[:].keys),
             -1.0, ALU.mult)
        for tb in range(4):
            self.load_xb(tb)
            sl = slice(tb * 512, (tb + 1) * 512)
            self.proj_fm(w2, 256, 16, tb, lambda ps: P.copy(adT[0:16, sl], ps[0:16, :], eng="act"))
            for c in range(2):
                def evg(ps, c=c):
                    t = self.nt()
                    P.act(t[:], ps[:, :], AF.Silu)
                    P.ts(self.yT[:, c, sl], t[:], self.col("gla_norm_g%d" % l, c), ALU.mult)
                self.proj_fm(w2, c * 128, 128, tb, evg)
            for b4 in range(4):
                bk = tb * 4 + b4
                self.proj_tm(w1, 256, 256, bk * 128, lambda ps, bk=bk: P.copy(v128[:, bk, :], ps[:, 0:256], eng="act"))
        import os
        stop = int(os.environ.get('GLA_STOP', '99'))
        if stop <= 1:
            return
        ones_b = self.C("ones", rows=64, c0=0, c1=1).ap.to_broadcast([64, S])
        for g in range(2):
            for tb in range(4):
                sl = slice(tb * 512, (tb + 1) * 512)
                ps = self.nextps()
                P.mm(ps[0:64, :], G["aup"][:, g * 64:(g + 1) * 64], adT[0:16, sl])
                P.act(Gt[0:64, sl], ps[0:64, :], AF.Exp, bias=G["nab"][:, g:g + 1], scale=-1.0)
                P.act(Gt[0:64, sl], Gt[0:64, sl], AF.Ln, bias=1.0)
            P.generic("dve", lambda e: e.tensor_tensor_scan(Gt.ap[0:64, :], ones_b, Gt.ap[0:64, :], 0.0, ALU.mult, ALU.add),
                      [self.consts[:], Gt[0:64, :]], [Gt[0:64, :]])
            if stop <= 2:
                continue
            P.memset(G["gps"][:, 0:1], 0.0)
            P.copy(G["gps"][:, 1:32], Gt3[0:64, 0:31, 63])
            P.tt(Gt3[0:64, :, :], Gt3[0:64, :, :], V(G["gps"].h[:, :].unsqueeze(2).to_broadcast([64, 32, 64]), G["gps"][:].keys),
                 ALU.subtract)
            P.copy(G["bl"][:], Gt3[0:64, :, 63])
            P.act(G["dec"][:, g, :], G["bl"][:], AF.Exp, scale=-1.0 / 16)
            for tb in range(4):
                sl = slice(tb * 512, (tb + 1) * 512)
                self.load_xb(tb)
                Eb = self.nt(); Enb = self.nt()
                P.act(Eb[0:64, :], Gt[0:64, sl], AF.Exp, scale=-1.0 / 16)
                P.act(Enb[0:64, :], Gt[0:64, sl], AF.Exp, scale=1.0 / 16)
                self.proj_fm(w1, g * 64, 64, tb,
                             lambda ps: P.stt(qe[0:64, g, sl], ps[0:64, :], 32.0 ** -0.5, Eb[0:64, :], ALU.mult, ALU.mult))
                self.proj_fm(w1, 128 + g * 64, 64, tb,
                             lambda ps: P.tt(ke[0:64, g, sl], ps[0:64, :], Enb[0:64, :], ALU.mult))
            if stop <= 3:
                continue
            for bk in range(16):
                ps = self.nextps()
                pbf = V(ps.h[:, :].bitcast(BF16), ps[:].keys)
                P.transpose(V(pbf.ap[:, 0:64], pbf.keys), ke[0:64, g, bk * 128:(bk + 1) * 128], self.ident_bf[0:64, 0:64])
                P.copy(ketm[:, bk, g * 64:(g + 1) * 64], V(pbf.ap[:, 0:64], pbf.keys), eng="act")
        self.debug_dump("gl_qe", qe[0:64, :, :], [64, 2, S])
        self.debug_dump("gl_ke", ke[0:64, :, :], [64, 2, S])
        if stop <= 4:
            return
        for g in range(2):
            P.memset(G["S"][g][:], 0.0)
            P.memset(G["Sbf"][g][0][:], 0.0)
        for bk in range(16):
            for g in range(2):
                attm = []
                for hh in range(2):
                    ps = self.nextps()
                    P.mm(ps[:, 0:128], ke[hh * 32:hh * 32 + 32, g, bk * 128:(bk + 1) * 128],
                         qe[hh * 32:hh * 32 + 32, g, bk * 128:(bk + 1) * 128])
                    am = self.nb()
                    P.tt(am[:, 0:128], ps[:, 0:128], self.C("mask2"), ALU.mult)
                    attm.append(am)
                if stop <= 5:
                    continue
                for cc in range(2):
                    n = 2 * bk + cc
                    if n == 31:
                        break
                    psU = self.nextps()
                    r0 = cc * 64
                    P.mm(psU[0:64, 0:128], ketm[r0:r0 + 64, bk, g * 64:(g + 1) * 64], v128[r0:r0 + 64, bk, g * 128:(g + 1) * 128])
                    P.tt(G["S"][g][:], G["S"][g][:], psU[0:64, 0:128], ALU.add)
                    P.ts(G["S"][g][:], G["S"][g][:], G["dec"][:, g, n:n + 1], ALU.mult)
                    P.copy(G["Sbf"][g][(n + 1) % 4][:], G["S"][g][:], eng="act")
                if stop <= 6:
                    continue
                psO = self.nextps()
                for hh in range(2):
                    c0 = hh * 128
                    P.mm(psO[:, c0:c0 + 128], v128[:, bk, g * 128:(g + 1) * 128], attm[hh][:, 0:128], start=True, stop=False)
                    for cc in range(2):
                        n = 2 * bk + cc
                        P.mm(psO[:, c0 + cc * 64:c0 + (cc + 1) * 64], G["Sbf"][g][n % 4][hh * 32:hh * 32 + 32, :],
                             qe[hh * 32:hh * 32 + 32, g, n * 64:(n + 1) * 64], start=False, stop=(cc == 1))
                if stop <= 7:
                    continue
                osb = G["osb"][g]
                osq = self.nb()
                for hh in range(2):
                    r0 = hh * 64
                    var = os.environ.get('GLA_VAR', 'ab')
                    if 'a' in var:
                        P.copy(osb[r0:r0 + 64, :], psO[r0:r0 + 64, hh * 128:(hh + 1) * 128], eng="dve")
                    if 'b' in var:
                        P.act(osq[r0:r0 + 64, 0:128], psO[r0:r0 + 64, hh * 128:(hh + 1) * 128], AF.Square)
                if stop <= 8:
                    continue
                pss = self.nextps()
                P.mm(pss[:, 0:128], self.blk64_bf[:, :], osq[:, 0:128])
                if stop <= 9:
                    continue
                rs = self.nf()
                P.act(rs[:, 0:128], pss[:, 0:128], AF.Ln, bias=1e-5)
                P.act(rs[:, 0:128], rs[:, 0:128], AF.Exp, scale=-0.5)
                if stop <= 10:
                    continue
                P.tt(osb[:, :], osb[:, :], rs[:, 0:128], ALU.mult)
                if stop <= 11:
                    continue
                yv = self.yT[:, g, bk * 128:(bk + 1) * 128]
                P.tt(yv, osb[:, :], yv, ALU.mult)


    def mixer_rw(self, l):
        P = self.P
        wA = self.load_w(self.h_w_in[l][:, 512:1024], 8, 512, "h_w_in")
        wB = self.load_w(self.h_w_in[l][:, 1024:1536], 8, 512, "h_w_in")
        if not hasattr(self, "rw"):
            self.rw = dict(
                sm=P.sb("rw_sm", [128, 512], BF16),
                omm=P.sb("rw_omm", [128, 24], F32),
                nwa=P.sb("rw_nwa", [64, 8], F32),
                carry=P.sb("rw_carry", [128, 8], F32),
                maskq=P.sb("rw_maskq", [64, 128], F32),
                PC=P.sb("rw_PC", [64, 32], F32),
                gs=P.sb("rw_gs", [64, 4], F32),
                S32=P.sb("rw_S32", [64, 64], F32),
                Sb=[P.sb("rw_Sb%d" % i, [64, 64], BF16) for i in range(2)],
                Nn=[P.sb("rw_Nn%d" % i, [64, 64], BF16) for i in range(2)],
                XN=[[P.sb("rw_XN%d_%d" % (i, j), [64, 128], BF16) for j in range(2)] for i in range(2)],
                Wt=[[P.sb("rw_Wt%d_%d" % (i, j), [64, 64], BF16) for j in range(2)] for i in range(2)],
                Wf=[P.sb("rw_Wf%d" % i, [64, 64], BF16) for i in range(4)],
                RU=[P.sb("rw_RU%d" % i, [64, 128], BF16) for i in range(2)],
            )
            P.copy(self.rw["maskq"][:, 0:64], self.C("tri_le", rows=64, c0=0, c1=64))
            P.copy(self.rw["maskq"][:, 64:128], self.C("tri_lt", rows=64, c0=0, c1=64))
        R = self.rw
        scr = self.scr
        self.scr_phase(["rw_twa", "rw_sgd", "rw_KB", "rw_RA", "rw_VT", "rw_bv"])
        class _C:
            def __init__(s_, ap, key):
                s_.ap = ap; s_.key = key
            def __getitem__(s_, idx):
                return V(s_.ap[idx], (s_.key,))
        twa = _C(scr.h[:, 0:2048], "rw_twa")
        sgd = _C(scr.h[:, 2048:4096], "rw_sgd")
        KB = _C(scr.h[:, 4096:8192].rearrange("p (n c) -> p n c", n=32), "rw_KB")
        RA = _C(scr.h[:, 8192:12288].rearrange("p (n c) -> p n c", n=32), "rw_RA")
        VT = _C(scr.h[:, 12288:14336], "rw_VT")
        bv = _C(scr.h[:, 14336:16384], "rw_bv")
        y1 = self.yT.h[0:64, 1, :]
        class _A:
            def __init__(s_, c0, w, key):
                s_.c0 = c0; s_.w = w; s_.key = key
            def __getitem__(s_, idx):
                p, c = idx
                c = slice(s_.c0 + (c.start or 0), s_.c0 + (s_.w if c.stop is None else c.stop))
                return V(y1[:, c], (s_.key,))
        RQA = [_A(i * 128, 128, "rwq_QA%d" % i) for i in range(4)]
        RQB = [_A(512 + i * 128, 128, "rwq_QB%d" % i) for i in range(4)]
        RTM = [_A(1024 + i * 192, 192, "rwq_TM%d" % i) for i in range(4)]
        ring_keys = tuple(x.key for x in RQA + RQB + RTM)
        P.add("pool", lambda e: e.memset(self._dummy.h[:, :], 0.0), [V(None, ("yT",))], [V(None, ring_keys + (self._dummy.name,))])
        P.dma(R["sm"][0:64, 0:256], self.hv(self.h_rw_w2[l], "h_rw_w2"), eng="pool")
        P.dma(R["sm"][64:128, 0:256], self.hv(self.h_rw_a2[l], "h_rw_a2"), eng="pool")
        P.dma(R["sm"][:, 256:512], self.hv(self.h_rw_g2[l], "h_rw_g2"), eng="pool")
        o_mu, _ = COL_LAYOUT["rw_mu%d" % l]; o_mu64, _ = COL_LAYOUT["rw_mu64_%d" % l]
        mu128 = lambda c: self.colp[:, o_mu + c:o_mu + c + 1]
        mu64 = lambda c: self.colp[0:64, o_mu64 + c:o_mu64 + c + 1]
        P.ts(R["omm"][:, 0:8], self.colp[:, o_mu:o_mu + 8], -1.0, ALU.mult, 1.0, ALU.add)
        P.ts(R["omm"][0:64, 8:24], self.colp[0:64, o_mu64:o_mu64 + 16], -1.0, ALU.mult, 1.0, ALU.add)
        o_h, _ = COL_LAYOUT["rw_h64_%d" % l]
        hcol = lambda h, j: self.colp[0:64, o_h + h * 9 + j:o_h + h * 9 + j + 1]
        for h in range(4):
            P.ts(R["nwa"][:, 2 * h:2 * h + 1], hcol(h, 0), -1.0, ALU.mult)
            P.ts(R["nwa"][:, 2 * h + 1:2 * h + 2], hcol(h, 1), -1.0, ALU.mult)
        pool_tiles = self.t512 + self.st512
        def tmp(i, rows=64):
            t = pool_tiles[i // 2]
            c0 = (i % 2) * 256
            class _T:
                name = t.name
                def __getitem__(s_, idx):
                    if not isinstance(idx, tuple):
                        idx = (idx, slice(0, 256))
                    p, c = idx
                    c = slice(c0 + (c.start or 0), c0 + (256 if c.stop is None else c.stop))
                    return V(t.h[p, c], (t.name,))
                def v3(s_, rows_):
                    return V(t.h[0:rows_, c0:c0 + 256].rearrange("p (n c) -> p n c", n=4), (t.name,))
            return _T()

        def shiftmix(ps, M, mu_ap, omm_ap, cslot, out_t, blk):
            zr = pool_tiles[6]
            if blk == 0:
                P.memset(zr[0:M, 0:1], 0.0)
            else:
                P.copy(zr[0:M, 0:1], R["carry"][0:M, cslot:cslot + 1])
            P.copy(zr[0:M, 1:257], ps[0:M, 0:256], eng="act")
            P.copy(R["carry"][0:M, cslot:cslot + 1], zr[0:M, 256:257])
            P.act(out_t[0:M, :], ps[0:M, 0:256], AF.Copy, scale=omm_ap)
            P.stt(out_t[0:M, :], zr[0:M, 0:256], mu_ap, out_t[0:M, :], ALU.mult, ALU.add)

        class _XB:
            def __init__(s_, xb, t0):
                s_.xb = xb; s_.t0 = t0
            def __getitem__(s_, idx):
                p, k, c = idx
                return V(s_.xb.h[p, k, slice(s_.t0 + c.start, s_.t0 + c.stop)], (s_.xb.name,))

        for blk in range(8):
            self.cur_xb = _XB(self.xb, blk * 256)
            sl = slice(blk * 256, (blk + 1) * 256)
            z = tmp(0)
            self.proj_fm(wB, 256, 128, 0, lambda ps: shiftmix(ps, 128, mu128(6), R["omm"][:, 6:7], 0, z, blk), ncol=256)
            P.act(twa[0:64, sl], z[0:64, :], AF.Tanh)
            P.copy(twa[64:128, sl], z[64:128, :], eng="pool")
            z2 = tmp(1)
            self.proj_fm(wB, 384, 128, 0, lambda ps: shiftmix(ps, 128, mu128(7), R["omm"][:, 7:8], 1, z2, blk), ncol=256)
            P.act(z2[:, :], z2[:, :], AF.Exp, scale=-1.0)
            P.act(z2[:, :], z2[:, :], AF.Ln, bias=1.0)
            P.act(sgd[:, sl], z2[:, :], AF.Exp, scale=-1.0)
        import os
        rstop = int(os.environ.get("RW_STOP", "99"))
        nheads = int(os.environ.get("RW_HEADS", "4"))
        for h in range(nheads):
            for blk in range(8):
                self.cur_xb = _XB(self.xb, blk * 256)
                sl = slice(blk * 256, (blk + 1) * 256)
                n0 = blk * 4
                zr_ = tmp(0); zk_ = tmp(1); zv_ = tmp(2)
                self.proj_fm(wA, h * 64, 64, 0, lambda ps: shiftmix(ps, 64, mu64(h), R["omm"][0:64, 8 + h:9 + h], 2, zr_, blk), ncol=256)
                self.proj_fm(wA, 256 + h * 64, 64, 0, lambda ps: shiftmix(ps, 64, mu64(4 + h), R["omm"][0:64, 12 + h:13 + h], 3, zk_, blk), ncol=256)
                self.proj_fm(wB, h * 64, 64, 0, lambda ps: shiftmix(ps, 64, mu64(8 + h), R["omm"][0:64, 16 + h:17 + h], 4, zv_, blk), ncol=256)
                P.copy(VT[0:64, sl], zv_[0:64, :], eng="pool")
                LD = tmp(3)
                ps = self.nextps()
                P.mm(ps[0:64, 0:256], R["sm"][0:64, h * 64:(h + 1) * 64], twa[0:64, sl])
                P.act(LD[0:64, :], ps[0:64, 0:256], AF.Exp, bias=R["nwa"][:, 2 * h:2 * h + 1], scale=-1.0)
                P.act(LD[0:64, :], LD[0:64, :], AF.Ln, bias=1.0)
                P.act(LD[0:64, :], LD[0:64, :], AF.Exp, bias=-0.5, scale=-1.0)
                A = tmp(4)
                ps = self.nextps()
                P.mm(ps[0:64, 0:256], R["sm"][64:128, h * 64:(h + 1) * 64], twa[64:128, sl])
                P.act(A[0:64, :], ps[0:64, 0:256], AF.Exp, bias=R["nwa"][:, 2 * h + 1:2 * h + 2], scale=-1.0)
                P.act(A[0:64, :], A[0:64, :], AF.Ln, bias=1.0)
                P.act(A[0:64, :], A[0:64, :], AF.Exp, scale=-1.0)
                KK = tmp(5)
                P.ts(KK[0:64, :], zk_[0:64, :], hcol(h, 2), ALU.mult)
                sq = self.nb()
                P.act(sq[0:64, 0:256], KK[0:64, :], AF.Square)
                ps = self.nextps()
                P.mm(ps[0:64, 0:256], self.blk64_bf[0:64, 0:64], sq[0:64, 0:256])
                RN = tmp(10)
                P.act(RN[0:64, :], ps[0:64, 0:256], AF.Ln, bias=1e-24, scale=64.0)
                P.act(RN[0:64, :], RN[0:64, :], AF.Exp, scale=-0.5)
                P.tt(KK[0:64, :], KK[0:64, :], RN[0:64, :], ALU.mult)
                K2 = tmp(6)
                P.ts(K2[0:64, :], A[0:64, :], -1.0, ALU.add, hcol(h, 3), ALU.mult)
                P.stt(K2[0:64, :], K2[0:64, :], 1.0, zk_[0:64, :], ALU.add, ALU.mult)
                KKA = tmp(7)
                P.tt(KKA[0:64, :], KK[0:64, :], A[0:64, :], ALU.mult)
                pr = self.nb()
                P.stt(pr[0:64, 0:256], zr_[0:64, :], hcol(h, 4), K2[0:64, :], ALU.mult, ALU.mult)
                ps = self.nextps()
                P.mm(ps[0:64, 0:256], self.blk64_bf[0:64, 0:64], pr[0:64, 0:256])
                P.stt(bv[0:64, sl], ps[0:64, 0:256], 64.0, zv_[0:64, :], ALU.mult, ALU.mult)
                Gb = tmp(8)
                ones_b = self.C("ones", rows=64, c0=0, c1=1).ap.to_broadcast([64, 256])
                P.generic("dve", lambda e, Gb=Gb, LD=LD: e.tensor_tensor_scan(Gb[0:64, :].ap, ones_b, LD[0:64, :].ap, 0.0, ALU.mult, ALU.add),
                          [self.consts[:], LD[0:64, :]], [Gb[0:64, :]])
                P.memset(R["gs"][:, 0:1], 0.0)
                G3 = Gb.v3(64)
                P.copy(R["gs"][:, 1:4], V(G3.ap[:, 0:3, 63], G3.keys))
                P.tt(G3, G3, V(R["gs"].h[:, :].unsqueeze(2).to_broadcast([64, 4, 64]), R["gs"][:].keys), ALU.subtract)
                P.act(R["PC"][:, n0:n0 + 4], V(G3.ap[:, :, 63], G3.keys), AF.Exp, scale=-1.0)
                Ep = tmp(10); Em = tmp(11); Epm1 = tmp(9)
                P.act(Ep[0:64, :], Gb[0:64, :], AF.Exp, scale=-1.0)
                P.act(Em[0:64, :], Gb[0:64, :], AF.Exp)
                P.tt(Epm1[0:64, :], LD[0:64, :], Gb[0:64, :], ALU.subtract)
                P.act(Epm1[0:64, :], Epm1[0:64, :], AF.Exp)
                P.tt(V(RA.ap[0:64, n0:n0 + 4, 0:64], ("rw_RA",)), zr_.v3(64), Ep.v3(64), ALU.mult)
                P.stt(V(RA.ap[0:64, n0:n0 + 4, 64:128], ("rw_RA",)), KK.v3(64), -1.0, Epm1.v3(64), ALU.mult, ALU.mult)
                P.tt(V(KB.ap[0:64, n0:n0 + 4, 0:64], ("rw_KB",)), K2.v3(64), Em.v3(64), ALU.mult)
                P.tt(V(KB.ap[0:64, n0:n0 + 4, 64:128], ("rw_KB",)), KKA.v3(64), Em.v3(64), ALU.mult)
            if h == 0:
                self.debug_dump("rw_RA", RA[0:64, :, :], [64, 32, 128])
                self.debug_dump("rw_KB", KB[0:64, :, :], [64, 32, 128])
                self.debug_dump("rw_PC", R["PC"][:, :], [64, 32])
            if rstop <= 1:
                continue
            P.memset(R["S32"][:], 0.0)
            P.memset(R["Sb"][0][:], 0.0)
            self._psY = None
            BS = 2
            def slot(n):
                return n % (2 * BS)
            def pre_rounds(ns):
                rounds = []
                def r0():
                    for n in ns:
                        s4 = slot(n)
                        QA = RQA[s4]; QB = RQB[s4]; TM = RTM[s4]; Nn = R["Nn"][n % BS]
                        Kt = KB[0:64, n, 0:64]; Bt = KB[0:64, n, 64:128]; At = RA[0:64, n, 64:128]
                        ps = self.nextps()
                        P.mm(ps[0:64, 0:128], Kt, RA[0:64, n, :])
                        P.mm(ps[0:64, 128:256], Bt, RA[0:64, n, :])
                        P.mm(ps[0:64, 256:320], At, Bt)
                        P.tt(QA[:, :], ps[0:64, 0:128], R["maskq"][:, :], ALU.mult)
                        P.tt(QB[:, :], ps[0:64, 128:256], R["maskq"][:, :], ALU.mult)
                        P.tt(Nn[:, :], ps[0:64, 256:320], self.C("tri_gt", rows=64, c0=0, c1=64), ALU.mult)
                        ps2 = self.nextps()
                        pbf = lambda a_, b_, ps2=ps2: V(ps2.h[0:64, :].bitcast(BF16)[:, a_:b_], ps2[:].keys)
                        P.transpose(pbf(0, 64), Kt, self.ident_bf[0:64, 0:64])
                        P.transpose(pbf(64, 128), Bt, self.ident_bf[0:64, 0:64])
                        P.transpose(pbf(128, 192), VT[0:64, n * 64:(n + 1) * 64], self.ident_bf[0:64, 0:64])
                        P.copy(TM[:, :], pbf(0, 192), eng="act")
                rounds.append(r0)
                st = {}
                def r1():
                    for n in ns:
                        QB = RQB[slot(n)]
                        W = R["Wt"][n % BS][0]
                        P.tt(W[:, :], QB[:, 64:128], self.C("ident", rows=64, c0=0, c1=64), ALU.add)
                        st[n] = dict(W=W, Xp=QB[:, 64:128], Np=R["Nn"][n % BS][:, :], wi=0)
                rounds.append(r1)
                for lev in range(5):
                    def ra(lev=lev):
                        for n in ns:
                            d = st[n]
                            XN = R["XN"][n % BS][lev % 2]
                            ps = self.nextps()
                            P.mm(ps[0:64, 64:128], d["Xp"], d["Np"])
                            if lev < 4:
                                P.mm(ps[0:64, 0:64], d["Np"], d["Xp"])
                                P.copy(XN[:, :], ps[0:64, 0:128], eng="act")
                            else:
                                P.copy(XN[:, 64:128], ps[0:64, 64:128], eng="act")
                            d["XN"] = XN
                    def rb(lev=lev):
                        for n in ns:
                            d = st[n]
                            XN = d["XN"]
                            ps2 = self.nextps()
                            P.mm(ps2[0:64, 0:64], XN[:, 64:128], d["W"][:, :])
                            if lev < 4:
                                d["wi"] += 1
                                Wn = R["Wt"][n % BS][d["wi"] % 2]
                            else:
                                Wn = R["Wf"][slot(n)]
                            P.tt(Wn[:, :], ps2[0:64, 0:64], d["W"][:, :], ALU.add)
                            d["W"] = Wn
                            d["Xp"] = XN[:, 0:64]; d["Np"] = XN[:, 64:128]
                    rounds.append(ra); rounds.append(rb)
                return rounds

            def chain_hops(ns):
                hops = []
                for n in ns:
                    s4 = slot(n)
                    QA = RQA[s4]; QB = RQB[s4]; TM = RTM[s4]; W = R["Wf"][s4]
                    Rt = RA[0:64, n, 0:64]; At = RA[0:64, n, 64:128]
                    Sb = R["Sb"][n % 2]; Sbn = R["Sb"][(n + 1) % 2]
                    RU = R["RU"][n % 2]
                    def h1(n=n, QA=QA, TM=TM, At=At, Sb=Sb, RU=RU):
                        psA = self.nextps()
                        P.mm(psA[0:64, 0:64], At, Sb[:, :], start=True, stop=False)
                        P.mm(psA[0:64, 0:64], QA[:, 64:128], TM[:, 128:192], start=False, stop=True)
                        P.copy(RU[:, 0:64], psA[0:64, 0:64], eng="act")
                        P.ts(R["S32"][:, :], R["S32"][:, :], R["PC"][:, n:n + 1], ALU.mult)
                    def h2(n=n, W=W, RU=RU):
                        psU = self.nextps()
                        P.mm(psU[0:64, 0:64], W[:, :], RU[:, 0:64])
                        P.copy(RU[:, 64:128], psU[0:64, 0:64], eng="act")
                    def h3(n=n, QA=QA, QB=QB, TM=TM, Rt=Rt, Sb=Sb, Sbn=Sbn, RU=RU):
                        psS = self.nextps()
                        P.mm(psS[0:64, 0:64], TM[:, 0:64], TM[:, 128:192], start=True, stop=False)
                        P.mm(psS[0:64, 0:64], TM[:, 64:128], RU[:, 64:128], start=False, stop=True)
                        P.stt(Sbn[:, :], psS[0:64, 0:64], R["PC"][:, n:n + 1], R["S32"][:, :], ALU.mult, ALU.add)
                        P.stt(R["S32"][:, :], psS[0:64, 0:64], R["PC"][:, n:n + 1], R["S32"][:, :], ALU.mult, ALU.add)
                        if n % 8 == 0:
                            self._psY = self.nextacc()
                        psY = self._psY
                        yc = slice((n % 8) * 64, (n % 8 + 1) * 64)
                        P.mm(psY[0:64, yc], Sb[:, :], Rt, start=True, stop=False)
                        P.mm(psY[0:64, yc], TM[:, 128:192], QA[:, 0:64], start=False, stop=False)
                        P.mm(psY[0:64, yc], RU[:, 64:128], QB[:, 0:64], start=False, stop=True)
                        if n % 8 == 7:
                            post(n // 8, psY)
                    hops += [h1, h2, h3]
                return hops

            def post(tb, psY):
                sl = slice(tb * 512, (tb + 1) * 512)
                Y = self.nt()
                P.copy(Y[0:64, :], psY[0:64, :], eng="act")
                if h == 0:
                    self.debug_dump("rw_scan%d" % tb, Y[0:64, :], [64, 512])
                ysq = self.nb(); ybf = self.nb()
                P.act(ysq[0:64, :], Y[0:64, :], AF.Square)
                P.copy(ybf[0:64, :], Y[0:64, :], eng="pool")
                psm = self.nextps(); psq = self.nextps()
                P.mm(psm[0:64, :], self.blk64_bf[0:64, 0:64], ybf[0:64, :])
                P.mm(psq[0:64, :], self.blk64_bf[0:64, 0:64], ysq[0:64, :])
                m2 = self.nt()
                P.tt(Y[0:64, :], Y[0:64, :], psm[0:64, :], ALU.subtract)
                P.act(m2[0:64, :], psm[0:64, :], AF.Square)
                P.tt(m2[0:64, :], psq[0:64, :], m2[0:64, :], ALU.subtract)
                P.act(m2[0:64, :], m2[0:64, :], AF.Ln, bias=64e-5)
                P.act(m2[0:64, :], m2[0:64, :], AF.Exp, scale=-0.5)
                P.stt(Y[0:64, :], Y[0:64, :], hcol(h, 5), m2[0:64, :], ALU.mult, ALU.mult)
                P.stt(Y[0:64, :], Y[0:64, :], hcol(h, 6), bv[0:64, sl], ALU.add, ALU.add)
                psg = self.nextps()
                P.mm(psg[0:64, :], R["sm"][:, 256 + h * 64:256 + (h + 1) * 64], sgd[:, sl])
                P.tt(self.yT[0:64, 0, sl], Y[0:64, :], psg[0:64, :], ALU.mult)

            nb_ = 32 // BS
            for k in range(nb_ + 1):
                pr = pre_rounds(list(range(k * BS, (k + 1) * BS))) if k < nb_ else []
                ch = chain_hops(list(range((k - 1) * BS, k * BS))) if k >= 1 else []
                i = j = 0
                while i < len(pr) or j < len(ch):
                    if j < len(ch):
                        ch[j](); j += 1
                    for _ in range(2):
                        if i < len(pr):
                            pr[i](); i += 1
            if rstop <= 2:
                continue
            self.debug_dump("y_rwh%d" % h, self.yT[0:64, 0, :], [64, S])
            if "ln1" in self.stages:
                self.acc_out_rw(l, h)
        P.add("pool", lambda e: e.memset(self._dummy.h[:, :], 0.0), [V(None, ring_keys)], [V(None, ("yT", self._dummy.name))])

    def acc_out_rw(self, l, h):
        P = self.P
        if not hasattr(self, "rw_wo"):
            self.rw_wo = P.sb("rw_wo", [64, 1024], BF16)
        t = self.rw_wo
        wv = WV(t, 1, 1024)
        P.dma(V(wv.ap[0:64, :, :], (t.name,)),
              V(self.h_w_out[l][256 + h * 64:256 + (h + 1) * 64, :].rearrange("(k p) c -> p k c", p=64), ("h_w_out",)), eng="pool")
        for tb in range(4):
            for o in range(8):
                ps = self.nextps()
                P.mm(ps[:, :], V(wv.ap[0:64, 0, o * 128:(o + 1) * 128], (t.name,)), self.yT[0:64, 0, tb * 512:(tb + 1) * 512])
                xv = self.xT[:, o, tb * 512:(tb + 1) * 512]
                if self._first_acc:
                    P.stt(xv, xv, ALPHA, ps[:, :], ALU.mult, ALU.add)
                else:
                    P.tt(xv, xv, ps[:, :], ALU.add)
        self._first_acc = False


    def mem_ln(self):
        P = self.P
        self.memnb = P.sb("memnb", [128, 8, NMEM], BF16)
        self.scr_phase(["mem_raw"])
        class _C:
            def __init__(s_, ap, key):
                s_.ap = ap; s_.key = key
            def __getitem__(s_, idx):
                return V(s_.ap[idx], (s_.key,))
        raw = _C(self.scr.h[:, :].bitcast(F32)[:, 0:8 * NMEM].rearrange("p (c t) -> p c t", c=8), "mem_raw")
        P.dma(raw[:, :, :], self.hv(self.h_memT.rearrange("(c p) t -> p c t", p=128), "h_memT"))
        self.layernorm("mem_ln_g", "mem_ln_b", src=raw, dst32=False, dstb=self.memnb, ntok=NMEM)

    def cross_attn(self, l):
        P = self.P
        self.scr_phase(["ca_KT", "ca_V", "ca_oT"])
        scr = self.scr
        class _C:
            def __init__(s_, ap, key):
                s_.ap = ap; s_.key = key
            def __getitem__(s_, idx):
                return V(s_.ap[idx], (s_.key,))
        KT = _C(scr.h[:, 0:2048].rearrange("p (c m) -> p c m", c=8), "ca_KT")
        Vt = _C(scr.h[:, 2048:4096].rearrange("p (b c) -> p b c", b=2), "ca_V")
        oT = _C(scr.h[:, 4096:20480].rearrange("p (c t) -> p c t", c=8), "ca_oT")
        if not hasattr(self, "ones_bf"):
            self.ones_bf = P.sb("ones_bf", [128, 128], BF16)
            P.copy(self.ones_bf[:], self.C("ones"))
        stages = []
        def ld_k(half):
            return self.load_w(self.h_ca_wk[l][:, half * 512:(half + 1) * 512], 8, 512, "h_ca_wk")
        def cp_k(wk, half):
            for oc in range(4):
                ps = self.nextps()
                for kc in range(8):
                    P.mm(ps[:, 0:NMEM], wk[:, kc, oc * 128:(oc + 1) * 128], self.memnb[:, kc, :], start=(kc == 0), stop=(kc == 7))
                P.copy(KT[:, half * 4 + oc, :], ps[:, 0:NMEM], eng="act")
        def ld_v(half):
            return self.load_w(self.h_ca_wv[l][:, half * 512:(half + 1) * 512], 8, 512, "h_ca_wv")
        def cp_v(wv, half):
            for mb in range(2):
                ps = self.nextps()
                for kc in range(8):
                    P.mm(ps[:, :], self.memnb[:, kc, mb * 128:(mb + 1) * 128], wv[:, kc, :], start=(kc == 0), stop=(kc == 7))
                P.copy(Vt[:, mb, half * 512:(half + 1) * 512], ps[:, :], eng="dve")
        def ld_q(h):
            return self.load_w(self.h_ca_wq[l][:, h * 256:(h + 1) * 256], 8, 256, "h_ca_wq")
        def cp_q(wq, h):
            for tb in range(4):
                self.load_xb(tb)
                sl = slice(tb * 512, (tb + 1) * 512)
                qT = [self.nb(), self.nb()]
                for c in range(2):
                    self.proj_fm(wq, c * 128, 128, tb, lambda ps, c=c: P.act(qT[c][:, :], ps[:, :], AF.Copy, scale=1.0 / 16))
                PT = []
                for mb in range(2):
                    ps = self.nextps()
                    for c in range(2):
                        P.mm(ps[:, :], KT[:, h * 2 + c, mb * 128:(mb + 1) * 128], qT[c][:, :], start=(c == 0), stop=(c == 1))
                    pt = self.nb()
                    P.act(pt[:, :], ps[:, :], AF.Exp)
                    PT.append(pt)
                den = self.nextps()
                for mb in range(2):
                    P.mm(den[:, :], self.ones_bf[:, :], PT[mb][:, :], start=(mb == 0), stop=(mb == 1))
                rden = self.nt()
                P.recip(rden[:, :], den[:, :])
                for c2 in range(2):
                    ps = self.nextps()
                    for mb in range(2):
                        P.mm(ps[:, :], Vt[:, mb, h * 256 + c2 * 128:h * 256 + (c2 + 1) * 128], PT[mb][:, :], start=(mb == 0), stop=(mb == 1))
                    P.tt(oT[:, h * 2 + c2, sl], ps[:, :], rden[:, :], ALU.mult)
        def ld_o(half):
            return self.load_w(self.h_ca_wo[l][:, half * 512:(half + 1) * 512], 8, 512, "h_ca_wo")
        def cp_o(wo, half):
            for tb in range(4):
                sl = slice(tb * 512, (tb + 1) * 512)
                for oc in range(4):
                    ps = self.nextps()
                    for kc in range(8):
                        P.mm(ps[:, :], wo[:, kc, oc * 128:(oc + 1) * 128], oT[:, kc, sl], start=(kc == 0), stop=(kc == 7))
                    xv = self.xT[:, half * 4 + oc, sl]
                    P.stt(xv, xv, ALPHA, ps[:, :], ALU.mult, ALU.add)
        for half in range(2):
            stages.append((ld_k, cp_k, half))
        for half in range(2):
            stages.append((ld_v, cp_v, half))
        for h in range(4):
            stages.append((ld_q, cp_q, h))
        for half in range(2):
            stages.append((ld_o, cp_o, half))
        cur = stages[0][0](stages[0][2])
        for i, (ld, cp, arg) in enumerate(stages):
            nxt = None
            if i + 1 < len(stages):
                nxt = stages[i + 1][0](stages[i + 1][2])
            cp(cur, arg)
            cur = nxt

    def conv_ffn(self, l):
        P = self.P
        self.scr_phase(["ff_w0", "ff_w1", "ff_w2", "ff_w3", "ff_pr0", "ff_pr1"])
        scr = self.scr
        if not hasattr(self, "ff"):
            self.ff = dict(halo=P.sb("ff_halo", [128, 8, 2], F32))
        y32 = self.yT.h[:, :, :].rearrange("p a b -> p (a b)").bitcast(F32)
        class _H:
            def __init__(s_, i):
                s_.i = i
            def __getitem__(s_, idx):
                p, c = idx
                return V(y32[p, slice(s_.i * 520 + c.start, s_.i * 520 + c.stop)], ("ffh%d" % s_.i,))
        self.ffh = [_H(i) for i in range(3)]
        self.P.add("pool", lambda e: e.memset(self._dummy.h[:, :], 0.0), [V(None, ("yT",))],
                   [V(None, ("ffh0", "ffh1", "ffh2", self._dummy.name))])
        class _WB:
            def __init__(s_, ap, key, kc, cols):
                s_.ap = ap[:, 0:kc * cols].rearrange("p (k c) -> p k c", k=kc); s_.key = key; s_.kc = kc
            def __getitem__(s_, idx):
                return V(s_.ap[idx], (s_.key,))
        bufs = [(self.wb[0].h[:, :], "wb0"), (self.wb[1].h[:, :], "wb1")] + \
               [(scr.h[:, i * 4096:(i + 1) * 4096], "ff_w%d" % i) for i in range(4)]
        prs = [(scr.h[:, 16384 + i * 2048:16384 + (i + 1) * 2048].rearrange("p (j t) -> p j t", j=4), "ff_pr%d" % i) for i in range(2)]
        def loadw(bi, hbm_ap, kc, cols, key):
            ap, k = bufs[bi]
            wv = _WB(ap, k, kc, cols)
            src = hbm_ap.rearrange("(k p) c -> p k c", p=128)
            step = 4 if cols <= 512 else 2
            kk = 0
            while kk < kc:
                k2 = min(kc, kk + step)
                P.dma(V(wv.ap[:, kk:k2, :], (k,)), V(src[:, kk:k2, :], (key,)), eng="pool")
                kk = k2
            return wv
        o_ub, _ = COL_LAYOUT["ffn_up_b"]; o_cb, _ = COL_LAYOUT["ffn_conv_b"]; o_cw, _ = COL_LAYOUT["ffn_conv"]
        colv = lambda o: self.colp[:, o:o + 1]
        npg = 6
        def load_pg(pg):
            nj = 4 if pg < 5 else 2
            b0 = (pg % 2) * 3
            wg = loadw(b0, self.h_ffn_up[l][:, pg * 512:pg * 512 + nj * 128], 8, nj * 128, "h_ffn_up")
            wvv = loadw(b0 + 1, self.h_ffn_up[l][:, DFF + pg * 512:DFF + pg * 512 + nj * 128], 8, nj * 128, "h_ffn_up")
            wd = loadw(b0 + 2, self.h_ffn_down[l][pg * 512:pg * 512 + nj * 128, :], nj, 1024, "h_ffn_down")
            return wg, wvv, wd
        nxt = load_pg(0)
        for pg in range(npg):
            nj = 4 if pg < 5 else 2
            wg, wvv, wd = nxt
            if pg + 1 < npg:
                nxt = load_pg(pg + 1)
            for tb in range(4):
                self.load_xb(tb)
                sl = slice(tb * 512, (tb + 1) * 512)
                prap, prk = prs[(pg * 4 + tb) % 2]
                for jj in range(nj):
                    res = []
                    for part, wsrc in enumerate((wg, wvv)):
                        ch = part * 22 + pg * 4 + jj
                        hb = self.nt()
                        hs = part * 4 + jj
                        ps = self.nextps()
                        for kc in range(8):
                            P.mm(ps[:, :], wsrc[:, kc, jj * 128:(jj + 1) * 128], self.cur_xb[:, kc, 0:512], start=(kc == 0), stop=(kc == 7))
                        hbuf = self.ffh[self._ffh_i % 3]; self._ffh_i += 1
                        if tb == 0:
                            P.memset(hbuf[:, 0:2], 0.0, eng="dve")
                        else:
                            P.copy(hbuf[:, 0:2], self.ff["halo"][:, hs, :], eng="dve")
                        P.act(hbuf[:, 2:514], ps[:, :], AF.Identity, bias=colv(o_ub + ch))
                        P.copy(self.ff["halo"][:, hs, :], hbuf[:, 512:514], eng="dve")
                        P.act(hb[:, :], hbuf[:, 0:512], AF.Identity, bias=colv(o_cb + ch), scale=colv(o_cw + ch))
                        P.stt(hb[:, :], hbuf[:, 1:513], colv(o_cw + 44 + ch), hb[:, :], ALU.mult, ALU.add)
                        P.stt(hb[:, :], hbuf[:, 2:514], colv(o_cw + 88 + ch), hb[:, :], ALU.mult, ALU.add)
                        res.append(hb)
                    P.act(res[0][:, :], res[0][:, :], AF.Gelu)
                    P.tt(V(prap[:, jj, :], (prk,)), res[0][:, :], res[1][:, :], ALU.mult)
                for o in range(8):
                    ps = self.nextps()
                    for jj in range(nj):
                        P.mm(ps[:, :], wd[:, jj, o * 128:(o + 1) * 128], V(prap[:, jj, :], (prk,)), start=(jj == 0), stop=(jj == nj - 1))
                    xv = self.xT[:, o, sl]
                    if pg == 0:
                        P.stt(xv, xv, ALPHA, ps[:, :], ALU.mult, ALU.add)
                    else:
                        P.tt(xv, xv, ps[:, :], ALU.add)

    def build(self):
        P = self.P
        with ExitStack() as st:
            P.enter(st)
            self.decl()
            self.alloc()
            P.dma(self.consts[:], self.hv(self.h_consts, "h_consts"))
            P.dma(self.colp[:], self.hv(self.h_colp[0], "h_colp"))
            xsrc = self.h_xT.rearrange("(c p) t -> p c t", p=128)
            for tb in range(4):
                sl = slice(tb * 512, (tb + 1) * 512)
                P.dma(self.xT[:, :, sl], self.hv(xsrc[:, :, sl], "h_xT"))
                P.copy(self.xb[:, :, sl], self.xT[:, :, sl], eng=("act" if tb % 2 else "dve"))
            P.copy(self.ident_bf[:], self.C("ident"))
            P.ts(self.ones_s[:], self.C("ones"), 1.0 / 1024, ALU.mult)
            P.copy(self.blk64_bf[:], self.C("blk64"))
            if "ca" in self.stages:
                self.mem_ln()
            for l in range(self.nlayers):
                self.layer(l)
            osrc = self.h_outT.rearrange("(c p) t -> p c t", p=128)
            for tb in range(4):
                sl = slice(tb * 512, (tb + 1) * 512)
                P.dma(self.hv(osrc[:, :, sl], "h_outT"), self.xT[:, :, sl])
            P.finalize()
            P.emit(st)
        return self.nc

    def layer(self, l):
        P = self.P
        if l > 0:
            P.dma(self.colp[:], self.hv(self.h_colp[l], "h_colp"))
        self._first_acc = True
        if l > 0 and "ffn" in self.stages:
            self.P.add("pool", lambda e: e.memset(self._dummy.h[:, :], 0.0), [V(None, ("ffh0", "ffh1", "ffh2"))],
                       [V(None, ("yT", self._dummy.name))])
        for m, (name, fn) in enumerate([("sg", self.mixer_sg), ("rw", self.mixer_rw), ("gla", self.mixer_gla), ("fox", self.mixer_fox)]):
            if name in self.stages and fn is not None:
                fn(l)
                if name == "rw":
                    continue
                self.debug_dump("y_%s%d" % (name, l), self.yT[:], [128, 2, S])
                if "ln1" in self.stages:
                    self.acc_out(self.h_w_out[l][m * 256:(m + 1) * 256, :], self.yT, 2, "h_w_out", self._first_acc)
                    self._first_acc = False
        if "ln1" in self.stages:
            self.layernorm("ln1_g%d" % l, "ln1_b%d" % l)
            self.debug_dump("x1_%d" % l, self.xT[:], [128, 8, S])
        if "ca" in self.stages:
            self.cross_attn(l)
            self.layernorm("ln2_g%d" % l, "ln2_b%d" % l)
            self.debug_dump("x2_%d" % l, self.xT[:], [128, 8, S])
        if "ffn" in self.stages:
            self.conv_ffn(l)
            self.layernorm("ln3_g%d" % l, "ln3_b%d" % l)


_CACHE = {}


def kernel(**inputs):
    inp = {k: np.ascontiguousarray(np.asarray(v, dtype=np.float32)) for k, v in inputs.items()}
    n = 8
    nc = bass.Bass("TRN2", target_bir_lowering=False)
    mk = MK(nc)
    mk.build()
    consts = make_consts()
    colp = make_colp(inp)
    rowp = make_rowp(inp)
    sgwT = np.ascontiguousarray(inp["sg_w"].transpose(0, 3, 1, 2))
    shared = dict(consts=consts, colp=colp, rowp=rowp, sgwT=sgwT,
                  w_in=inp["w_in"], w_out=inp["w_out"], gla_a_up=inp["gla_a_up"],
                  rw_w2=inp["rw_w2"], rw_a2=inp["rw_a2"], rw_g2=inp["rw_g2"],
                  ca_wq=inp["ca_wq"], ca_wk=inp["ca_wk"], ca_wv=inp["ca_wv"], ca_wo=inp["ca_wo"],
                  ffn_up=inp["ffn_up"], ffn_down=inp["ffn_down"])
    maps = []
    for b in range(n):
        m = dict(shared)
        m["xT"] = np.ascontiguousarray(inp["x"][b].T)
        m["memT"] = np.ascontiguousarray(inp["mem"][b].T)
        maps.append(m)
    res = run_bass_kernel_spmd(nc, maps, core_ids=list(range(n)))
    out = np.stack([np.asarray(res.results[b]["outT"]).T for b in range(n)]).astype(np.float32)
    return np.ascontiguousarray(out)
```

```python
from contextlib import ExitStack
from concourse.bass_utils import run_bass_kernel_spmd
import numpy as np
import concourse.bass as bass
import concourse.mybir as mybir

F32 = mybir.dt.float32
BF16 = mybir.dt.bfloat16
AF = mybir.ActivationFunctionType
ALU = mybir.AluOpType
AX = mybir.AxisListType

ENGS = ("pe", "act", "dve", "pool", "sp")


class V:
    __slots__ = ("ap", "keys")

    def __init__(self, ap, keys):
        self.ap = ap
        self.keys = keys


class T:
    def __init__(self, handle, name, shape):
        self.h = handle
        self.name = name
        self.shape = shape

    def __getitem__(self, idx):
        return V(self.h[idx], (self.name,))

    def k(self, sub):
        return _TK(self, sub)


class _TK:
    def __init__(self, t, sub):
        self.t = t
        self.sub = sub

    def __getitem__(self, idx):
        return V(self.t.h[idx], ((self.t.name, self.sub),))


class Op:
    __slots__ = ("eng", "fn", "reads", "writes", "idx", "deps", "signal", "sigidx",
                 "is_dma", "dslot", "dcnt", "snap", "xreads")

    def __init__(self, eng, fn, reads, writes, is_dma=False):
        self.eng = eng
        self.fn = fn
        self.reads = reads
        self.writes = writes
        self.is_dma = is_dma
        self.deps = []
        self.signal = False
        self.sigidx = 0
        self.dslot = -1
        self.dcnt = 0
        self.snap = None


class Prog:
    N_DSEM = 24

    def __init__(self, nc):
        self.nc = nc
        self.ops = []
        self._stack = None
        self.ntile = 0
        self.psum_names = set()

    def enter(self, stack):
        self._stack = stack

    def sb(self, name, shape, dt=F32):
        h = self._stack.enter_context(self.nc.sbuf_tensor("s_" + name, list(shape), dt))
        return T(h, name, shape)

    def ps(self, name, shape, dt=F32):
        h = self._stack.enter_context(self.nc.psum_tensor("p_" + name, list(shape), dt))
        self.psum_names.add(name)
        return T(h, name, shape)

    def _keys(self, vs):
        ks = []
        for v in vs:
            if v is None or isinstance(v, (int, float)):
                continue
            ks.extend(v.keys)
        return ks

    def add(self, eng, fn, reads, writes, is_dma=False):
        op = Op(eng, fn, self._keys(reads), self._keys(writes), is_dma)
        op.xreads = [k for k in op.reads if (k if isinstance(k, str) else k[0]) in self.psum_names and k not in op.writes]
        self.ops.append(op)
        return op

    def dma(self, out, in_, eng="sp", **kw):
        def fn(e, out=out, in_=in_):
            return e.dma_start(out=out.ap, in_=in_.ap, **kw)
        return self.add(eng, fn, [in_], [out], is_dma=True)

    def mm(self, out, lhsT, rhs, start=True, stop=True, **kw):
        def fn(e):
            return e.matmul(out.ap, lhsT.ap, rhs.ap, start=start, stop=stop, **kw)
        return self.add("pe", fn, [lhsT, rhs], [out])

    def transpose(self, out, in_, ident):
        def fn(e):
            return e.transpose(out.ap, in_.ap, ident.ap)
        return self.add("pe", fn, [in_, ident], [out])

    def act(self, out, in_, func, bias=0.0, scale=1.0, eng="act", accum_out=None):
        def fn(e):
            kw = {}
            if accum_out is not None:
                kw["accum_out"] = accum_out.ap
            return e.activation(out.ap, in_.ap, func,
                                bias=(bias.ap if isinstance(bias, V) else bias),
                                scale=(scale.ap if isinstance(scale, V) else scale), **kw)
        return self.add("act", fn, [in_, bias, scale], [out, accum_out])

    def tt(self, out, in0, in1, op, eng="dve"):
        def fn(e):
            return e.tensor_tensor(out.ap, in0.ap, in1.ap, op)
        return self.add(eng, fn, [in0, in1], [out])

    def ts(self, out, in0, s1, op0, s2=None, op1=None, eng="dve", accum_out=None):
        def fn(e):
            a1 = s1.ap if isinstance(s1, V) else s1
            a2 = s2.ap if isinstance(s2, V) else s2
            kw = {}
            if accum_out is not None:
                kw["accum_out"] = accum_out.ap
            if op1 is None:
                return e.tensor_scalar(out.ap, in0.ap, a1, None, op0, **kw)
            return e.tensor_scalar(out.ap, in0.ap, a1, a2, op0, op1, **kw)
        return self.add(eng, fn, [in0, s1, s2], [out, accum_out])

    def stt(self, out, in0, scalar, in1, op0, op1, eng="dve"):
        def fn(e):
            s = scalar.ap if isinstance(scalar, V) else scalar
            return e.scalar_tensor_tensor(out.ap, in0.ap, s, in1.ap, op0, op1)
        return self.add(eng, fn, [in0, scalar, in1], [out])

    def copy(self, out, in_, eng="dve"):
        if eng == "act":
            def fn(e):
                return e.copy(out.ap, in_.ap)
        else:
            def fn(e):
                return e.tensor_copy(out.ap, in_.ap)
        return self.add(eng, fn, [in_], [out])

    def memset(self, out, val, eng="dve"):
        def fn(e):
            return e.memset(out.ap, val)
        return self.add(eng, fn, [], [out])

    def reduce(self, out, in_, op, axis=AX.X, eng="dve"):
        def fn(e):
            return e.tensor_reduce(out.ap, in_.ap, axis, op)
        return self.add(eng, fn, [in_], [out])

    def recip(self, out, in_):
        def fn(e):
            return e.reciprocal(out.ap, in_.ap)
        return self.add("dve", fn, [in_], [out])

    def generic(self, eng, fn, reads, writes):
        return self.add(eng, fn, reads, writes)

    def finalize(self, out_keys=()):
        nc = self.nc
        ops = self.ops
        last_w = {}
        readers = {}
        for i, op in enumerate(ops):
            op.idx = i
            deps = set()
            for k in op.reads:
                w = last_w.get(k)
                if w is not None:
                    deps.add(w)
            for k in list(op.writes) + op.xreads:
                w = last_w.get(k)
                if w is not None:
                    deps.add(w)
                latest = {}
                for r in readers.get(k, ()):
                    ro = ops[r]
                    if ro.is_dma:
                        deps.add(r)
                    else:
                        latest[ro.eng] = r
                for r in latest.values():
                    deps.add(r)
            deps.discard(i)
            op.deps = sorted(deps)
            for k in op.reads:
                lst = readers.setdefault(k, [])
                if not op.is_dma:
                    lst[:] = [r for r in lst if ops[r].is_dma or ops[r].eng != op.eng]
                lst.append(i)
            for k in op.writes:
                last_w[k] = i
                readers[k] = []
            for k in op.xreads:
                readers[k] = [i]
        for op in ops:
            need = []
            for d in op.deps:
                p = ops[d]
                if p.is_dma:
                    need.append(d)
                    continue
                if p.eng == op.eng:
                    if op.is_dma:
                        need.append(d)
                        continue
                    if op.eng == "pe":
                        continue
                    raw = any(k in p.writes for k in op.reads)
                    if raw:
                        need.append(d)
                    continue
                need.append(d)
            op.deps = need
            for d in need:
                if not ops[d].is_dma:
                    ops[d].signal = True
        cnt = {e: 0 for e in ENGS}
        for op in ops:
            if op.is_dma:
                continue
            if op.signal:
                cnt[op.eng] += 1
                op.sigidx = cnt[op.eng]
        dcount = [0] * self.N_DSEM
        nd = 0
        for op in ops:
            if op.is_dma:
                op.dslot = nd % self.N_DSEM
                dcount[op.dslot] += 1
                op.dcnt = dcount[op.dslot]
                nd += 1
        self.n_dma = nd
        self.sig_counts = cnt
        return self

    def emit(self, stack):
        nc = self.nc
        ops = self.ops
        sems = {e: stack.enter_context(nc.semaphore("S_" + e)) for e in ENGS if e != "sp"}
        dsems = [stack.enter_context(nc.semaphore("D%d" % i)) for i in range(self.N_DSEM)]
        block = stack.enter_context(nc.Block())
        seen = {e: {x: 0 for x in ENGS} for e in ENGS}
        seen_d = {e: [0] * self.N_DSEM for e in ENGS}
        plan = {e: [] for e in ENGS}
        last_dma_on_slot = [None] * self.N_DSEM
        for op in ops:
            e = op.eng
            waits = []
            if op.is_dma:
                prev = last_dma_on_slot[op.dslot]
                if prev is not None and seen_d[e][op.dslot] < prev.dcnt * 16:
                    waits.append((dsems[op.dslot], prev.dcnt * 16))
                    seen_d[e][op.dslot] = prev.dcnt * 16
                last_dma_on_slot[op.dslot] = op
            for d in op.deps:
                p = ops[d]
                if p.is_dma:
                    v = p.dcnt * 16
                    if seen_d[e][p.dslot] < v:
                        waits.append((dsems[p.dslot], v))
                        seen_d[e][p.dslot] = v
                else:
                    v = p.sigidx
                    if seen[e][p.eng] < v:
                        waits.append((sems[p.eng], v))
                        seen[e][p.eng] = v
                        for x in ENGS:
                            if p.snap[x] > seen[e][x]:
                                seen[e][x] = p.snap[x]
            if not op.is_dma:
                snap = dict(seen[e])
                if op.signal:
                    snap[e] = max(snap[e], op.sigidx)
                op.snap = snap
            plan[e].append((waits, op))
        final_waits = []
        for s in range(self.N_DSEM):
            lp = last_dma_on_slot[s]
            if lp is not None:
                final_waits.append((dsems[s], lp.dcnt * 16))

        def run(engname, e):
            for waits, op in plan[engname]:
                for (s, v) in waits:
                    e.wait_ge(s, v)
                ins = op.fn(e)
                if op.is_dma:
                    ins.then_inc(dsems[op.dslot], 16)
                elif op.signal:
                    ins.then_inc(sems[op.eng], 1)

        @block.tensor
        def _(e):
            run("pe", e)

        @block.scalar
        def _(e):
            run("act", e)

        @block.vector
        def _(e):
            run("dve", e)

        @block.gpsimd
        def _(e):
            run("pool", e)

        @block.sync
        def _(e):
            run("sp", e)
            for (s, v) in final_waits:
                e.wait_ge(s, v)
        self.stats = {e: len(plan[e]) for e in ENGS}
        self.nwaits = {e: sum(len(w) for w, _ in plan[e]) for e in ENGS}


S = 2048
D = 1024
L = 4
NMEM = 256
DFF = 2816
ALPHA = (2.0 * L) ** 0.25
LN_EPS = 1e-5
NEG = -30000.0

CONST_LAYOUT = {}
_off = 0
for _n, _w in [("ident", 128), ("tri_le", 128), ("tri_lt", 64), ("negmask", 128), ("selneg", 4 * 128),
               ("ones", 128), ("sel_even", 64), ("sel_odd", 128), ("tri_gt", 64), ("blk64", 128), ("mask2", 128)]:
    CONST_LAYOUT[_n] = (_off, _w)
    _off += _w
NCONST = _off


def make_consts():
    c = np.zeros((128, NCONST), np.float32)
    def put(n, a):
        o, w = CONST_LAYOUT[n]
        c[:a.shape[0], o:o + w] = a
    i = np.arange(128)
    put("ident", np.eye(128, dtype=np.float32))
    put("tri_le", (i[:, None] <= i[None, :]).astype(np.float32))
    put("tri_lt", (i[:, None] < i[None, :]).astype(np.float32)[:, :64])
    put("tri_gt", (i[:, None] > i[None, :]).astype(np.float32)[:, :64])
    put("negmask", np.where(i[:, None] <= i[None, :], 0.0, NEG).astype(np.float32))
    sn = np.zeros((12, 4 * 128), np.float32)
    for h in range(4):
        for j in range(3):
            sn[j * 4 + h, h * 128:(h + 1) * 128] = -1.0
    put("selneg", sn)
    put("ones", np.ones((128, 128), np.float32))
    se = np.zeros((65, 64), np.float32); se[64, :] = 1.0
    put("sel_even", se)
    so = np.zeros((128, 128), np.float32); so[0, 64:128] = 1.0
    put("sel_odd", so)
    put("blk64", ((i[:, None] // 64) == (i[None, :] // 64)).astype(np.float32) / 64.0)
    put("mask2", ((i[:, None] <= i[None, :]) & ((i[:, None] // 64) == (i[None, :] // 64))).astype(np.float32))
    return c


def col_layout():
    lay = {}
    off = 0
    def add(n, w):
        nonlocal off
        lay[n] = (off, w)
        off += w
    add("mem_ln_g", 8); add("mem_ln_b", 8)
    for n in ("ln1_g", "ln1_b", "ln2_g", "ln2_b", "ln3_g", "ln3_b"):
        add(n, 8)
    add("ffn_up_b", 44); add("ffn_conv_b", 44); add("ffn_conv", 132)
    add("fox_fb", 1)
    add("gla_a_b", 2)
    add("gla_norm_g", 2)
    add("rw_mu", 8)
    add("rw_mu64_", 16)
    add("rw_h64_", 4 * 9)
    for n in list(lay.keys()):
        for l in range(L):
            lay["%s%d" % (n, l)] = lay[n]
    return lay, off


COL_LAYOUT, NCOL = col_layout()


def chunkcols(v, p=128):
    return np.ascontiguousarray(v.reshape(-1, p).T)


def make_colp(inp):
    call = np.zeros((L, 128, NCOL), np.float32)
    for l in range(L):
        c = call[l]
        def put(n, a):
            o, w = COL_LAYOUT[n]
            assert a.shape[1] == w, (n, a.shape, w)
            c[:a.shape[0], o:o + w] = a
        put("mem_ln_g", chunkcols(inp["mem_ln_g"])); put("mem_ln_b", chunkcols(inp["mem_ln_b"]))
        for n in ("ln1_g", "ln1_b", "ln2_g", "ln2_b", "ln3_g", "ln3_b"):
            put(n, chunkcols(inp[n][l]))
        put("ffn_up_b", chunkcols(inp["ffn_up_b"][l]))
        put("ffn_conv_b", chunkcols(inp["ffn_conv_b"][l]))
        put("ffn_conv", np.concatenate([chunkcols(inp["ffn_conv"][l][j]) for j in range(3)], 1))
        put("fox_fb", inp["fox_fb"][l].reshape(4, 1))
        put("gla_a_b", chunkcols(inp["gla_a_b"][l], 64))
        put("gla_norm_g", chunkcols(inp["gla_norm_g"][l]))
        put("rw_mu", chunkcols(inp["rw_mu"][l]))
        put("rw_mu64_", chunkcols(inp["rw_mu"][l], 64))
        h64 = np.zeros((64, 36), np.float32)
        for h in range(4):
            sl = slice(h * 64, (h + 1) * 64)
            for j, nm in enumerate(["rw_w0", "rw_a0", "rw_kk", "rw_ka", None, "rw_lnx_g", "rw_lnx_b"]):
                if nm is not None:
                    h64[:, h * 9 + j] = inp[nm][l][sl]
            h64[:, h * 9 + 4] = inp["rw_rk"][l][h]
        put("rw_h64_", h64)
    return call


ROW_LAYOUT = {}
_off = 0
for _n, _w in [("sg_ln_g", 256), ("sg_ln_b", 256), ("sgb", 256)]:
    ROW_LAYOUT[_n] = (_off, _w)
    _off += _w
NROW = _off


def make_rowp(inp):
    r = np.zeros((L, 128, NROW), np.float32)
    for l in range(L):
        def put(n, a):
            o, w = ROW_LAYOUT[n]
            r[l, :, o:o + w] = a
        put("sg_ln_g", np.tile(inp["sg_ln_g"][l][None], (128, 1)))
        put("sg_ln_b", np.tile(inp["sg_ln_b"][l][None], (128, 1)))
        sgb = np.zeros((128, 2, 128), np.float32)
        for p in range(128):
            for pair in range(2):
                sgb[p, pair] = inp["sg_b"][l][pair * 2 + p // 64]
        put("sgb", sgb.reshape(128, 256))
    return r


class WV:
    def __init__(self, tile, kc, cols):
        self.tile = tile
        self.kc = kc
        self.cols = cols
        self.ap = tile.h[:, 0:kc * cols].rearrange("p (k c) -> p k c", k=kc)

    def __getitem__(self, idx):
        return V(self.ap[idx], (self.tile.name,))


class MK:
    def __init__(self, nc, nlayers=L, stages=("sg", "fox", "gla", "rw", "ln1", "ca", "ln2", "ffn", "ln3"), dbg=()):
        self.nc = nc
        self.nlayers = nlayers
        self.stages = stages
        self.dbg = dbg
        self.P = Prog(nc)
        self.dbg_out = {}

    def decl(self):
        nc = self.nc
        di = lambda n, shp: nc.dram_tensor(n, list(shp), F32, kind="ExternalInput").ap()
        self.h_xT = di("xT", [D, S])
        self.h_memT = di("memT", [D, NMEM])
        self.h_consts = di("consts", [128, NCONST])
        self.h_colp = di("colp", [L, 128, NCOL])
        self.h_rowp = di("rowp", [L, 128, NROW])
        self.h_w_in = di("w_in", [L, D, 3092])
        self.h_w_out = di("w_out", [L, D, D])
        self.h_sgwT = di("sgwT", [L, 128, 4, 128])
        self.h_gla_a_up = di("gla_a_up", [L, 16, 128])
        self.h_rw_w2 = di("rw_w2", [L, 64, 256])
        self.h_rw_a2 = di("rw_a2", [L, 64, 256])
        self.h_rw_g2 = di("rw_g2", [L, 128, 256])
        self.h_ca_wq = di("ca_wq", [L, D, D]); self.h_ca_wk = di("ca_wk", [L, D, D])
        self.h_ca_wv = di("ca_wv", [L, D, D]); self.h_ca_wo = di("ca_wo", [L, D, D])
        self.h_ffn_up = di("ffn_up", [L, D, 2 * DFF]); self.h_ffn_down = di("ffn_down", [L, DFF, D])
        self.h_outT = nc.dram_tensor("outT", [D, S], F32, kind="ExternalOutput").ap()

    def hv(self, ap, key):
        return V(ap, (key,))

    def alloc(self):
        P = self.P
        self.xT = P.sb("xT", [128, 8, S], F32)
        self.xb = P.sb("xb", [128, 8, S], BF16)
        self.cur_xb = None
        self.yT = P.sb("yT", [128, 2, S], BF16)
        self.consts = P.sb("consts", [128, NCONST], F32)
        self.colp = P.sb("colp", [128, NCOL], F32)
        self.ident_bf = P.sb("ident_bf", [128, 128], BF16)
        self.ones_s = P.sb("ones_s", [128, 128], BF16)
        self.blk64_bf = P.sb("blk64_bf", [128, 128], BF16)
        self.wb = [P.sb("wb%d" % i, [128, 4096], BF16) for i in range(2)]
        self.pb = [P.ps("pb%d" % i, [128, 512], F32) for i in range(8)]
        self.t512 = [P.sb("t512_%d" % i, [128, 512], F32) for i in range(5)]
        self.b512 = [P.sb("b512_%d" % i, [128, 512], BF16) for i in range(4)]
        self.st512 = [P.sb("st512_%d" % i, [128, 512], F32) for i in range(2)]
        self.scr = P.sb("scr", [128, 20480], BF16)
        self._ffh_i = 0
        self._ps_i = 0
        self._wb_i = 0
        self._wo_i = 0
        self._t_i = 0
        self._b_i = 0

    def load_xb(self, tb, eng="pool"):
        xb = self.xb
        class _B:
            def __getitem__(s_, idx):
                p, k, c = idx
                return V(xb.h[p, k, slice(tb * 512 + c.start, tb * 512 + c.stop)], (xb.name,))
        self.cur_xb = _B()
        return self.cur_xb

    def sub256(self, t):
        class _S:
            name = t.name
            class _H:
                def __getitem__(s2, idx):
                    p, c = idx
                    c = slice(c.start or 0, 256 if c.stop is None else c.stop)
                    return t.h[p, c]
            h = _H()
            def __getitem__(s_, idx):
                if not isinstance(idx, tuple):
                    idx = (idx, slice(0, 256))
                p, c = idx
                c = slice(c.start or 0, 256 if c.stop is None else c.stop)
                return V(t.h[p, c], (t.name,))
        return _S()

    def nextps(self):
        p = self.pb[self._ps_i % 6]
        self._ps_i += 1
        return p

    def nextacc(self):
        self._acc_i = getattr(self, "_acc_i", 0) + 1
        return self.pb[6 + self._acc_i % 2]

    def nt(self):
        t = self.t512[self._t_i % 5]
        self._t_i += 1
        return t

    def nf(self):
        return self.nt()

    def nb(self):
        t = self.b512[self._b_i % 4]
        self._b_i += 1
        return t

    def C(self, name, rows=128, c0=0, c1=None):
        o, w = CONST_LAYOUT[name]
        if c1 is None:
            c1 = w
        return self.consts[0:rows, o + c0:o + c1]

    def col(self, name, j, rows=128):
        o, w = COL_LAYOUT[name]
        return self.colp[0:rows, o + j:o + j + 1]

    def row(self, name, c0=0, c1=None):
        o, w = ROW_LAYOUT[name]
        if c1 is None:
            c1 = w
        return V(self.scr.h[:, :].bitcast(F32)[:, 2560 + o + c0:2560 + o + c1], ("sg_rowp",))

    def load_w(self, hbm_ap, kc, cols, key, ring="wb"):
        P = self.P
        t = self.wb[self._wb_i % 2]; self._wb_i += 1
        wv = WV(t, kc, cols)
        src = hbm_ap.rearrange("(k p) c -> p k c", p=128)
        step = max(1, (2048 // cols) if cols <= 2048 else 1)
        k = 0
        while k < kc:
            k2 = min(kc, k + step)
            P.dma(V(wv.ap[:, k:k2, :], (t.name,)), V(src[:, k:k2, :], (key,)), eng="pool")
            k = k2
        return wv

    def scr_phase(self, new_keys):
        old = getattr(self, "_scr_keys", [])
        if not hasattr(self, "_dummy"):
            self._dummy = self.P.sb("phase_dummy", [128, 8], F32)
        d = self._dummy
        self.P.add("pool", lambda e: e.memset(d.h[:, :], 0.0), [V(None, tuple(old))], [V(None, tuple(new_keys) + (d.name,))])
        self._scr_keys = list(new_keys)

    def debug_dump(self, name, view, shape):
        if name not in self.dbg:
            return
        h = self.nc.dram_tensor("dbg_" + name, list(shape), view.ap.dtype, kind="ExternalOutput").ap()
        self.P.dma(V(h, ("dbg_" + name,)), view)
        self.dbg_out[name] = shape

    def proj_fm(self, w, c0, M, tb, evac, xsrc=None, ncol=512):
        P = self.P
        ps = self.nextps()
        for kc in range(w.kc):
            if xsrc is None:
                rhs = self.cur_xb[:, kc, 0:ncol]
            else:
                rhs = xsrc[:, kc, tb * ncol:(tb + 1) * ncol]
            P.mm(ps[0:M, 0:ncol], w[:, kc, c0:c0 + M], rhs,
                 start=(kc == 0), stop=(kc == w.kc - 1))
        evac(ps)

    def proj_tm(self, w, c0, N, t0, evac, ntok=128):
        P = self.P
        ps = self.nextps()
        for kc in range(w.kc):
            P.mm(ps[0:ntok, 0:N], self.cur_xb[:, kc, (t0 % 512):(t0 % 512) + ntok], w[:, kc, c0:c0 + N],
                 start=(kc == 0), stop=(kc == w.kc - 1))
        evac(ps)

    def acc_out(self, hbm_w, yT, nck, key, first):
        P = self.P
        w = self.load_w(hbm_w, nck, 1024, key, ring="wb")
        for tb in range(4):
            for o in range(8):
                ps = self.nextps()
                for c in range(nck):
                    P.mm(ps[:, :], w[:, c, o * 128:(o + 1) * 128], yT[:, c, tb * 512:(tb + 1) * 512],
                         start=(c == 0), stop=(c == nck - 1))
                xv = self.xT[:, o, tb * 512:(tb + 1) * 512]
                if first:
                    P.stt(xv, xv, ALPHA, ps[:, :], ALU.mult, ALU.add)
                else:
                    P.tt(xv, xv, ps[:, :], ALU.add)

    def layernorm(self, gname, bname, src=None, dst32=None, dstb=None, ntok=S, eps=LN_EPS):
        P = self.P
        src = self.xT if src is None else src
        dst32 = self.xT if dst32 is None else (None if dst32 is False else dst32)
        dstb = self.xb if dstb is None else dstb
        nblk = (ntok + 511) // 512
        for tb in range(nblk):
            w = min(512, ntok - tb * 512)
            sl = slice(tb * 512, tb * 512 + w)
            psm = self.nextps(); psq = self.nextps()
            for c in range(8):
                xb_ = self.nb(); sq = self.nb()
                P.copy(xb_[:, 0:w], src[:, c, sl], eng="dve")
                P.act(sq[:, 0:w], src[:, c, sl], AF.Square)
                P.mm(psm[:, 0:w], self.ones_s[:, :], xb_[:, 0:w], start=(c == 0), stop=(c == 7))
                P.mm(psq[:, 0:w], self.ones_s[:, :], sq[:, 0:w], start=(c == 0), stop=(c == 7))
            mean = self.st512[0]; rstd = self.st512[1]
            P.copy(mean[:, 0:w], psm[:, 0:w])
            msq = self.nt()
            P.tt(msq[:, 0:w], mean[:, 0:w], mean[:, 0:w], ALU.mult)
            P.tt(msq[:, 0:w], psq[:, 0:w], msq[:, 0:w], ALU.subtract)
            P.act(msq[:, 0:w], msq[:, 0:w], AF.Ln, bias=eps)
            P.act(rstd[:, 0:w], msq[:, 0:w], AF.Exp, scale=-0.5)
            for c in range(8):
                u = self.nt()
                P.tt(u[:, 0:w], src[:, c, sl], mean[:, 0:w], ALU.subtract)
                P.stt(u[:, 0:w], u[:, 0:w], self.col(gname, c), rstd[:, 0:w], ALU.mult, ALU.mult)
                if dst32 is not None:
                    P.act(dst32[:, c, sl], u[:, 0:w], AF.Identity, bias=self.col(bname, c))
                if dstb is not None:
                    P.act(dstb[:, c, sl], u[:, 0:w], AF.Identity, bias=self.col(bname, c))

    def mixer_sg(self, l):
        P = self.P
        w = self.load_w(self.h_w_in[l][:, 0:512], 8, 512, "h_w_in")
        if not hasattr(self, "sg_t"):
            self.sg_t = dict(
                wmT=P.sb("sg_wmT", [128, 4, 128], BF16),
                stat=[P.sb("sg_stat%d" % i, [128, 16], F32) for i in range(2)],
                vnp=[P.sb("sg_vnp%d" % i, [128, 2, 2, 128], BF16) for i in range(2)],
            )
            for v_ in self.sg_t["vnp"]:
                P.memset(v_[:], 0.0, eng="pool")
        self.scr_phase(["sg_u0", "sg_u1", "sg_wraw", "sg_rowp"])
        P.dma(V(self.scr.h[:, :].bitcast(F32)[:, 2560:2560 + NROW], ("sg_rowp",)), self.hv(self.h_rowp[l], "h_rowp"))
        scr32 = self.scr.h[:, :].bitcast(F32)
        class _C:
            def __init__(s_, ap, key):
                s_.ap = ap; s_.key = key
            def __getitem__(s_, idx):
                return V(s_.ap[idx], (s_.key,))
        usb = [_C(scr32[:, i * 1024:(i + 1) * 1024].rearrange("p (c t) -> p c t", c=2), "sg_u%d" % i) for i in range(2)]
        wraw = _C(scr32[:, 2048:2560].rearrange("p (h t) -> p h t", h=4), "sg_wraw")
        T_ = self.sg_t
        P.dma(wraw[:], self.hv(self.h_sgwT[l], "h_sgwT"))
        tri = V(self.C("tri_le").ap.unsqueeze(1).to_broadcast([128, 4, 128]), self.consts[:].keys)
        P.tt(T_["wmT"][:], wraw[:], tri, ALU.mult)
        for tb in range(4):
            self.load_xb(tb)
            u_sb = usb[tb % 2]
            for c in range(2):
                self.proj_fm(w, c * 128, 128, tb, lambda ps, c=c: P.copy(u_sb[:, c, :], ps[:, :], eng="act"))
            for n4 in range(4):
                n = tb * 4 + n4
                vs = self.sub256(self.nf()); sq = self.sub256(self.nf()); stat = T_["stat"][n % 2]; vnp = T_["vnp"][n % 2]
                def ev(ps):
                    P.copy(vs[:], ps[:, 0:256], eng="act")
                    P.act(sq[:], ps[:, 0:256], AF.Square)
                self.proj_tm(w, 256, 256, n * 128, ev)
                v3 = V(vs.h[:, :].rearrange("p (h d) -> p h d", h=4), vs[:].keys)
                q3 = V(sq.h[:, :].rearrange("p (h d) -> p h d", h=4), sq[:].keys)
                P.reduce(stat[:, 0:4], v3, ALU.add)
                P.reduce(stat[:, 4:8], q3, ALU.add)
                P.ts(stat[:, 0:4], stat[:, 0:4], 1.0 / 64, ALU.mult)
                P.tt(stat[:, 8:12], stat[:, 0:4], stat[:, 0:4], ALU.mult)
                P.stt(stat[:, 4:8], stat[:, 4:8], 1.0 / 64, stat[:, 8:12], ALU.mult, ALU.subtract)
                P.act(stat[:, 4:8], stat[:, 4:8], AF.Ln, bias=LN_EPS)
                P.act(stat[:, 12:16], stat[:, 4:8], AF.Exp, scale=-0.5)
                for h in range(4):
                    P.ts(sq[:, h * 64:(h + 1) * 64], vs[:, h * 64:(h + 1) * 64], stat[:, h:h + 1], ALU.subtract,
                         stat[:, 12 + h:13 + h], ALU.mult)
                P.tt(sq[:], sq[:], self.row("sg_ln_g"), ALU.mult)
                s4 = sq.h[:, :].rearrange("p (a b d) -> p a b d", a=2, b=2)
                rb = self.row("sg_ln_b").ap.rearrange("p (a b d) -> p a b d", a=2, b=2)
                for hh in range(2):
                    P.tt(V(vnp.h[:, :, hh, hh * 64:(hh + 1) * 64], vnp[:].keys), V(s4[:, :, hh, :], sq[:].keys),
                         V(rb[:, :, hh, :], ("sg_rowp",)), ALU.add)
                for pair in range(2):
                    ps = self.nextps()
                    for hh in range(2):
                        P.mm(ps[:, 0:128], V(vnp.h[:, pair, hh, :], vnp[:].keys), T_["wmT"][:, pair * 2 + hh, :],
                             start=(hh == 0), stop=(hh == 1))
                    t2 = self.nf()
                    P.tt(t2[:, 0:128], ps[:, 0:128], self.row("sgb", pair * 128, (pair + 1) * 128), ALU.add)
                    P.tt(self.yT[:, pair, n * 128:(n + 1) * 128], t2[:, 0:128], u_sb[:, pair, n4 * 128:(n4 + 1) * 128], ALU.mult)

    def mixer_fox(self, l):
        P = self.P
        w = self.load_w(self.h_w_in[l][:, 2320:2832], 8, 512, "h_w_in")
        w2 = self.load_w(self.h_w_in[l][:, 2832:3092], 8, 260, "h_w_in")
        if not hasattr(self, "fx"):
            self.fx = dict(
                negcT=P.sb("fx_negcT", [128, 16, 4], F32),
                nfb=P.sb("fx_nfb", [4, 1], F32),
            )
        F = self.fx
        scr = self.scr
        qT = V(scr.h[:, 0:4096].rearrange("p (c t) -> p c t", c=2), ("fx_qT",))
        kT = V(scr.h[:, 4096:8192].rearrange("p (c t) -> p c t", c=2), ("fx_kT",))
        v1ap = scr.h[:, 8192:16384].rearrange("p (n h m) -> p n h m", n=16, h=4)
        v1 = lambda idx: V(v1ap[idx], ("fx_v1",))
        self.scr_phase(["fx_qT", "fx_kT", "fx_v1", "fx_negc"])
        negc_ap = scr.h[:, 16384:20480].bitcast(F32)
        class _N:
            def __getitem__(s_, idx):
                return V(negc_ap[idx], ("fx_negc",))
        F["negc"] = _N(); F["nlf"] = F["negc"]
        P.memset(v1((slice(None),)), 0.0, eng="dve")
        for h in range(4):
            col = 64 if h % 2 == 0 else 0
            P.memset(v1((slice(None), slice(None), h, slice(col, col + 1))), 1.0, eng="dve")
        P.ts(F["nfb"][:], self.col("fox_fb%d" % l, 0, rows=4), -1.0, ALU.mult)
        for tb in range(4):
            self.load_xb(tb)
            sl = slice(tb * 512, (tb + 1) * 512)
            for c in range(2):
                self.proj_fm(w, c * 128, 128, tb,
                             lambda ps, c=c: P.act(V(qT.ap[:, c, sl], qT.keys), ps[:, :], AF.Copy, scale=0.125))
                self.proj_fm(w, 256 + c * 128, 128, tb,
                             lambda ps, c=c: P.copy(V(kT.ap[:, c, sl], kT.keys), ps[:, :], eng="dve"))
            def evf(ps):
                P.act(F["nlf"][0:4, sl], ps[0:4, :], AF.Exp, bias=F["nfb"][:, 0:1], scale=-1.0)
                P.act(F["nlf"][0:4, sl], F["nlf"][0:4, sl], AF.Ln, bias=1.0)
            self.proj_fm(w2, 256, 4, tb, evf)
            for n4 in range(4):
                n = tb * 4 + n4
                def evv(ps, n=n):
                    for h in range(4):
                        col = 0 if h % 2 == 0 else 64
                        P.copy(v1((slice(None), n, h, slice(col, col + 64))), ps[:, h * 64:(h + 1) * 64],
                               eng=("act" if h % 2 else "dve"))
                self.proj_tm(w2, 0, 256, n * 128, evv)
        ones_b = self.C("ones", rows=4, c0=0, c1=1).ap.to_broadcast([4, S])
        P.generic("dve", lambda e: e.tensor_tensor_scan(negc_ap[0:4, :], ones_b, negc_ap[0:4, :], 0.0,
                                                         ALU.mult, ALU.add),
                  [self.consts[:], F["negc"][0:4, :]], [F["negc"][0:4, :]])
        self.debug_dump("fox_negc", F["negc"][0:4, :], [4, S])
        for J in range(16):
            ps = self.nextps()
            P.transpose(ps[:, 0:4], F["negc"][0:4, J * 128:(J + 1) * 128], self.C("ident", rows=4, c0=0, c1=4))
            P.copy(F["negcT"][:, J, :], ps[:, 0:4])
        if "sel12" not in F:
            F["sel12"] = P.sb("fx_sel12", [12, 512], BF16)
            P.copy(F["sel12"][:], self.C("selneg", rows=12))
        cs_t = w2.tile
        cs_ap = cs_t.h[0:12, 0:S]
        cs3 = lambda r0, r1, c0, c1: V(cs_ap[r0:r1, c0:c1], (cs_t.name,))
        for tb in range(4):
            c0 = tb * 512; c1 = c0 + 512
            r1 = self.nt(); r2 = self.nt(); m_ = self.nb(); l_ = self.nb()
            P.copy(cs3(0, 4, c0, c1), F["negc"][0:4, c0:c1])
            P.tt(r1[0:4, :], F["negc"][0:4, c0:c1], cs3(0, 4, c0, c1), ALU.subtract)
            P.copy(m_[0:4, :], r1[0:4, :])
            P.tt(r2[0:4, :], r1[0:4, :], m_[0:4, :], ALU.subtract)
            P.copy(l_[0:4, :], r2[0:4, :])
            P.dma(cs3(4, 8, c0, c1), m_[0:4, :])
            P.dma(cs3(8, 12, c0, c1), l_[0:4, :])
        for h in range(4):
            c = h // 2
            p0 = (h % 2) * 64
            even = (h % 2 == 0)
            for Q in range(4):
                ops_ = self.nextacc()
                nJ = 4 * Q + 4
                for J in range(nJ):
                    c_lo = max(0, (J - 4 * Q) * 128)
                    lg = self.nextps()
                    P.mm(lg[:, c_lo:512], V(kT.ap[p0:p0 + 64, c, J * 128:(J + 1) * 128], kT.keys),
                         V(qT.ap[p0:p0 + 64, c, Q * 512 + c_lo:(Q + 1) * 512], qT.keys), start=True, stop=False)
                    P.mm(lg[:, c_lo:512], F["sel12"][:, h * 128:(h + 1) * 128],
                         cs3(0, 12, Q * 512 + c_lo, (Q + 1) * 512), start=False, stop=True)
                    if J >= 4 * Q:
                        P.tt(lg[:, c_lo:c_lo + 128], lg[:, c_lo:c_lo + 128], self.C("negmask"), ALU.add)
                    pT = self.nb()
                    P.act(pT[:, c_lo:512], lg[:, c_lo:512], AF.Exp, bias=F["negcT"][:, J, h:h + 1])
                    M = 65 if even else 128
                    P.mm(ops_[0:M, c_lo:512], v1((slice(None), J, h, slice(0, M))), pT[:, c_lo:512],
                         start=(J == 0), stop=(J == nJ - 1))
                osb = self.nt()
                if even:
                    P.copy(osb[0:65, :], ops_[0:65, :], eng="act")
                    P.recip(osb[64:65, :], osb[64:65, :])
                    bp = self.nextps()
                    P.mm(bp[0:64, :], self.C("sel_even", rows=65), osb[0:65, :])
                    P.tt(self.yT[0:64, c, Q * 512:(Q + 1) * 512], osb[0:64, :], bp[0:64, :], ALU.mult)
                else:
                    P.copy(osb[:, :], ops_[:, :], eng="act")
                    P.recip(osb[0:1, :], osb[0:1, :])
                    bp = self.nextps()
                    P.mm(bp[:, :], self.C("sel_odd"), osb[:, :])
                    P.tt(self.yT[64:128, c, Q * 512:(Q + 1) * 512], osb[64:128, :], bp[64:128, :], ALU.mult)


    def mixer_gla(self, l):
        P = self.P
        w1 = self.load_w(self.h_w_in[l][:, 1536:2048], 8, 512, "h_w_in")
        w2 = self.load_w(self.h_w_in[l][:, 2048:2320], 8, 272, "h_w_in")
        if not hasattr(self, "gl"):
            self.gl = dict(
                aup=P.sb("gl_aup", [16, 128], BF16),
                nab=P.sb("gl_nab", [64, 2], F32),
                gps=P.sb("gl_gps", [64, 32], F32),
                bl=P.sb("gl_bl", [64, 32], F32),
                dec=P.sb("gl_dec", [64, 2, 32], F32),
                S=[P.sb("gl_S%d" % g, [64, 128], F32) for g in range(2)],
                Sbf=[[P.sb("gl_Sbf%d_%d" % (g, i), [64, 128], BF16) for i in range(4)] for g in range(2)],
                osb=[P.sb("gl_osb%d" % i, [128, 128], F32) for i in range(2)],
            )
        G = self.gl
        scr = self.scr
        self.scr_phase(["gl_qe", "gl_ke", "gl_v", "gl_ketm", "gl_G", "gl_adT"])
        class _C:
            def __init__(s_, ap, key):
                s_.ap = ap; s_.key = key
            def __getitem__(s_, idx):
                return V(s_.ap[idx], (s_.key,))
        qe = _C(scr.h[:, 0:4096].rearrange("p (g t) -> p g t", g=2), "gl_qe")
        ke = _C(scr.h[:, 4096:8192].rearrange("p (g t) -> p g t", g=2), "gl_ke")
        v128 = _C(scr.h[:, 8192:12288].rearrange("p (b c) -> p b c", b=16), "gl_v")
        ketm = _C(scr.h[:, 12288:14336].rearrange("p (b c) -> p b c", b=16), "gl_ketm")
        Gt = _C(scr.h[:, 14336:18432].bitcast(F32), "gl_G")
        Gt3 = _C(scr.h[:, 14336:18432].bitcast(F32).rearrange("p (n c) -> p n c", n=32), "gl_G")
        adT = _C(scr.h[:, 18432:20480], "gl_adT")
        P.dma(G["aup"][:], self.hv(self.h_gla_a_up[l], "h_gla_a_up"), eng="pool")
        P.ts(G["nab"][:], V(self.colp.h[0:64, COL_LAYOUT["gla_a_b%d" % l][0]:COL_LAYOUT["gla_a_b%d" % l][0] + 2], self.colp[:].keys),
             -1.0, ALU.mult)
        for tb in range(4):
            self.load_xb(tb)
            sl = slice(tb * 512, (tb + 1) * 512)
            self.proj_fm(w2, 256, 16, tb, lambda ps: P.copy(adT[0:16, sl], ps[0:16, :], eng="act"))
            for c in range(2):
                def evg(ps, c=c):
                    t = self.nt()
                    P.act(t[:], ps[:, :], AF.Silu)
                    P.ts(self.yT[:, c, sl], t[:], self.col("gla_norm_g%d" % l, c), ALU.mult)
                self.proj_fm(w2, c * 128, 128, tb, evg)
            for b4 in range(4):
                bk = tb * 4 + b4
                self.proj_tm(w1, 256, 256, bk * 128, lambda ps, bk=bk: P.copy(v128[:, bk, :], ps[:, 0:256], eng="act"))
        import os
        stop = int(os.environ.get('GLA_STOP', '99'))
        if stop <= 1:
            return
        ones_b = self.C("ones", rows=64, c0=0, c1=1).ap.to_broadcast([64, S])
        for g in range(2):
            for tb in range(4):
                sl = slice(tb * 512, (tb + 1) * 512)
                ps = self.nextps()
                P.mm(ps[0:64, :], G["aup"][:, g * 64:(g + 1) * 64], adT[0:16, sl])
                P.act(Gt[0:64, sl], ps[0:64, :], AF.Exp, bias=G["nab"][:, g:g + 1], scale=-1.0)
                P.act(Gt[0:64, sl], Gt[0:64, sl], AF.Ln, bias=1.0)
            P.generic("dve", lambda e: e.tensor_tensor_scan(Gt.ap[0:64, :], ones_b, Gt.ap[0:64, :], 0.0, ALU.mult, ALU.add),
                      [self.consts[:], Gt[0:64, :]], [Gt[0:64, :]])
            if stop <= 2:
                continue
            P.memset(G["gps"][:, 0:1], 0.0)
            P.copy(G["gps"][:, 1:32], Gt3[0:64, 0:31, 63])
            P.tt(Gt3[0:64, :, :], Gt3[0:64, :, :], V(G["gps"].h[:, :].unsqueeze(2).to_broadcast([64, 32, 64]), G["gps"][:].keys),
                 ALU.subtract)
            P.copy(G["bl"][:], Gt3[0:64, :, 63])
            P.act(G["dec"][:, g, :], G["bl"][:], AF.Exp, scale=-1.0 / 16)
            for tb in range(4):
                sl = slice(tb * 512, (tb + 1) * 512)
                self.load_xb(tb)
                Eb = self.nt(); Enb = self.nt()
                P.act(Eb[0:64, :], Gt[0:64, sl], AF.Exp, scale=-1.0 / 16)
                P.act(Enb[0:64, :], Gt[0:64, sl], AF.Exp, scale=1.0 / 16)
                self.proj_fm(w1, g * 64, 64, tb,
                             lambda ps: P.stt(qe[0:64, g, sl], ps[0:64, :], 32.0 ** -0.5, Eb[0:64, :], ALU.mult, ALU.mult))
                self.proj_fm(w1, 128 + g * 64, 64, tb,
                             lambda ps: P.tt(ke[0:64, g, sl], ps[0:64, :], Enb[0:64, :], ALU.mult))
            if stop <= 3:
                continue
            for bk in range(16):
                ps = self.nextps()
                pbf = V(ps.h[:, :].bitcast(BF16), ps[:].keys)
                P.transpose(V(pbf.ap[:, 0:64], pbf.keys), ke[0:64, g, bk * 128:(bk + 1) * 128], self.ident_bf[0:64, 0:64])
                P.copy(ketm[:, bk, g * 64:(g + 1) * 64], V(pbf.ap[:, 0:64], pbf.keys), eng="act")
        self.debug_dump("gl_qe", qe[0:64, :, :], [64, 2, S])
        self.debug_dump("gl_ke", ke[0:64, :, :], [64, 2, S])
        if stop <= 4:
            return
        for g in range(2):
            P.memset(G["S"][g][:], 0.0)
            P.memset(G["Sbf"][g][0][:], 0.0)
        for bk in range(16):
            for g in range(2):
                attm = []
                for hh in range(2):
                    ps = self.nextps()
                    P.mm(ps[:, 0:128], ke[hh * 32:hh * 32 + 32, g, bk * 128:(bk + 1) * 128],
                         qe[hh * 32:hh * 32 + 32, g, bk * 128:(bk + 1) * 128])
                    am = self.nb()
                    P.tt(am[:, 0:128], ps[:, 0:128], self.C("mask2"), ALU.mult)
                    attm.append(am)
                if stop <= 5:
                    continue
                for cc in range(2):
                    n = 2 * bk + cc
                    if n == 31:
                        break
                    psU = self.nextps()
                    r0 = cc * 64
                    P.mm(psU[0:64, 0:128], ketm[r0:r0 + 64, bk, g * 64:(g + 1) * 64], v128[r0:r0 + 64, bk, g * 128:(g + 1) * 128])
                    P.tt(G["S"][g][:], G["S"][g][:], psU[0:64, 0:128], ALU.add)
                    P.ts(G["S"][g][:], G["S"][g][:], G["dec"][:, g, n:n + 1], ALU.mult)
                    P.copy(G["Sbf"][g][(n + 1) % 4][:], G["S"][g][:], eng="act")
                if stop <= 6:
                    continue
                psO = self.nextps()
                for hh in range(2):
                    c0 = hh * 128
                    P.mm(psO[:, c0:c0 + 128], v128[:, bk, g * 128:(g + 1) * 128], attm[hh][:, 0:128], start=True, stop=False)
                    for cc in range(2):
                        n = 2 * bk + cc
                        P.mm(psO[:, c0 + cc * 64:c0 + (cc + 1) * 64], G["Sbf"][g][n % 4][hh * 32:hh * 32 + 32, :],
                             qe[hh * 32:hh * 32 + 32, g, n * 64:(n + 1) * 64], start=False, stop=(cc == 1))
                if stop <= 7:
                    continue
                osb = G["osb"][g]
                osq = self.nb()
                for hh in range(2):
                    r0 = hh * 64
                    var = os.environ.get('GLA_VAR', 'ab')
                    if 'a' in var:
                        P.copy(osb[r0:r0 + 64, :], psO[r0:r0 + 64, hh * 128:(hh + 1) * 128], eng="dve")
                    if 'b' in var:
                        P.act(osq[r0:r0 + 64, 0:128], psO[r0:r0 + 64, hh * 128:(hh + 1) * 128], AF.Square)
                if stop <= 8:
                    continue
                pss = self.nextps()
                P.mm(pss[:, 0:128], self.blk64_bf[:, :], osq[:, 0:128])
                if stop <= 9:
                    continue
                rs = self.nf()
                P.act(rs[:, 0:128], pss[:, 0:128], AF.Ln, bias=1e-5)
                P.act(rs[:, 0:128], rs[:, 0:128], AF.Exp, scale=-0.5)
                if stop <= 10:
                    continue
                P.tt(osb[:, :], osb[:, :], rs[:, 0:128], ALU.mult)
                if stop <= 11:
                    continue
                yv = self.yT[:, g, bk * 128:(bk + 1) * 128]
                P.tt(yv, osb[:, :], yv, ALU.mult)


    def mixer_rw(self, l):
        P = self.P
        wA = self.load_w(self.h_w_in[l][:, 512:1024], 8, 512, "h_w_in")
        wB = self.load_w(self.h_w_in[l][:, 1024:1536], 8, 512, "h_w_in")
        if not hasattr(self, "rw"):
            self.rw = dict(
                sm=P.sb("rw_sm", [128, 512], BF16),
                omm=P.sb("rw_omm", [128, 24], F32),
                nwa=P.sb("rw_nwa", [64, 8], F32),
                carry=P.sb("rw_carry", [128, 8], F32),
                maskq=P.sb("rw_maskq", [64, 128], F32),
                PC=P.sb("rw_PC", [64, 32], F32),
                gs=P.sb("rw_gs", [64, 4], F32),
                S32=P.sb("rw_S32", [64, 64], F32),
                Sb=[P.sb("rw_Sb%d" % i, [64, 64], BF16) for i in range(2)],
                Nn=[P.sb("rw_Nn%d" % i, [64, 64], BF16) for i in range(2)],
                XN=[[P.sb("rw_XN%d_%d" % (i, j), [64, 128], BF16) for j in range(2)] for i in range(2)],
                Wt=[[P.sb("rw_Wt%d_%d" % (i, j), [64, 64], BF16) for j in range(2)] for i in range(2)],
                Wf=[P.sb("rw_Wf%d" % i, [64, 64], BF16) for i in range(4)],
                RU=[P.sb("rw_RU%d" % i, [64, 128], BF16) for i in range(2)],
            )
            P.copy(self.rw["maskq"][:, 0:64], self.C("tri_le", rows=64, c0=0, c1=64))
            P.copy(self.rw["maskq"][:, 64:128], self.C("tri_lt", rows=64, c0=0, c1=64))
        R = self.rw
        scr = self.scr
        self.scr_phase(["rw_twa", "rw_sgd", "rw_KB", "rw_RA", "rw_VT", "rw_bv"])
        class _C:
            def __init__(s_, ap, key):
                s_.ap = ap; s_.key = key
            def __getitem__(s_, idx):
                return V(s_.ap[idx], (s_.key,))
        twa = _C(scr.h[:, 0:2048], "rw_twa")
        sgd = _C(scr.h[:, 2048:4096], "rw_sgd")
        KB = _C(scr.h[:, 4096:8192].rearrange("p (n c) -> p n c", n=32), "rw_KB")
        RA = _C(scr.h[:, 8192:12288].rearrange("p (n c) -> p n c", n=32), "rw_RA")
        VT = _C(scr.h[:, 12288:14336], "rw_VT")
        bv = _C(scr.h[:, 14336:16384], "rw_bv")
        y1 = self.yT.h[0:64, 1, :]
        class _A:
            def __init__(s_, c0, w, key):
                s_.c0 = c0; s_.w = w; s_.key = key
            def __getitem__(s_, idx):
                p, c = idx
                c = slice(s_.c0 + (c.start or 0), s_.c0 + (s_.w if c.stop is None else c.stop))
                return V(y1[:, c], (s_.key,))
        RQA = [_A(i * 128, 128, "rwq_QA%d" % i) for i in range(4)]
        RQB = [_A(512 + i * 128, 128, "rwq_QB%d" % i) for i in range(4)]
        RTM = [_A(1024 + i * 192, 192, "rwq_TM%d" % i) for i in range(4)]
        ring_keys = tuple(x.key for x in RQA + RQB + RTM)
        P.add("pool", lambda e: e.memset(self._dummy.h[:, :], 0.0), [V(None, ("yT",))], [V(None, ring_keys + (self._dummy.name,))])
        P.dma(R["sm"][0:64, 0:256], self.hv(self.h_rw_w2[l], "h_rw_w2"), eng="pool")
        P.dma(R["sm"][64:128, 0:256], self.hv(self.h_rw_a2[l], "h_rw_a2"), eng="pool")
        P.dma(R["sm"][:, 256:512], self.hv(self.h_rw_g2[l], "h_rw_g2"), eng="pool")
        o_mu, _ = COL_LAYOUT["rw_mu%d" % l]; o_mu64, _ = COL_LAYOUT["rw_mu64_%d" % l]
        mu128 = lambda c: self.colp[:, o_mu + c:o_mu + c + 1]
        mu64 = lambda c: self.colp[0:64, o_mu64 + c:o_mu64 + c + 1]
        P.ts(R["omm"][:, 0:8], self.colp[:, o_mu:o_mu + 8], -1.0, ALU.mult, 1.0, ALU.add)
        P.ts(R["omm"][0:64, 8:24], self.colp[0:64, o_mu64:o_mu64 + 16], -1.0, ALU.mult, 1.0, ALU.add)
        o_h, _ = COL_LAYOUT["rw_h64_%d" % l]
        hcol = lambda h, j: self.colp[0:64, o_h + h * 9 + j:o_h + h * 9 + j + 1]
        for h in range(4):
            P.ts(R["nwa"][:, 2 * h:2 * h + 1], hcol(h, 0), -1.0, ALU.mult)
            P.ts(R["nwa"][:, 2 * h + 1:2 * h + 2], hcol(h, 1), -1.0, ALU.mult)
        pool_tiles = self.t512 + self.st512
        def tmp(i, rows=64):
            t = pool_tiles[i // 2]
            c0 = (i % 2) * 256
            class _T:
                name = t.name
                def __getitem__(s_, idx):
                    if not isinstance(idx, tuple):
                        idx = (idx, slice(0, 256))
                    p, c = idx
                    c = slice(c0 + (c.start or 0), c0 + (256 if c.stop is None else c.stop))
                    return V(t.h[p, c], (t.name,))
                def v3(s_, rows_):
                    return V(t.h[0:rows_, c0:c0 + 256].rearrange("p (n c) -> p n c", n=4), (t.name,))
            return _T()

        def shiftmix(ps, M, mu_ap, omm_ap, cslot, out_t, blk):
            zr = pool_tiles[6]
            if blk == 0:
                P.memset(zr[0:M, 0:1], 0.0)
            else:
                P.copy(zr[0:M, 0:1], R["carry"][0:M, cslot:cslot + 1])
            P.copy(zr[0:M, 1:257], ps[0:M, 0:256], eng="act")
            P.copy(R["carry"][0:M, cslot:cslot + 1], zr[0:M, 256:257])
            P.act(out_t[0:M, :], ps[0:M, 0:256], AF.Copy, scale=omm_ap)
            P.stt(out_t[0:M, :], zr[0:M, 0:256], mu_ap, out_t[0:M, :], ALU.mult, ALU.add)

        class _XB:
            def __init__(s_, xb, t0):
                s_.xb = xb; s_.t0 = t0
            def __getitem__(s_, idx):
                p, k, c = idx
                return V(s_.xb.h[p, k, slice(s_.t0 + c.start, s_.t0 + c.stop)], (s_.xb.name,))

        for blk in range(8):
            self.cur_xb = _XB(self.xb, blk * 256)
            sl = slice(blk * 256, (blk + 1) * 256)
            z = tmp(0)
            self.proj_fm(wB, 256, 128, 0, lambda ps: shiftmix(ps, 128, mu128(6), R["omm"][:, 6:7], 0, z, blk), ncol=256)
            P.act(twa[0:64, sl], z[0:64, :], AF.Tanh)
            P.copy(twa[64:128, sl], z[64:128, :], eng="pool")
            z2 = tmp(1)
            self.proj_fm(wB, 384, 128, 0, lambda ps: shiftmix(ps, 128, mu128(7), R["omm"][:, 7:8], 1, z2, blk), ncol=256)
            P.act(z2[:, :], z2[:, :], AF.Exp, scale=-1.0)
            P.act(z2[:, :], z2[:, :], AF.Ln, bias=1.0)
            P.act(sgd[:, sl], z2[:, :], AF.Exp, scale=-1.0)
        import os
        rstop = int(os.environ.get("RW_STOP", "99"))
        nheads = int(os.environ.get("RW_HEADS", "4"))
        pending_acc = []
        for h in range(nheads):
            for blk in range(8):
                for _ in range(4):
                    if pending_acc:
                        pending_acc.pop(0)()
                self.cur_xb = _XB(self.xb, blk * 256)
                sl = slice(blk * 256, (blk + 1) * 256)
                n0 = blk * 4
                zr_ = tmp(0); zk_ = tmp(1); zv_ = tmp(2)
                self.proj_fm(wA, h * 64, 64, 0, lambda ps: shiftmix(ps, 64, mu64(h), R["omm"][0:64, 8 + h:9 + h], 2, zr_, blk), ncol=256)
                self.proj_fm(wA, 256 + h * 64, 64, 0, lambda ps: shiftmix(ps, 64, mu64(4 + h), R["omm"][0:64, 12 + h:13 + h], 3, zk_, blk), ncol=256)
                self.proj_fm(wB, h * 64, 64, 0, lambda ps: shiftmix(ps, 64, mu64(8 + h), R["omm"][0:64, 16 + h:17 + h], 4, zv_, blk), ncol=256)
                P.copy(VT[0:64, sl], zv_[0:64, :], eng="pool")
                LD = tmp(3)
                ps = self.nextps()
                P.mm(ps[0:64, 0:256], R["sm"][0:64, h * 64:(h + 1) * 64], twa[0:64, sl])
                P.act(LD[0:64, :], ps[0:64, 0:256], AF.Exp, bias=R["nwa"][:, 2 * h:2 * h + 1], scale=-1.0)
                P.act(LD[0:64, :], LD[0:64, :], AF.Ln, bias=1.0)
                P.act(LD[0:64, :], LD[0:64, :], AF.Exp, bias=-0.5, scale=-1.0)
                A = tmp(4)
                ps = self.nextps()
                P.mm(ps[0:64, 0:256], R["sm"][64:128, h * 64:(h + 1) * 64], twa[64:128, sl])
                P.act(A[0:64, :], ps[0:64, 0:256], AF.Exp, bias=R["nwa"][:, 2 * h + 1:2 * h + 2], scale=-1.0)
                P.act(A[0:64, :], A[0:64, :], AF.Ln, bias=1.0)
                P.act(A[0:64, :], A[0:64, :], AF.Exp, scale=-1.0)
                KK = tmp(5)
                P.ts(KK[0:64, :], zk_[0:64, :], hcol(h, 2), ALU.mult)
                sq = self.nb()
                P.act(sq[0:64, 0:256], KK[0:64, :], AF.Square)
                ps = self.nextps()
                P.mm(ps[0:64, 0:256], self.blk64_bf[0:64, 0:64], sq[0:64, 0:256])
                RN = tmp(10)
                P.act(RN[0:64, :], ps[0:64, 0:256], AF.Ln, bias=1e-24, scale=64.0)
                P.act(RN[0:64, :], RN[0:64, :], AF.Exp, scale=-0.5)
                P.tt(KK[0:64, :], KK[0:64, :], RN[0:64, :], ALU.mult)
                K2 = tmp(6)
                P.ts(K2[0:64, :], A[0:64, :], -1.0, ALU.add, hcol(h, 3), ALU.mult)
                P.stt(K2[0:64, :], K2[0:64, :], 1.0, zk_[0:64, :], ALU.add, ALU.mult)
                KKA = tmp(7)
                P.tt(KKA[0:64, :], KK[0:64, :], A[0:64, :], ALU.mult)
                pr = self.nb()
                P.stt(pr[0:64, 0:256], zr_[0:64, :], hcol(h, 4), K2[0:64, :], ALU.mult, ALU.mult)
                ps = self.nextps()
                P.mm(ps[0:64, 0:256], self.blk64_bf[0:64, 0:64], pr[0:64, 0:256])
                P.stt(bv[0:64, sl], ps[0:64, 0:256], 64.0, zv_[0:64, :], ALU.mult, ALU.mult)
                Gb = tmp(8)
                ones_b = self.C("ones", rows=64, c0=0, c1=1).ap.to_broadcast([64, 256])
                P.generic("dve", lambda e, Gb=Gb, LD=LD: e.tensor_tensor_scan(Gb[0:64, :].ap, ones_b, LD[0:64, :].ap, 0.0, ALU.mult, ALU.add),
                          [self.consts[:], LD[0:64, :]], [Gb[0:64, :]])
                P.memset(R["gs"][:, 0:1], 0.0)
                G3 = Gb.v3(64)
                P.copy(R["gs"][:, 1:4], V(G3.ap[:, 0:3, 63], G3.keys))
                P.tt(G3, G3, V(R["gs"].h[:, :].unsqueeze(2).to_broadcast([64, 4, 64]), R["gs"][:].keys), ALU.subtract)
                P.act(R["PC"][:, n0:n0 + 4], V(G3.ap[:, :, 63], G3.keys), AF.Exp, scale=-1.0)
                Ep = tmp(10); Em = tmp(11); Epm1 = tmp(9)
                P.act(Ep[0:64, :], Gb[0:64, :], AF.Exp, scale=-1.0)
                P.act(Em[0:64, :], Gb[0:64, :], AF.Exp)
                P.tt(Epm1[0:64, :], LD[0:64, :], Gb[0:64, :], ALU.subtract)
                P.act(Epm1[0:64, :], Epm1[0:64, :], AF.Exp)
                P.tt(V(RA.ap[0:64, n0:n0 + 4, 0:64], ("rw_RA",)), zr_.v3(64), Ep.v3(64), ALU.mult)
                P.stt(V(RA.ap[0:64, n0:n0 + 4, 64:128], ("rw_RA",)), KK.v3(64), -1.0, Epm1.v3(64), ALU.mult, ALU.mult)
                P.tt(V(KB.ap[0:64, n0:n0 + 4, 0:64], ("rw_KB",)), K2.v3(64), Em.v3(64), ALU.mult)
                P.tt(V(KB.ap[0:64, n0:n0 + 4, 64:128], ("rw_KB",)), KKA.v3(64), Em.v3(64), ALU.mult)
            if h == 0:
                self.debug_dump("rw_RA", RA[0:64, :, :], [64, 32, 128])
                self.debug_dump("rw_KB", KB[0:64, :, :], [64, 32, 128])
                self.debug_dump("rw_PC", R["PC"][:, :], [64, 32])
            if rstop <= 1:
                continue
            P.memset(R["S32"][:], 0.0)
            P.memset(R["Sb"][0][:], 0.0)
            self._psY = None
            BS = 2
            def slot(n):
                return n % (2 * BS)
            def pre_rounds(ns):
                rounds = []
                def r0():
                    for n in ns:
                        s4 = slot(n)
                        QA = RQA[s4]; QB = RQB[s4]; TM = RTM[s4]; Nn = R["Nn"][n % BS]
                        Kt = KB[0:64, n, 0:64]; Bt = KB[0:64, n, 64:128]; At = RA[0:64, n, 64:128]
                        ps = self.nextps()
                        P.mm(ps[0:64, 0:128], Kt, RA[0:64, n, :])
                        P.mm(ps[0:64, 128:256], Bt, RA[0:64, n, :])
                        P.mm(ps[0:64, 256:320], At, Bt)
                        P.tt(QA[:, :], ps[0:64, 0:128], R["maskq"][:, :], ALU.mult)
                        P.tt(QB[:, :], ps[0:64, 128:256], R["maskq"][:, :], ALU.mult)
                        P.tt(Nn[:, :], ps[0:64, 256:320], self.C("tri_gt", rows=64, c0=0, c1=64), ALU.mult)
                        ps2 = self.nextps()
                        pbf = lambda a_, b_, ps2=ps2: V(ps2.h[0:64, :].bitcast(BF16)[:, a_:b_], ps2[:].keys)
                        P.transpose(pbf(0, 64), Kt, self.ident_bf[0:64, 0:64])
                        P.transpose(pbf(64, 128), Bt, self.ident_bf[0:64, 0:64])
                        P.transpose(pbf(128, 192), VT[0:64, n * 64:(n + 1) * 64], self.ident_bf[0:64, 0:64])
                        P.copy(TM[:, :], pbf(0, 192), eng="act")
                rounds.append(r0)
                st = {}
                def r1():
                    for n in ns:
                        QB = RQB[slot(n)]
                        W = R["Wt"][n % BS][0]
                        P.tt(W[:, :], QB[:, 64:128], self.C("ident", rows=64, c0=0, c1=64), ALU.add)
                        st[n] = dict(W=W, Xp=QB[:, 64:128], Np=R["Nn"][n % BS][:, :], wi=0)
                rounds.append(r1)
                for lev in range(5):
                    def ra(lev=lev):
                        for n in ns:
                            d = st[n]
                            XN = R["XN"][n % BS][lev % 2]
                            ps = self.nextps()
                            P.mm(ps[0:64, 64:128], d["Xp"], d["Np"])
                            if lev < 4:
                                P.mm(ps[0:64, 0:64], d["Np"], d["Xp"])
                                P.copy(XN[:, :], ps[0:64, 0:128], eng="act")
                            else:
                                P.copy(XN[:, 64:128], ps[0:64, 64:128], eng="act")
                            d["XN"] = XN
                    def rb(lev=lev):
                        for n in ns:
                            d = st[n]
                            XN = d["XN"]
                            ps2 = self.nextps()
                            P.mm(ps2[0:64, 0:64], XN[:, 64:128], d["W"][:, :])
                            if lev < 4:
                                d["wi"] += 1
                                Wn = R["Wt"][n % BS][d["wi"] % 2]
                            else:
                                Wn = R["Wf"][slot(n)]
                            P.tt(Wn[:, :], ps2[0:64, 0:64], d["W"][:, :], ALU.add)
                            d["W"] = Wn
                            d["Xp"] = XN[:, 0:64]; d["Np"] = XN[:, 64:128]
                    rounds.append(ra); rounds.append(rb)
                return rounds

            def chain_hops(ns):
                hops = []
                for n in ns:
                    s4 = slot(n)
                    QA = RQA[s4]; QB = RQB[s4]; TM = RTM[s4]; W = R["Wf"][s4]
                    Rt = RA[0:64, n, 0:64]; At = RA[0:64, n, 64:128]
                    Sb = R["Sb"][n % 2]; Sbn = R["Sb"][(n + 1) % 2]
                    RU = R["RU"][n % 2]
                    def h1(n=n, QA=QA, TM=TM, At=At, Sb=Sb, RU=RU):
                        psA = self.nextps()
                        P.mm(psA[0:64, 0:64], At, Sb[:, :], start=True, stop=False)
                        P.mm(psA[0:64, 0:64], QA[:, 64:128], TM[:, 128:192], start=False, stop=True)
                        P.copy(RU[:, 0:64], psA[0:64, 0:64], eng="act")
                        P.ts(R["S32"][:, :], R["S32"][:, :], R["PC"][:, n:n + 1], ALU.mult)
                    def h2(n=n, W=W, RU=RU):
                        psU = self.nextps()
                        P.mm(psU[0:64, 0:64], W[:, :], RU[:, 0:64])
                        P.copy(RU[:, 64:128], psU[0:64, 0:64], eng="act")
                    def h3(n=n, QA=QA, QB=QB, TM=TM, Rt=Rt, Sb=Sb, Sbn=Sbn, RU=RU):
                        psS = self.nextps()
                        P.mm(psS[0:64, 0:64], TM[:, 0:64], TM[:, 128:192], start=True, stop=False)
                        P.mm(psS[0:64, 0:64], TM[:, 64:128], RU[:, 64:128], start=False, stop=True)
                        P.stt(Sbn[:, :], psS[0:64, 0:64], R["PC"][:, n:n + 1], R["S32"][:, :], ALU.mult, ALU.add)
                        P.stt(R["S32"][:, :], psS[0:64, 0:64], R["PC"][:, n:n + 1], R["S32"][:, :], ALU.mult, ALU.add)
                        if n % 8 == 0:
                            self._psY = self.nextacc()
                        psY = self._psY
                        yc = slice((n % 8) * 64, (n % 8 + 1) * 64)
                        P.mm(psY[0:64, yc], Sb[:, :], Rt, start=True, stop=False)
                        P.mm(psY[0:64, yc], TM[:, 128:192], QA[:, 0:64], start=False, stop=False)
                        P.mm(psY[0:64, yc], RU[:, 64:128], QB[:, 0:64], start=False, stop=True)
                        if n % 8 == 7:
                            post(n // 8, psY)
                    hops += [h1, h2, h3]
                return hops

            def post(tb, psY):
                sl = slice(tb * 512, (tb + 1) * 512)
                Y = self.nt()
                P.copy(Y[0:64, :], psY[0:64, :], eng="act")
                if h == 0:
                    self.debug_dump("rw_scan%d" % tb, Y[0:64, :], [64, 512])
                ysq = self.nb(); ybf = self.nb()
                P.act(ysq[0:64, :], Y[0:64, :], AF.Square)
                P.copy(ybf[0:64, :], Y[0:64, :], eng="pool")
                psm = self.nextps(); psq = self.nextps()
                P.mm(psm[0:64, :], self.blk64_bf[0:64, 0:64], ybf[0:64, :])
                P.mm(psq[0:64, :], self.blk64_bf[0:64, 0:64], ysq[0:64, :])
                m2 = self.nt()
                P.tt(Y[0:64, :], Y[0:64, :], psm[0:64, :], ALU.subtract)
                P.act(m2[0:64, :], psm[0:64, :], AF.Square)
                P.tt(m2[0:64, :], psq[0:64, :], m2[0:64, :], ALU.subtract)
                P.act(m2[0:64, :], m2[0:64, :], AF.Ln, bias=64e-5)
                P.act(m2[0:64, :], m2[0:64, :], AF.Exp, scale=-0.5)
                P.stt(Y[0:64, :], Y[0:64, :], hcol(h, 5), m2[0:64, :], ALU.mult, ALU.mult)
                P.stt(Y[0:64, :], Y[0:64, :], hcol(h, 6), bv[0:64, sl], ALU.add, ALU.add)
                psg = self.nextps()
                P.mm(psg[0:64, :], R["sm"][:, 256 + h * 64:256 + (h + 1) * 64], sgd[:, sl])
                P.tt(self.yT[0:64, 0, sl], Y[0:64, :], psg[0:64, :], ALU.mult)

            nb_ = 32 // BS
            for k in range(nb_ + 1):
                pr = pre_rounds(list(range(k * BS, (k + 1) * BS))) if k < nb_ else []
                ch = chain_hops(list(range((k - 1) * BS, k * BS))) if k >= 1 else []
                i = j = 0
                while i < len(pr) or j < len(ch):
                    if j < len(ch):
                        ch[j](); j += 1
                    for _ in range(2):
                        if i < len(pr):
                            pr[i](); i += 1
            if rstop <= 2:
                continue
            self.debug_dump("y_rwh%d" % h, self.yT[0:64, 0, :], [64, S])
            if "ln1" in self.stages:
                pending_acc = self.acc_out_rw(l, h)
        for th in pending_acc:
            th()
        P.add("pool", lambda e: e.memset(self._dummy.h[:, :], 0.0), [V(None, ring_keys)], [V(None, ("yT", self._dummy.name))])

    def acc_out_rw(self, l, h):
        P = self.P
        if not hasattr(self, "rw_wo"):
            self.rw_wo = P.sb("rw_wo", [64, 1024], BF16)
        t = self.rw_wo
        wv = WV(t, 1, 1024)
        P.dma(V(wv.ap[0:64, :, :], (t.name,)),
              V(self.h_w_out[l][256 + h * 64:256 + (h + 1) * 64, :].rearrange("(k p) c -> p k c", p=64), ("h_w_out",)), eng="pool")
        first = self._first_acc
        self._first_acc = False
        thunks = []
        for tb in range(4):
            for o in range(8):
                def th(tb=tb, o=o):
                    ps = self.nextps()
                    P.mm(ps[:, :], V(wv.ap[0:64, 0, o * 128:(o + 1) * 128], (t.name,)), self.yT[0:64, 0, tb * 512:(tb + 1) * 512])
                    xv = self.xT[:, o, tb * 512:(tb + 1) * 512]
                    if first:
                        P.stt(xv, xv, ALPHA, ps[:, :], ALU.mult, ALU.add)
                    else:
                        P.tt(xv, xv, ps[:, :], ALU.add)
                thunks.append(th)
        return thunks


    def mem_ln(self):
        P = self.P
        self.memnb = P.sb("memnb", [128, 8, NMEM], BF16)
        self.scr_phase(["mem_raw"])
        class _C:
            def __init__(s_, ap, key):
                s_.ap = ap; s_.key = key
            def __getitem__(s_, idx):
                return V(s_.ap[idx], (s_.key,))
        raw = _C(self.scr.h[:, :].bitcast(F32)[:, 0:8 * NMEM].rearrange("p (c t) -> p c t", c=8), "mem_raw")
        P.dma(raw[:, :, :], self.hv(self.h_memT.rearrange("(c p) t -> p c t", p=128), "h_memT"))
        self.layernorm("mem_ln_g", "mem_ln_b", src=raw, dst32=False, dstb=self.memnb, ntok=NMEM)

    def cross_attn(self, l):
        P = self.P
        self.scr_phase(["ca_KT", "ca_V", "ca_oT"])
        scr = self.scr
        class _C:
            def __init__(s_, ap, key):
                s_.ap = ap; s_.key = key
            def __getitem__(s_, idx):
                return V(s_.ap[idx], (s_.key,))
        KT = _C(scr.h[:, 0:2048].rearrange("p (c m) -> p c m", c=8), "ca_KT")
        Vt = _C(scr.h[:, 2048:4096].rearrange("p (b c) -> p b c", b=2), "ca_V")
        oT = _C(scr.h[:, 4096:20480].rearrange("p (c t) -> p c t", c=8), "ca_oT")
        if not hasattr(self, "ones_bf"):
            self.ones_bf = P.sb("ones_bf", [128, 128], BF16)
            P.copy(self.ones_bf[:], self.C("ones"))
        stages = []
        def ld_k(half):
            return self.load_w(self.h_ca_wk[l][:, half * 512:(half + 1) * 512], 8, 512, "h_ca_wk")
        def cp_k(wk, half):
            for oc in range(4):
                ps = self.nextps()
                for kc in range(8):
                    P.mm(ps[:, 0:NMEM], wk[:, kc, oc * 128:(oc + 1) * 128], self.memnb[:, kc, :], start=(kc == 0), stop=(kc == 7))
                P.copy(KT[:, half * 4 + oc, :], ps[:, 0:NMEM], eng="act")
        def ld_v(half):
            return self.load_w(self.h_ca_wv[l][:, half * 512:(half + 1) * 512], 8, 512, "h_ca_wv")
        def cp_v(wv, half):
            for mb in range(2):
                ps = self.nextps()
                for kc in range(8):
                    P.mm(ps[:, :], self.memnb[:, kc, mb * 128:(mb + 1) * 128], wv[:, kc, :], start=(kc == 0), stop=(kc == 7))
                P.copy(Vt[:, mb, half * 512:(half + 1) * 512], ps[:, :], eng="dve")
        def ld_q(h):
            return self.load_w(self.h_ca_wq[l][:, h * 256:(h + 1) * 256], 8, 256, "h_ca_wq")
        def cp_q(wq, h):
            for tb in range(4):
                self.load_xb(tb)
                sl = slice(tb * 512, (tb + 1) * 512)
                qT = [self.nb(), self.nb()]
                for c in range(2):
                    self.proj_fm(wq, c * 128, 128, tb, lambda ps, c=c: P.act(qT[c][:, :], ps[:, :], AF.Copy, scale=1.0 / 16))
                PT = []
                for mb in range(2):
                    ps = self.nextps()
                    for c in range(2):
                        P.mm(ps[:, :], KT[:, h * 2 + c, mb * 128:(mb + 1) * 128], qT[c][:, :], start=(c == 0), stop=(c == 1))
                    pt = self.nb()
                    P.act(pt[:, :], ps[:, :], AF.Exp)
                    PT.append(pt)
                den = self.nextps()
                for mb in range(2):
                    P.mm(den[:, :], self.ones_bf[:, :], PT[mb][:, :], start=(mb == 0), stop=(mb == 1))
                rden = self.nt()
                P.recip(rden[:, :], den[:, :])
                for c2 in range(2):
                    ps = self.nextps()
                    for mb in range(2):
                        P.mm(ps[:, :], Vt[:, mb, h * 256 + c2 * 128:h * 256 + (c2 + 1) * 128], PT[mb][:, :], start=(mb == 0), stop=(mb == 1))
                    P.tt(oT[:, h * 2 + c2, sl], ps[:, :], rden[:, :], ALU.mult)
        def ld_o(half):
            return self.load_w(self.h_ca_wo[l][:, half * 512:(half + 1) * 512], 8, 512, "h_ca_wo")
        def cp_o(wo, half):
            for tb in range(4):
                sl = slice(tb * 512, (tb + 1) * 512)
                for oc in range(4):
                    ps = self.nextps()
                    for kc in range(8):
                        P.mm(ps[:, :], wo[:, kc, oc * 128:(oc + 1) * 128], oT[:, kc, sl], start=(kc == 0), stop=(kc == 7))
                    xv = self.xT[:, half * 4 + oc, sl]
                    P.stt(xv, xv, ALPHA, ps[:, :], ALU.mult, ALU.add)
        for half in range(2):
            stages.append((ld_k, cp_k, half))
        for half in range(2):
            stages.append((ld_v, cp_v, half))
        for h in range(4):
            stages.append((ld_q, cp_q, h))
        for half in range(2):
            stages.append((ld_o, cp_o, half))
        cur = stages[0][0](stages[0][2])
        for i, (ld, cp, arg) in enumerate(stages):
            nxt = None
            if i + 1 < len(stages):
                nxt = stages[i + 1][0](stages[i + 1][2])
            cp(cur, arg)
            cur = nxt

    def conv_ffn(self, l):
        P = self.P
        self.scr_phase(["ff_w0", "ff_w1", "ff_w2", "ff_w3", "ff_pr0", "ff_pr1"])
        scr = self.scr
        if not hasattr(self, "ff"):
            self.ff = dict(halo=P.sb("ff_halo", [128, 8, 2], F32))
        y32 = self.yT.h[:, :, :].rearrange("p a b -> p (a b)").bitcast(F32)
        class _H:
            def __init__(s_, i):
                s_.i = i
            def __getitem__(s_, idx):
                p, c = idx
                return V(y32[p, slice(s_.i * 520 + c.start, s_.i * 520 + c.stop)], ("ffh%d" % s_.i,))
        self.ffh = [_H(i) for i in range(3)]
        self.P.add("pool", lambda e: e.memset(self._dummy.h[:, :], 0.0), [V(None, ("yT",))],
                   [V(None, ("ffh0", "ffh1", "ffh2", self._dummy.name))])
        class _WB:
            def __init__(s_, ap, key, kc, cols):
                s_.ap = ap[:, 0:kc * cols].rearrange("p (k c) -> p k c", k=kc); s_.key = key; s_.kc = kc
            def __getitem__(s_, idx):
                return V(s_.ap[idx], (s_.key,))
        bufs = [(self.wb[0].h[:, :], "wb0"), (self.wb[1].h[:, :], "wb1")] + \
               [(scr.h[:, i * 4096:(i + 1) * 4096], "ff_w%d" % i) for i in range(4)]
        prs = [(scr.h[:, 16384 + i * 2048:16384 + (i + 1) * 2048].rearrange("p (j t) -> p j t", j=4), "ff_pr%d" % i) for i in range(2)]
        def loadw(bi, hbm_ap, kc, cols, key):
            ap, k = bufs[bi]
            wv = _WB(ap, k, kc, cols)
            src = hbm_ap.rearrange("(k p) c -> p k c", p=128)
            step = 4 if cols <= 512 else 2
            kk = 0
            while kk < kc:
                k2 = min(kc, kk + step)
                P.dma(V(wv.ap[:, kk:k2, :], (k,)), V(src[:, kk:k2, :], (key,)), eng="pool")
                kk = k2
            return wv
        o_ub, _ = COL_LAYOUT["ffn_up_b"]; o_cb, _ = COL_LAYOUT["ffn_conv_b"]; o_cw, _ = COL_LAYOUT["ffn_conv"]
        colv = lambda o: self.colp[:, o:o + 1]
        npg = 6
        def load_pg(pg):
            nj = 4 if pg < 5 else 2
            b0 = (pg % 2) * 3
            wg = loadw(b0, self.h_ffn_up[l][:, pg * 512:pg * 512 + nj * 128], 8, nj * 128, "h_ffn_up")
            wvv = loadw(b0 + 1, self.h_ffn_up[l][:, DFF + pg * 512:DFF + pg * 512 + nj * 128], 8, nj * 128, "h_ffn_up")
            wd = loadw(b0 + 2, self.h_ffn_down[l][pg * 512:pg * 512 + nj * 128, :], nj, 1024, "h_ffn_down")
            return wg, wvv, wd
        nxt = load_pg(0)
        pend_dn = []
        for pg in range(npg):
            nj = 4 if pg < 5 else 2
            wg, wvv, wd = nxt
            if pg + 1 < npg:
                nxt = load_pg(pg + 1)
            for tb in range(4):
                self.load_xb(tb)
                sl = slice(tb * 512, (tb + 1) * 512)
                prap, prk = prs[(pg * 4 + tb) % 2]
                carry_dn = pend_dn[:]
                del pend_dn[:]
                for jj in range(nj):
                    for _ in range((8 + nj - 1) // nj):
                        if carry_dn:
                            carry_dn.pop(0)()
                    res = []
                    for part, wsrc in enumerate((wg, wvv)):
                        ch = part * 22 + pg * 4 + jj
                        hb = self.nt()
                        hs = part * 4 + jj
                        ps = self.nextps()
                        for kc in range(8):
                            P.mm(ps[:, :], wsrc[:, kc, jj * 128:(jj + 1) * 128], self.cur_xb[:, kc, 0:512], start=(kc == 0), stop=(kc == 7))
                        hbuf = self.ffh[self._ffh_i % 3]; self._ffh_i += 1
                        if tb == 0:
                            P.memset(hbuf[:, 0:2], 0.0, eng="dve")
                        else:
                            P.copy(hbuf[:, 0:2], self.ff["halo"][:, hs, :], eng="dve")
                        P.act(hbuf[:, 2:514], ps[:, :], AF.Identity, bias=colv(o_ub + ch))
                        P.copy(self.ff["halo"][:, hs, :], hbuf[:, 512:514], eng="dve")
                        P.act(hb[:, :], hbuf[:, 0:512], AF.Identity, bias=colv(o_cb + ch), scale=colv(o_cw + ch))
                        P.stt(hb[:, :], hbuf[:, 1:513], colv(o_cw + 44 + ch), hb[:, :], ALU.mult, ALU.add)
                        P.stt(hb[:, :], hbuf[:, 2:514], colv(o_cw + 88 + ch), hb[:, :], ALU.mult, ALU.add)
                        res.append(hb)
                    P.act(res[0][:, :], res[0][:, :], AF.Gelu)
                    P.tt(V(prap[:, jj, :], (prk,)), res[0][:, :], res[1][:, :], ALU.mult)
                for o in range(8):
                    def dn(o=o, nj=nj, wd=wd, prap=prap, prk=prk, sl=sl, pg=pg):
                        ps = self.nextps()
                        for jj in range(nj):
                            P.mm(ps[:, :], wd[:, jj, o * 128:(o + 1) * 128], V(prap[:, jj, :], (prk,)), start=(jj == 0), stop=(jj == nj - 1))
                        xv = self.xT[:, o, sl]
                        if pg == 0:
                            P.stt(xv, xv, ALPHA, ps[:, :], ALU.mult, ALU.add)
                        else:
                            P.tt(xv, xv, ps[:, :], ALU.add)
                    pend_dn.append(dn)
            for dn in pend_dn:
                dn()
            del pend_dn[:]

    def build(self):
        P = self.P
        with ExitStack() as st:
            P.enter(st)
            self.decl()
            self.alloc()
            P.dma(self.consts[:], self.hv(self.h_consts, "h_consts"))
            P.dma(self.colp[:], self.hv(self.h_colp[0], "h_colp"))
            xsrc = self.h_xT.rearrange("(c p) t -> p c t", p=128)
            for tb in range(4):
                sl = slice(tb * 512, (tb + 1) * 512)
                P.dma(self.xT[:, :, sl], self.hv(xsrc[:, :, sl], "h_xT"))
                P.copy(self.xb[:, :, sl], self.xT[:, :, sl], eng=("act" if tb % 2 else "dve"))
            P.copy(self.ident_bf[:], self.C("ident"))
            P.ts(self.ones_s[:], self.C("ones"), 1.0 / 1024, ALU.mult)
            P.copy(self.blk64_bf[:], self.C("blk64"))
            if "ca" in self.stages:
                self.mem_ln()
            for l in range(self.nlayers):
                self.layer(l)
            osrc = self.h_outT.rearrange("(c p) t -> p c t", p=128)
            for tb in range(4):
                sl = slice(tb * 512, (tb + 1) * 512)
                P.dma(self.hv(osrc[:, :, sl], "h_outT"), self.xT[:, :, sl])
            P.finalize()
            P.emit(st)
        return self.nc

    def layer(self, l):
        P = self.P
        if l > 0:
            P.dma(self.colp[:], self.hv(self.h_colp[l], "h_colp"))
        self._first_acc = True
        if l > 0 and "ffn" in self.stages:
            self.P.add("pool", lambda e: e.memset(self._dummy.h[:, :], 0.0), [V(None, ("ffh0", "ffh1", "ffh2"))],
                       [V(None, ("yT", self._dummy.name))])
        for m, (name, fn) in enumerate([("sg", self.mixer_sg), ("rw", self.mixer_rw), ("gla", self.mixer_gla), ("fox", self.mixer_fox)]):
            if name in self.stages and fn is not None:
                fn(l)
                if name == "rw":
                    continue
                self.debug_dump("y_%s%d" % (name, l), self.yT[:], [128, 2, S])
                if "ln1" in self.stages:
                    self.acc_out(self.h_w_out[l][m * 256:(m + 1) * 256, :], self.yT, 2, "h_w_out", self._first_acc)
                    self._first_acc = False
        if "ln1" in self.stages:
            self.layernorm("ln1_g%d" % l, "ln1_b%d" % l)
            self.debug_dump("x1_%d" % l, self.xT[:], [128, 8, S])
        if "ca" in self.stages:
            self.cross_attn(l)
            self.layernorm("ln2_g%d" % l, "ln2_b%d" % l)
            self.debug_dump("x2_%d" % l, self.xT[:], [128, 8, S])
        if "ffn" in self.stages:
            self.conv_ffn(l)
            self.layernorm("ln3_g%d" % l, "ln3_b%d" % l)


_CACHE = {}


def kernel(**inputs):
    inp = {k: np.ascontiguousarray(np.asarray(v, dtype=np.float32)) for k, v in inputs.items()}
    n = 8
    nc = bass.Bass("TRN2", target_bir_lowering=False)
    mk = MK(nc)
    mk.build()
    consts = make_consts()
    colp = make_colp(inp)
    rowp = make_rowp(inp)
    sgwT = np.ascontiguousarray(inp["sg_w"].transpose(0, 3, 1, 2))
    shared = dict(consts=consts, colp=colp, rowp=rowp, sgwT=sgwT,
                  w_in=inp["w_in"], w_out=inp["w_out"], gla_a_up=inp["gla_a_up"],
                  rw_w2=inp["rw_w2"], rw_a2=inp["rw_a2"], rw_g2=inp["rw_g2"],
                  ca_wq=inp["ca_wq"], ca_wk=inp["ca_wk"], ca_wv=inp["ca_wv"], ca_wo=inp["ca_wo"],
                  ffn_up=inp["ffn_up"], ffn_down=inp["ffn_down"])
    maps = []
    for b in range(n):
        m = dict(shared)
        m["xT"] = np.ascontiguousarray(inp["x"][b].T)
        m["memT"] = np.ascontiguousarray(inp["mem"][b].T)
        maps.append(m)
    res = run_bass_kernel_spmd(nc, maps, core_ids=list(range(n)))
    out = np.stack([np.asarray(res.results[b]["outT"]).T for b in range(n)]).astype(np.float32)
    return np.ascontiguousarray(out)
```

```python
from contextlib import ExitStack
from concourse.bass_utils import run_bass_kernel_spmd
import numpy as np
import concourse.bass as bass
import concourse.mybir as mybir

F32 = mybir.dt.float32
BF16 = mybir.dt.bfloat16
AF = mybir.ActivationFunctionType
ALU = mybir.AluOpType
AX = mybir.AxisListType

ENGS = ("pe", "act", "dve", "pool", "sp")


class V:
    __slots__ = ("ap", "keys")

    def __init__(self, ap, keys):
        self.ap = ap
        self.keys = keys


class T:
    def __init__(self, handle, name, shape):
        self.h = handle
        self.name = name
        self.shape = shape

    def __getitem__(self, idx):
        return V(self.h[idx], (self.name,))

    def k(self, sub):
        return _TK(self, sub)


class _TK:
    def __init__(self, t, sub):
        self.t = t
        self.sub = sub

    def __getitem__(self, idx):
        return V(self.t.h[idx], ((self.t.name, self.sub),))


class Op:
    __slots__ = ("eng", "fn", "reads", "writes", "idx", "deps", "signal", "sigidx",
                 "is_dma", "dslot", "dcnt", "snap", "xreads")

    def __init__(self, eng, fn, reads, writes, is_dma=False):
        self.eng = eng
        self.fn = fn
        self.reads = reads
        self.writes = writes
        self.is_dma = is_dma
        self.deps = []
        self.signal = False
        self.sigidx = 0
        self.dslot = -1
        self.dcnt = 0
        self.snap = None


class Prog:
    N_DSEM = 24

    def __init__(self, nc):
        self.nc = nc
        self.ops = []
        self._stack = None
        self.ntile = 0
        self.psum_names = set()

    def enter(self, stack):
        self._stack = stack

    def sb(self, name, shape, dt=F32):
        h = self._stack.enter_context(self.nc.sbuf_tensor("s_" + name, list(shape), dt))
        return T(h, name, shape)

    def ps(self, name, shape, dt=F32):
        h = self._stack.enter_context(self.nc.psum_tensor("p_" + name, list(shape), dt))
        self.psum_names.add(name)
        return T(h, name, shape)

    def _keys(self, vs):
        ks = []
        for v in vs:
            if v is None or isinstance(v, (int, float)):
                continue
            ks.extend(v.keys)
        return ks

    def add(self, eng, fn, reads, writes, is_dma=False):
        op = Op(eng, fn, self._keys(reads), self._keys(writes), is_dma)
        op.xreads = [k for k in op.reads if (k if isinstance(k, str) else k[0]) in self.psum_names and k not in op.writes]
        self.ops.append(op)
        return op

    def dma(self, out, in_, eng="sp", **kw):
        def fn(e, out=out, in_=in_):
            return e.dma_start(out=out.ap, in_=in_.ap, **kw)
        return self.add(eng, fn, [in_], [out], is_dma=True)

    def mm(self, out, lhsT, rhs, start=True, stop=True, **kw):
        def fn(e):
            return e.matmul(out.ap, lhsT.ap, rhs.ap, start=start, stop=stop, **kw)
        return self.add("pe", fn, [lhsT, rhs], [out])

    def transpose(self, out, in_, ident):
        def fn(e):
            return e.transpose(out.ap, in_.ap, ident.ap)
        return self.add("pe", fn, [in_, ident], [out])

    def act(self, out, in_, func, bias=0.0, scale=1.0, eng="act", accum_out=None):
        def fn(e):
            kw = {}
            if accum_out is not None:
                kw["accum_out"] = accum_out.ap
            return e.activation(out.ap, in_.ap, func,
                                bias=(bias.ap if isinstance(bias, V) else bias),
                                scale=(scale.ap if isinstance(scale, V) else scale), **kw)
        return self.add("act", fn, [in_, bias, scale], [out, accum_out])

    def tt(self, out, in0, in1, op, eng="dve"):
        def fn(e):
            return e.tensor_tensor(out.ap, in0.ap, in1.ap, op)
        return self.add(eng, fn, [in0, in1], [out])

    def ts(self, out, in0, s1, op0, s2=None, op1=None, eng="dve", accum_out=None):
        def fn(e):
            a1 = s1.ap if isinstance(s1, V) else s1
            a2 = s2.ap if isinstance(s2, V) else s2
            kw = {}
            if accum_out is not None:
                kw["accum_out"] = accum_out.ap
            if op1 is None:
                return e.tensor_scalar(out.ap, in0.ap, a1, None, op0, **kw)
            return e.tensor_scalar(out.ap, in0.ap, a1, a2, op0, op1, **kw)
        return self.add(eng, fn, [in0, s1, s2], [out, accum_out])

    def stt(self, out, in0, scalar, in1, op0, op1, eng="dve"):
        def fn(e):
            s = scalar.ap if isinstance(scalar, V) else scalar
            return e.scalar_tensor_tensor(out.ap, in0.ap, s, in1.ap, op0, op1)
        return self.add(eng, fn, [in0, scalar, in1], [out])

    def copy(self, out, in_, eng="dve"):
        if eng == "act":
            def fn(e):
                return e.copy(out.ap, in_.ap)
        else:
            def fn(e):
                return e.tensor_copy(out.ap, in_.ap)
        return self.add(eng, fn, [in_], [out])

    def memset(self, out, val, eng="dve"):
        def fn(e):
            return e.memset(out.ap, val)
        return self.add(eng, fn, [], [out])

    def reduce(self, out, in_, op, axis=AX.X, eng="dve"):
        def fn(e):
            return e.tensor_reduce(out.ap, in_.ap, axis, op)
        return self.add(eng, fn, [in_], [out])

    def recip(self, out, in_):
        def fn(e):
            return e.reciprocal(out.ap, in_.ap)
        return self.add("dve", fn, [in_], [out])

    def generic(self, eng, fn, reads, writes):
        return self.add(eng, fn, reads, writes)

    def finalize(self, out_keys=()):
        nc = self.nc
        ops = self.ops
        last_w = {}
        readers = {}
        for i, op in enumerate(ops):
            op.idx = i
            deps = set()
            for k in op.reads:
                w = last_w.get(k)
                if w is not None:
                    deps.add(w)
            for k in list(op.writes) + op.xreads:
                w = last_w.get(k)
                if w is not None:
                    deps.add(w)
                latest = {}
                for r in readers.get(k, ()):
                    ro = ops[r]
                    if ro.is_dma:
                        deps.add(r)
                    else:
                        latest[ro.eng] = r
                for r in latest.values():
                    deps.add(r)
            deps.discard(i)
            op.deps = sorted(deps)
            for k in op.reads:
                lst = readers.setdefault(k, [])
                if not op.is_dma:
                    lst[:] = [r for r in lst if ops[r].is_dma or ops[r].eng != op.eng]
                lst.append(i)
            for k in op.writes:
                last_w[k] = i
                readers[k] = []
            for k in op.xreads:
                readers[k] = [i]
        for op in ops:
            need = []
            for d in op.deps:
                p = ops[d]
                if p.is_dma:
                    need.append(d)
                    continue
                if p.eng == op.eng:
                    if op.is_dma:
                        need.append(d)
                        continue
                    if op.eng == "pe":
                        continue
                    raw = any(k in p.writes for k in op.reads)
                    if raw:
                        need.append(d)
                    continue
                need.append(d)
            op.deps = need
            for d in need:
                if not ops[d].is_dma:
                    ops[d].signal = True
        cnt = {e: 0 for e in ENGS}
        for op in ops:
            if op.is_dma:
                continue
            if op.signal:
                cnt[op.eng] += 1
                op.sigidx = cnt[op.eng]
        dcount = [0] * self.N_DSEM
        nd = 0
        for op in ops:
            if op.is_dma:
                op.dslot = nd % self.N_DSEM
                dcount[op.dslot] += 1
                op.dcnt = dcount[op.dslot]
                nd += 1
        self.n_dma = nd
        self.sig_counts = cnt
        return self

    def emit(self, stack):
        nc = self.nc
        ops = self.ops
        sems = {e: stack.enter_context(nc.semaphore("S_" + e)) for e in ENGS if e != "sp"}
        dsems = [stack.enter_context(nc.semaphore("D%d" % i)) for i in range(self.N_DSEM)]
        block = stack.enter_context(nc.Block())
        seen = {e: {x: 0 for x in ENGS} for e in ENGS}
        seen_d = {e: [0] * self.N_DSEM for e in ENGS}
        plan = {e: [] for e in ENGS}
        last_dma_on_slot = [None] * self.N_DSEM
        for op in ops:
            e = op.eng
            waits = []
            if op.is_dma:
                prev = last_dma_on_slot[op.dslot]
                if prev is not None and seen_d[e][op.dslot] < prev.dcnt * 16:
                    waits.append((dsems[op.dslot], prev.dcnt * 16))
                    seen_d[e][op.dslot] = prev.dcnt * 16
                last_dma_on_slot[op.dslot] = op
            for d in op.deps:
                p = ops[d]
                if p.is_dma:
                    v = p.dcnt * 16
                    if seen_d[e][p.dslot] < v:
                        waits.append((dsems[p.dslot], v))
                        seen_d[e][p.dslot] = v
                else:
                    v = p.sigidx
                    if seen[e][p.eng] < v:
                        waits.append((sems[p.eng], v))
                        seen[e][p.eng] = v
                        for x in ENGS:
                            if p.snap[x] > seen[e][x]:
                                seen[e][x] = p.snap[x]
            if not op.is_dma:
                snap = dict(seen[e])
                if op.signal:
                    snap[e] = max(snap[e], op.sigidx)
                op.snap = snap
            plan[e].append((waits, op))
        final_waits = []
        for s in range(self.N_DSEM):
            lp = last_dma_on_slot[s]
            if lp is not None:
                final_waits.append((dsems[s], lp.dcnt * 16))

        def run(engname, e):
            for waits, op in plan[engname]:
                for (s, v) in waits:
                    e.wait_ge(s, v)
                ins = op.fn(e)
                if op.is_dma:
                    ins.then_inc(dsems[op.dslot], 16)
                elif op.signal:
                    ins.then_inc(sems[op.eng], 1)

        @block.tensor
        def _(e):
            run("pe", e)

        @block.scalar
        def _(e):
            run("act", e)

        @block.vector
        def _(e):
            run("dve", e)

        @block.gpsimd
        def _(e):
            run("pool", e)

        @block.sync
        def _(e):
            run("sp", e)
            for (s, v) in final_waits:
                e.wait_ge(s, v)
        self.stats = {e: len(plan[e]) for e in ENGS}
        self.nwaits = {e: sum(len(w) for w, _ in plan[e]) for e in ENGS}


S = 2048
D = 1024
L = 4
NMEM = 256
DFF = 2816
ALPHA = (2.0 * L) ** 0.25
LN_EPS = 1e-5
NEG = -30000.0

CONST_LAYOUT = {}
_off = 0
for _n, _w in [("ident", 128), ("tri_le", 128), ("tri_lt", 64), ("negmask", 128), ("selneg", 4 * 128),
               ("ones", 128), ("sel_even", 64), ("sel_odd", 128), ("tri_gt", 64), ("blk64", 128), ("mask2", 128)]:
    CONST_LAYOUT[_n] = (_off, _w)
    _off += _w
NCONST = _off


def make_consts():
    c = np.zeros((128, NCONST), np.float32)
    def put(n, a):
        o, w = CONST_LAYOUT[n]
        c[:a.shape[0], o:o + w] = a
    i = np.arange(128)
    put("ident", np.eye(128, dtype=np.float32))
    put("tri_le", (i[:, None] <= i[None, :]).astype(np.float32))
    put("tri_lt", (i[:, None] < i[None, :]).astype(np.float32)[:, :64])
    put("tri_gt", (i[:, None] > i[None, :]).astype(np.float32)[:, :64])
    put("negmask", np.where(i[:, None] <= i[None, :], 0.0, NEG).astype(np.float32))
    sn = np.zeros((12, 4 * 128), np.float32)
    for h in range(4):
        for j in range(3):
            sn[j * 4 + h, h * 128:(h + 1) * 128] = -1.0
    put("selneg", sn)
    put("ones", np.ones((128, 128), np.float32))
    se = np.zeros((65, 64), np.float32); se[64, :] = 1.0
    put("sel_even", se)
    so = np.zeros((128, 128), np.float32); so[0, 64:128] = 1.0
    put("sel_odd", so)
    put("blk64", ((i[:, None] // 64) == (i[None, :] // 64)).astype(np.float32) / 64.0)
    put("mask2", ((i[:, None] <= i[None, :]) & ((i[:, None] // 64) == (i[None, :] // 64))).astype(np.float32))
    return c


def col_layout():
    lay = {}
    off = 0
    def add(n, w):
        nonlocal off
        lay[n] = (off, w)
        off += w
    add("mem_ln_g", 8); add("mem_ln_b", 8)
    for n in ("ln1_g", "ln1_b", "ln2_g", "ln2_b", "ln3_g", "ln3_b"):
        add(n, 8)
    add("ffn_up_b", 44); add("ffn_conv_b", 44); add("ffn_conv", 132)
    add("fox_fb", 1)
    add("gla_a_b", 2)
    add("gla_norm_g", 2)
    add("rw_mu", 8)
    add("rw_mu64_", 16)
    add("rw_h64_", 4 * 9)
    for n in list(lay.keys()):
        for l in range(L):
            lay["%s%d" % (n, l)] = lay[n]
    return lay, off


COL_LAYOUT, NCOL = col_layout()


def chunkcols(v, p=128):
    return np.ascontiguousarray(v.reshape(-1, p).T)


def make_colp(inp):
    call = np.zeros((L, 128, NCOL), np.float32)
    for l in range(L):
        c = call[l]
        def put(n, a):
            o, w = COL_LAYOUT[n]
            assert a.shape[1] == w, (n, a.shape, w)
            c[:a.shape[0], o:o + w] = a
        put("mem_ln_g", chunkcols(inp["mem_ln_g"])); put("mem_ln_b", chunkcols(inp["mem_ln_b"]))
        for n in ("ln1_g", "ln1_b", "ln2_g", "ln2_b", "ln3_g", "ln3_b"):
            put(n, chunkcols(inp[n][l]))
        put("ffn_up_b", chunkcols(inp["ffn_up_b"][l]))
        put("ffn_conv_b", chunkcols(inp["ffn_conv_b"][l]))
        put("ffn_conv", np.concatenate([chunkcols(inp["ffn_conv"][l][j]) for j in range(3)], 1))
        put("fox_fb", inp["fox_fb"][l].reshape(4, 1))
        put("gla_a_b", chunkcols(inp["gla_a_b"][l], 64))
        put("gla_norm_g", chunkcols(inp["gla_norm_g"][l]))
        put("rw_mu", chunkcols(inp["rw_mu"][l]))
        put("rw_mu64_", chunkcols(inp["rw_mu"][l], 64))
        h64 = np.zeros((64, 36), np.float32)
        for h in range(4):
            sl = slice(h * 64, (h + 1) * 64)
            for j, nm in enumerate(["rw_w0", "rw_a0", "rw_kk", "rw_ka", None, "rw_lnx_g", "rw_lnx_b"]):
                if nm is not None:
                    h64[:, h * 9 + j] = inp[nm][l][sl]
            h64[:, h * 9 + 4] = inp["rw_rk"][l][h]
        put("rw_h64_", h64)
    return call


ROW_LAYOUT = {}
_off = 0
for _n, _w in [("sg_ln_g", 256), ("sg_ln_b", 256), ("sgb", 256)]:
    ROW_LAYOUT[_n] = (_off, _w)
    _off += _w
NROW = _off


def make_rowp(inp):
    r = np.zeros((L, 128, NROW), np.float32)
    for l in range(L):
        def put(n, a):
            o, w = ROW_LAYOUT[n]
            r[l, :, o:o + w] = a
        put("sg_ln_g", np.tile(inp["sg_ln_g"][l][None], (128, 1)))
        put("sg_ln_b", np.tile(inp["sg_ln_b"][l][None], (128, 1)))
        sgb = np.zeros((128, 2, 128), np.float32)
        for p in range(128):
            for pair in range(2):
                sgb[p, pair] = inp["sg_b"][l][pair * 2 + p // 64]
        put("sgb", sgb.reshape(128, 256))
    return r


class WV:
    def __init__(self, tile, kc, cols):
        self.tile = tile
        self.kc = kc
        self.cols = cols
        self.ap = tile.h[:, 0:kc * cols].rearrange("p (k c) -> p k c", k=kc)

    def __getitem__(self, idx):
        return V(self.ap[idx], (self.tile.name,))


class MK:
    def __init__(self, nc, nlayers=L, stages=("sg", "fox", "gla", "rw", "ln1", "ca", "ln2", "ffn", "ln3"), dbg=()):
        self.nc = nc
        self.nlayers = nlayers
        self.stages = stages
        self.dbg = dbg
        self.P = Prog(nc)
        self.dbg_out = {}

    def decl(self):
        nc = self.nc
        di = lambda n, shp: nc.dram_tensor(n, list(shp), F32, kind="ExternalInput").ap()
        self.h_xT = di("xT", [D, S])
        self.h_memT = di("memT", [D, NMEM])
        self.h_consts = di("consts", [128, NCONST])
        self.h_colp = di("colp", [L, 128, NCOL])
        self.h_rowp = di("rowp", [L, 128, NROW])
        self.h_w_in = di("w_in", [L, D, 3092])
        self.h_w_out = di("w_out", [L, D, D])
        self.h_sgwT = di("sgwT", [L, 128, 4, 128])
        self.h_gla_a_up = di("gla_a_up", [L, 16, 128])
        self.h_rw_w2 = di("rw_w2", [L, 64, 256])
        self.h_rw_a2 = di("rw_a2", [L, 64, 256])
        self.h_rw_g2 = di("rw_g2", [L, 128, 256])
        self.h_ca_wq = di("ca_wq", [L, D, D]); self.h_ca_wk = di("ca_wk", [L, D, D])
        self.h_ca_wv = di("ca_wv", [L, D, D]); self.h_ca_wo = di("ca_wo", [L, D, D])
        self.h_ffn_up = di("ffn_up", [L, D, 2 * DFF]); self.h_ffn_down = di("ffn_down", [L, DFF, D])
        self.h_outT = nc.dram_tensor("outT", [D, S], F32, kind="ExternalOutput").ap()

    def hv(self, ap, key):
        return V(ap, (key,))

    def alloc(self):
        P = self.P
        self.xT = P.sb("xT", [128, 8, S], F32)
        self.xb = P.sb("xb", [128, 8, S], BF16)
        self.cur_xb = None
        self.yT = P.sb("yT", [128, 2, S], BF16)
        self.consts = P.sb("consts", [128, NCONST], F32)
        self.colp = P.sb("colp", [128, NCOL], F32)
        self.ident_bf = P.sb("ident_bf", [128, 128], BF16)
        self.ones_s = P.sb("ones_s", [128, 128], BF16)
        self.blk64_bf = P.sb("blk64_bf", [128, 128], BF16)
        self.wb = [P.sb("wb%d" % i, [128, 4096], BF16) for i in range(2)]
        self.pb = [P.ps("pb%d" % i, [128, 512], F32) for i in range(8)]
        self.t512 = [P.sb("t512_%d" % i, [128, 512], F32) for i in range(5)]
        self.b512 = [P.sb("b512_%d" % i, [128, 512], BF16) for i in range(4)]
        self.st512 = [P.sb("st512_%d" % i, [128, 512], F32) for i in range(2)]
        self.scr = P.sb("scr", [128, 20480], BF16)
        self._ffh_i = 0
        self._ps_i = 0
        self._wb_i = 0
        self._wo_i = 0
        self._t_i = 0
        self._b_i = 0

    def load_xb(self, tb, eng="pool"):
        xb = self.xb
        class _B:
            def __getitem__(s_, idx):
                p, k, c = idx
                return V(xb.h[p, k, slice(tb * 512 + c.start, tb * 512 + c.stop)], (xb.name,))
        self.cur_xb = _B()
        return self.cur_xb

    def sub256(self, t):
        class _S:
            name = t.name
            class _H:
                def __getitem__(s2, idx):
                    p, c = idx
                    c = slice(c.start or 0, 256 if c.stop is None else c.stop)
                    return t.h[p, c]
            h = _H()
            def __getitem__(s_, idx):
                if not isinstance(idx, tuple):
                    idx = (idx, slice(0, 256))
                p, c = idx
                c = slice(c.start or 0, 256 if c.stop is None else c.stop)
                return V(t.h[p, c], (t.name,))
        return _S()

    def nextps(self):
        p = self.pb[self._ps_i % 6]
        self._ps_i += 1
        return p

    def nextacc(self):
        self._acc_i = getattr(self, "_acc_i", 0) + 1
        return self.pb[6 + self._acc_i % 2]

    def nt(self):
        t = self.t512[self._t_i % 5]
        self._t_i += 1
        return t

    def nf(self):
        return self.nt()

    def nb(self):
        t = self.b512[self._b_i % 4]
        self._b_i += 1
        return t

    def C(self, name, rows=128, c0=0, c1=None):
        o, w = CONST_LAYOUT[name]
        if c1 is None:
            c1 = w
        return self.consts[0:rows, o + c0:o + c1]

    def col(self, name, j, rows=128):
        o, w = COL_LAYOUT[name]
        return self.colp[0:rows, o + j:o + j + 1]

    def row(self, name, c0=0, c1=None):
        o, w = ROW_LAYOUT[name]
        if c1 is None:
            c1 = w
        return V(self.scr.h[:, :].bitcast(F32)[:, 2560 + o + c0:2560 + o + c1], ("sg_rowp",))

    def load_w(self, hbm_ap, kc, cols, key, ring="wb"):
        P = self.P
        t = self.wb[self._wb_i % 2]; self._wb_i += 1
        wv = WV(t, kc, cols)
        src = hbm_ap.rearrange("(k p) c -> p k c", p=128)
        step = max(1, (2048 // cols) if cols <= 2048 else 1)
        k = 0
        while k < kc:
            k2 = min(kc, k + step)
            P.dma(V(wv.ap[:, k:k2, :], (t.name,)), V(src[:, k:k2, :], (key,)), eng="pool")
            k = k2
        return wv

    def scr_phase(self, new_keys):
        old = getattr(self, "_scr_keys", [])
        if not hasattr(self, "_dummy"):
            self._dummy = self.P.sb("phase_dummy", [128, 8], F32)
        d = self._dummy
        self.P.add("pool", lambda e: e.memset(d.h[:, :], 0.0), [V(None, tuple(old))], [V(None, tuple(new_keys) + (d.name,))])
        self._scr_keys = list(new_keys)

    def debug_dump(self, name, view, shape):
        if name not in self.dbg:
            return
        h = self.nc.dram_tensor("dbg_" + name, list(shape), view.ap.dtype, kind="ExternalOutput").ap()
        self.P.dma(V(h, ("dbg_" + name,)), view)
        self.dbg_out[name] = shape

    def proj_fm(self, w, c0, M, tb, evac, xsrc=None, ncol=512):
        P = self.P
        ps = self.nextps()
        for kc in range(w.kc):
            if xsrc is None:
                rhs = self.cur_xb[:, kc, 0:ncol]
            else:
                rhs = xsrc[:, kc, tb * ncol:(tb + 1) * ncol]
            P.mm(ps[0:M, 0:ncol], w[:, kc, c0:c0 + M], rhs,
                 start=(kc == 0), stop=(kc == w.kc - 1))
        evac(ps)

    def proj_tm(self, w, c0, N, t0, evac, ntok=128):
        P = self.P
        ps = self.nextps()
        for kc in range(w.kc):
            P.mm(ps[0:ntok, 0:N], self.cur_xb[:, kc, (t0 % 512):(t0 % 512) + ntok], w[:, kc, c0:c0 + N],
                 start=(kc == 0), stop=(kc == w.kc - 1))
        evac(ps)

    def acc_out(self, hbm_w, yT, nck, key, first):
        P = self.P
        w = self.load_w(hbm_w, nck, 1024, key, ring="wb")
        for tb in range(4):
            for o in range(8):
                ps = self.nextps()
                for c in range(nck):
                    P.mm(ps[:, :], w[:, c, o * 128:(o + 1) * 128], yT[:, c, tb * 512:(tb + 1) * 512],
                         start=(c == 0), stop=(c == nck - 1))
                xv = self.xT[:, o, tb * 512:(tb + 1) * 512]
                if first:
                    P.stt(xv, xv, ALPHA, ps[:, :], ALU.mult, ALU.add)
                else:
                    P.tt(xv, xv, ps[:, :], ALU.add)

    def layernorm(self, gname, bname, src=None, dst32=None, dstb=None, ntok=S, eps=LN_EPS):
        P = self.P
        src = self.xT if src is None else src
        dst32 = self.xT if dst32 is None else (None if dst32 is False else dst32)
        dstb = self.xb if dstb is None else dstb
        nblk = (ntok + 511) // 512
        for tb in range(nblk):
            w = min(512, ntok - tb * 512)
            sl = slice(tb * 512, tb * 512 + w)
            psm = self.nextps(); psq = self.nextps()
            for c in range(8):
                xb_ = self.nb(); sq = self.nb()
                P.copy(xb_[:, 0:w], src[:, c, sl], eng="dve")
                P.act(sq[:, 0:w], src[:, c, sl], AF.Square)
                P.mm(psm[:, 0:w], self.ones_s[:, :], xb_[:, 0:w], start=(c == 0), stop=(c == 7))
                P.mm(psq[:, 0:w], self.ones_s[:, :], sq[:, 0:w], start=(c == 0), stop=(c == 7))
            mean = self.st512[0]; rstd = self.st512[1]
            P.copy(mean[:, 0:w], psm[:, 0:w])
            msq = self.nt()
            P.tt(msq[:, 0:w], mean[:, 0:w], mean[:, 0:w], ALU.mult)
            P.tt(msq[:, 0:w], psq[:, 0:w], msq[:, 0:w], ALU.subtract)
            P.act(msq[:, 0:w], msq[:, 0:w], AF.Ln, bias=eps)
            P.act(rstd[:, 0:w], msq[:, 0:w], AF.Exp, scale=-0.5)
            for c in range(8):
                u = self.nt()
                P.tt(u[:, 0:w], src[:, c, sl], mean[:, 0:w], ALU.subtract)
                P.stt(u[:, 0:w], u[:, 0:w], self.col(gname, c), rstd[:, 0:w], ALU.mult, ALU.mult)
                if dst32 is not None:
                    P.act(dst32[:, c, sl], u[:, 0:w], AF.Identity, bias=self.col(bname, c))
                if dstb is not None:
                    P.act(dstb[:, c, sl], u[:, 0:w], AF.Identity, bias=self.col(bname, c))

    def mixer_sg(self, l):
        P = self.P
        w = self.load_w(self.h_w_in[l][:, 0:512], 8, 512, "h_w_in")
        if not hasattr(self, "sg_t"):
            self.sg_t = dict(
                wmT=P.sb("sg_wmT", [128, 4, 128], BF16),
                stat=[P.sb("sg_stat%d" % i, [128, 16], F32) for i in range(2)],
                vnp=[P.sb("sg_vnp%d" % i, [128, 2, 2, 128], BF16) for i in range(2)],
            )
            for v_ in self.sg_t["vnp"]:
                P.memset(v_[:], 0.0, eng="pool")
        self.scr_phase(["sg_u0", "sg_u1", "sg_wraw", "sg_rowp"])
        P.dma(V(self.scr.h[:, :].bitcast(F32)[:, 2560:2560 + NROW], ("sg_rowp",)), self.hv(self.h_rowp[l], "h_rowp"))
        scr32 = self.scr.h[:, :].bitcast(F32)
        class _C:
            def __init__(s_, ap, key):
                s_.ap = ap; s_.key = key
            def __getitem__(s_, idx):
                return V(s_.ap[idx], (s_.key,))
        usb = [_C(scr32[:, i * 1024:(i + 1) * 1024].rearrange("p (c t) -> p c t", c=2), "sg_u%d" % i) for i in range(2)]
        wraw = _C(scr32[:, 2048:2560].rearrange("p (h t) -> p h t", h=4), "sg_wraw")
        T_ = self.sg_t
        P.dma(wraw[:], self.hv(self.h_sgwT[l], "h_sgwT"))
        tri = V(self.C("tri_le").ap.unsqueeze(1).to_broadcast([128, 4, 128]), self.consts[:].keys)
        P.tt(T_["wmT"][:], wraw[:], tri, ALU.mult)
        for tb in range(4):
            self.load_xb(tb)
            u_sb = usb[tb % 2]
            for c in range(2):
                self.proj_fm(w, c * 128, 128, tb, lambda ps, c=c: P.copy(u_sb[:, c, :], ps[:, :], eng="act"))
            for n4 in range(4):
                n = tb * 4 + n4
                vs = self.sub256(self.nf()); sq = self.sub256(self.nf()); stat = T_["stat"][n % 2]; vnp = T_["vnp"][n % 2]
                def ev(ps):
                    P.copy(vs[:], ps[:, 0:256], eng="act")
                    P.act(sq[:], ps[:, 0:256], AF.Square)
                self.proj_tm(w, 256, 256, n * 128, ev)
                v3 = V(vs.h[:, :].rearrange("p (h d) -> p h d", h=4), vs[:].keys)
                q3 = V(sq.h[:, :].rearrange("p (h d) -> p h d", h=4), sq[:].keys)
                P.reduce(stat[:, 0:4], v3, ALU.add)
                P.reduce(stat[:, 4:8], q3, ALU.add)
                P.ts(stat[:, 0:4], stat[:, 0:4], 1.0 / 64, ALU.mult)
                P.tt(stat[:, 8:12], stat[:, 0:4], stat[:, 0:4], ALU.mult)
                P.stt(stat[:, 4:8], stat[:, 4:8], 1.0 / 64, stat[:, 8:12], ALU.mult, ALU.subtract)
                P.act(stat[:, 4:8], stat[:, 4:8], AF.Ln, bias=LN_EPS)
                P.act(stat[:, 12:16], stat[:, 4:8], AF.Exp, scale=-0.5)
                for h in range(4):
                    P.ts(sq[:, h * 64:(h + 1) * 64], vs[:, h * 64:(h + 1) * 64], stat[:, h:h + 1], ALU.subtract,
                         stat[:, 12 + h:13 + h], ALU.mult)
                P.tt(sq[:], sq[:], self.row("sg_ln_g"), ALU.mult)
                s4 = sq.h[:, :].rearrange("p (a b d) -> p a b d", a=2, b=2)
                rb = self.row("sg_ln_b").ap.rearrange("p (a b d) -> p a b d", a=2, b=2)
                for hh in range(2):
                    P.tt(V(vnp.h[:, :, hh, hh * 64:(hh + 1) * 64], vnp[:].keys), V(s4[:, :, hh, :], sq[:].keys),
                         V(rb[:, :, hh, :], ("sg_rowp",)), ALU.add)
                for pair in range(2):
                    ps = self.nextps()
                    for hh in range(2):
                        P.mm(ps[:, 0:128], V(vnp.h[:, pair, hh, :], vnp[:].keys), T_["wmT"][:, pair * 2 + hh, :],
                             start=(hh == 0), stop=(hh == 1))
                    t2 = self.nf()
                    P.tt(t2[:, 0:128], ps[:, 0:128], self.row("sgb", pair * 128, (pair + 1) * 128), ALU.add)
                    P.tt(self.yT[:, pair, n * 128:(n + 1) * 128], t2[:, 0:128], u_sb[:, pair, n4 * 128:(n4 + 1) * 128], ALU.mult)

    def mixer_fox(self, l):
        P = self.P
        w = self.load_w(self.h_w_in[l][:, 2320:2832], 8, 512, "h_w_in")
        w2 = self.load_w(self.h_w_in[l][:, 2832:3092], 8, 260, "h_w_in")
        if not hasattr(self, "fx"):
            self.fx = dict(
                negcT=P.sb("fx_negcT", [128, 16, 4], F32),
                nfb=P.sb("fx_nfb", [4, 1], F32),
            )
        F = self.fx
        scr = self.scr
        qT = V(scr.h[:, 0:4096].rearrange("p (c t) -> p c t", c=2), ("fx_qT",))
        kT = V(scr.h[:, 4096:8192].rearrange("p (c t) -> p c t", c=2), ("fx_kT",))
        v1ap = scr.h[:, 8192:16384].rearrange("p (n h m) -> p n h m", n=16, h=4)
        v1 = lambda idx: V(v1ap[idx], ("fx_v1",))
        self.scr_phase(["fx_qT", "fx_kT", "fx_v1", "fx_negc"])
        negc_ap = scr.h[:, 16384:20480].bitcast(F32)
        class _N:
            def __getitem__(s_, idx):
                return V(negc_ap[idx], ("fx_negc",))
        F["negc"] = _N(); F["nlf"] = F["negc"]
        P.memset(v1((slice(None),)), 0.0, eng="dve")
        for h in range(4):
            col = 64 if h % 2 == 0 else 0
            P.memset(v1((slice(None), slice(None), h, slice(col, col + 1))), 1.0, eng="dve")
        P.ts(F["nfb"][:], self.col("fox_fb%d" % l, 0, rows=4), -1.0, ALU.mult)
        for tb in range(4):
            self.load_xb(tb)
            sl = slice(tb * 512, (tb + 1) * 512)
            for c in range(2):
                self.proj_fm(w, c * 128, 128, tb,
                             lambda ps, c=c: P.act(V(qT.ap[:, c, sl], qT.keys), ps[:, :], AF.Copy, scale=0.125))
                self.proj_fm(w, 256 + c * 128, 128, tb,
                             lambda ps, c=c: P.copy(V(kT.ap[:, c, sl], kT.keys), ps[:, :], eng="dve"))
            def evf(ps):
                P.act(F["nlf"][0:4, sl], ps[0:4, :], AF.Exp, bias=F["nfb"][:, 0:1], scale=-1.0)
                P.act(F["nlf"][0:4, sl], F["nlf"][0:4, sl], AF.Ln, bias=1.0)
            self.proj_fm(w2, 256, 4, tb, evf)
            for n4 in range(4):
                n = tb * 4 + n4
                def evv(ps, n=n):
                    for h in range(4):
                        col = 0 if h % 2 == 0 else 64
                        P.copy(v1((slice(None), n, h, slice(col, col + 64))), ps[:, h * 64:(h + 1) * 64],
                               eng=("act" if h % 2 else "dve"))
                self.proj_tm(w2, 0, 256, n * 128, evv)
        ones_b = self.C("ones", rows=4, c0=0, c1=1).ap.to_broadcast([4, S])
        P.generic("dve", lambda e: e.tensor_tensor_scan(negc_ap[0:4, :], ones_b, negc_ap[0:4, :], 0.0,
                                                         ALU.mult, ALU.add),
                  [self.consts[:], F["negc"][0:4, :]], [F["negc"][0:4, :]])
        self.debug_dump("fox_negc", F["negc"][0:4, :], [4, S])
        for J in range(16):
            ps = self.nextps()
            P.transpose(ps[:, 0:4], F["negc"][0:4, J * 128:(J + 1) * 128], self.C("ident", rows=4, c0=0, c1=4))
            P.copy(F["negcT"][:, J, :], ps[:, 0:4])
        if "sel12" not in F:
            F["sel12"] = P.sb("fx_sel12", [12, 512], BF16)
            P.copy(F["sel12"][:], self.C("selneg", rows=12))
        cs_t = w2.tile
        cs_ap = cs_t.h[0:12, 0:S]
        cs3 = lambda r0, r1, c0, c1: V(cs_ap[r0:r1, c0:c1], (cs_t.name,))
        for tb in range(4):
            c0 = tb * 512; c1 = c0 + 512
            r1 = self.nt(); r2 = self.nt(); m_ = self.nb(); l_ = self.nb()
            P.copy(cs3(0, 4, c0, c1), F["negc"][0:4, c0:c1])
            P.tt(r1[0:4, :], F["negc"][0:4, c0:c1], cs3(0, 4, c0, c1), ALU.subtract)
            P.copy(m_[0:4, :], r1[0:4, :])
            P.tt(r2[0:4, :], r1[0:4, :], m_[0:4, :], ALU.subtract)
            P.copy(l_[0:4, :], r2[0:4, :])
            P.dma(cs3(4, 8, c0, c1), m_[0:4, :])
            P.dma(cs3(8, 12, c0, c1), l_[0:4, :])
        for h in range(4):
            c = h // 2
            p0 = (h % 2) * 64
            even = (h % 2 == 0)
            for Q in range(4):
                ops_ = self.nextacc()
                nJ = 4 * Q + 4
                def issue_lg(J):
                    c_lo = max(0, (J - 4 * Q) * 128)
                    lg = self.nextps()
                    P.mm(lg[:, c_lo:512], V(kT.ap[p0:p0 + 64, c, J * 128:(J + 1) * 128], kT.keys),
                         V(qT.ap[p0:p0 + 64, c, Q * 512 + c_lo:(Q + 1) * 512], qT.keys), start=True, stop=False)
                    P.mm(lg[:, c_lo:512], F["sel12"][:, h * 128:(h + 1) * 128],
                         cs3(0, 12, Q * 512 + c_lo, (Q + 1) * 512), start=False, stop=True)
                    if J >= 4 * Q:
                        P.tt(lg[:, c_lo:c_lo + 128], lg[:, c_lo:c_lo + 128], self.C("negmask"), ALU.add)
                    return lg, c_lo
                def finish(J, lg, c_lo):
                    pT = self.nb()
                    P.act(pT[:, c_lo:512], lg[:, c_lo:512], AF.Exp, bias=F["negcT"][:, J, h:h + 1])
                    M = 65 if even else 128
                    P.mm(ops_[0:M, c_lo:512], v1((slice(None), J, h, slice(0, M))), pT[:, c_lo:512],
                         start=(J == 0), stop=(J == nJ - 1))
                prev = issue_lg(0)
                for J in range(nJ):
                    nxt = issue_lg(J + 1) if J + 1 < nJ else None
                    finish(J, *prev)
                    prev = nxt
                osb = self.nt()
                if even:
                    P.copy(osb[0:65, :], ops_[0:65, :], eng="act")
                    P.recip(osb[64:65, :], osb[64:65, :])
                    bp = self.nextps()
                    P.mm(bp[0:64, :], self.C("sel_even", rows=65), osb[0:65, :])
                    P.tt(self.yT[0:64, c, Q * 512:(Q + 1) * 512], osb[0:64, :], bp[0:64, :], ALU.mult)
                else:
                    P.copy(osb[:, :], ops_[:, :], eng="act")
                    P.recip(osb[0:1, :], osb[0:1, :])
                    bp = self.nextps()
                    P.mm(bp[:, :], self.C("sel_odd"), osb[:, :])
                    P.tt(self.yT[64:128, c, Q * 512:(Q + 1) * 512], osb[64:128, :], bp[64:128, :], ALU.mult)


    def mixer_gla(self, l):
        P = self.P
        w1 = self.load_w(self.h_w_in[l][:, 1536:2048], 8, 512, "h_w_in")
        w2 = self.load_w(self.h_w_in[l][:, 2048:2320], 8, 272, "h_w_in")
        if not hasattr(self, "gl"):
            self.gl = dict(
                aup=P.sb("gl_aup", [16, 128], BF16),
                nab=P.sb("gl_nab", [64, 2], F32),
                gps=P.sb("gl_gps", [64, 32], F32),
                bl=P.sb("gl_bl", [64, 32], F32),
                dec=P.sb("gl_dec", [64, 2, 32], F32),
                S=[P.sb("gl_S%d" % g, [64, 128], F32) for g in range(2)],
                Sbf=[[P.sb("gl_Sbf%d_%d" % (g, i), [64, 128], BF16) for i in range(4)] for g in range(2)],
                osb=[P.sb("gl_osb%d" % i, [128, 128], F32) for i in range(2)],
            )
        G = self.gl
        scr = self.scr
        self.scr_phase(["gl_qe", "gl_ke", "gl_v", "gl_ketm", "gl_G", "gl_adT"])
        class _C:
            def __init__(s_, ap, key):
                s_.ap = ap; s_.key = key
            def __getitem__(s_, idx):
                return V(s_.ap[idx], (s_.key,))
        qe = _C(scr.h[:, 0:4096].rearrange("p (g t) -> p g t", g=2), "gl_qe")
        ke = _C(scr.h[:, 4096:8192].rearrange("p (g t) -> p g t", g=2), "gl_ke")
        v128 = _C(scr.h[:, 8192:12288].rearrange("p (b c) -> p b c", b=16), "gl_v")
        ketm = _C(scr.h[:, 12288:14336].rearrange("p (b c) -> p b c", b=16), "gl_ketm")
        Gt = _C(scr.h[:, 14336:18432].bitcast(F32), "gl_G")
        Gt3 = _C(scr.h[:, 14336:18432].bitcast(F32).rearrange("p (n c) -> p n c", n=32), "gl_G")
        adT = _C(scr.h[:, 18432:20480], "gl_adT")
        P.dma(G["aup"][:], self.hv(self.h_gla_a_up[l], "h_gla_a_up"), eng="pool")
        P.ts(G["nab"][:], V(self.colp.h[0:64, COL_LAYOUT["gla_a_b%d" % l][0]:COL_LAYOUT["gla_a_b%d" % l][0] + 2], self.colp[:].keys),
             -1.0, ALU.mult)
        for tb in range(4):
            self.load_xb(tb)
            sl = slice(tb * 512, (tb + 1) * 512)
            self.proj_fm(w2, 256, 16, tb, lambda ps: P.copy(adT[0:16, sl], ps[0:16, :], eng="act"))
            for c in range(2):
                def evg(ps, c=c):
                    t = self.nt()
                    P.act(t[:], ps[:, :], AF.Silu)
                    P.ts(self.yT[:, c, sl], t[:], self.col("gla_norm_g%d" % l, c), ALU.mult)
                self.proj_fm(w2, c * 128, 128, tb, evg)
            for b4 in range(4):
                bk = tb * 4 + b4
                self.proj_tm(w1, 256, 256, bk * 128, lambda ps, bk=bk: P.copy(v128[:, bk, :], ps[:, 0:256], eng="act"))
        import os
        stop = int(os.environ.get('GLA_STOP', '99'))
        if stop <= 1:
            return
        ones_b = self.C("ones", rows=64, c0=0, c1=1).ap.to_broadcast([64, S])
        for g in range(2):
            for tb in range(4):
                sl = slice(tb * 512, (tb + 1) * 512)
                ps = self.nextps()
                P.mm(ps[0:64, :], G["aup"][:, g * 64:(g + 1) * 64], adT[0:16, sl])
                P.act(Gt[0:64, sl], ps[0:64, :], AF.Exp, bias=G["nab"][:, g:g + 1], scale=-1.0)
                P.act(Gt[0:64, sl], Gt[0:64, sl], AF.Ln, bias=1.0)
            P.generic("dve", lambda e: e.tensor_tensor_scan(Gt.ap[0:64, :], ones_b, Gt.ap[0:64, :], 0.0, ALU.mult, ALU.add),
                      [self.consts[:], Gt[0:64, :]], [Gt[0:64, :]])
            if stop <= 2:
                continue
            P.memset(G["gps"][:, 0:1], 0.0)
            P.copy(G["gps"][:, 1:32], Gt3[0:64, 0:31, 63])
            P.tt(Gt3[0:64, :, :], Gt3[0:64, :, :], V(G["gps"].h[:, :].unsqueeze(2).to_broadcast([64, 32, 64]), G["gps"][:].keys),
                 ALU.subtract)
            P.copy(G["bl"][:], Gt3[0:64, :, 63])
            P.act(G["dec"][:, g, :], G["bl"][:], AF.Exp, scale=-1.0 / 16)
            for tb in range(4):
                sl = slice(tb * 512, (tb + 1) * 512)
                self.load_xb(tb)
                Eb = self.nt(); Enb = self.nt()
                P.act(Eb[0:64, :], Gt[0:64, sl], AF.Exp, scale=-1.0 / 16)
                P.act(Enb[0:64, :], Gt[0:64, sl], AF.Exp, scale=1.0 / 16)
                self.proj_fm(w1, g * 64, 64, tb,
                             lambda ps: P.stt(qe[0:64, g, sl], ps[0:64, :], 32.0 ** -0.5, Eb[0:64, :], ALU.mult, ALU.mult))
                self.proj_fm(w1, 128 + g * 64, 64, tb,
                             lambda ps: P.tt(ke[0:64, g, sl], ps[0:64, :], Enb[0:64, :], ALU.mult))
            if stop <= 3:
                continue
            for bk in range(16):
                ps = self.nextps()
                pbf = V(ps.h[:, :].bitcast(BF16), ps[:].keys)
                P.transpose(V(pbf.ap[:, 0:64], pbf.keys), ke[0:64, g, bk * 128:(bk + 1) * 128], self.ident_bf[0:64, 0:64])
                P.copy(ketm[:, bk, g * 64:(g + 1) * 64], V(pbf.ap[:, 0:64], pbf.keys), eng="act")
        self.debug_dump("gl_qe", qe[0:64, :, :], [64, 2, S])
        self.debug_dump("gl_ke", ke[0:64, :, :], [64, 2, S])
        if stop <= 4:
            return
        for g in range(2):
            P.memset(G["S"][g][:], 0.0)
            P.memset(G["Sbf"][g][0][:], 0.0)
        for bk in range(16):
            for g in range(2):
                attm = []
                for hh in range(2):
                    ps = self.nextps()
                    P.mm(ps[:, 0:128], ke[hh * 32:hh * 32 + 32, g, bk * 128:(bk + 1) * 128],
                         qe[hh * 32:hh * 32 + 32, g, bk * 128:(bk + 1) * 128])
                    am = self.nb()
                    P.tt(am[:, 0:128], ps[:, 0:128], self.C("mask2"), ALU.mult)
                    attm.append(am)
                if stop <= 5:
                    continue
                for cc in range(2):
                    n = 2 * bk + cc
                    if n == 31:
                        break
                    psU = self.nextps()
                    r0 = cc * 64
                    P.mm(psU[0:64, 0:128], ketm[r0:r0 + 64, bk, g * 64:(g + 1) * 64], v128[r0:r0 + 64, bk, g * 128:(g + 1) * 128])
                    P.tt(G["S"][g][:], G["S"][g][:], psU[0:64, 0:128], ALU.add)
                    P.ts(G["S"][g][:], G["S"][g][:], G["dec"][:, g, n:n + 1], ALU.mult)
                    P.copy(G["Sbf"][g][(n + 1) % 4][:], G["S"][g][:], eng="act")
                if stop <= 6:
                    continue
                psO = self.nextps()
                for hh in range(2):
                    c0 = hh * 128
                    P.mm(psO[:, c0:c0 + 128], v128[:, bk, g * 128:(g + 1) * 128], attm[hh][:, 0:128], start=True, stop=False)
                    for cc in range(2):
                        n = 2 * bk + cc
                        P.mm(psO[:, c0 + cc * 64:c0 + (cc + 1) * 64], G["Sbf"][g][n % 4][hh * 32:hh * 32 + 32, :],
                             qe[hh * 32:hh * 32 + 32, g, n * 64:(n + 1) * 64], start=False, stop=(cc == 1))
                if stop <= 7:
                    continue
                osb = G["osb"][g]
                osq = self.nb()
                for hh in range(2):
                    r0 = hh * 64
                    var = os.environ.get('GLA_VAR', 'ab')
                    if 'a' in var:
                        P.copy(osb[r0:r0 + 64, :], psO[r0:r0 + 64, hh * 128:(hh + 1) * 128], eng="dve")
                    if 'b' in var:
                        P.act(osq[r0:r0 + 64, 0:128], psO[r0:r0 + 64, hh * 128:(hh + 1) * 128], AF.Square)
                if stop <= 8:
                    continue
                pss = self.nextps()
                P.mm(pss[:, 0:128], self.blk64_bf[:, :], osq[:, 0:128])
                if stop <= 9:
                    continue
                rs = self.nf()
                P.act(rs[:, 0:128], pss[:, 0:128], AF.Ln, bias=1e-5)
                P.act(rs[:, 0:128], rs[:, 0:128], AF.Exp, scale=-0.5)
                if stop <= 10:
                    continue
                P.tt(osb[:, :], osb[:, :], rs[:, 0:128], ALU.mult)
                if stop <= 11:
                    continue
                yv = self.yT[:, g, bk * 128:(bk + 1) * 128]
                P.tt(yv, osb[:, :], yv, ALU.mult)


    def mixer_rw(self, l):
        P = self.P
        wA = self.load_w(self.h_w_in[l][:, 512:1024], 8, 512, "h_w_in")
        wB = self.load_w(self.h_w_in[l][:, 1024:1536], 8, 512, "h_w_in")
        if not hasattr(self, "rw"):
            self.rw = dict(
                sm=P.sb("rw_sm", [128, 512], BF16),
                omm=P.sb("rw_omm", [128, 24], F32),
                nwa=P.sb("rw_nwa", [64, 8], F32),
                carry=P.sb("rw_carry", [128, 8], F32),
                maskq=P.sb("rw_maskq", [64, 128], F32),
                PC=P.sb("rw_PC", [64, 32], F32),
                gs=P.sb("rw_gs", [64, 4], F32),
                S32=P.sb("rw_S32", [64, 64], F32),
                Sb=[P.sb("rw_Sb%d" % i, [64, 64], BF16) for i in range(2)],
                Nn=[P.sb("rw_Nn%d" % i, [64, 64], BF16) for i in range(2)],
                XN=[[P.sb("rw_XN%d_%d" % (i, j), [64, 128], BF16) for j in range(2)] for i in range(2)],
                Wt=[[P.sb("rw_Wt%d_%d" % (i, j), [64, 64], BF16) for j in range(2)] for i in range(2)],
                Wf=[P.sb("rw_Wf%d" % i, [64, 64], BF16) for i in range(4)],
                RU=[P.sb("rw_RU%d" % i, [64, 128], BF16) for i in range(2)],
            )
            P.copy(self.rw["maskq"][:, 0:64], self.C("tri_le", rows=64, c0=0, c1=64))
            P.copy(self.rw["maskq"][:, 64:128], self.C("tri_lt", rows=64, c0=0, c1=64))
        R = self.rw
        scr = self.scr
        self.scr_phase(["rw_twa", "rw_sgd", "rw_KB", "rw_RA", "rw_VT", "rw_bv"])
        class _C:
            def __init__(s_, ap, key):
                s_.ap = ap; s_.key = key
            def __getitem__(s_, idx):
                return V(s_.ap[idx], (s_.key,))
        twa = _C(scr.h[:, 0:2048], "rw_twa")
        sgd = _C(scr.h[:, 2048:4096], "rw_sgd")
        KB = _C(scr.h[:, 4096:8192].rearrange("p (n c) -> p n c", n=32), "rw_KB")
        RA = _C(scr.h[:, 8192:12288].rearrange("p (n c) -> p n c", n=32), "rw_RA")
        VT = _C(scr.h[:, 12288:14336], "rw_VT")
        bv = _C(scr.h[:, 14336:16384], "rw_bv")
        y1 = self.yT.h[0:64, 1, :]
        class _A:
            def __init__(s_, c0, w, key):
                s_.c0 = c0; s_.w = w; s_.key = key
            def __getitem__(s_, idx):
                p, c = idx
                c = slice(s_.c0 + (c.start or 0), s_.c0 + (s_.w if c.stop is None else c.stop))
                return V(y1[:, c], (s_.key,))
        RQA = [_A(i * 128, 128, "rwq_QA%d" % i) for i in range(4)]
        RQB = [_A(512 + i * 128, 128, "rwq_QB%d" % i) for i in range(4)]
        RTM = [_A(1024 + i * 192, 192, "rwq_TM%d" % i) for i in range(4)]
        ring_keys = tuple(x.key for x in RQA + RQB + RTM)
        P.add("pool", lambda e: e.memset(self._dummy.h[:, :], 0.0), [V(None, ("yT",))], [V(None, ring_keys + (self._dummy.name,))])
        P.dma(R["sm"][0:64, 0:256], self.hv(self.h_rw_w2[l], "h_rw_w2"), eng="pool")
        P.dma(R["sm"][64:128, 0:256], self.hv(self.h_rw_a2[l], "h_rw_a2"), eng="pool")
        P.dma(R["sm"][:, 256:512], self.hv(self.h_rw_g2[l], "h_rw_g2"), eng="pool")
        o_mu, _ = COL_LAYOUT["rw_mu%d" % l]; o_mu64, _ = COL_LAYOUT["rw_mu64_%d" % l]
        mu128 = lambda c: self.colp[:, o_mu + c:o_mu + c + 1]
        mu64 = lambda c: self.colp[0:64, o_mu64 + c:o_mu64 + c + 1]
        P.ts(R["omm"][:, 0:8], self.colp[:, o_mu:o_mu + 8], -1.0, ALU.mult, 1.0, ALU.add)
        P.ts(R["omm"][0:64, 8:24], self.colp[0:64, o_mu64:o_mu64 + 16], -1.0, ALU.mult, 1.0, ALU.add)
        o_h, _ = COL_LAYOUT["rw_h64_%d" % l]
        hcol = lambda h, j: self.colp[0:64, o_h + h * 9 + j:o_h + h * 9 + j + 1]
        for h in range(4):
            P.ts(R["nwa"][:, 2 * h:2 * h + 1], hcol(h, 0), -1.0, ALU.mult)
            P.ts(R["nwa"][:, 2 * h + 1:2 * h + 2], hcol(h, 1), -1.0, ALU.mult)
        pool_tiles = self.t512 + self.st512
        def tmp(i, rows=64):
            t = pool_tiles[i // 2]
            c0 = (i % 2) * 256
            class _T:
                name = t.name
                def __getitem__(s_, idx):
                    if not isinstance(idx, tuple):
                        idx = (idx, slice(0, 256))
                    p, c = idx
                    c = slice(c0 + (c.start or 0), c0 + (256 if c.stop is None else c.stop))
                    return V(t.h[p, c], (t.name,))
                def v3(s_, rows_):
                    return V(t.h[0:rows_, c0:c0 + 256].rearrange("p (n c) -> p n c", n=4), (t.name,))
            return _T()

        def shiftmix(ps, M, mu_ap, omm_ap, cslot, out_t, blk):
            zr = pool_tiles[6]
            if blk == 0:
                P.memset(zr[0:M, 0:1], 0.0)
            else:
                P.copy(zr[0:M, 0:1], R["carry"][0:M, cslot:cslot + 1])
            P.copy(zr[0:M, 1:257], ps[0:M, 0:256], eng="act")
            P.copy(R["carry"][0:M, cslot:cslot + 1], zr[0:M, 256:257])
            P.act(out_t[0:M, :], ps[0:M, 0:256], AF.Copy, scale=omm_ap)
            P.stt(out_t[0:M, :], zr[0:M, 0:256], mu_ap, out_t[0:M, :], ALU.mult, ALU.add)

        class _XB:
            def __init__(s_, xb, t0):
                s_.xb = xb; s_.t0 = t0
            def __getitem__(s_, idx):
                p, k, c = idx
                return V(s_.xb.h[p, k, slice(s_.t0 + c.start, s_.t0 + c.stop)], (s_.xb.name,))

        for blk in range(8):
            self.cur_xb = _XB(self.xb, blk * 256)
            sl = slice(blk * 256, (blk + 1) * 256)
            z = tmp(0)
            self.proj_fm(wB, 256, 128, 0, lambda ps: shiftmix(ps, 128, mu128(6), R["omm"][:, 6:7], 0, z, blk), ncol=256)
            P.act(twa[0:64, sl], z[0:64, :], AF.Tanh)
            P.copy(twa[64:128, sl], z[64:128, :], eng="pool")
            z2 = tmp(1)
            self.proj_fm(wB, 384, 128, 0, lambda ps: shiftmix(ps, 128, mu128(7), R["omm"][:, 7:8], 1, z2, blk), ncol=256)
            P.act(z2[:, :], z2[:, :], AF.Exp, scale=-1.0)
            P.act(z2[:, :], z2[:, :], AF.Ln, bias=1.0)
            P.act(sgd[:, sl], z2[:, :], AF.Exp, scale=-1.0)
        import os
        rstop = int(os.environ.get("RW_STOP", "99"))
        nheads = int(os.environ.get("RW_HEADS", "4"))
        pending_acc = []
        for h in range(nheads):
            for blk in range(8):
                for _ in range(4):
                    if pending_acc:
                        pending_acc.pop(0)()
                self.cur_xb = _XB(self.xb, blk * 256)
                sl = slice(blk * 256, (blk + 1) * 256)
                n0 = blk * 4
                zr_ = tmp(0); zk_ = tmp(1); zv_ = tmp(2)
                self.proj_fm(wA, h * 64, 64, 0, lambda ps: shiftmix(ps, 64, mu64(h), R["omm"][0:64, 8 + h:9 + h], 2, zr_, blk), ncol=256)
                self.proj_fm(wA, 256 + h * 64, 64, 0, lambda ps: shiftmix(ps, 64, mu64(4 + h), R["omm"][0:64, 12 + h:13 + h], 3, zk_, blk), ncol=256)
                self.proj_fm(wB, h * 64, 64, 0, lambda ps: shiftmix(ps, 64, mu64(8 + h), R["omm"][0:64, 16 + h:17 + h], 4, zv_, blk), ncol=256)
                P.copy(VT[0:64, sl], zv_[0:64, :], eng="pool")
                LD = tmp(3)
                ps = self.nextps()
                P.mm(ps[0:64, 0:256], R["sm"][0:64, h * 64:(h + 1) * 64], twa[0:64, sl])
                P.act(LD[0:64, :], ps[0:64, 0:256], AF.Exp, bias=R["nwa"][:, 2 * h:2 * h + 1], scale=-1.0)
                P.act(LD[0:64, :], LD[0:64, :], AF.Ln, bias=1.0)
                P.act(LD[0:64, :], LD[0:64, :], AF.Exp, bias=-0.5, scale=-1.0)
                A = tmp(4)
                ps = self.nextps()
                P.mm(ps[0:64, 0:256], R["sm"][64:128, h * 64:(h + 1) * 64], twa[64:128, sl])
                P.act(A[0:64, :], ps[0:64, 0:256], AF.Exp, bias=R["nwa"][:, 2 * h + 1:2 * h + 2], scale=-1.0)
                P.act(A[0:64, :], A[0:64, :], AF.Ln, bias=1.0)
                P.act(A[0:64, :], A[0:64, :], AF.Exp, scale=-1.0)
                KK = tmp(5)
                P.ts(KK[0:64, :], zk_[0:64, :], hcol(h, 2), ALU.mult)
                sq = self.nb()
                P.act(sq[0:64, 0:256], KK[0:64, :], AF.Square)
                ps = self.nextps()
                P.mm(ps[0:64, 0:256], self.blk64_bf[0:64, 0:64], sq[0:64, 0:256])
                RN = tmp(10)
                P.act(RN[0:64, :], ps[0:64, 0:256], AF.Ln, bias=1e-24, scale=64.0)
                P.act(RN[0:64, :], RN[0:64, :], AF.Exp, scale=-0.5)
                P.tt(KK[0:64, :], KK[0:64, :], RN[0:64, :], ALU.mult)
                K2 = tmp(6)
                P.ts(K2[0:64, :], A[0:64, :], -1.0, ALU.add, hcol(h, 3), ALU.mult)
                P.stt(K2[0:64, :], K2[0:64, :], 1.0, zk_[0:64, :], ALU.add, ALU.mult)
                KKA = tmp(7)
                P.tt(KKA[0:64, :], KK[0:64, :], A[0:64, :], ALU.mult)
                pr = self.nb()
                P.stt(pr[0:64, 0:256], zr_[0:64, :], hcol(h, 4), K2[0:64, :], ALU.mult, ALU.mult)
                ps = self.nextps()
                P.mm(ps[0:64, 0:256], self.blk64_bf[0:64, 0:64], pr[0:64, 0:256])
                P.stt(bv[0:64, sl], ps[0:64, 0:256], 64.0, zv_[0:64, :], ALU.mult, ALU.mult)
                Gb = tmp(8)
                ones_b = self.C("ones", rows=64, c0=0, c1=1).ap.to_broadcast([64, 256])
                P.generic("dve", lambda e, Gb=Gb, LD=LD: e.tensor_tensor_scan(Gb[0:64, :].ap, ones_b, LD[0:64, :].ap, 0.0, ALU.mult, ALU.add),
                          [self.consts[:], LD[0:64, :]], [Gb[0:64, :]])
                P.memset(R["gs"][:, 0:1], 0.0)
                G3 = Gb.v3(64)
                P.copy(R["gs"][:, 1:4], V(G3.ap[:, 0:3, 63], G3.keys))
                P.tt(G3, G3, V(R["gs"].h[:, :].unsqueeze(2).to_broadcast([64, 4, 64]), R["gs"][:].keys), ALU.subtract)
                P.act(R["PC"][:, n0:n0 + 4], V(G3.ap[:, :, 63], G3.keys), AF.Exp, scale=-1.0)
                Ep = tmp(10); Em = tmp(11); Epm1 = tmp(9)
                P.act(Ep[0:64, :], Gb[0:64, :], AF.Exp, scale=-1.0)
                P.act(Em[0:64, :], Gb[0:64, :], AF.Exp)
                P.tt(Epm1[0:64, :], LD[0:64, :], Gb[0:64, :], ALU.subtract)
                P.act(Epm1[0:64, :], Epm1[0:64, :], AF.Exp)
                P.tt(V(RA.ap[0:64, n0:n0 + 4, 0:64], ("rw_RA",)), zr_.v3(64), Ep.v3(64), ALU.mult)
                P.stt(V(RA.ap[0:64, n0:n0 + 4, 64:128], ("rw_RA",)), KK.v3(64), -1.0, Epm1.v3(64), ALU.mult, ALU.mult)
                P.tt(V(KB.ap[0:64, n0:n0 + 4, 0:64], ("rw_KB",)), K2.v3(64), Em.v3(64), ALU.mult)
                P.tt(V(KB.ap[0:64, n0:n0 + 4, 64:128], ("rw_KB",)), KKA.v3(64), Em.v3(64), ALU.mult)
            if h == 0:
                self.debug_dump("rw_RA", RA[0:64, :, :], [64, 32, 128])
                self.debug_dump("rw_KB", KB[0:64, :, :], [64, 32, 128])
                self.debug_dump("rw_PC", R["PC"][:, :], [64, 32])
            if rstop <= 1:
                continue
            P.memset(R["S32"][:], 0.0)
            P.memset(R["Sb"][0][:], 0.0)
            self._psY = None
            BS = 2
            def slot(n):
                return n % (2 * BS)
            def pre_rounds(ns):
                rounds = []
                def r0():
                    for n in ns:
                        s4 = slot(n)
                        QA = RQA[s4]; QB = RQB[s4]; TM = RTM[s4]; Nn = R["Nn"][n % BS]
                        Kt = KB[0:64, n, 0:64]; Bt = KB[0:64, n, 64:128]; At = RA[0:64, n, 64:128]
                        ps = self.nextps()
                        P.mm(ps[0:64, 0:128], Kt, RA[0:64, n, :])
                        P.mm(ps[0:64, 128:256], Bt, RA[0:64, n, :])
                        P.mm(ps[0:64, 256:320], At, Bt)
                        P.tt(QA[:, :], ps[0:64, 0:128], R["maskq"][:, :], ALU.mult)
                        P.tt(QB[:, :], ps[0:64, 128:256], R["maskq"][:, :], ALU.mult)
                        P.tt(Nn[:, :], ps[0:64, 256:320], self.C("tri_gt", rows=64, c0=0, c1=64), ALU.mult)
                        ps2 = self.nextps()
                        pbf = lambda a_, b_, ps2=ps2: V(ps2.h[0:64, :].bitcast(BF16)[:, a_:b_], ps2[:].keys)
                        P.transpose(pbf(0, 64), Kt, self.ident_bf[0:64, 0:64])
                        P.transpose(pbf(64, 128), Bt, self.ident_bf[0:64, 0:64])
                        P.transpose(pbf(128, 192), VT[0:64, n * 64:(n + 1) * 64], self.ident_bf[0:64, 0:64])
                        P.copy(TM[:, :], pbf(0, 192), eng="act")
                rounds.append(r0)
                st = {}
                def r1():
                    for n in ns:
                        QB = RQB[slot(n)]
                        W = R["Wt"][n % BS][0]
                        P.tt(W[:, :], QB[:, 64:128], self.C("ident", rows=64, c0=0, c1=64), ALU.add)
                        st[n] = dict(W=W, Xp=QB[:, 64:128], Np=R["Nn"][n % BS][:, :], wi=0)
                rounds.append(r1)
                for lev in range(5):
                    def ra(lev=lev):
                        for n in ns:
                            d = st[n]
                            XN = R["XN"][n % BS][lev % 2]
                            ps = self.nextps()
                            P.mm(ps[0:64, 64:128], d["Xp"], d["Np"])
                            if lev < 4:
                                P.mm(ps[0:64, 0:64], d["Np"], d["Xp"])
                                P.copy(XN[:, :], ps[0:64, 0:128], eng="act")
                            else:
                                P.copy(XN[:, 64:128], ps[0:64, 64:128], eng="act")
                            d["XN"] = XN
                    def rb(lev=lev):
                        for n in ns:
                            d = st[n]
                            XN = d["XN"]
                            ps2 = self.nextps()
                            P.mm(ps2[0:64, 0:64], XN[:, 64:128], d["W"][:, :])
                            if lev < 4:
                                d["wi"] += 1
                                Wn = R["Wt"][n % BS][d["wi"] % 2]
                            else:
                                Wn = R["Wf"][slot(n)]
                            P.tt(Wn[:, :], ps2[0:64, 0:64], d["W"][:, :], ALU.add)
                            d["W"] = Wn
                            d["Xp"] = XN[:, 0:64]; d["Np"] = XN[:, 64:128]
                    rounds.append(ra); rounds.append(rb)
                return rounds

            def chain_hops(ns):
                hops = []
                for n in ns:
                    s4 = slot(n)
                    QA = RQA[s4]; QB = RQB[s4]; TM = RTM[s4]; W = R["Wf"][s4]
                    Rt = RA[0:64, n, 0:64]; At = RA[0:64, n, 64:128]
                    Sb = R["Sb"][n % 2]; Sbn = R["Sb"][(n + 1) % 2]
                    RU = R["RU"][n % 2]
                    def h1(n=n, QA=QA, TM=TM, At=At, Sb=Sb, RU=RU):
                        psA = self.nextps()
                        P.mm(psA[0:64, 0:64], At, Sb[:, :], start=True, stop=False)
                        P.mm(psA[0:64, 0:64], QA[:, 64:128], TM[:, 128:192], start=False, stop=True)
                        P.copy(RU[:, 0:64], psA[0:64, 0:64], eng="act")
                        P.ts(R["S32"][:, :], R["S32"][:, :], R["PC"][:, n:n + 1], ALU.mult)
                    def h2(n=n, W=W, RU=RU):
                        psU = self.nextps()
                        P.mm(psU[0:64, 0:64], W[:, :], RU[:, 0:64])
                        P.copy(RU[:, 64:128], psU[0:64, 0:64], eng="act")
                    def h3(n=n, QA=QA, QB=QB, TM=TM, Rt=Rt, Sb=Sb, Sbn=Sbn, RU=RU):
                        psS = self.nextps()
                        P.mm(psS[0:64, 0:64], TM[:, 0:64], TM[:, 128:192], start=True, stop=False)
                        P.mm(psS[0:64, 0:64], TM[:, 64:128], RU[:, 64:128], start=False, stop=True)
                        P.stt(Sbn[:, :], psS[0:64, 0:64], R["PC"][:, n:n + 1], R["S32"][:, :], ALU.mult, ALU.add)
                        P.stt(R["S32"][:, :], psS[0:64, 0:64], R["PC"][:, n:n + 1], R["S32"][:, :], ALU.mult, ALU.add)
                        if n % 8 == 0:
                            self._psY = self.nextacc()
                        psY = self._psY
                        yc = slice((n % 8) * 64, (n % 8 + 1) * 64)
                        P.mm(psY[0:64, yc], Sb[:, :], Rt, start=True, stop=False)
                        P.mm(psY[0:64, yc], TM[:, 128:192], QA[:, 0:64], start=False, stop=False)
                        P.mm(psY[0:64, yc], RU[:, 64:128], QB[:, 0:64], start=False, stop=True)
                        if n % 8 == 7:
                            post(n // 8, psY)
                    hops += [h1, h2, h3]
                return hops

            def post(tb, psY):
                sl = slice(tb * 512, (tb + 1) * 512)
                Y = self.nt()
                P.copy(Y[0:64, :], psY[0:64, :], eng="act")
                if h == 0:
                    self.debug_dump("rw_scan%d" % tb, Y[0:64, :], [64, 512])
                ysq = self.nb(); ybf = self.nb()
                P.act(ysq[0:64, :], Y[0:64, :], AF.Square)
                P.copy(ybf[0:64, :], Y[0:64, :], eng="pool")
                psm = self.nextps(); psq = self.nextps()
                P.mm(psm[0:64, :], self.blk64_bf[0:64, 0:64], ybf[0:64, :])
                P.mm(psq[0:64, :], self.blk64_bf[0:64, 0:64], ysq[0:64, :])
                m2 = self.nt()
                P.tt(Y[0:64, :], Y[0:64, :], psm[0:64, :], ALU.subtract)
                P.act(m2[0:64, :], psm[0:64, :], AF.Square)
                P.tt(m2[0:64, :], psq[0:64, :], m2[0:64, :], ALU.subtract)
                P.act(m2[0:64, :], m2[0:64, :], AF.Ln, bias=64e-5)
                P.act(m2[0:64, :], m2[0:64, :], AF.Exp, scale=-0.5)
                P.stt(Y[0:64, :], Y[0:64, :], hcol(h, 5), m2[0:64, :], ALU.mult, ALU.mult)
                P.stt(Y[0:64, :], Y[0:64, :], hcol(h, 6), bv[0:64, sl], ALU.add, ALU.add)
                psg = self.nextps()
                P.mm(psg[0:64, :], R["sm"][:, 256 + h * 64:256 + (h + 1) * 64], sgd[:, sl])
                P.tt(self.yT[0:64, 0, sl], Y[0:64, :], psg[0:64, :], ALU.mult)

            nb_ = 32 // BS
            for k in range(nb_ + 1):
                pr = pre_rounds(list(range(k * BS, (k + 1) * BS))) if k < nb_ else []
                ch = chain_hops(list(range((k - 1) * BS, k * BS))) if k >= 1 else []
                i = j = 0
                while i < len(pr) or j < len(ch):
                    if j < len(ch):
                        ch[j](); j += 1
                    for _ in range(2):
                        if i < len(pr):
                            pr[i](); i += 1
            if rstop <= 2:
                continue
            self.debug_dump("y_rwh%d" % h, self.yT[0:64, 0, :], [64, S])
            if "ln1" in self.stages:
                pending_acc = self.acc_out_rw(l, h)
        for th in pending_acc:
            th()
        P.add("pool", lambda e: e.memset(self._dummy.h[:, :], 0.0), [V(None, ring_keys)], [V(None, ("yT", self._dummy.name))])

    def acc_out_rw(self, l, h):
        P = self.P
        if not hasattr(self, "rw_wo"):
            self.rw_wo = P.sb("rw_wo", [64, 1024], BF16)
        t = self.rw_wo
        wv = WV(t, 1, 1024)
        P.dma(V(wv.ap[0:64, :, :], (t.name,)),
              V(self.h_w_out[l][256 + h * 64:256 + (h + 1) * 64, :].rearrange("(k p) c -> p k c", p=64), ("h_w_out",)), eng="pool")
        first = self._first_acc
        self._first_acc = False
        thunks = []
        for tb in range(4):
            for o in range(8):
                def th(tb=tb, o=o):
                    ps = self.nextps()
                    P.mm(ps[:, :], V(wv.ap[0:64, 0, o * 128:(o + 1) * 128], (t.name,)), self.yT[0:64, 0, tb * 512:(tb + 1) * 512])
                    xv = self.xT[:, o, tb * 512:(tb + 1) * 512]
                    if first:
                        P.stt(xv, xv, ALPHA, ps[:, :], ALU.mult, ALU.add)
                    else:
                        P.tt(xv, xv, ps[:, :], ALU.add)
                thunks.append(th)
        return thunks


    def mem_ln(self):
        P = self.P
        self.memnb = P.sb("memnb", [128, 8, NMEM], BF16)
        self.scr_phase(["mem_raw"])
        class _C:
            def __init__(s_, ap, key):
                s_.ap = ap; s_.key = key
            def __getitem__(s_, idx):
                return V(s_.ap[idx], (s_.key,))
        raw = _C(self.scr.h[:, :].bitcast(F32)[:, 0:8 * NMEM].rearrange("p (c t) -> p c t", c=8), "mem_raw")
        P.dma(raw[:, :, :], self.hv(self.h_memT.rearrange("(c p) t -> p c t", p=128), "h_memT"))
        self.layernorm("mem_ln_g", "mem_ln_b", src=raw, dst32=False, dstb=self.memnb, ntok=NMEM)

    def cross_attn(self, l):
        P = self.P
        self.scr_phase(["ca_KT", "ca_V", "ca_oT"])
        scr = self.scr
        class _C:
            def __init__(s_, ap, key):
                s_.ap = ap; s_.key = key
            def __getitem__(s_, idx):
                return V(s_.ap[idx], (s_.key,))
        KT = _C(scr.h[:, 0:2048].rearrange("p (c m) -> p c m", c=8), "ca_KT")
        Vt = _C(scr.h[:, 2048:4096].rearrange("p (b c) -> p b c", b=2), "ca_V")
        oT = _C(scr.h[:, 4096:20480].rearrange("p (c t) -> p c t", c=8), "ca_oT")
        if not hasattr(self, "ones_bf"):
            self.ones_bf = P.sb("ones_bf", [128, 128], BF16)
            P.copy(self.ones_bf[:], self.C("ones"))
        stages = []
        def ld_k(half):
            return self.load_w(self.h_ca_wk[l][:, half * 512:(half + 1) * 512], 8, 512, "h_ca_wk")
        def cp_k(wk, half):
            for oc in range(4):
                ps = self.nextps()
                for kc in range(8):
                    P.mm(ps[:, 0:NMEM], wk[:, kc, oc * 128:(oc + 1) * 128], self.memnb[:, kc, :], start=(kc == 0), stop=(kc == 7))
                P.copy(KT[:, half * 4 + oc, :], ps[:, 0:NMEM], eng="act")
        def ld_v(half):
            return self.load_w(self.h_ca_wv[l][:, half * 512:(half + 1) * 512], 8, 512, "h_ca_wv")
        def cp_v(wv, half):
            for mb in range(2):
                ps = self.nextps()
                for kc in range(8):
                    P.mm(ps[:, :], self.memnb[:, kc, mb * 128:(mb + 1) * 128], wv[:, kc, :], start=(kc == 0), stop=(kc == 7))
                P.copy(Vt[:, mb, half * 512:(half + 1) * 512], ps[:, :], eng="dve")
        def ld_q(h):
            return self.load_w(self.h_ca_wq[l][:, h * 256:(h + 1) * 256], 8, 256, "h_ca_wq")
        def cp_q(wq, h):
            def front(tb):
                self.load_xb(tb)
                qT = [self.nb(), self.nb()]
                for c in range(2):
                    self.proj_fm(wq, c * 128, 128, tb, lambda ps, c=c: P.act(qT[c][:, :], ps[:, :], AF.Copy, scale=1.0 / 16))
                lps = []
                for mb in range(2):
                    ps = self.nextps()
                    for c in range(2):
                        P.mm(ps[:, :], KT[:, h * 2 + c, mb * 128:(mb + 1) * 128], qT[c][:, :], start=(c == 0), stop=(c == 1))
                    lps.append(ps)
                return lps
            def back(tb, lps):
                sl = slice(tb * 512, (tb + 1) * 512)
                PT = []
                for mb in range(2):
                    pt = self.nb()
                    P.act(pt[:, :], lps[mb][:, :], AF.Exp)
                    PT.append(pt)
                den = self.nextps()
                for mb in range(2):
                    P.mm(den[:, :], self.ones_bf[:, :], PT[mb][:, :], start=(mb == 0), stop=(mb == 1))
                rden = self.nt()
                P.recip(rden[:, :], den[:, :])
                for c2 in range(2):
                    ps = self.nextps()
                    for mb in range(2):
                        P.mm(ps[:, :], Vt[:, mb, h * 256 + c2 * 128:h * 256 + (c2 + 1) * 128], PT[mb][:, :], start=(mb == 0), stop=(mb == 1))
                    P.tt(oT[:, h * 2 + c2, sl], ps[:, :], rden[:, :], ALU.mult)
            cur = front(0)
            for tb in range(4):
                sl = slice(tb * 512, (tb + 1) * 512)
                PT = []
                for mb in range(2):
                    pt = self.nb()
                    P.act(pt[:, :], cur[mb][:, :], AF.Exp)
                    PT.append(pt)
                nxt = front(tb + 1) if tb + 1 < 4 else None
                den = self.nextps()
                for mb in range(2):
                    P.mm(den[:, :], self.ones_bf[:, :], PT[mb][:, :], start=(mb == 0), stop=(mb == 1))
                rden = self.nt()
                P.recip(rden[:, :], den[:, :])
                for c2 in range(2):
                    ps = self.nextps()
                    for mb in range(2):
                        P.mm(ps[:, :], Vt[:, mb, h * 256 + c2 * 128:h * 256 + (c2 + 1) * 128], PT[mb][:, :], start=(mb == 0), stop=(mb == 1))
                    P.tt(oT[:, h * 2 + c2, sl], ps[:, :], rden[:, :], ALU.mult)
                cur = nxt
        def ld_o(half):
            return self.load_w(self.h_ca_wo[l][:, half * 512:(half + 1) * 512], 8, 512, "h_ca_wo")
        def cp_o(wo, half):
            for tb in range(4):
                sl = slice(tb * 512, (tb + 1) * 512)
                for oc in range(4):
                    ps = self.nextps()
                    for kc in range(8):
                        P.mm(ps[:, :], wo[:, kc, oc * 128:(oc + 1) * 128], oT[:, kc, sl], start=(kc == 0), stop=(kc == 7))
                    xv = self.xT[:, half * 4 + oc, sl]
                    P.stt(xv, xv, ALPHA, ps[:, :], ALU.mult, ALU.add)
        for half in range(2):
            stages.append((ld_k, cp_k, half))
        for half in range(2):
            stages.append((ld_v, cp_v, half))
        for h in range(4):
            stages.append((ld_q, cp_q, h))
        for half in range(2):
            stages.append((ld_o, cp_o, half))
        cur = stages[0][0](stages[0][2])
        for i, (ld, cp, arg) in enumerate(stages):
            nxt = None
            if i + 1 < len(stages):
                nxt = stages[i + 1][0](stages[i + 1][2])
            cp(cur, arg)
            cur = nxt

    def conv_ffn(self, l):
        P = self.P
        self.scr_phase(["ff_w0", "ff_w1", "ff_w2", "ff_w3", "ff_pr0", "ff_pr1"])
        scr = self.scr
        if not hasattr(self, "ff"):
            self.ff = dict(halo=P.sb("ff_halo", [128, 8, 2], F32))
        y32 = self.yT.h[:, :, :].rearrange("p a b -> p (a b)").bitcast(F32)
        class _H:
            def __init__(s_, i):
                s_.i = i
            def __getitem__(s_, idx):
                p, c = idx
                return V(y32[p, slice(s_.i * 520 + c.start, s_.i * 520 + c.stop)], ("ffh%d" % s_.i,))
        self.ffh = [_H(i) for i in range(3)]
        self.P.add("pool", lambda e: e.memset(self._dummy.h[:, :], 0.0), [V(None, ("yT",))],
                   [V(None, ("ffh0", "ffh1", "ffh2", self._dummy.name))])
        class _WB:
            def __init__(s_, ap, key, kc, cols):
                s_.ap = ap[:, 0:kc * cols].rearrange("p (k c) -> p k c", k=kc); s_.key = key; s_.kc = kc
            def __getitem__(s_, idx):
                return V(s_.ap[idx], (s_.key,))
        bufs = [(self.wb[0].h[:, :], "wb0"), (self.wb[1].h[:, :], "wb1")] + \
               [(scr.h[:, i * 4096:(i + 1) * 4096], "ff_w%d" % i) for i in range(4)]
        prs = [(scr.h[:, 16384 + i * 2048:16384 + (i + 1) * 2048].rearrange("p (j t) -> p j t", j=4), "ff_pr%d" % i) for i in range(2)]
        def loadw(bi, hbm_ap, kc, cols, key):
            ap, k = bufs[bi]
            wv = _WB(ap, k, kc, cols)
            src = hbm_ap.rearrange("(k p) c -> p k c", p=128)
            step = 4 if cols <= 512 else 2
            kk = 0
            while kk < kc:
                k2 = min(kc, kk + step)
                P.dma(V(wv.ap[:, kk:k2, :], (k,)), V(src[:, kk:k2, :], (key,)), eng="pool")
                kk = k2
            return wv
        o_ub, _ = COL_LAYOUT["ffn_up_b"]; o_cb, _ = COL_LAYOUT["ffn_conv_b"]; o_cw, _ = COL_LAYOUT["ffn_conv"]
        colv = lambda o: self.colp[:, o:o + 1]
        npg = 6
        def load_pg(pg):
            nj = 4 if pg < 5 else 2
            b0 = (pg % 2) * 3
            wg = loadw(b0, self.h_ffn_up[l][:, pg * 512:pg * 512 + nj * 128], 8, nj * 128, "h_ffn_up")
            wvv = loadw(b0 + 1, self.h_ffn_up[l][:, DFF + pg * 512:DFF + pg * 512 + nj * 128], 8, nj * 128, "h_ffn_up")
            wd = loadw(b0 + 2, self.h_ffn_down[l][pg * 512:pg * 512 + nj * 128, :], nj, 1024, "h_ffn_down")
            return wg, wvv, wd
        nxt = load_pg(0)
        pend_dn = []
        for pg in range(npg):
            nj = 4 if pg < 5 else 2
            wg, wvv, wd = nxt
            if pg + 1 < npg:
                nxt = load_pg(pg + 1)
            for tb in range(4):
                self.load_xb(tb)
                sl = slice(tb * 512, (tb + 1) * 512)
                prap, prk = prs[(pg * 4 + tb) % 2]
                carry_dn = pend_dn[:]
                del pend_dn[:]
                for jj in range(nj):
                    for _ in range((8 + nj - 1) // nj):
                        if carry_dn:
                            carry_dn.pop(0)()
                    res = []
                    for part, wsrc in enumerate((wg, wvv)):
                        ch = part * 22 + pg * 4 + jj
                        hb = self.nt()
                        hs = part * 4 + jj
                        ps = self.nextps()
                        for kc in range(8):
                            P.mm(ps[:, :], wsrc[:, kc, jj * 128:(jj + 1) * 128], self.cur_xb[:, kc, 0:512], start=(kc == 0), stop=(kc == 7))
                        hbuf = self.ffh[self._ffh_i % 3]; self._ffh_i += 1
                        if tb == 0:
                            P.memset(hbuf[:, 0:2], 0.0, eng="dve")
                        else:
                            P.copy(hbuf[:, 0:2], self.ff["halo"][:, hs, :], eng="dve")
                        P.act(hbuf[:, 2:514], ps[:, :], AF.Identity, bias=colv(o_ub + ch))
                        P.copy(self.ff["halo"][:, hs, :], hbuf[:, 512:514], eng="dve")
                        P.act(hb[:, :], hbuf[:, 0:512], AF.Identity, bias=colv(o_cb + ch), scale=colv(o_cw + ch))
                        P.stt(hb[:, :], hbuf[:, 1:513], colv(o_cw + 44 + ch), hb[:, :], ALU.mult, ALU.add)
                        P.stt(hb[:, :], hbuf[:, 2:514], colv(o_cw + 88 + ch), hb[:, :], ALU.mult, ALU.add)
                        res.append(hb)
                    P.act(res[0][:, :], res[0][:, :], AF.Gelu)
                    P.tt(V(prap[:, jj, :], (prk,)), res[0][:, :], res[1][:, :], ALU.mult)
                for o in range(8):
                    def dn(o=o, nj=nj, wd=wd, prap=prap, prk=prk, sl=sl, pg=pg):
                        ps = self.nextps()
                        for jj in range(nj):
                            P.mm(ps[:, :], wd[:, jj, o * 128:(o + 1) * 128], V(prap[:, jj, :], (prk,)), start=(jj == 0), stop=(jj == nj - 1))
                        xv = self.xT[:, o, sl]
                        if pg == 0:
                            P.stt(xv, xv, ALPHA, ps[:, :], ALU.mult, ALU.add)
                        else:
                            P.tt(xv, xv, ps[:, :], ALU.add)
                    pend_dn.append(dn)
            for dn in pend_dn:
                dn()
            del pend_dn[:]

    def build(self):
        P = self.P
        with ExitStack() as st:
            P.enter(st)
            self.decl()
            self.alloc()
            P.dma(self.consts[:], self.hv(self.h_consts, "h_consts"))
            P.dma(self.colp[:], self.hv(self.h_colp[0], "h_colp"))
            xsrc = self.h_xT.rearrange("(c p) t -> p c t", p=128)
            for tb in range(4):
                sl = slice(tb * 512, (tb + 1) * 512)
                P.dma(self.xT[:, :, sl], self.hv(xsrc[:, :, sl], "h_xT"))
                P.copy(self.xb[:, :, sl], self.xT[:, :, sl], eng=("act" if tb % 2 else "dve"))
            P.copy(self.ident_bf[:], self.C("ident"))
            P.ts(self.ones_s[:], self.C("ones"), 1.0 / 1024, ALU.mult)
            P.copy(self.blk64_bf[:], self.C("blk64"))
            if "ca" in self.stages:
                self.mem_ln()
            for l in range(self.nlayers):
                self.layer(l)
            osrc = self.h_outT.rearrange("(c p) t -> p c t", p=128)
            for tb in range(4):
                sl = slice(tb * 512, (tb + 1) * 512)
                P.dma(self.hv(osrc[:, :, sl], "h_outT"), self.xT[:, :, sl])
            P.finalize()
            P.emit(st)
        return self.nc

    def layer(self, l):
        P = self.P
        if l > 0:
            P.dma(self.colp[:], self.hv(self.h_colp[l], "h_colp"))
        self._first_acc = True
        if l > 0 and "ffn" in self.stages:
            self.P.add("pool", lambda e: e.memset(self._dummy.h[:, :], 0.0), [V(None, ("ffh0", "ffh1", "ffh2"))],
                       [V(None, ("yT", self._dummy.name))])
        for m, (name, fn) in enumerate([("sg", self.mixer_sg), ("rw", self.mixer_rw), ("gla", self.mixer_gla), ("fox", self.mixer_fox)]):
            if name in self.stages and fn is not None:
                fn(l)
                if name == "rw":
                    continue
                self.debug_dump("y_%s%d" % (name, l), self.yT[:], [128, 2, S])
                if "ln1" in self.stages:
                    self.acc_out(self.h_w_out[l][m * 256:(m + 1) * 256, :], self.yT, 2, "h_w_out", self._first_acc)
                    self._first_acc = False
        if "ln1" in self.stages:
            self.layernorm("ln1_g%d" % l, "ln1_b%d" % l)
            self.debug_dump("x1_%d" % l, self.xT[:], [128, 8, S])
        if "ca" in self.stages:
            self.cross_attn(l)
            self.layernorm("ln2_g%d" % l, "ln2_b%d" % l)
            self.debug_dump("x2_%d" % l, self.xT[:], [128, 8, S])
        if "ffn" in self.stages:
            self.conv_ffn(l)
            self.layernorm("ln3_g%d" % l, "ln3_b%d" % l)


_CACHE = {}


def kernel(**inputs):
    inp = {k: np.ascontiguousarray(np.asarray(v, dtype=np.float32)) for k, v in inputs.items()}
    n = 8
    nc = bass.Bass("TRN2", target_bir_lowering=False)
    mk = MK(nc)
    mk.build()
    consts = make_consts()
    colp = make_colp(inp)
    rowp = make_rowp(inp)
    sgwT = np.ascontiguousarray(inp["sg_w"].transpose(0, 3, 1, 2))
    shared = dict(consts=consts, colp=colp, rowp=rowp, sgwT=sgwT,
                  w_in=inp["w_in"], w_out=inp["w_out"], gla_a_up=inp["gla_a_up"],
                  rw_w2=inp["rw_w2"], rw_a2=inp["rw_a2"], rw_g2=inp["rw_g2"],
                  ca_wq=inp["ca_wq"], ca_wk=inp["ca_wk"], ca_wv=inp["ca_wv"], ca_wo=inp["ca_wo"],
                  ffn_up=inp["ffn_up"], ffn_down=inp["ffn_down"])
    maps = []
    for b in range(n):
        m = dict(shared)
        m["xT"] = np.ascontiguousarray(inp["x"][b].T)
        m["memT"] = np.ascontiguousarray(inp["mem"][b].T)
        maps.append(m)
    res = run_bass_kernel_spmd(nc, maps, core_ids=list(range(n)))
    out = np.stack([np.asarray(res.results[b]["outT"]).T for b in range(n)]).astype(np.float32)
    return np.ascontiguousarray(out)
```

```python
from contextlib import ExitStack
from concourse.bass_utils import run_bass_kernel_spmd
import numpy as np
import concourse.bass as bass
import concourse.mybir as mybir

F32 = mybir.dt.float32
BF16 = mybir.dt.bfloat16
AF = mybir.ActivationFunctionType
ALU = mybir.AluOpType
AX = mybir.AxisListType

ENGS = ("pe", "act", "dve", "pool", "sp")


class V:
    __slots__ = ("ap", "keys")

    def __init__(self, ap, keys):
        self.ap = ap
        self.keys = keys


class T:
    def __init__(self, handle, name, shape):
        self.h = handle
        self.name = name
        self.shape = shape

    def __getitem__(self, idx):
        return V(self.h[idx], (self.name,))

    def k(self, sub):
        return _TK(self, sub)


class _TK:
    def __init__(self, t, sub):
        self.t = t
        self.sub = sub

    def __getitem__(self, idx):
        return V(self.t.h[idx], ((self.t.name, self.sub),))


class Op:
    __slots__ = ("eng", "fn", "reads", "writes", "idx", "deps", "signal", "sigidx",
                 "is_dma", "dslot", "dcnt", "snap", "xreads")

    def __init__(self, eng, fn, reads, writes, is_dma=False):
        self.eng = eng
        self.fn = fn
        self.reads = reads
        self.writes = writes
        self.is_dma = is_dma
        self.deps = []
        self.signal = False
        self.sigidx = 0
        self.dslot = -1
        self.dcnt = 0
        self.snap = None


class Prog:
    N_DSEM = 24

    def __init__(self, nc):
        self.nc = nc
        self.ops = []
        self._stack = None
        self.ntile = 0
        self.psum_names = set()

    def enter(self, stack):
        self._stack = stack

    def sb(self, name, shape, dt=F32):
        h = self._stack.enter_context(self.nc.sbuf_tensor("s_" + name, list(shape), dt))
        return T(h, name, shape)

    def ps(self, name, shape, dt=F32):
        h = self._stack.enter_context(self.nc.psum_tensor("p_" + name, list(shape), dt))
        self.psum_names.add(name)
        return T(h, name, shape)

    def _keys(self, vs):
        ks = []
        for v in vs:
            if v is None or isinstance(v, (int, float)):
                continue
            ks.extend(v.keys)
        return ks

    def add(self, eng, fn, reads, writes, is_dma=False):
        op = Op(eng, fn, self._keys(reads), self._keys(writes), is_dma)
        op.xreads = [k for k in op.reads if (k if isinstance(k, str) else k[0]) in self.psum_names and k not in op.writes]
        self.ops.append(op)
        return op

    def dma(self, out, in_, eng="sp", **kw):
        def fn(e, out=out, in_=in_):
            return e.dma_start(out=out.ap, in_=in_.ap, **kw)
        return self.add(eng, fn, [in_], [out], is_dma=True)

    def mm(self, out, lhsT, rhs, start=True, stop=True, **kw):
        def fn(e):
            return e.matmul(out.ap, lhsT.ap, rhs.ap, start=start, stop=stop, **kw)
        return self.add("pe", fn, [lhsT, rhs], [out])

    def transpose(self, out, in_, ident):
        def fn(e):
            return e.transpose(out.ap, in_.ap, ident.ap)
        return self.add("pe", fn, [in_, ident], [out])

    def act(self, out, in_, func, bias=0.0, scale=1.0, eng="act", accum_out=None):
        def fn(e):
            kw = {}
            if accum_out is not None:
                kw["accum_out"] = accum_out.ap
            return e.activation(out.ap, in_.ap, func,
                                bias=(bias.ap if isinstance(bias, V) else bias),
                                scale=(scale.ap if isinstance(scale, V) else scale), **kw)
        return self.add("act", fn, [in_, bias, scale], [out, accum_out])

    def tt(self, out, in0, in1, op, eng="dve"):
        def fn(e):
            return e.tensor_tensor(out.ap, in0.ap, in1.ap, op)
        return self.add(eng, fn, [in0, in1], [out])

    def ts(self, out, in0, s1, op0, s2=None, op1=None, eng="dve", accum_out=None):
        def fn(e):
            a1 = s1.ap if isinstance(s1, V) else s1
            a2 = s2.ap if isinstance(s2, V) else s2
            kw = {}
            if accum_out is not None:
                kw["accum_out"] = accum_out.ap
            if op1 is None:
                return e.tensor_scalar(out.ap, in0.ap, a1, None, op0, **kw)
            return e.tensor_scalar(out.ap, in0.ap, a1, a2, op0, op1, **kw)
        return self.add(eng, fn, [in0, s1, s2], [out, accum_out])

    def stt(self, out, in0, scalar, in1, op0, op1, eng="dve"):
        def fn(e):
            s = scalar.ap if isinstance(scalar, V) else scalar
            return e.scalar_tensor_tensor(out.ap, in0.ap, s, in1.ap, op0, op1)
        return self.add(eng, fn, [in0, scalar, in1], [out])

    def copy(self, out, in_, eng="dve"):
        if eng == "act":
            def fn(e):
                return e.copy(out.ap, in_.ap)
        else:
            def fn(e):
                return e.tensor_copy(out.ap, in_.ap)
        return self.add(eng, fn, [in_], [out])

    def memset(self, out, val, eng="dve"):
        def fn(e):
            return e.memset(out.ap, val)
        return self.add(eng, fn, [], [out])

    def reduce(self, out, in_, op, axis=AX.X, eng="dve"):
        def fn(e):
            return e.tensor_reduce(out.ap, in_.ap, axis, op)
        return self.add(eng, fn, [in_], [out])

    def recip(self, out, in_):
        def fn(e):
            return e.reciprocal(out.ap, in_.ap)
        return self.add("dve", fn, [in_], [out])

    def generic(self, eng, fn, reads, writes):
        return self.add(eng, fn, reads, writes)

    def finalize(self, out_keys=()):
        nc = self.nc
        ops = self.ops
        last_w = {}
        readers = {}
        for i, op in enumerate(ops):
            op.idx = i
            deps = set()
            for k in op.reads:
                w = last_w.get(k)
                if w is not None:
                    deps.add(w)
            for k in list(op.writes) + op.xreads:
                w = last_w.get(k)
                if w is not None:
                    deps.add(w)
                latest = {}
                for r in readers.get(k, ()):
                    ro = ops[r]
                    if ro.is_dma:
                        deps.add(r)
                    else:
                        latest[ro.eng] = r
                for r in latest.values():
                    deps.add(r)
            deps.discard(i)
            op.deps = sorted(deps)
            for k in op.reads:
                lst = readers.setdefault(k, [])
                if not op.is_dma:
                    lst[:] = [r for r in lst if ops[r].is_dma or ops[r].eng != op.eng]
                lst.append(i)
            for k in op.writes:
                last_w[k] = i
                readers[k] = []
            for k in op.xreads:
                readers[k] = [i]
        for op in ops:
            need = []
            for d in op.deps:
                p = ops[d]
                if p.is_dma:
                    need.append(d)
                    continue
                if p.eng == op.eng:
                    if op.is_dma:
                        need.append(d)
                        continue
                    if op.eng == "pe":
                        continue
                    raw = any(k in p.writes for k in op.reads)
                    if raw:
                        need.append(d)
                    continue
                need.append(d)
            op.deps = need
            for d in need:
                if not ops[d].is_dma:
                    ops[d].signal = True
        cnt = {e: 0 for e in ENGS}
        for op in ops:
            if op.is_dma:
                continue
            if op.signal:
                cnt[op.eng] += 1
                op.sigidx = cnt[op.eng]
        dcount = [0] * self.N_DSEM
        nd = 0
        for op in ops:
            if op.is_dma:
                op.dslot = nd % self.N_DSEM
                dcount[op.dslot] += 1
                op.dcnt = dcount[op.dslot]
                nd += 1
        self.n_dma = nd
        self.sig_counts = cnt
        return self

    def emit(self, stack):
        nc = self.nc
        ops = self.ops
        sems = {e: stack.enter_context(nc.semaphore("S_" + e)) for e in ENGS if e != "sp"}
        dsems = [stack.enter_context(nc.semaphore("D%d" % i)) for i in range(self.N_DSEM)]
        block = stack.enter_context(nc.Block())
        seen = {e: {x: 0 for x in ENGS} for e in ENGS}
        seen_d = {e: [0] * self.N_DSEM for e in ENGS}
        plan = {e: [] for e in ENGS}
        last_dma_on_slot = [None] * self.N_DSEM
        for op in ops:
            e = op.eng
            waits = []
            if op.is_dma:
                prev = last_dma_on_slot[op.dslot]
                if prev is not None and seen_d[e][op.dslot] < prev.dcnt * 16:
                    waits.append((dsems[op.dslot], prev.dcnt * 16))
                    seen_d[e][op.dslot] = prev.dcnt * 16
                last_dma_on_slot[op.dslot] = op
            for d in op.deps:
                p = ops[d]
                if p.is_dma:
                    v = p.dcnt * 16
                    if seen_d[e][p.dslot] < v:
                        waits.append((dsems[p.dslot], v))
                        seen_d[e][p.dslot] = v
                else:
                    v = p.sigidx
                    if seen[e][p.eng] < v:
                        waits.append((sems[p.eng], v))
                        seen[e][p.eng] = v
                        for x in ENGS:
                            if p.snap[x] > seen[e][x]:
                                seen[e][x] = p.snap[x]
            if not op.is_dma:
                snap = dict(seen[e])
                if op.signal:
                    snap[e] = max(snap[e], op.sigidx)
                op.snap = snap
            plan[e].append((waits, op))
        final_waits = []
        for s in range(self.N_DSEM):
            lp = last_dma_on_slot[s]
            if lp is not None:
                final_waits.append((dsems[s], lp.dcnt * 16))

        def run(engname, e):
            for waits, op in plan[engname]:
                for (s, v) in waits:
                    e.wait_ge(s, v)
                ins = op.fn(e)
                if op.is_dma:
                    ins.then_inc(dsems[op.dslot], 16)
                elif op.signal:
                    ins.then_inc(sems[op.eng], 1)

        @block.tensor
        def _(e):
            run("pe", e)

        @block.scalar
        def _(e):
            run("act", e)

        @block.vector
        def _(e):
            run("dve", e)

        @block.gpsimd
        def _(e):
            run("pool", e)

        @block.sync
        def _(e):
            run("sp", e)
            for (s, v) in final_waits:
                e.wait_ge(s, v)
        self.stats = {e: len(plan[e]) for e in ENGS}
        self.nwaits = {e: sum(len(w) for w, _ in plan[e]) for e in ENGS}


S = 2048
D = 1024
L = 4
NMEM = 256
DFF = 2816
ALPHA = (2.0 * L) ** 0.25
LN_EPS = 1e-5
NEG = -30000.0

CONST_LAYOUT = {}
_off = 0
for _n, _w in [("ident", 128), ("tri_le", 128), ("tri_lt", 64), ("negmask", 128), ("selneg", 4 * 128),
               ("ones", 128), ("sel_even", 64), ("sel_odd", 128), ("tri_gt", 64), ("blk64", 128), ("mask2", 128)]:
    CONST_LAYOUT[_n] = (_off, _w)
    _off += _w
NCONST = _off


def make_consts():
    c = np.zeros((128, NCONST), np.float32)
    def put(n, a):
        o, w = CONST_LAYOUT[n]
        c[:a.shape[0], o:o + w] = a
    i = np.arange(128)
    put("ident", np.eye(128, dtype=np.float32))
    put("tri_le", (i[:, None] <= i[None, :]).astype(np.float32))
    put("tri_lt", (i[:, None] < i[None, :]).astype(np.float32)[:, :64])
    put("tri_gt", (i[:, None] > i[None, :]).astype(np.float32)[:, :64])
    put("negmask", np.where(i[:, None] <= i[None, :], 0.0, NEG).astype(np.float32))
    sn = np.zeros((12, 4 * 128), np.float32)
    for h in range(4):
        for j in range(3):
            sn[j * 4 + h, h * 128:(h + 1) * 128] = -1.0
    put("selneg", sn)
    put("ones", np.ones((128, 128), np.float32))
    se = np.zeros((65, 64), np.float32); se[64, :] = 1.0
    put("sel_even", se)
    so = np.zeros((128, 128), np.float32); so[0, 64:128] = 1.0
    put("sel_odd", so)
    put("blk64", ((i[:, None] // 64) == (i[None, :] // 64)).astype(np.float32) / 64.0)
    put("mask2", ((i[:, None] <= i[None, :]) & ((i[:, None] // 64) == (i[None, :] // 64))).astype(np.float32))
    return c


def col_layout():
    lay = {}
    off = 0
    def add(n, w):
        nonlocal off
        lay[n] = (off, w)
        off += w
    add("mem_ln_g", 8); add("mem_ln_b", 8)
    for n in ("ln1_g", "ln1_b", "ln2_g", "ln2_b", "ln3_g", "ln3_b"):
        add(n, 8)
    add("ffn_up_b", 44); add("ffn_conv_b", 44); add("ffn_conv", 132)
    add("fox_fb", 1)
    add("gla_a_b", 2)
    add("gla_norm_g", 2)
    add("rw_mu", 8)
    add("rw_mu64_", 16)
    add("rw_h64_", 4 * 9)
    for n in list(lay.keys()):
        for l in range(L):
            lay["%s%d" % (n, l)] = lay[n]
    return lay, off


COL_LAYOUT, NCOL = col_layout()


def chunkcols(v, p=128):
    return np.ascontiguousarray(v.reshape(-1, p).T)


def make_colp(inp):
    call = np.zeros((L, 128, NCOL), np.float32)
    for l in range(L):
        c = call[l]
        def put(n, a):
            o, w = COL_LAYOUT[n]
            assert a.shape[1] == w, (n, a.shape, w)
            c[:a.shape[0], o:o + w] = a
        put("mem_ln_g", chunkcols(inp["mem_ln_g"])); put("mem_ln_b", chunkcols(inp["mem_ln_b"]))
        for n in ("ln1_g", "ln1_b", "ln2_g", "ln2_b", "ln3_g", "ln3_b"):
            put(n, chunkcols(inp[n][l]))
        put("ffn_up_b", chunkcols(inp["ffn_up_b"][l]))
        put("ffn_conv_b", chunkcols(inp["ffn_conv_b"][l]))
        put("ffn_conv", np.concatenate([chunkcols(inp["ffn_conv"][l][j]) for j in range(3)], 1))
        put("fox_fb", inp["fox_fb"][l].reshape(4, 1))
        put("gla_a_b", chunkcols(inp["gla_a_b"][l], 64))
        put("gla_norm_g", chunkcols(inp["gla_norm_g"][l]))
        put("rw_mu", chunkcols(inp["rw_mu"][l]))
        put("rw_mu64_", chunkcols(inp["rw_mu"][l], 64))
        h64 = np.zeros((64, 36), np.float32)
        for h in range(4):
            sl = slice(h * 64, (h + 1) * 64)
            for j, nm in enumerate(["rw_w0", "rw_a0", "rw_kk", "rw_ka", None, "rw_lnx_g", "rw_lnx_b"]):
                if nm is not None:
                    h64[:, h * 9 + j] = inp[nm][l][sl]
            h64[:, h * 9 + 4] = inp["rw_rk"][l][h]
        put("rw_h64_", h64)
    return call


ROW_LAYOUT = {}
_off = 0
for _n, _w in [("sg_ln_g", 256), ("sg_ln_b", 256), ("sgb", 256)]:
    ROW_LAYOUT[_n] = (_off, _w)
    _off += _w
NROW = _off


def make_rowp(inp):
    r = np.zeros((L, 128, NROW), np.float32)
    for l in range(L):
        def put(n, a):
            o, w = ROW_LAYOUT[n]
            r[l, :, o:o + w] = a
        put("sg_ln_g", np.tile(inp["sg_ln_g"][l][None], (128, 1)))
        put("sg_ln_b", np.tile(inp["sg_ln_b"][l][None], (128, 1)))
        sgb = np.zeros((128, 2, 128), np.float32)
        for p in range(128):
            for pair in range(2):
                sgb[p, pair] = inp["sg_b"][l][pair * 2 + p // 64]
        put("sgb", sgb.reshape(128, 256))
    return r


class WV:
    def __init__(self, tile, kc, cols):
        self.tile = tile
        self.kc = kc
        self.cols = cols
        self.ap = tile.h[:, 0:kc * cols].rearrange("p (k c) -> p k c", k=kc)

    def __getitem__(self, idx):
        return V(self.ap[idx], (self.tile.name,))


class MK:
    def __init__(self, nc, nlayers=L, stages=("sg", "fox", "gla", "rw", "ln1", "ca", "ln2", "ffn", "ln3"), dbg=()):
        self.nc = nc
        self.nlayers = nlayers
        self.stages = stages
        self.dbg = dbg
        self.P = Prog(nc)
        self.dbg_out = {}

    def decl(self):
        nc = self.nc
        di = lambda n, shp: nc.dram_tensor(n, list(shp), F32, kind="ExternalInput").ap()
        self.h_xT = di("xT", [D, S])
        self.h_memT = di("memT", [D, NMEM])
        self.h_consts = di("consts", [128, NCONST])
        self.h_colp = di("colp", [L, 128, NCOL])
        self.h_rowp = di("rowp", [L, 128, NROW])
        self.h_w_in = di("w_in", [L, D, 3092])
        self.h_w_out = di("w_out", [L, D, D])
        self.h_sgwT = di("sgwT", [L, 128, 4, 128])
        self.h_gla_a_up = di("gla_a_up", [L, 16, 128])
        self.h_rw_w2 = di("rw_w2", [L, 64, 256])
        self.h_rw_a2 = di("rw_a2", [L, 64, 256])
        self.h_rw_g2 = di("rw_g2", [L, 128, 256])
        self.h_ca_wq = di("ca_wq", [L, D, D]); self.h_ca_wk = di("ca_wk", [L, D, D])
        self.h_ca_wv = di("ca_wv", [L, D, D]); self.h_ca_wo = di("ca_wo", [L, D, D])
        self.h_ffn_up = di("ffn_up", [L, D, 2 * DFF]); self.h_ffn_down = di("ffn_down", [L, DFF, D])
        self.h_outT = nc.dram_tensor("outT", [D, S], F32, kind="ExternalOutput").ap()

    def hv(self, ap, key):
        return V(ap, (key,))

    def alloc(self):
        P = self.P
        self.xT = P.sb("xT", [128, 8, S], F32)
        self.xb = P.sb("xb", [128, 8, S], BF16)
        self.cur_xb = None
        self.yT = P.sb("yT", [128, 2, S], BF16)
        self.consts = P.sb("consts", [128, NCONST], F32)
        self.colp = P.sb("colp", [128, NCOL], F32)
        self.ident_bf = P.sb("ident_bf", [128, 128], BF16)
        self.ones_s = P.sb("ones_s", [128, 128], BF16)
        self.blk64_bf = P.sb("blk64_bf", [128, 128], BF16)
        self.wb = [P.sb("wb%d" % i, [128, 4096], BF16) for i in range(2)]
        self.pb = [P.ps("pb%d" % i, [128, 512], F32) for i in range(8)]
        self.t512 = [P.sb("t512_%d" % i, [128, 512], F32) for i in range(5)]
        self.b512 = [P.sb("b512_%d" % i, [128, 512], BF16) for i in range(4)]
        self.st512 = [P.sb("st512_%d" % i, [128, 512], F32) for i in range(2)]
        self.scr = P.sb("scr", [128, 20480], BF16)
        self._ffh_i = 0
        self._ps_i = 0
        self._wb_i = 0
        self._wo_i = 0
        self._t_i = 0
        self._b_i = 0

    def load_xb(self, tb, eng="pool"):
        xb = self.xb
        class _B:
            def __getitem__(s_, idx):
                p, k, c = idx
                return V(xb.h[p, k, slice(tb * 512 + c.start, tb * 512 + c.stop)], (xb.name,))
        self.cur_xb = _B()
        return self.cur_xb

    def sub256(self, t):
        class _S:
            name = t.name
            class _H:
                def __getitem__(s2, idx):
                    p, c = idx
                    c = slice(c.start or 0, 256 if c.stop is None else c.stop)
                    return t.h[p, c]
            h = _H()
            def __getitem__(s_, idx):
                if not isinstance(idx, tuple):
                    idx = (idx, slice(0, 256))
                p, c = idx
                c = slice(c.start or 0, 256 if c.stop is None else c.stop)
                return V(t.h[p, c], (t.name,))
        return _S()

    def nextps(self):
        p = self.pb[self._ps_i % 6]
        self._ps_i += 1
        return p

    def nextacc(self):
        self._acc_i = getattr(self, "_acc_i", 0) + 1
        return self.pb[6 + self._acc_i % 2]

    def nt(self):
        t = self.t512[self._t_i % 5]
        self._t_i += 1
        return t

    def nf(self):
        return self.nt()

    def nb(self):
        t = self.b512[self._b_i % 4]
        self._b_i += 1
        return t

    def C(self, name, rows=128, c0=0, c1=None):
        o, w = CONST_LAYOUT[name]
        if c1 is None:
            c1 = w
        return self.consts[0:rows, o + c0:o + c1]

    def col(self, name, j, rows=128):
        o, w = COL_LAYOUT[name]
        return self.colp[0:rows, o + j:o + j + 1]

    def row(self, name, c0=0, c1=None):
        o, w = ROW_LAYOUT[name]
        if c1 is None:
            c1 = w
        return V(self.scr.h[:, :].bitcast(F32)[:, 2560 + o + c0:2560 + o + c1], ("sg_rowp",))

    def load_w(self, hbm_ap, kc, cols, key, ring="wb"):
        P = self.P
        t = self.wb[self._wb_i % 2]; self._wb_i += 1
        wv = WV(t, kc, cols)
        src = hbm_ap.rearrange("(k p) c -> p k c", p=128)
        step = max(1, (2048 // cols) if cols <= 2048 else 1)
        k = 0
        while k < kc:
            k2 = min(kc, k + step)
            P.dma(V(wv.ap[:, k:k2, :], (t.name,)), V(src[:, k:k2, :], (key,)), eng="pool")
            k = k2
        return wv

    def scr_phase(self, new_keys):
        old = getattr(self, "_scr_keys", [])
        if not hasattr(self, "_dummy"):
            self._dummy = self.P.sb("phase_dummy", [128, 8], F32)
        d = self._dummy
        self.P.add("pool", lambda e: e.memset(d.h[:, :], 0.0), [V(None, tuple(old))], [V(None, tuple(new_keys) + (d.name,))])
        self._scr_keys = list(new_keys)

    def debug_dump(self, name, view, shape):
        if name not in self.dbg:
            return
        h = self.nc.dram_tensor("dbg_" + name, list(shape), view.ap.dtype, kind="ExternalOutput").ap()
        self.P.dma(V(h, ("dbg_" + name,)), view)
        self.dbg_out[name] = shape

    def proj_fm(self, w, c0, M, tb, evac, xsrc=None, ncol=512):
        P = self.P
        ps = self.nextps()
        for kc in range(w.kc):
            if xsrc is None:
                rhs = self.cur_xb[:, kc, 0:ncol]
            else:
                rhs = xsrc[:, kc, tb * ncol:(tb + 1) * ncol]
            P.mm(ps[0:M, 0:ncol], w[:, kc, c0:c0 + M], rhs,
                 start=(kc == 0), stop=(kc == w.kc - 1))
        evac(ps)

    def proj_tm(self, w, c0, N, t0, evac, ntok=128):
        P = self.P
        ps = self.nextps()
        for kc in range(w.kc):
            P.mm(ps[0:ntok, 0:N], self.cur_xb[:, kc, (t0 % 512):(t0 % 512) + ntok], w[:, kc, c0:c0 + N],
                 start=(kc == 0), stop=(kc == w.kc - 1))
        evac(ps)

    def acc_out(self, hbm_w, yT, nck, key, first):
        P = self.P
        w = self.load_w(hbm_w, nck, 1024, key, ring="wb")
        for tb in range(4):
            for o in range(8):
                ps = self.nextps()
                for c in range(nck):
                    P.mm(ps[:, :], w[:, c, o * 128:(o + 1) * 128], yT[:, c, tb * 512:(tb + 1) * 512],
                         start=(c == 0), stop=(c == nck - 1))
                xv = self.xT[:, o, tb * 512:(tb + 1) * 512]
                if first:
                    P.stt(xv, xv, ALPHA, ps[:, :], ALU.mult, ALU.add)
                else:
                    P.tt(xv, xv, ps[:, :], ALU.add)

    def layernorm(self, gname, bname, src=None, dst32=None, dstb=None, ntok=S, eps=LN_EPS):
        P = self.P
        src = self.xT if src is None else src
        dst32 = self.xT if dst32 is None else (None if dst32 is False else dst32)
        dstb = self.xb if dstb is None else dstb
        nblk = (ntok + 511) // 512
        for tb in range(nblk):
            w = min(512, ntok - tb * 512)
            sl = slice(tb * 512, tb * 512 + w)
            psm = self.nextps(); psq = self.nextps()
            for c in range(8):
                xb_ = self.nb(); sq = self.nb()
                P.copy(xb_[:, 0:w], src[:, c, sl], eng="dve")
                P.act(sq[:, 0:w], src[:, c, sl], AF.Square)
                P.mm(psm[:, 0:w], self.ones_s[:, :], xb_[:, 0:w], start=(c == 0), stop=(c == 7))
                P.mm(psq[:, 0:w], self.ones_s[:, :], sq[:, 0:w], start=(c == 0), stop=(c == 7))
            mean = self.st512[0]; rstd = self.st512[1]
            P.copy(mean[:, 0:w], psm[:, 0:w])
            msq = self.nt()
            P.tt(msq[:, 0:w], mean[:, 0:w], mean[:, 0:w], ALU.mult)
            P.tt(msq[:, 0:w], psq[:, 0:w], msq[:, 0:w], ALU.subtract)
            P.act(msq[:, 0:w], msq[:, 0:w], AF.Ln, bias=eps)
            P.act(rstd[:, 0:w], msq[:, 0:w], AF.Exp, scale=-0.5)
            for c in range(8):
                u = self.nt()
                P.tt(u[:, 0:w], src[:, c, sl], mean[:, 0:w], ALU.subtract)
                P.stt(u[:, 0:w], u[:, 0:w], self.col(gname, c), rstd[:, 0:w], ALU.mult, ALU.mult)
                if dst32 is not None:
                    P.act(dst32[:, c, sl], u[:, 0:w], AF.Identity, bias=self.col(bname, c))
                if dstb is not None:
                    P.act(dstb[:, c, sl], u[:, 0:w], AF.Identity, bias=self.col(bname, c))

    def mixer_sg(self, l):
        P = self.P
        w = self.load_w(self.h_w_in[l][:, 0:512], 8, 512, "h_w_in")
        if not hasattr(self, "sg_t"):
            self.sg_t = dict(
                wmT=P.sb("sg_wmT", [128, 4, 128], BF16),
                stat=[P.sb("sg_stat%d" % i, [128, 16], F32) for i in range(2)],
                vnp=[P.sb("sg_vnp%d" % i, [128, 2, 2, 128], BF16) for i in range(2)],
            )
            for v_ in self.sg_t["vnp"]:
                P.memset(v_[:], 0.0, eng="pool")
        self.scr_phase(["sg_u0", "sg_u1", "sg_wraw", "sg_rowp"])
        P.dma(V(self.scr.h[:, :].bitcast(F32)[:, 2560:2560 + NROW], ("sg_rowp",)), self.hv(self.h_rowp[l], "h_rowp"))
        scr32 = self.scr.h[:, :].bitcast(F32)
        class _C:
            def __init__(s_, ap, key):
                s_.ap = ap; s_.key = key
            def __getitem__(s_, idx):
                return V(s_.ap[idx], (s_.key,))
        usb = [_C(scr32[:, i * 1024:(i + 1) * 1024].rearrange("p (c t) -> p c t", c=2), "sg_u%d" % i) for i in range(2)]
        wraw = _C(scr32[:, 2048:2560].rearrange("p (h t) -> p h t", h=4), "sg_wraw")
        T_ = self.sg_t
        P.dma(wraw[:], self.hv(self.h_sgwT[l], "h_sgwT"))
        tri = V(self.C("tri_le").ap.unsqueeze(1).to_broadcast([128, 4, 128]), self.consts[:].keys)
        P.tt(T_["wmT"][:], wraw[:], tri, ALU.mult)
        for tb in range(4):
            self.load_xb(tb)
            u_sb = usb[tb % 2]
            for c in range(2):
                self.proj_fm(w, c * 128, 128, tb, lambda ps, c=c: P.copy(u_sb[:, c, :], ps[:, :], eng="act"))
            for n4 in range(4):
                n = tb * 4 + n4
                vs = self.sub256(self.nf()); sq = self.sub256(self.nf()); stat = T_["stat"][n % 2]; vnp = T_["vnp"][n % 2]
                def ev(ps):
                    P.copy(vs[:], ps[:, 0:256], eng="act")
                    P.act(sq[:], ps[:, 0:256], AF.Square)
                self.proj_tm(w, 256, 256, n * 128, ev)
                v3 = V(vs.h[:, :].rearrange("p (h d) -> p h d", h=4), vs[:].keys)
                q3 = V(sq.h[:, :].rearrange("p (h d) -> p h d", h=4), sq[:].keys)
                P.reduce(stat[:, 0:4], v3, ALU.add)
                P.reduce(stat[:, 4:8], q3, ALU.add)
                P.ts(stat[:, 0:4], stat[:, 0:4], 1.0 / 64, ALU.mult)
                P.tt(stat[:, 8:12], stat[:, 0:4], stat[:, 0:4], ALU.mult)
                P.stt(stat[:, 4:8], stat[:, 4:8], 1.0 / 64, stat[:, 8:12], ALU.mult, ALU.subtract)
                P.act(stat[:, 4:8], stat[:, 4:8], AF.Ln, bias=LN_EPS)
                P.act(stat[:, 12:16], stat[:, 4:8], AF.Exp, scale=-0.5)
                for h in range(4):
                    P.ts(sq[:, h * 64:(h + 1) * 64], vs[:, h * 64:(h + 1) * 64], stat[:, h:h + 1], ALU.subtract,
                         stat[:, 12 + h:13 + h], ALU.mult)
                P.tt(sq[:], sq[:], self.row("sg_ln_g"), ALU.mult)
                s4 = sq.h[:, :].rearrange("p (a b d) -> p a b d", a=2, b=2)
                rb = self.row("sg_ln_b").ap.rearrange("p (a b d) -> p a b d", a=2, b=2)
                for hh in range(2):
                    P.tt(V(vnp.h[:, :, hh, hh * 64:(hh + 1) * 64], vnp[:].keys), V(s4[:, :, hh, :], sq[:].keys),
                         V(rb[:, :, hh, :], ("sg_rowp",)), ALU.add)
                for pair in range(2):
                    ps = self.nextps()
                    for hh in range(2):
                        P.mm(ps[:, 0:128], V(vnp.h[:, pair, hh, :], vnp[:].keys), T_["wmT"][:, pair * 2 + hh, :],
                             start=(hh == 0), stop=(hh == 1))
                    t2 = self.nf()
                    P.tt(t2[:, 0:128], ps[:, 0:128], self.row("sgb", pair * 128, (pair + 1) * 128), ALU.add)
                    P.tt(self.yT[:, pair, n * 128:(n + 1) * 128], t2[:, 0:128], u_sb[:, pair, n4 * 128:(n4 + 1) * 128], ALU.mult)

    def mixer_fox(self, l):
        P = self.P
        w = self.load_w(self.h_w_in[l][:, 2320:2832], 8, 512, "h_w_in")
        w2 = self.load_w(self.h_w_in[l][:, 2832:3092], 8, 260, "h_w_in")
        if not hasattr(self, "fx"):
            self.fx = dict(
                negcT=P.sb("fx_negcT", [128, 16, 4], F32),
                nfb=P.sb("fx_nfb", [4, 1], F32),
            )
        F = self.fx
        scr = self.scr
        qT = V(scr.h[:, 0:4096].rearrange("p (c t) -> p c t", c=2), ("fx_qT",))
        kT = V(scr.h[:, 4096:8192].rearrange("p (c t) -> p c t", c=2), ("fx_kT",))
        v1ap = scr.h[:, 8192:16384].rearrange("p (n h m) -> p n h m", n=16, h=4)
        v1 = lambda idx: V(v1ap[idx], ("fx_v1",))
        self.scr_phase(["fx_qT", "fx_kT", "fx_v1", "fx_negc"])
        negc_ap = scr.h[:, 16384:20480].bitcast(F32)
        class _N:
            def __getitem__(s_, idx):
                return V(negc_ap[idx], ("fx_negc",))
        F["negc"] = _N(); F["nlf"] = F["negc"]
        P.memset(v1((slice(None),)), 0.0, eng="dve")
        for h in range(4):
            col = 64 if h % 2 == 0 else 0
            P.memset(v1((slice(None), slice(None), h, slice(col, col + 1))), 1.0, eng="dve")
        P.ts(F["nfb"][:], self.col("fox_fb%d" % l, 0, rows=4), -1.0, ALU.mult)
        for tb in range(4):
            self.load_xb(tb)
            sl = slice(tb * 512, (tb + 1) * 512)
            for c in range(2):
                self.proj_fm(w, c * 128, 128, tb,
                             lambda ps, c=c: P.act(V(qT.ap[:, c, sl], qT.keys), ps[:, :], AF.Copy, scale=0.125))
                self.proj_fm(w, 256 + c * 128, 128, tb,
                             lambda ps, c=c: P.copy(V(kT.ap[:, c, sl], kT.keys), ps[:, :], eng="dve"))
            def evf(ps):
                P.act(F["nlf"][0:4, sl], ps[0:4, :], AF.Exp, bias=F["nfb"][:, 0:1], scale=-1.0)
                P.act(F["nlf"][0:4, sl], F["nlf"][0:4, sl], AF.Ln, bias=1.0)
            self.proj_fm(w2, 256, 4, tb, evf)
            for n4 in range(4):
                n = tb * 4 + n4
                def evv(ps, n=n):
                    for h in range(4):
                        col = 0 if h % 2 == 0 else 64
                        P.copy(v1((slice(None), n, h, slice(col, col + 64))), ps[:, h * 64:(h + 1) * 64],
                               eng=("act" if h % 2 else "dve"))
                self.proj_tm(w2, 0, 256, n * 128, evv)
        ones_b = self.C("ones", rows=4, c0=0, c1=1).ap.to_broadcast([4, S])
        P.generic("dve", lambda e: e.tensor_tensor_scan(negc_ap[0:4, :], ones_b, negc_ap[0:4, :], 0.0,
                                                         ALU.mult, ALU.add),
                  [self.consts[:], F["negc"][0:4, :]], [F["negc"][0:4, :]])
        self.debug_dump("fox_negc", F["negc"][0:4, :], [4, S])
        for J in range(16):
            ps = self.nextps()
            P.transpose(ps[:, 0:4], F["negc"][0:4, J * 128:(J + 1) * 128], self.C("ident", rows=4, c0=0, c1=4))
            P.copy(F["negcT"][:, J, :], ps[:, 0:4])
        if "sel12" not in F:
            F["sel12"] = P.sb("fx_sel12", [12, 512], BF16)
            P.copy(F["sel12"][:], self.C("selneg", rows=12))
        cs_t = w2.tile
        cs_ap = cs_t.h[0:12, 0:S]
        cs3 = lambda r0, r1, c0, c1: V(cs_ap[r0:r1, c0:c1], (cs_t.name,))
        for tb in range(4):
            c0 = tb * 512; c1 = c0 + 512
            r1 = self.nt(); r2 = self.nt(); m_ = self.nb(); l_ = self.nb()
            P.copy(cs3(0, 4, c0, c1), F["negc"][0:4, c0:c1])
            P.tt(r1[0:4, :], F["negc"][0:4, c0:c1], cs3(0, 4, c0, c1), ALU.subtract)
            P.copy(m_[0:4, :], r1[0:4, :])
            P.tt(r2[0:4, :], r1[0:4, :], m_[0:4, :], ALU.subtract)
            P.copy(l_[0:4, :], r2[0:4, :])
            P.dma(cs3(4, 8, c0, c1), m_[0:4, :])
            P.dma(cs3(8, 12, c0, c1), l_[0:4, :])
        for h in range(4):
            c = h // 2
            p0 = (h % 2) * 64
            even = (h % 2 == 0)
            for Q in range(4):
                ops_ = self.nextacc()
                nJ = 4 * Q + 4
                def issue_lg(J):
                    c_lo = max(0, (J - 4 * Q) * 128)
                    lg = self.nextps()
                    P.mm(lg[:, c_lo:512], V(kT.ap[p0:p0 + 64, c, J * 128:(J + 1) * 128], kT.keys),
                         V(qT.ap[p0:p0 + 64, c, Q * 512 + c_lo:(Q + 1) * 512], qT.keys), start=True, stop=False)
                    P.mm(lg[:, c_lo:512], F["sel12"][:, h * 128:(h + 1) * 128],
                         cs3(0, 12, Q * 512 + c_lo, (Q + 1) * 512), start=False, stop=True)
                    if J >= 4 * Q:
                        P.tt(lg[:, c_lo:c_lo + 128], lg[:, c_lo:c_lo + 128], self.C("negmask"), ALU.add)
                    return lg, c_lo
                def finish(J, lg, c_lo):
                    pT = self.nb()
                    P.act(pT[:, c_lo:512], lg[:, c_lo:512], AF.Exp, bias=F["negcT"][:, J, h:h + 1])
                    M = 65 if even else 128
                    P.mm(ops_[0:M, c_lo:512], v1((slice(None), J, h, slice(0, M))), pT[:, c_lo:512],
                         start=(J == 0), stop=(J == nJ - 1))
                prev = issue_lg(0)
                for J in range(nJ):
                    nxt = issue_lg(J + 1) if J + 1 < nJ else None
                    finish(J, *prev)
                    prev = nxt
                osb = self.nt()
                if even:
                    P.copy(osb[0:65, :], ops_[0:65, :], eng="act")
                    P.recip(osb[64:65, :], osb[64:65, :])
                    bp = self.nextps()
                    P.mm(bp[0:64, :], self.C("sel_even", rows=65), osb[0:65, :])
                    P.tt(self.yT[0:64, c, Q * 512:(Q + 1) * 512], osb[0:64, :], bp[0:64, :], ALU.mult)
                else:
                    P.copy(osb[:, :], ops_[:, :], eng="act")
                    P.recip(osb[0:1, :], osb[0:1, :])
                    bp = self.nextps()
                    P.mm(bp[:, :], self.C("sel_odd"), osb[:, :])
                    P.tt(self.yT[64:128, c, Q * 512:(Q + 1) * 512], osb[64:128, :], bp[64:128, :], ALU.mult)


    def mixer_gla(self, l):
        P = self.P
        w1 = self.load_w(self.h_w_in[l][:, 1536:2048], 8, 512, "h_w_in")
        w2 = self.load_w(self.h_w_in[l][:, 2048:2320], 8, 272, "h_w_in")
        if not hasattr(self, "gl"):
            self.gl = dict(
                aup=P.sb("gl_aup", [16, 128], BF16),
                nab=P.sb("gl_nab", [64, 2], F32),
                gps=P.sb("gl_gps", [64, 32], F32),
                bl=P.sb("gl_bl", [64, 32], F32),
                dec=P.sb("gl_dec", [64, 2, 32], F32),
                S=[P.sb("gl_S%d" % g, [64, 128], F32) for g in range(2)],
                Sbf=[[P.sb("gl_Sbf%d_%d" % (g, i), [64, 128], BF16) for i in range(4)] for g in range(2)],
                osb=[P.sb("gl_osb%d" % i, [128, 128], F32) for i in range(2)],
            )
        G = self.gl
        scr = self.scr
        self.scr_phase(["gl_qe", "gl_ke", "gl_v", "gl_ketm", "gl_G", "gl_adT"])
        class _C:
            def __init__(s_, ap, key):
                s_.ap = ap; s_.key = key
            def __getitem__(s_, idx):
                return V(s_.ap[idx], (s_.key,))
        qe = _C(scr.h[:, 0:4096].rearrange("p (g t) -> p g t", g=2), "gl_qe")
        ke = _C(scr.h[:, 4096:8192].rearrange("p (g t) -> p g t", g=2), "gl_ke")
        v128 = _C(scr.h[:, 8192:12288].rearrange("p (b c) -> p b c", b=16), "gl_v")
        ketm = _C(scr.h[:, 12288:14336].rearrange("p (b c) -> p b c", b=16), "gl_ketm")
        Gt = _C(scr.h[:, 14336:18432].bitcast(F32), "gl_G")
        Gt3 = _C(scr.h[:, 14336:18432].bitcast(F32).rearrange("p (n c) -> p n c", n=32), "gl_G")
        adT = _C(scr.h[:, 18432:20480], "gl_adT")
        P.dma(G["aup"][:], self.hv(self.h_gla_a_up[l], "h_gla_a_up"), eng="pool")
        P.ts(G["nab"][:], V(self.colp.h[0:64, COL_LAYOUT["gla_a_b%d" % l][0]:COL_LAYOUT["gla_a_b%d" % l][0] + 2], self.colp[:].keys),
             -1.0, ALU.mult)
        for tb in range(4):
            self.load_xb(tb)
            sl = slice(tb * 512, (tb + 1) * 512)
            self.proj_fm(w2, 256, 16, tb, lambda ps: P.copy(adT[0:16, sl], ps[0:16, :], eng="act"))
            for c in range(2):
                def evg(ps, c=c):
                    t = self.nt()
                    P.act(t[:], ps[:, :], AF.Silu)
                    P.ts(self.yT[:, c, sl], t[:], self.col("gla_norm_g%d" % l, c), ALU.mult)
                self.proj_fm(w2, c * 128, 128, tb, evg)
            for b4 in range(4):
                bk = tb * 4 + b4
                self.proj_tm(w1, 256, 256, bk * 128, lambda ps, bk=bk: P.copy(v128[:, bk, :], ps[:, 0:256], eng="act"))
        import os
        stop = int(os.environ.get('GLA_STOP', '99'))
        if stop <= 1:
            return
        ones_b = self.C("ones", rows=64, c0=0, c1=1).ap.to_broadcast([64, S])
        for g in range(2):
            for tb in range(4):
                sl = slice(tb * 512, (tb + 1) * 512)
                ps = self.nextps()
                P.mm(ps[0:64, :], G["aup"][:, g * 64:(g + 1) * 64], adT[0:16, sl])
                P.act(Gt[0:64, sl], ps[0:64, :], AF.Exp, bias=G["nab"][:, g:g + 1], scale=-1.0)
                P.act(Gt[0:64, sl], Gt[0:64, sl], AF.Ln, bias=1.0)
            P.generic("dve", lambda e: e.tensor_tensor_scan(Gt.ap[0:64, :], ones_b, Gt.ap[0:64, :], 0.0, ALU.mult, ALU.add),
                      [self.consts[:], Gt[0:64, :]], [Gt[0:64, :]])
            if stop <= 2:
                continue
            P.memset(G["gps"][:, 0:1], 0.0)
            P.copy(G["gps"][:, 1:32], Gt3[0:64, 0:31, 63])
            P.tt(Gt3[0:64, :, :], Gt3[0:64, :, :], V(G["gps"].h[:, :].unsqueeze(2).to_broadcast([64, 32, 64]), G["gps"][:].keys),
                 ALU.subtract)
            P.copy(G["bl"][:], Gt3[0:64, :, 63])
            P.act(G["dec"][:, g, :], G["bl"][:], AF.Exp, scale=-1.0 / 16)
            for tb in range(4):
                sl = slice(tb * 512, (tb + 1) * 512)
                self.load_xb(tb)
                Eb = self.nt(); Enb = self.nt()
                P.act(Eb[0:64, :], Gt[0:64, sl], AF.Exp, scale=-1.0 / 16)
                P.act(Enb[0:64, :], Gt[0:64, sl], AF.Exp, scale=1.0 / 16)
                self.proj_fm(w1, g * 64, 64, tb,
                             lambda ps: P.stt(qe[0:64, g, sl], ps[0:64, :], 32.0 ** -0.5, Eb[0:64, :], ALU.mult, ALU.mult))
                self.proj_fm(w1, 128 + g * 64, 64, tb,
                             lambda ps: P.tt(ke[0:64, g, sl], ps[0:64, :], Enb[0:64, :], ALU.mult))
            if stop <= 3:
                continue
            for bk in range(16):
                ps = self.nextps()
                pbf = V(ps.h[:, :].bitcast(BF16), ps[:].keys)
                P.transpose(V(pbf.ap[:, 0:64], pbf.keys), ke[0:64, g, bk * 128:(bk + 1) * 128], self.ident_bf[0:64, 0:64])
                P.copy(ketm[:, bk, g * 64:(g + 1) * 64], V(pbf.ap[:, 0:64], pbf.keys), eng="act")
        self.debug_dump("gl_qe", qe[0:64, :, :], [64, 2, S])
        self.debug_dump("gl_ke", ke[0:64, :, :], [64, 2, S])
        if stop <= 4:
            return
        for g in range(2):
            P.memset(G["S"][g][:], 0.0)
            P.memset(G["Sbf"][g][0][:], 0.0)
        for bk in range(16):
            for g in range(2):
                attm = []
                for hh in range(2):
                    ps = self.nextps()
                    P.mm(ps[:, 0:128], ke[hh * 32:hh * 32 + 32, g, bk * 128:(bk + 1) * 128],
                         qe[hh * 32:hh * 32 + 32, g, bk * 128:(bk + 1) * 128])
                    am = self.nb()
                    P.tt(am[:, 0:128], ps[:, 0:128], self.C("mask2"), ALU.mult)
                    attm.append(am)
                if stop <= 5:
                    continue
                for cc in range(2):
                    n = 2 * bk + cc
                    if n == 31:
                        break
                    psU = self.nextps()
                    r0 = cc * 64
                    P.mm(psU[0:64, 0:128], ketm[r0:r0 + 64, bk, g * 64:(g + 1) * 64], v128[r0:r0 + 64, bk, g * 128:(g + 1) * 128])
                    P.tt(G["S"][g][:], G["S"][g][:], psU[0:64, 0:128], ALU.add)
                    P.ts(G["S"][g][:], G["S"][g][:], G["dec"][:, g, n:n + 1], ALU.mult)
                    P.copy(G["Sbf"][g][(n + 1) % 4][:], G["S"][g][:], eng="act")
                if stop <= 6:
                    continue
                psO = self.nextps()
                for hh in range(2):
                    c0 = hh * 128
                    P.mm(psO[:, c0:c0 + 128], v128[:, bk, g * 128:(g + 1) * 128], attm[hh][:, 0:128], start=True, stop=False)
                    for cc in range(2):
                        n = 2 * bk + cc
                        P.mm(psO[:, c0 + cc * 64:c0 + (cc + 1) * 64], G["Sbf"][g][n % 4][hh * 32:hh * 32 + 32, :],
                             qe[hh * 32:hh * 32 + 32, g, n * 64:(n + 1) * 64], start=False, stop=(cc == 1))
                if stop <= 7:
                    continue
                osb = G["osb"][g]
                osq = self.nb()
                for hh in range(2):
                    r0 = hh * 64
                    var = os.environ.get('GLA_VAR', 'ab')
                    if 'a' in var:
                        P.copy(osb[r0:r0 + 64, :], psO[r0:r0 + 64, hh * 128:(hh + 1) * 128], eng="dve")
                    if 'b' in var:
                        P.act(osq[r0:r0 + 64, 0:128], psO[r0:r0 + 64, hh * 128:(hh + 1) * 128], AF.Square)
                if stop <= 8:
                    continue
                pss = self.nextps()
                P.mm(pss[:, 0:128], self.blk64_bf[:, :], osq[:, 0:128])
                if stop <= 9:
                    continue
                rs = self.nf()
                P.act(rs[:, 0:128], pss[:, 0:128], AF.Ln, bias=1e-5)
                P.act(rs[:, 0:128], rs[:, 0:128], AF.Exp, scale=-0.5)
                if stop <= 10:
                    continue
                P.tt(osb[:, :], osb[:, :], rs[:, 0:128], ALU.mult)
                if stop <= 11:
                    continue
                yv = self.yT[:, g, bk * 128:(bk + 1) * 128]
                P.tt(yv, osb[:, :], yv, ALU.mult)


    def mixer_rw(self, l):
        P = self.P
        wA = self.load_w(self.h_w_in[l][:, 512:1024], 8, 512, "h_w_in")
        wB = self.load_w(self.h_w_in[l][:, 1024:1536], 8, 512, "h_w_in")
        if not hasattr(self, "rw"):
            self.rw = dict(
                sm=P.sb("rw_sm", [128, 512], BF16),
                omm=P.sb("rw_omm", [128, 24], F32),
                nwa=P.sb("rw_nwa", [64, 8], F32),
                carry=P.sb("rw_carry", [128, 8], F32),
                maskq=P.sb("rw_maskq", [64, 128], F32),
                PC=P.sb("rw_PC", [64, 32], F32),
                gs=P.sb("rw_gs", [64, 4], F32),
                S32=P.sb("rw_S32", [64, 64], F32),
                Sb=[P.sb("rw_Sb%d" % i, [64, 64], BF16) for i in range(2)],
                Nn=[P.sb("rw_Nn%d" % i, [64, 64], BF16) for i in range(2)],
                XN=[[P.sb("rw_XN%d_%d" % (i, j), [64, 128], BF16) for j in range(2)] for i in range(2)],
                Wt=[[P.sb("rw_Wt%d_%d" % (i, j), [64, 64], BF16) for j in range(2)] for i in range(2)],
                Wf=[P.sb("rw_Wf%d" % i, [64, 64], BF16) for i in range(4)],
                RU=[P.sb("rw_RU%d" % i, [64, 128], BF16) for i in range(2)],
            )
            P.copy(self.rw["maskq"][:, 0:64], self.C("tri_le", rows=64, c0=0, c1=64))
            P.copy(self.rw["maskq"][:, 64:128], self.C("tri_lt", rows=64, c0=0, c1=64))
        R = self.rw
        scr = self.scr
        self.scr_phase(["rw_twa", "rw_sgd", "rw_KB", "rw_RA", "rw_VT", "rw_bv"])
        class _C:
            def __init__(s_, ap, key):
                s_.ap = ap; s_.key = key
            def __getitem__(s_, idx):
                return V(s_.ap[idx], (s_.key,))
        twa = _C(scr.h[:, 0:2048], "rw_twa")
        sgd = _C(scr.h[:, 2048:4096], "rw_sgd")
        KB = _C(scr.h[:, 4096:8192].rearrange("p (n c) -> p n c", n=32), "rw_KB")
        RA = _C(scr.h[:, 8192:12288].rearrange("p (n c) -> p n c", n=32), "rw_RA")
        VT = _C(scr.h[:, 12288:14336], "rw_VT")
        bv = _C(scr.h[:, 14336:16384], "rw_bv")
        y1 = self.yT.h[0:64, 1, :]
        class _A:
            def __init__(s_, c0, w, key):
                s_.c0 = c0; s_.w = w; s_.key = key
            def __getitem__(s_, idx):
                p, c = idx
                c = slice(s_.c0 + (c.start or 0), s_.c0 + (s_.w if c.stop is None else c.stop))
                return V(y1[:, c], (s_.key,))
        RQA = [_A(i * 128, 128, "rwq_QA%d" % i) for i in range(4)]
        RQB = [_A(512 + i * 128, 128, "rwq_QB%d" % i) for i in range(4)]
        RTM = [_A(1024 + i * 192, 192, "rwq_TM%d" % i) for i in range(4)]
        ring_keys = tuple(x.key for x in RQA + RQB + RTM)
        P.add("pool", lambda e: e.memset(self._dummy.h[:, :], 0.0), [V(None, ("yT",))], [V(None, ring_keys + (self._dummy.name,))])
        P.dma(R["sm"][0:64, 0:256], self.hv(self.h_rw_w2[l], "h_rw_w2"), eng="pool")
        P.dma(R["sm"][64:128, 0:256], self.hv(self.h_rw_a2[l], "h_rw_a2"), eng="pool")
        P.dma(R["sm"][:, 256:512], self.hv(self.h_rw_g2[l], "h_rw_g2"), eng="pool")
        o_mu, _ = COL_LAYOUT["rw_mu%d" % l]; o_mu64, _ = COL_LAYOUT["rw_mu64_%d" % l]
        mu128 = lambda c: self.colp[:, o_mu + c:o_mu + c + 1]
        mu64 = lambda c: self.colp[0:64, o_mu64 + c:o_mu64 + c + 1]
        P.ts(R["omm"][:, 0:8], self.colp[:, o_mu:o_mu + 8], -1.0, ALU.mult, 1.0, ALU.add)
        P.ts(R["omm"][0:64, 8:24], self.colp[0:64, o_mu64:o_mu64 + 16], -1.0, ALU.mult, 1.0, ALU.add)
        o_h, _ = COL_LAYOUT["rw_h64_%d" % l]
        hcol = lambda h, j: self.colp[0:64, o_h + h * 9 + j:o_h + h * 9 + j + 1]
        for h in range(4):
            P.ts(R["nwa"][:, 2 * h:2 * h + 1], hcol(h, 0), -1.0, ALU.mult)
            P.ts(R["nwa"][:, 2 * h + 1:2 * h + 2], hcol(h, 1), -1.0, ALU.mult)
        pool_tiles = self.t512 + self.st512
        def tmp(i, rows=64):
            t = pool_tiles[i // 2]
            c0 = (i % 2) * 256
            class _T:
                name = t.name
                def __getitem__(s_, idx):
                    if not isinstance(idx, tuple):
                        idx = (idx, slice(0, 256))
                    p, c = idx
                    c = slice(c0 + (c.start or 0), c0 + (256 if c.stop is None else c.stop))
                    return V(t.h[p, c], (t.name,))
                def v3(s_, rows_):
                    return V(t.h[0:rows_, c0:c0 + 256].rearrange("p (n c) -> p n c", n=4), (t.name,))
            return _T()

        def shiftmix(ps, M, mu_ap, omm_ap, cslot, out_t, blk):
            zr = pool_tiles[6]
            if blk == 0:
                P.memset(zr[0:M, 0:1], 0.0)
            else:
                P.copy(zr[0:M, 0:1], R["carry"][0:M, cslot:cslot + 1])
            P.copy(zr[0:M, 1:257], ps[0:M, 0:256], eng="act")
            P.copy(R["carry"][0:M, cslot:cslot + 1], zr[0:M, 256:257])
            P.act(out_t[0:M, :], ps[0:M, 0:256], AF.Copy, scale=omm_ap)
            P.stt(out_t[0:M, :], zr[0:M, 0:256], mu_ap, out_t[0:M, :], ALU.mult, ALU.add)

        class _XB:
            def __init__(s_, xb, t0):
                s_.xb = xb; s_.t0 = t0
            def __getitem__(s_, idx):
                p, k, c = idx
                return V(s_.xb.h[p, k, slice(s_.t0 + c.start, s_.t0 + c.stop)], (s_.xb.name,))

        for blk in range(8):
            self.cur_xb = _XB(self.xb, blk * 256)
            sl = slice(blk * 256, (blk + 1) * 256)
            z = tmp(0)
            self.proj_fm(wB, 256, 128, 0, lambda ps: shiftmix(ps, 128, mu128(6), R["omm"][:, 6:7], 0, z, blk), ncol=256)
            P.act(twa[0:64, sl], z[0:64, :], AF.Tanh)
            P.copy(twa[64:128, sl], z[64:128, :], eng="pool")
            z2 = tmp(1)
            self.proj_fm(wB, 384, 128, 0, lambda ps: shiftmix(ps, 128, mu128(7), R["omm"][:, 7:8], 1, z2, blk), ncol=256)
            P.act(z2[:, :], z2[:, :], AF.Exp, scale=-1.0)
            P.act(z2[:, :], z2[:, :], AF.Ln, bias=1.0)
            P.act(sgd[:, sl], z2[:, :], AF.Exp, scale=-1.0)
        import os
        rstop = int(os.environ.get("RW_STOP", "99"))
        nheads = int(os.environ.get("RW_HEADS", "4"))
        pending_acc = []
        for h in range(nheads):
            for blk in range(8):
                for _ in range(4):
                    if pending_acc:
                        pending_acc.pop(0)()
                self.cur_xb = _XB(self.xb, blk * 256)
                sl = slice(blk * 256, (blk + 1) * 256)
                n0 = blk * 4
                zr_ = tmp(0); zk_ = tmp(1); zv_ = tmp(2)
                self.proj_fm(wA, h * 64, 64, 0, lambda ps: shiftmix(ps, 64, mu64(h), R["omm"][0:64, 8 + h:9 + h], 2, zr_, blk), ncol=256)
                self.proj_fm(wA, 256 + h * 64, 64, 0, lambda ps: shiftmix(ps, 64, mu64(4 + h), R["omm"][0:64, 12 + h:13 + h], 3, zk_, blk), ncol=256)
                self.proj_fm(wB, h * 64, 64, 0, lambda ps: shiftmix(ps, 64, mu64(8 + h), R["omm"][0:64, 16 + h:17 + h], 4, zv_, blk), ncol=256)
                P.copy(VT[0:64, sl], zv_[0:64, :], eng="pool")
                LD = tmp(3); A = tmp(4); KK = tmp(5); K2 = tmp(6); KKA = tmp(7); Gb = tmp(8)
                Ep = tmp(10); Em = tmp(11); Epm1 = tmp(9); RN = tmp(10)
                st_ = {}
                def l1():
                    ps = self.nextps()
                    P.mm(ps[0:64, 0:256], R["sm"][0:64, h * 64:(h + 1) * 64], twa[0:64, sl])
                    P.act(LD[0:64, :], ps[0:64, 0:256], AF.Exp, bias=R["nwa"][:, 2 * h:2 * h + 1], scale=-1.0)
                def l2():
                    P.act(LD[0:64, :], LD[0:64, :], AF.Ln, bias=1.0)
                def l3():
                    P.act(LD[0:64, :], LD[0:64, :], AF.Exp, bias=-0.5, scale=-1.0)
                def a1():
                    ps = self.nextps()
                    P.mm(ps[0:64, 0:256], R["sm"][64:128, h * 64:(h + 1) * 64], twa[64:128, sl])
                    P.act(A[0:64, :], ps[0:64, 0:256], AF.Exp, bias=R["nwa"][:, 2 * h + 1:2 * h + 2], scale=-1.0)
                def a2():
                    P.act(A[0:64, :], A[0:64, :], AF.Ln, bias=1.0)
                def a3():
                    P.act(A[0:64, :], A[0:64, :], AF.Exp, scale=-1.0)
                def k1():
                    P.ts(KK[0:64, :], zk_[0:64, :], hcol(h, 2), ALU.mult)
                    sq = self.nb(); st_["sq"] = sq
                    P.act(sq[0:64, 0:256], KK[0:64, :], AF.Square)
                def k2():
                    ps = self.nextps()
                    P.mm(ps[0:64, 0:256], self.blk64_bf[0:64, 0:64], st_["sq"][0:64, 0:256])
                    P.act(RN[0:64, :], ps[0:64, 0:256], AF.Ln, bias=1e-24, scale=64.0)
                def k3():
                    P.act(RN[0:64, :], RN[0:64, :], AF.Exp, scale=-0.5)
                    P.tt(KK[0:64, :], KK[0:64, :], RN[0:64, :], ALU.mult)
                def rr(*chains):
                    chains = [list(c) for c in chains]
                    while any(chains):
                        for c in chains:
                            if c:
                                c.pop(0)()
                rr([l1, l2, l3], [a1, a2, a3], [k1, k2, k3])
                ones_b = self.C("ones", rows=64, c0=0, c1=1).ap.to_broadcast([64, 256])
                G3 = Gb.v3(64)
                def s1():
                    P.generic("dve", lambda e, Gb=Gb, LD=LD: e.tensor_tensor_scan(Gb[0:64, :].ap, ones_b, LD[0:64, :].ap, 0.0, ALU.mult, ALU.add),
                              [self.consts[:], LD[0:64, :]], [Gb[0:64, :]])
                    P.memset(R["gs"][:, 0:1], 0.0)
                def s2():
                    P.copy(R["gs"][:, 1:4], V(G3.ap[:, 0:3, 63], G3.keys))
                def s3():
                    P.tt(G3, G3, V(R["gs"].h[:, :].unsqueeze(2).to_broadcast([64, 4, 64]), R["gs"][:].keys), ALU.subtract)
                def s4():
                    P.act(R["PC"][:, n0:n0 + 4], V(G3.ap[:, :, 63], G3.keys), AF.Exp, scale=-1.0)
                    P.tt(Epm1[0:64, :], LD[0:64, :], Gb[0:64, :], ALU.subtract)
                def s5():
                    P.act(Ep[0:64, :], Gb[0:64, :], AF.Exp, scale=-1.0)
                    P.act(Em[0:64, :], Gb[0:64, :], AF.Exp)
                def s6():
                    P.act(Epm1[0:64, :], Epm1[0:64, :], AF.Exp)
                def c1():
                    P.ts(K2[0:64, :], A[0:64, :], -1.0, ALU.add, hcol(h, 3), ALU.mult)
                def c2():
                    P.stt(K2[0:64, :], K2[0:64, :], 1.0, zk_[0:64, :], ALU.add, ALU.mult)
                    P.tt(KKA[0:64, :], KK[0:64, :], A[0:64, :], ALU.mult)
                def c3():
                    pr = self.nb(); st_["pr"] = pr
                    P.stt(pr[0:64, 0:256], zr_[0:64, :], hcol(h, 4), K2[0:64, :], ALU.mult, ALU.mult)
                def c4():
                    ps = self.nextps()
                    P.mm(ps[0:64, 0:256], self.blk64_bf[0:64, 0:64], st_["pr"][0:64, 0:256])
                    P.stt(bv[0:64, sl], ps[0:64, 0:256], 64.0, zv_[0:64, :], ALU.mult, ALU.mult)
                rr([s1, s2, s3, s4, s5, s6], [c1, c2, c3, c4])
                P.tt(V(RA.ap[0:64, n0:n0 + 4, 0:64], ("rw_RA",)), zr_.v3(64), Ep.v3(64), ALU.mult)
                P.tt(V(KB.ap[0:64, n0:n0 + 4, 0:64], ("rw_KB",)), K2.v3(64), Em.v3(64), ALU.mult)
                P.tt(V(KB.ap[0:64, n0:n0 + 4, 64:128], ("rw_KB",)), KKA.v3(64), Em.v3(64), ALU.mult)
                P.stt(V(RA.ap[0:64, n0:n0 + 4, 64:128], ("rw_RA",)), KK.v3(64), -1.0, Epm1.v3(64), ALU.mult, ALU.mult)
            if h == 0:
                self.debug_dump("rw_RA", RA[0:64, :, :], [64, 32, 128])
                self.debug_dump("rw_KB", KB[0:64, :, :], [64, 32, 128])
                self.debug_dump("rw_PC", R["PC"][:, :], [64, 32])
            if rstop <= 1:
                continue
            P.memset(R["S32"][:], 0.0)
            P.memset(R["Sb"][0][:], 0.0)
            self._psY = None
            BS = 2
            def slot(n):
                return n % (2 * BS)
            def pre_rounds(ns):
                rounds = []
                def r0():
                    for n in ns:
                        s4 = slot(n)
                        QA = RQA[s4]; QB = RQB[s4]; TM = RTM[s4]; Nn = R["Nn"][n % BS]
                        Kt = KB[0:64, n, 0:64]; Bt = KB[0:64, n, 64:128]; At = RA[0:64, n, 64:128]
                        ps = self.nextps()
                        P.mm(ps[0:64, 0:128], Kt, RA[0:64, n, :])
                        P.mm(ps[0:64, 128:256], Bt, RA[0:64, n, :])
                        P.mm(ps[0:64, 256:320], At, Bt)
                        P.tt(QA[:, :], ps[0:64, 0:128], R["maskq"][:, :], ALU.mult)
                        P.tt(QB[:, :], ps[0:64, 128:256], R["maskq"][:, :], ALU.mult)
                        P.tt(Nn[:, :], ps[0:64, 256:320], self.C("tri_gt", rows=64, c0=0, c1=64), ALU.mult)
                        ps2 = self.nextps()
                        pbf = lambda a_, b_, ps2=ps2: V(ps2.h[0:64, :].bitcast(BF16)[:, a_:b_], ps2[:].keys)
                        P.transpose(pbf(0, 64), Kt, self.ident_bf[0:64, 0:64])
                        P.transpose(pbf(64, 128), Bt, self.ident_bf[0:64, 0:64])
                        P.transpose(pbf(128, 192), VT[0:64, n * 64:(n + 1) * 64], self.ident_bf[0:64, 0:64])
                        P.copy(TM[:, :], pbf(0, 192), eng="act")
                rounds.append(r0)
                st = {}
                def r1():
                    for n in ns:
                        QB = RQB[slot(n)]
                        W = R["Wt"][n % BS][0]
                        P.tt(W[:, :], QB[:, 64:128], self.C("ident", rows=64, c0=0, c1=64), ALU.add)
                        st[n] = dict(W=W, Xp=QB[:, 64:128], Np=R["Nn"][n % BS][:, :], wi=0)
                rounds.append(r1)
                for lev in range(5):
                    def ra(lev=lev):
                        for n in ns:
                            d = st[n]
                            XN = R["XN"][n % BS][lev % 2]
                            ps = self.nextps()
                            P.mm(ps[0:64, 64:128], d["Xp"], d["Np"])
                            if lev < 4:
                                P.mm(ps[0:64, 0:64], d["Np"], d["Xp"])
                                P.copy(XN[:, :], ps[0:64, 0:128], eng="act")
                            else:
                                P.copy(XN[:, 64:128], ps[0:64, 64:128], eng="act")
                            d["XN"] = XN
                    def rb(lev=lev):
                        for n in ns:
                            d = st[n]
                            XN = d["XN"]
                            ps2 = self.nextps()
                            P.mm(ps2[0:64, 0:64], XN[:, 64:128], d["W"][:, :])
                            if lev < 4:
                                d["wi"] += 1
                                Wn = R["Wt"][n % BS][d["wi"] % 2]
                            else:
                                Wn = R["Wf"][slot(n)]
                            P.tt(Wn[:, :], ps2[0:64, 0:64], d["W"][:, :], ALU.add)
                            d["W"] = Wn
                            d["Xp"] = XN[:, 0:64]; d["Np"] = XN[:, 64:128]
                    rounds.append(ra); rounds.append(rb)
                return rounds

            def chain_hops(ns):
                hops = []
                for n in ns:
                    s4 = slot(n)
                    QA = RQA[s4]; QB = RQB[s4]; TM = RTM[s4]; W = R["Wf"][s4]
                    Rt = RA[0:64, n, 0:64]; At = RA[0:64, n, 64:128]
                    Sb = R["Sb"][n % 2]; Sbn = R["Sb"][(n + 1) % 2]
                    RU = R["RU"][n % 2]
                    def h1(n=n, QA=QA, TM=TM, At=At, Sb=Sb, RU=RU):
                        psA = self.nextps()
                        P.mm(psA[0:64, 0:64], At, Sb[:, :], start=True, stop=False)
                        P.mm(psA[0:64, 0:64], QA[:, 64:128], TM[:, 128:192], start=False, stop=True)
                        P.copy(RU[:, 0:64], psA[0:64, 0:64], eng="act")
                        P.ts(R["S32"][:, :], R["S32"][:, :], R["PC"][:, n:n + 1], ALU.mult)
                    def h2(n=n, W=W, RU=RU):
                        psU = self.nextps()
                        P.mm(psU[0:64, 0:64], W[:, :], RU[:, 0:64])
                        P.copy(RU[:, 64:128], psU[0:64, 0:64], eng="act")
                    def h3(n=n, QA=QA, QB=QB, TM=TM, Rt=Rt, Sb=Sb, Sbn=Sbn, RU=RU):
                        psS = self.nextps()
                        P.mm(psS[0:64, 0:64], TM[:, 0:64], TM[:, 128:192], start=True, stop=False)
                        P.mm(psS[0:64, 0:64], TM[:, 64:128], RU[:, 64:128], start=False, stop=True)
                        P.stt(Sbn[:, :], psS[0:64, 0:64], R["PC"][:, n:n + 1], R["S32"][:, :], ALU.mult, ALU.add)
                        P.stt(R["S32"][:, :], psS[0:64, 0:64], R["PC"][:, n:n + 1], R["S32"][:, :], ALU.mult, ALU.add)
                        if n % 8 == 0:
                            self._psY = self.nextacc()
                        psY = self._psY
                        yc = slice((n % 8) * 64, (n % 8 + 1) * 64)
                        P.mm(psY[0:64, yc], Sb[:, :], Rt, start=True, stop=False)
                        P.mm(psY[0:64, yc], TM[:, 128:192], QA[:, 0:64], start=False, stop=False)
                        P.mm(psY[0:64, yc], RU[:, 64:128], QB[:, 0:64], start=False, stop=True)
                        if n % 8 == 7:
                            post(n // 8, psY)
                    hops += [h1, h2, h3]
                return hops

            def post(tb, psY):
                sl = slice(tb * 512, (tb + 1) * 512)
                Y = self.nt()
                P.copy(Y[0:64, :], psY[0:64, :], eng="act")
                if h == 0:
                    self.debug_dump("rw_scan%d" % tb, Y[0:64, :], [64, 512])
                ysq = self.nb(); ybf = self.nb()
                P.act(ysq[0:64, :], Y[0:64, :], AF.Square)
                P.copy(ybf[0:64, :], Y[0:64, :], eng="pool")
                psm = self.nextps(); psq = self.nextps()
                P.mm(psm[0:64, :], self.blk64_bf[0:64, 0:64], ybf[0:64, :])
                P.mm(psq[0:64, :], self.blk64_bf[0:64, 0:64], ysq[0:64, :])
                m2 = self.nt()
                P.tt(Y[0:64, :], Y[0:64, :], psm[0:64, :], ALU.subtract)
                P.act(m2[0:64, :], psm[0:64, :], AF.Square)
                P.tt(m2[0:64, :], psq[0:64, :], m2[0:64, :], ALU.subtract)
                P.act(m2[0:64, :], m2[0:64, :], AF.Ln, bias=64e-5)
                P.act(m2[0:64, :], m2[0:64, :], AF.Exp, scale=-0.5)
                P.stt(Y[0:64, :], Y[0:64, :], hcol(h, 5), m2[0:64, :], ALU.mult, ALU.mult)
                P.stt(Y[0:64, :], Y[0:64, :], hcol(h, 6), bv[0:64, sl], ALU.add, ALU.add)
                psg = self.nextps()
                P.mm(psg[0:64, :], R["sm"][:, 256 + h * 64:256 + (h + 1) * 64], sgd[:, sl])
                P.tt(self.yT[0:64, 0, sl], Y[0:64, :], psg[0:64, :], ALU.mult)

            nb_ = 32 // BS
            for k in range(nb_ + 1):
                pr = pre_rounds(list(range(k * BS, (k + 1) * BS))) if k < nb_ else []
                ch = chain_hops(list(range((k - 1) * BS, k * BS))) if k >= 1 else []
                i = j = 0
                while i < len(pr) or j < len(ch):
                    if j < len(ch):
                        ch[j](); j += 1
                    for _ in range(2):
                        if i < len(pr):
                            pr[i](); i += 1
            if rstop <= 2:
                continue
            self.debug_dump("y_rwh%d" % h, self.yT[0:64, 0, :], [64, S])
            if "ln1" in self.stages:
                pending_acc = self.acc_out_rw(l, h)
        for th in pending_acc:
            th()
        P.add("pool", lambda e: e.memset(self._dummy.h[:, :], 0.0), [V(None, ring_keys)], [V(None, ("yT", self._dummy.name))])

    def acc_out_rw(self, l, h):
        P = self.P
        if not hasattr(self, "rw_wo"):
            self.rw_wo = P.sb("rw_wo", [64, 1024], BF16)
        t = self.rw_wo
        wv = WV(t, 1, 1024)
        P.dma(V(wv.ap[0:64, :, :], (t.name,)),
              V(self.h_w_out[l][256 + h * 64:256 + (h + 1) * 64, :].rearrange("(k p) c -> p k c", p=64), ("h_w_out",)), eng="pool")
        first = self._first_acc
        self._first_acc = False
        thunks = []
        for tb in range(4):
            for o in range(8):
                def th(tb=tb, o=o):
                    ps = self.nextps()
                    P.mm(ps[:, :], V(wv.ap[0:64, 0, o * 128:(o + 1) * 128], (t.name,)), self.yT[0:64, 0, tb * 512:(tb + 1) * 512])
                    xv = self.xT[:, o, tb * 512:(tb + 1) * 512]
                    if first:
                        P.stt(xv, xv, ALPHA, ps[:, :], ALU.mult, ALU.add)
                    else:
                        P.tt(xv, xv, ps[:, :], ALU.add)
                thunks.append(th)
        return thunks


    def mem_ln(self):
        P = self.P
        self.memnb = P.sb("memnb", [128, 8, NMEM], BF16)
        self.scr_phase(["mem_raw"])
        class _C:
            def __init__(s_, ap, key):
                s_.ap = ap; s_.key = key
            def __getitem__(s_, idx):
                return V(s_.ap[idx], (s_.key,))
        raw = _C(self.scr.h[:, :].bitcast(F32)[:, 0:8 * NMEM].rearrange("p (c t) -> p c t", c=8), "mem_raw")
        P.dma(raw[:, :, :], self.hv(self.h_memT.rearrange("(c p) t -> p c t", p=128), "h_memT"))
        self.layernorm("mem_ln_g", "mem_ln_b", src=raw, dst32=False, dstb=self.memnb, ntok=NMEM)

    def cross_attn(self, l):
        P = self.P
        self.scr_phase(["ca_KT", "ca_V", "ca_oT"])
        scr = self.scr
        class _C:
            def __init__(s_, ap, key):
                s_.ap = ap; s_.key = key
            def __getitem__(s_, idx):
                return V(s_.ap[idx], (s_.key,))
        KT = _C(scr.h[:, 0:2048].rearrange("p (c m) -> p c m", c=8), "ca_KT")
        Vt = _C(scr.h[:, 2048:4096].rearrange("p (b c) -> p b c", b=2), "ca_V")
        oT = _C(scr.h[:, 4096:20480].rearrange("p (c t) -> p c t", c=8), "ca_oT")
        if not hasattr(self, "ones_bf"):
            self.ones_bf = P.sb("ones_bf", [128, 128], BF16)
            P.copy(self.ones_bf[:], self.C("ones"))
        stages = []
        def ld_k(half):
            return self.load_w(self.h_ca_wk[l][:, half * 512:(half + 1) * 512], 8, 512, "h_ca_wk")
        def cp_k(wk, half):
            for oc in range(4):
                ps = self.nextps()
                for kc in range(8):
                    P.mm(ps[:, 0:NMEM], wk[:, kc, oc * 128:(oc + 1) * 128], self.memnb[:, kc, :], start=(kc == 0), stop=(kc == 7))
                P.copy(KT[:, half * 4 + oc, :], ps[:, 0:NMEM], eng="act")
        def ld_v(half):
            return self.load_w(self.h_ca_wv[l][:, half * 512:(half + 1) * 512], 8, 512, "h_ca_wv")
        def cp_v(wv, half):
            for mb in range(2):
                ps = self.nextps()
                for kc in range(8):
                    P.mm(ps[:, :], self.memnb[:, kc, mb * 128:(mb + 1) * 128], wv[:, kc, :], start=(kc == 0), stop=(kc == 7))
                P.copy(Vt[:, mb, half * 512:(half + 1) * 512], ps[:, :], eng="dve")
        def ld_q(h):
            return self.load_w(self.h_ca_wq[l][:, h * 256:(h + 1) * 256], 8, 256, "h_ca_wq")
        def cp_q(wq, h):
            def front(tb):
                self.load_xb(tb)
                qT = [self.nb(), self.nb()]
                for c in range(2):
                    self.proj_fm(wq, c * 128, 128, tb, lambda ps, c=c: P.act(qT[c][:, :], ps[:, :], AF.Copy, scale=1.0 / 16))
                lps = []
                for mb in range(2):
                    ps = self.nextps()
                    for c in range(2):
                        P.mm(ps[:, :], KT[:, h * 2 + c, mb * 128:(mb + 1) * 128], qT[c][:, :], start=(c == 0), stop=(c == 1))
                    lps.append(ps)
                return lps
            def back(tb, lps):
                sl = slice(tb * 512, (tb + 1) * 512)
                PT = []
                for mb in range(2):
                    pt = self.nb()
                    P.act(pt[:, :], lps[mb][:, :], AF.Exp)
                    PT.append(pt)
                den = self.nextps()
                for mb in range(2):
                    P.mm(den[:, :], self.ones_bf[:, :], PT[mb][:, :], start=(mb == 0), stop=(mb == 1))
                rden = self.nt()
                P.recip(rden[:, :], den[:, :])
                for c2 in range(2):
                    ps = self.nextps()
                    for mb in range(2):
                        P.mm(ps[:, :], Vt[:, mb, h * 256 + c2 * 128:h * 256 + (c2 + 1) * 128], PT[mb][:, :], start=(mb == 0), stop=(mb == 1))
                    P.tt(oT[:, h * 2 + c2, sl], ps[:, :], rden[:, :], ALU.mult)
            cur = front(0)
            for tb in range(4):
                sl = slice(tb * 512, (tb + 1) * 512)
                PT = []
                for mb in range(2):
                    pt = self.nb()
                    P.act(pt[:, :], cur[mb][:, :], AF.Exp)
                    PT.append(pt)
                nxt = front(tb + 1) if tb + 1 < 4 else None
                den = self.nextps()
                for mb in range(2):
                    P.mm(den[:, :], self.ones_bf[:, :], PT[mb][:, :], start=(mb == 0), stop=(mb == 1))
                rden = self.nt()
                P.recip(rden[:, :], den[:, :])
                for c2 in range(2):
                    ps = self.nextps()
                    for mb in range(2):
                        P.mm(ps[:, :], Vt[:, mb, h * 256 + c2 * 128:h * 256 + (c2 + 1) * 128], PT[mb][:, :], start=(mb == 0), stop=(mb == 1))
                    P.tt(oT[:, h * 2 + c2, sl], ps[:, :], rden[:, :], ALU.mult)
                cur = nxt
        def ld_o(half):
            return self.load_w(self.h_ca_wo[l][:, half * 512:(half + 1) * 512], 8, 512, "h_ca_wo")
        def cp_o(wo, half):
            for tb in range(4):
                sl = slice(tb * 512, (tb + 1) * 512)
                for oc in range(4):
                    ps = self.nextps()
                    for kc in range(8):
                        P.mm(ps[:, :], wo[:, kc, oc * 128:(oc + 1) * 128], oT[:, kc, sl], start=(kc == 0), stop=(kc == 7))
                    xv = self.xT[:, half * 4 + oc, sl]
                    P.stt(xv, xv, ALPHA, ps[:, :], ALU.mult, ALU.add)
        for half in range(2):
            stages.append((ld_k, cp_k, half))
        for half in range(2):
            stages.append((ld_v, cp_v, half))
        for h in range(4):
            stages.append((ld_q, cp_q, h))
        for half in range(2):
            stages.append((ld_o, cp_o, half))
        cur = stages[0][0](stages[0][2])
        for i, (ld, cp, arg) in enumerate(stages):
            nxt = None
            if i + 1 < len(stages):
                nxt = stages[i + 1][0](stages[i + 1][2])
            cp(cur, arg)
            cur = nxt

    def conv_ffn(self, l):
        P = self.P
        self.scr_phase(["ff_w0", "ff_w1", "ff_w2", "ff_w3", "ff_pr0", "ff_pr1"])
        scr = self.scr
        if not hasattr(self, "ff"):
            self.ff = dict(halo=P.sb("ff_halo", [128, 8, 2], F32))
        y32 = self.yT.h[:, :, :].rearrange("p a b -> p (a b)").bitcast(F32)
        class _H:
            def __init__(s_, i):
                s_.i = i
            def __getitem__(s_, idx):
                p, c = idx
                return V(y32[p, slice(s_.i * 520 + c.start, s_.i * 520 + c.stop)], ("ffh%d" % s_.i,))
        self.ffh = [_H(i) for i in range(3)]
        self.P.add("pool", lambda e: e.memset(self._dummy.h[:, :], 0.0), [V(None, ("yT",))],
                   [V(None, ("ffh0", "ffh1", "ffh2", self._dummy.name))])
        class _WB:
            def __init__(s_, ap, key, kc, cols):
                s_.ap = ap[:, 0:kc * cols].rearrange("p (k c) -> p k c", k=kc); s_.key = key; s_.kc = kc
            def __getitem__(s_, idx):
                return V(s_.ap[idx], (s_.key,))
        bufs = [(self.wb[0].h[:, :], "wb0"), (self.wb[1].h[:, :], "wb1")] + \
               [(scr.h[:, i * 4096:(i + 1) * 4096], "ff_w%d" % i) for i in range(4)]
        prs = [(scr.h[:, 16384 + i * 2048:16384 + (i + 1) * 2048].rearrange("p (j t) -> p j t", j=4), "ff_pr%d" % i) for i in range(2)]
        def loadw(bi, hbm_ap, kc, cols, key):
            ap, k = bufs[bi]
            wv = _WB(ap, k, kc, cols)
            src = hbm_ap.rearrange("(k p) c -> p k c", p=128)
            step = 4 if cols <= 512 else 2
            kk = 0
            while kk < kc:
                k2 = min(kc, kk + step)
                P.dma(V(wv.ap[:, kk:k2, :], (k,)), V(src[:, kk:k2, :], (key,)), eng="pool")
                kk = k2
            return wv
        o_ub, _ = COL_LAYOUT["ffn_up_b"]; o_cb, _ = COL_LAYOUT["ffn_conv_b"]; o_cw, _ = COL_LAYOUT["ffn_conv"]
        colv = lambda o: self.colp[:, o:o + 1]
        npg = 6
        def load_pg(pg):
            nj = 4 if pg < 5 else 2
            b0 = (pg % 2) * 3
            wg = loadw(b0, self.h_ffn_up[l][:, pg * 512:pg * 512 + nj * 128], 8, nj * 128, "h_ffn_up")
            wvv = loadw(b0 + 1, self.h_ffn_up[l][:, DFF + pg * 512:DFF + pg * 512 + nj * 128], 8, nj * 128, "h_ffn_up")
            wd = loadw(b0 + 2, self.h_ffn_down[l][pg * 512:pg * 512 + nj * 128, :], nj, 1024, "h_ffn_down")
            return wg, wvv, wd
        nxt = load_pg(0)
        pend_dn = []
        for pg in range(npg):
            nj = 4 if pg < 5 else 2
            wg, wvv, wd = nxt
            if pg + 1 < npg:
                nxt = load_pg(pg + 1)
            for tb in range(4):
                self.load_xb(tb)
                sl = slice(tb * 512, (tb + 1) * 512)
                prap, prk = prs[(pg * 4 + tb) % 2]
                carry_dn = pend_dn[:]
                del pend_dn[:]
                for jj in range(nj):
                    for _ in range((8 + nj - 1) // nj):
                        if carry_dn:
                            carry_dn.pop(0)()
                    res = []
                    for part, wsrc in enumerate((wg, wvv)):
                        ch = part * 22 + pg * 4 + jj
                        hb = self.nt()
                        hs = part * 4 + jj
                        ps = self.nextps()
                        for kc in range(8):
                            P.mm(ps[:, :], wsrc[:, kc, jj * 128:(jj + 1) * 128], self.cur_xb[:, kc, 0:512], start=(kc == 0), stop=(kc == 7))
                        hbuf = self.ffh[self._ffh_i % 3]; self._ffh_i += 1
                        if tb == 0:
                            P.memset(hbuf[:, 0:2], 0.0, eng="dve")
                        else:
                            P.copy(hbuf[:, 0:2], self.ff["halo"][:, hs, :], eng="dve")
                        P.act(hbuf[:, 2:514], ps[:, :], AF.Identity, bias=colv(o_ub + ch))
                        P.copy(self.ff["halo"][:, hs, :], hbuf[:, 512:514], eng="dve")
                        P.act(hb[:, :], hbuf[:, 0:512], AF.Identity, bias=colv(o_cb + ch), scale=colv(o_cw + ch))
                        P.stt(hb[:, :], hbuf[:, 1:513], colv(o_cw + 44 + ch), hb[:, :], ALU.mult, ALU.add)
                        P.stt(hb[:, :], hbuf[:, 2:514], colv(o_cw + 88 + ch), hb[:, :], ALU.mult, ALU.add)
                        res.append(hb)
                    P.act(res[0][:, :], res[0][:, :], AF.Gelu)
                    P.tt(V(prap[:, jj, :], (prk,)), res[0][:, :], res[1][:, :], ALU.mult)
                for o in range(8):
                    def dn(o=o, nj=nj, wd=wd, prap=prap, prk=prk, sl=sl, pg=pg):
                        ps = self.nextps()
                        for jj in range(nj):
                            P.mm(ps[:, :], wd[:, jj, o * 128:(o + 1) * 128], V(prap[:, jj, :], (prk,)), start=(jj == 0), stop=(jj == nj - 1))
                        xv = self.xT[:, o, sl]
                        if pg == 0:
                            P.stt(xv, xv, ALPHA, ps[:, :], ALU.mult, ALU.add)
                        else:
                            P.tt(xv, xv, ps[:, :], ALU.add)
                    pend_dn.append(dn)
            for dn in pend_dn:
                dn()
            del pend_dn[:]

    def build(self):
        P = self.P
        with ExitStack() as st:
            P.enter(st)
            self.decl()
            self.alloc()
            P.dma(self.consts[:], self.hv(self.h_consts, "h_consts"))
            P.dma(self.colp[:], self.hv(self.h_colp[0], "h_colp"))
            xsrc = self.h_xT.rearrange("(c p) t -> p c t", p=128)
            for tb in range(4):
                sl = slice(tb * 512, (tb + 1) * 512)
                P.dma(self.xT[:, :, sl], self.hv(xsrc[:, :, sl], "h_xT"))
                P.copy(self.xb[:, :, sl], self.xT[:, :, sl], eng=("act" if tb % 2 else "dve"))
            P.copy(self.ident_bf[:], self.C("ident"))
            P.ts(self.ones_s[:], self.C("ones"), 1.0 / 1024, ALU.mult)
            P.copy(self.blk64_bf[:], self.C("blk64"))
            if "ca" in self.stages:
                self.mem_ln()
            for l in range(self.nlayers):
                self.layer(l)
            osrc = self.h_outT.rearrange("(c p) t -> p c t", p=128)
            for tb in range(4):
                sl = slice(tb * 512, (tb + 1) * 512)
                P.dma(self.hv(osrc[:, :, sl], "h_outT"), self.xT[:, :, sl])
            P.finalize()
            P.emit(st)
        return self.nc

    def layer(self, l):
        P = self.P
        if l > 0:
            P.dma(self.colp[:], self.hv(self.h_colp[l], "h_colp"))
        self._first_acc = True
        if l > 0 and "ffn" in self.stages:
            self.P.add("pool", lambda e: e.memset(self._dummy.h[:, :], 0.0), [V(None, ("ffh0", "ffh1", "ffh2"))],
                       [V(None, ("yT", self._dummy.name))])
        for m, (name, fn) in enumerate([("sg", self.mixer_sg), ("rw", self.mixer_rw), ("gla", self.mixer_gla), ("fox", self.mixer_fox)]):
            if name in self.stages and fn is not None:
                fn(l)
                if name == "rw":
                    continue
                self.debug_dump("y_%s%d" % (name, l), self.yT[:], [128, 2, S])
                if "ln1" in self.stages:
                    self.acc_out(self.h_w_out[l][m * 256:(m + 1) * 256, :], self.yT, 2, "h_w_out", self._first_acc)
                    self._first_acc = False
        if "ln1" in self.stages:
            self.layernorm("ln1_g%d" % l, "ln1_b%d" % l)
            self.debug_dump("x1_%d" % l, self.xT[:], [128, 8, S])
        if "ca" in self.stages:
            self.cross_attn(l)
            self.layernorm("ln2_g%d" % l, "ln2_b%d" % l)
            self.debug_dump("x2_%d" % l, self.xT[:], [128, 8, S])
        if "ffn" in self.stages:
            self.conv_ffn(l)
            self.layernorm("ln3_g%d" % l, "ln3_b%d" % l)


_CACHE = {}


def kernel(**inputs):
    inp = {k: np.ascontiguousarray(np.asarray(v, dtype=np.float32)) for k, v in inputs.items()}
    n = 8
    nc = bass.Bass("TRN2", target_bir_lowering=False)
    mk = MK(nc)
    mk.build()
    consts = make_consts()
    colp = make_colp(inp)
    rowp = make_rowp(inp)
    sgwT = np.ascontiguousarray(inp["sg_w"].transpose(0, 3, 1, 2))
    shared = dict(consts=consts, colp=colp, rowp=rowp, sgwT=sgwT,
                  w_in=inp["w_in"], w_out=inp["w_out"], gla_a_up=inp["gla_a_up"],
                  rw_w2=inp["rw_w2"], rw_a2=inp["rw_a2"], rw_g2=inp["rw_g2"],
                  ca_wq=inp["ca_wq"], ca_wk=inp["ca_wk"], ca_wv=inp["ca_wv"], ca_wo=inp["ca_wo"],
                  ffn_up=inp["ffn_up"], ffn_down=inp["ffn_down"])
    maps = []
    for b in range(n):
        m = dict(shared)
        m["xT"] = np.ascontiguousarray(inp["x"][b].T)
        m["memT"] = np.ascontiguousarray(inp["mem"][b].T)
        maps.append(m)
    res = run_bass_kernel_spmd(nc, maps, core_ids=list(range(n)))
    out = np.stack([np.asarray(res.results[b]["outT"]).T for b in range(n)]).astype(np.float32)
    return np.ascontiguousarray(out)
```
